# Optimizing a Trainium2 kernel written in Bass

```python
import jax, jax.numpy as jnp
from jax import lax
import numpy as np

D_MODEL = 1024
BATCH = 4
SEQ = 8192
DEPTH = 4

N_META = 16
ATT_BLOCK = 128
ATT_HEADS = 16
ATT_HEAD_DIM = 64
DN_HEADS = 8
DN_HEAD_DIM = 128
DN_CONV = 4
DN_CHUNK = 64
FFN_DIM = 2816
FFN_CONV = 3
N_ATTN_LAYERS = (DEPTH + 1) // 2
N_DN_LAYERS = DEPTH // 2
ATT_HD = ATT_HEADS * ATT_HEAD_DIM
DN_HD = DN_HEADS * DN_HEAD_DIM
ATT_IN = 4 * ATT_HD + ATT_HEADS
DN_IN = 4 * DN_HD + 2 * DN_HEADS
EPS = 1e-6
NEG = -1e30

kernel_name = "fox_gdn_hybrid_trunk"


def rmsnorm(x, g):
    xf = x.astype(jnp.float32)
    y = xf * lax.rsqrt(jnp.mean(xf * xf, axis=-1, keepdims=True) + EPS)
    return (y * g.astype(jnp.float32)).astype(x.dtype)


def l2norm(x):
    xf = x.astype(jnp.float32)
    return xf * lax.rsqrt(jnp.sum(xf * xf, axis=-1, keepdims=True) + EPS)


def causal_dwconv(x, w):
    K, C = w.shape
    return lax.conv_general_dilated(
        x, w[:, None, :].astype(x.dtype), window_strides=(1,), padding=[(K - 1, 0)],
        dimension_numbers=('NWC', 'WIO', 'NWC'), feature_group_count=C)


def forgetting_attention(h, w_in, b_forget, q_gain, k_gain, w_out):
    B, L, _ = h.shape
    pad = (-L) % ATT_BLOCK
    hp = jnp.pad(h, ((0, 0), (pad, 0), (0, 0)))
    Lp = L + pad
    proj = hp @ w_in
    q, k, v, og, fl = jnp.split(proj, [ATT_HD, 2 * ATT_HD, 3 * ATT_HD, 4 * ATT_HD], axis=-1)
    q = rmsnorm(q.reshape(B, Lp, ATT_HEADS, ATT_HEAD_DIM), q_gain).transpose(0, 2, 1, 3)
    k = rmsnorm(k.reshape(B, Lp, ATT_HEADS, ATT_HEAD_DIM), k_gain).transpose(0, 2, 1, 3)
    v = v.reshape(B, Lp, ATT_HEADS, ATT_HEAD_DIM).transpose(0, 2, 1, 3)
    logf = jax.nn.log_sigmoid(fl.astype(jnp.float32) + b_forget.astype(jnp.float32))
    c = jnp.cumsum(logf, axis=1).transpose(0, 2, 1)
    pos = jnp.arange(Lp)
    n_blk = Lp // ATT_BLOCK
    qb = q.reshape(B, ATT_HEADS, n_blk, ATT_BLOCK, ATT_HEAD_DIM).transpose(2, 0, 1, 3, 4)
    cb = c.reshape(B, ATT_HEADS, n_blk, ATT_BLOCK).transpose(2, 0, 1, 3)
    pb = pos.reshape(n_blk, ATT_BLOCK)
    scale = ATT_HEAD_DIM ** -0.5

    def one_block(args):
        q_i, c_i, p_i = args
        s = jnp.einsum('bhqd,bhkd->bhqk', q_i, k).astype(jnp.float32) * scale
        s = s + c_i[..., :, None] - c[:, :, None, :]
        mask = (pos[None, :] <= p_i[:, None]) & (pos[None, :] >= pad)
        p = jax.nn.softmax(jnp.where(mask, s, NEG), axis=-1)
        return jnp.einsum('bhqk,bhkd->bhqd', p.astype(v.dtype), v)

    o = lax.map(one_block, (qb, cb, pb))
    o = o.transpose(1, 0, 3, 2, 4).reshape(B, Lp, ATT_HD)
    o = o * jax.nn.sigmoid(og)
    return (o @ w_out)[:, pad:]


def gated_deltanet(h, w_in, conv_w, a_log, dt_bias, o_gain, w_out):
    B, L, _ = h.shape
    H, Dh, C = DN_HEADS, DN_HEAD_DIM, DN_CHUNK
    pad = (-L) % C
    hp = jnp.pad(h, ((0, 0), (pad, 0), (0, 0)))
    Lp = L + pad
    N = Lp // C
    proj = hp @ w_in
    qkv, og, b_logit, a_logit = jnp.split(proj, [3 * DN_HD, 4 * DN_HD, 4 * DN_HD + H], axis=-1)
    qkv = jax.nn.silu(causal_dwconv(qkv, conv_w))
    q, k, v = jnp.split(qkv, [DN_HD, 2 * DN_HD], axis=-1)
    q = l2norm(q.reshape(B, Lp, H, Dh)) * (Dh ** -0.5)
    k = l2norm(k.reshape(B, Lp, H, Dh))
    v = v.reshape(B, Lp, H, Dh).astype(jnp.float32)
    beta = jax.nn.sigmoid(b_logit.astype(jnp.float32))
    g = -jnp.exp(a_log.astype(jnp.float32)) * jax.nn.softplus(
        a_logit.astype(jnp.float32) + dt_bias.astype(jnp.float32))

    def chunk(t):
        return t.reshape(B, N, C, H, Dh).transpose(0, 3, 1, 2, 4)
    qc, kc, vc = chunk(q), chunk(k), chunk(v)
    bc = beta.reshape(B, N, C, H).transpose(0, 3, 1, 2)
    gc = jnp.cumsum(g.reshape(B, N, C, H).transpose(0, 3, 1, 2), axis=-1)
    idx = jnp.arange(C)
    strict = idx[:, None] > idx[None, :]
    incl = idx[:, None] >= idx[None, :]
    dec = jnp.exp(jnp.where(incl, gc[..., :, None] - gc[..., None, :], -jnp.inf))
    kk = jnp.einsum('bhnid,bhnjd->bhnij', kc, kc)
    A = jnp.where(strict, kk * dec * bc[..., :, None], 0.0)
    lhs = A + jnp.eye(C, dtype=jnp.float32)
    rhs = jnp.concatenate([kc * (bc * jnp.exp(gc))[..., None], vc * bc[..., None]], axis=-1)
    sol = lax.linalg.triangular_solve(lhs, rhs, left_side=True, lower=True, unit_diagonal=True)
    W, U0 = sol[..., :Dh], sol[..., Dh:]
    qk = jnp.einsum('bhnid,bhnjd->bhnij', qc, kc) * dec
    q_dec = qc * jnp.exp(gc)[..., None]
    k_dec = kc * jnp.exp(gc[..., -1:] - gc)[..., None]
    g_last = jnp.exp(gc[..., -1])

    def to_n(t):
        return jnp.moveaxis(t, 2, 0)

    def step(S, xs):
        W_n, U0_n, qk_n, qd_n, kd_n, gl_n = xs
        U = U0_n - jnp.einsum('bhcd,bhvd->bhcv', W_n, S)
        O = jnp.einsum('bhcd,bhvd->bhcv', qd_n, S) + jnp.einsum('bhij,bhjv->bhiv', qk_n, U)
        S = gl_n[..., None, None] * S + jnp.einsum('bhcv,bhcd->bhvd', U, kd_n)
        return S, O

    S0 = jnp.zeros((B, H, Dh, Dh), jnp.float32)
    _, O = lax.scan(step, S0, (to_n(W), to_n(U0), to_n(qk), to_n(q_dec), to_n(k_dec), to_n(g_last)))
    O = O.transpose(1, 0, 3, 2, 4).reshape(B, Lp, H, Dh)
    O = rmsnorm(O, o_gain) * jax.nn.silu(og.reshape(B, Lp, H, Dh).astype(jnp.float32))
    return (O.reshape(B, Lp, DN_HD).astype(h.dtype) @ w_out)[:, pad:]


def conv_ffn(h, w_up, conv_w, w_down):
    u = causal_dwconv(h @ w_up, conv_w)
    gate, up = jnp.split(u, [FFN_DIM], axis=-1)
    return (jax.nn.gelu(gate, approximate=True) * up) @ w_down


def setup_inputs(seed: int = 0) -> dict:
    key = jax.random.key(seed)
    ks = jax.random.split(key, 24)
    D = D_MODEL
    nrm = lambda k, shape, s: jax.random.normal(k, shape, jnp.float32) * s
    gain = lambda k, shape: 1.0 + 0.1 * jax.random.normal(k, shape, jnp.float32)
    dt = jnp.exp(jax.random.uniform(ks[12], (N_DN_LAYERS, DN_HEADS), jnp.float32,
                                    np.log(1e-3), np.log(1e-1)))
    return {
        "x": nrm(ks[0], (BATCH, SEQ, D), 1.0),
        "meta_tokens": nrm(ks[1], (N_META, D), 1.0),
        "norm_mix_pre": gain(ks[2], (DEPTH, D)),
        "norm_mix_post": gain(ks[3], (DEPTH, D)),
        "norm_ffn_pre": gain(ks[4], (DEPTH, D)),
        "norm_ffn_post": gain(ks[5], (DEPTH, D)),
        "attn_w_in": nrm(ks[6], (N_ATTN_LAYERS, D, ATT_IN), D ** -0.5),
        "attn_b_forget": jax.random.uniform(ks[7], (N_ATTN_LAYERS, ATT_HEADS), jnp.float32, 2.0, 6.0),
        "attn_q_norm": gain(ks[8], (N_ATTN_LAYERS, ATT_HEAD_DIM)),
        "attn_k_norm": gain(ks[9], (N_ATTN_LAYERS, ATT_HEAD_DIM)),
        "attn_w_out": nrm(ks[10], (N_ATTN_LAYERS, ATT_HD, D), ATT_HD ** -0.5),
        "dn_w_in": nrm(ks[11], (N_DN_LAYERS, D, DN_IN), D ** -0.5),
        "dn_conv": nrm(ks[13], (N_DN_LAYERS, DN_CONV, 3 * DN_HD), DN_CONV ** -0.5),
        "dn_a_log": jnp.log(jax.random.uniform(ks[14], (N_DN_LAYERS, DN_HEADS), jnp.float32, 1.0, 16.0)),
        "dn_dt_bias": dt + jnp.log(-jnp.expm1(-dt)),
        "dn_o_norm": gain(ks[15], (N_DN_LAYERS, DN_HEAD_DIM)),
        "dn_w_out": nrm(ks[16], (N_DN_LAYERS, DN_HD, D), DN_HD ** -0.5),
        "ffn_w_up": nrm(ks[17], (DEPTH, D, 2 * FFN_DIM), D ** -0.5),
        "ffn_conv": nrm(ks[18], (DEPTH, FFN_CONV, 2 * FFN_DIM), FFN_CONV ** -0.5),
        "ffn_w_down": nrm(ks[19], (DEPTH, FFN_DIM, D), FFN_DIM ** -0.5),
    }


def reference(x, meta_tokens, norm_mix_pre, norm_mix_post, norm_ffn_pre, norm_ffn_post,
              attn_w_in, attn_b_forget, attn_q_norm, attn_k_norm, attn_w_out,
              dn_w_in, dn_conv, dn_a_log, dn_dt_bias, dn_o_norm, dn_w_out,
              ffn_w_up, ffn_conv, ffn_w_down):
    B = x.shape[0]
    meta = jnp.broadcast_to(meta_tokens[None].astype(x.dtype), (B, N_META, x.shape[-1]))
    h = jnp.concatenate([meta, x], axis=1)
    for i in range(DEPTH):
        j = i // 2
        a = rmsnorm(h, norm_mix_pre[i])
        if i % 2 == 0:
            m = forgetting_attention(a, attn_w_in[j], attn_b_forget[j], attn_q_norm[j],
                                     attn_k_norm[j], attn_w_out[j])
        else:
            m = gated_deltanet(a, dn_w_in[j], dn_conv[j], dn_a_log[j], dn_dt_bias[j],
                               dn_o_norm[j], dn_w_out[j])
        h = h + rmsnorm(m, norm_mix_post[i])
        f = conv_ffn(rmsnorm(h, norm_ffn_pre[i]), ffn_w_up[i], ffn_conv[i], ffn_w_down[i])
        h = h + rmsnorm(f, norm_ffn_post[i])
    return h[:, N_META:]
```

```python
import numpy as np
from contextlib import ExitStack
import concourse.bass as bass
import concourse.mybir as mybir
from concourse.bass_utils import run_bass_kernel_spmd

F32 = mybir.dt.float32
BF16 = mybir.dt.bfloat16
AF = mybir.ActivationFunctionType
ALU = mybir.AluOpType
AX = mybir.AxisListType

D = 1024
NH_A = 16
HD_A = 64
NH_D = 8
HD_D = 128
FFN = 2816
EPS = 1e-6
NPAD = 112
NMETA = 16
NEGBIG = -30000.0


class Buf:
    def __init__(self, t, name, space):
        self.t = t
        self.name = name
        self.space = space
        self.w = {}
        self.r = {}
        self.sem = None

    def __getitem__(self, k):
        return View(self, self.t[k])

    @property
    def ap(self):
        return View(self, self.t[:])


class View:
    def __init__(self, buf, ap):
        self.buf = buf
        self.ap = ap

    def __getitem__(self, k):
        return View(self.buf, self.ap[k])

    def re(self, s, **kw):
        return View(self.buf, self.ap.rearrange(s, **kw))

    def bc(self, shape):
        return View(self.buf, self.ap.to_broadcast(shape))


def _ap(x):
    return x.ap if isinstance(x, View) else x


class KB:
    def __init__(self):
        self.nc = bass.Bass("TRN2", target_bir_lowering=False)
        nc = self.nc
        self.es = ExitStack()
        self.eng = {"pe": nc.tensor, "dve": nc.vector, "act": nc.scalar, "pool": nc.gpsimd, "sp": nc.sync}
        self.cnt = {}
        self.esem = {}
        for e in ("pe", "dve", "act", "pool"):
            self.esem[e] = self.es.enter_context(nc.semaphore("cnt_" + e))
            self.cnt[e] = 0
        self.seen = {e: {} for e in self.eng}
        self.sem_pool = []
        self.live_sems = []
        self.phase_stack = None
        self.phase_bufs = []
        self.uid = 0
        self.dq = 0

    def phase(self):
        self.phase_stack = ExitStack()
        self.phase_bufs = []

    def end_phase(self):
        self.barrier()
        for b in self.phase_bufs:
            if b.sem is not None:
                self.sem_pool.append(b.sem)
                b.sem = None
        self.phase_stack.close()
        self.phase_stack = None
        self.phase_bufs = []

    def carve(self, buf, slices):
        out = []
        for (a, b) in slices:
            sbuf = Buf(buf.t[:, a:b], buf.name + "_%d" % a, buf.space)
            sbuf.w = buf.w
            sbuf.r = buf.r
            self.phase_bufs.append(sbuf)
            out.append(sbuf)
        return out

    def sb(self, name, shape, dt):
        self.uid += 1
        t = self.phase_stack.enter_context(self.nc.sbuf_tensor("%s_%d" % (name, self.uid), list(shape), dt))
        b = Buf(t, name, "sb")
        self.phase_bufs.append(b)
        return b

    def ps(self, name, shape, dt):
        self.uid += 1
        t = self.phase_stack.enter_context(self.nc.psum_tensor("%s_%d" % (name, self.uid), list(shape), dt))
        b = Buf(t, name, "ps")
        self.phase_bufs.append(b)
        return b

    def _getsem(self, b):
        if b.sem is None:
            if self.sem_pool:
                b.sem = self.sem_pool.pop()
            else:
                s = self.es.enter_context(self.nc.semaphore("dsem%d" % len(self.live_sems)))
                b.sem = [s, 0]
                self.live_sems.append(b.sem)
        return b.sem

    def _wait(self, e, deps):
        seen = self.seen[e]
        for sid, (sem, val) in deps.items():
            if seen.get(sid, 0) < val:
                self.eng[e].wait_ge(sem, val)
                seen[sid] = val

    def _deps(self, outs, ins):
        deps = {}

        def add(d):
            for sid, (sem, val) in d.items():
                if sid not in deps or deps[sid][1] < val:
                    deps[sid] = (sem, val)
        for v in ins:
            if isinstance(v, View):
                add(v.buf.w)
        for v in outs:
            if isinstance(v, View):
                add(v.buf.w)
                add(v.buf.r)
        return deps

    def _record(self, outs, ins, sem, val):
        sid = id(sem)
        for v in ins:
            if isinstance(v, View):
                v.buf.r[sid] = (sem, val)
        for v in outs:
            if isinstance(v, View):
                v.buf.w[sid] = (sem, val)

    def op(self, e, fn, outs, ins, pe_acc=False):
        deps = self._deps(outs, ins)
        if e == "pe":
            deps.pop(id(self.esem["pe"]), None)
        self._wait(e, deps)
        ins_ = fn()
        self.cnt[e] += 1
        ins_.then_inc(self.esem[e], 1)
        self._record(outs, ins, self.esem[e], self.cnt[e])
        return ins_

    def dma(self, out, in_, q=None, **kw):
        if q is None:
            q = "sp"
        sbv = out if isinstance(out, View) else in_
        assert isinstance(sbv, View)
        outs = [out] if isinstance(out, View) else []
        ins = [in_] if isinstance(in_, View) else []
        deps = self._deps(outs, ins)
        self._wait(q, deps)
        se = self._getsem(sbv.buf)
        self.eng[q].dma_start(out=_ap(out), in_=_ap(in_), **kw).then_inc(se[0], 16)
        se[1] += 16
        self._record(outs, ins, se[0], se[1])

    def barrier(self):
        allsems = {}
        for e in ("pe", "dve", "act", "pool"):
            allsems[id(self.esem[e])] = (self.esem[e], self.cnt[e])
        for se in self.live_sems:
            allsems[id(se[0])] = (se[0], se[1])
        for e in self.eng:
            self._wait(e, {k: v for k, v in allsems.items() if v[1] > 0})

    def mm(self, out, lhsT, rhs, start=True, stop=True):
        return self.op("pe", lambda: self.nc.tensor.matmul(_ap(out), lhsT=_ap(lhsT), rhs=_ap(rhs), start=start, stop=stop),
                       [out], [lhsT, rhs])

    def tr(self, out, in_, ident):
        return self.op("pe", lambda: self.nc.tensor.transpose(_ap(out), _ap(in_), _ap(ident)), [out], [in_, ident])

    def act(self, out, in_, func, bias=None, scale=None, accum=None, e="act"):
        kw = {}
        ins = [in_]
        outs = [out]
        if bias is not None:
            kw["bias"] = _ap(bias)
            ins.append(bias)
        if scale is not None:
            kw["scale"] = _ap(scale)
            ins.append(scale)
        if accum is not None:
            kw["accum_out"] = _ap(accum)
            outs.append(accum)
        return self.op("act", lambda: self.nc.scalar.activation(out=_ap(out), in_=_ap(in_), func=func, **kw), outs, ins)

    def tt(self, e, out, a, b, op):
        return self.op(e, lambda: self.eng[e].tensor_tensor(out=_ap(out), in0=_ap(a), in1=_ap(b), op=op), [out], [a, b])

    def ts(self, e, out, a, s1, op0, s2=None, op1=None, accum=None):
        kw = {}
        outs = [out]
        if op1 is not None:
            kw["op1"] = op1
        if accum is not None:
            kw["accum_out"] = _ap(accum)
            outs.append(accum)
        return self.op(e, lambda: self.eng[e].tensor_scalar(out=_ap(out), in0=_ap(a), scalar1=_ap(s1), scalar2=_ap(s2), op0=op0, **kw),
                       outs, [a, s1, s2])

    def stt(self, e, out, a, s, b, op0, op1):
        return self.op(e, lambda: self.eng[e].scalar_tensor_tensor(out=_ap(out), in0=_ap(a), scalar=_ap(s), in1=_ap(b), op0=op0, op1=op1),
                       [out], [a, s, b])

    def copy(self, e, out, in_):
        if e == "act":
            return self.op("act", lambda: self.nc.scalar.copy(out=_ap(out), in_=_ap(in_)), [out], [in_])
        return self.op(e, lambda: self.eng[e].tensor_copy(out=_ap(out), in_=_ap(in_)), [out], [in_])

    def amul(self, out, in_, mul):
        return self.op("act", lambda: self.nc.scalar.mul(out=_ap(out), in_=_ap(in_), mul=_ap(mul)), [out], [in_, mul])

    def memset(self, e, out, val):
        return self.op(e, lambda: self.eng[e].memset(_ap(out), val), [out], [])

    def recip(self, out, in_):
        return self.op("dve", lambda: self.nc.vector.reciprocal(out=_ap(out), in_=_ap(in_)), [out], [in_])

    def asel(self, out, in_, pattern, cmp, fill, base, cm):
        return self.op("pool", lambda: self.nc.gpsimd.affine_select(out=_ap(out), in_=_ap(in_), pattern=pattern, compare_op=cmp,
                                                                      fill=fill, base=base, channel_multiplier=cm), [out], [in_])


class Prog:
    def __init__(self, NT, layers=(0, 1, 2, 3), final_out=True):
        self.NT = NT
        self.LP = NT * 128
        self.kb = KB()
        self.nc = self.kb.nc
        self.layers = layers
        nc = self.nc
        LP = self.LP
        S = LP - 128
        shapes = {"xpad": [LP, D], "norm_mix_pre": [4, D], "norm_mix_post": [4, D], "norm_ffn_pre": [4, D], "norm_ffn_post": [4, D],
                  "attn_w_in": [2, D, 4112], "attn_b_forget": [2, 16], "attn_q_norm": [2, 64], "attn_k_norm": [2, 64],
                  "attn_w_out": [2, D, D], "dn_w_in": [2, D, 4112], "dn_conv": [2, 4, 3072], "dn_a_log": [2, 8], "dn_dt_bias": [2, 8],
                  "dn_o_norm": [2, 128], "dn_w_out": [2, D, D], "ffn_w_up": [4, D, 2 * FFN], "ffn_conv": [4, 3, 2 * FFN],
                  "ffn_w_down": [4, FFN, D]}

        class Lazy(dict):
            def __missing__(d, name):
                d[name] = nc.dram_tensor(name, list(shapes[name]), F32, kind="ExternalInput").ap()
                return d[name]
        self.inp = Lazy()
        self.y = nc.dram_tensor("y", [S, D], F32, kind="ExternalOutput").ap()

        def dscr(name, shape, dt):
            return nc.dram_tensor(name, list(shape), dt, kind="Internal").ap()
        self.H = dscr("H", [LP, D], F32)
        self.HT = dscr("HT", [FFN, LP], BF16)
        self.OGT = dscr("OGT", [D, LP], BF16)
        self.SG = dscr("SG", [D, LP], BF16)
        self.QT = dscr("QT", [16, 65, LP], BF16)
        self.KT = dscr("KT", [16, 65, LP], BF16)
        self.VA = dscr("VA", [LP, 16, 65], BF16)
        self.CT = dscr("CT", [128, NT, 16], F32)
        self.QKVD = dscr("QKVD", [24, 128, LP], F32)
        self.SGD = dscr("SGD", [LP, D], BF16)
        self.GBD = dscr("GBD", [128, NT, 16], F32)
        self.CAR = dscr("CAR", [128, NT + 1, 16], F32)

    def consts(self):
        kb = self.kb
        kb.phase()
        es = kb.phase_stack
        self.idf = kb.sb("idf", [128, 128], F32)
        self.idb = kb.sb("idb", [128, 128], BF16)
        kb.memset("pool", self.idf.ap, 1.0)
        kb.asel(self.idf.ap, self.idf.ap, [[-1, 128]], ALU.is_equal, 0.0, 0, 1)
        kb.copy("dve", self.idb.ap, self.idf.ap)
        self.const_stack = es
        kb.phase_stack = None
        kb.phase_bufs = []

    def blocks(self):
        out = []
        t = 0
        while t < self.NT:
            n = min(4, self.NT - t)
            out.append((t, n))
            t += n
        return out

    def load_cols(self, dst, src2d, n, stage, pst):
        kb = self.kb
        kb.dma(stage[0:n, :], src2d)
        kb.tr(pst[:, 0:n], stage[0:n, :], self.idf[0:n, 0:n])
        kb.copy("dve", dst, pst[:, 0:n])

    def load_w(self, dst, W, nch, ncols, gcol, stages, col0=0):
        kb = self.kb
        SEG = 2048
        k = 0
        for c in range(nch):
            for s0 in range(0, ncols, SEG):
                w = min(SEG, ncols - s0)
                st = stages[k % len(stages)]
                e = ("dve", "pool")[k % 2]
                k += 1
                kb.dma(st[:, 0:w], W[c * 128:(c + 1) * 128, s0:s0 + w])
                if gcol is not None:
                    kb.ts(e, dst[:, c, col0 + s0:col0 + s0 + w], st[:, 0:w], gcol[:, c:c + 1], ALU.mult)
                else:
                    kb.copy(e, dst[:, c, col0 + s0:col0 + s0 + w], st[:, 0:w])

    def prenormT(self, ht, aT_dst, abf, ss, pst, junk, k):
        kb = self.kb
        kb.memset("dve", ss[:, 0:1], 0.0)
        kb.act(junk.ap, ht.ap, AF.Square, accum=ss[:, 0:1])
        kb.act(ss[:, 1:2], ss[:, 0:1], AF.Sqrt, bias=EPS, scale=1.0 / D)
        kb.recip(ss[:, 2:3], ss[:, 1:2])
        kb.ts("dve", abf.ap, ht.ap, ss[:, 2:3], ALU.mult)
        for c in range(8):
            kb.tr(pst[:, c * 128:(c + 1) * 128], abf[:, c * 128:(c + 1) * 128], self.idb.ap)
        kb.copy(("act", "dve")[k % 2], aT_dst, pst.ap.re("p (c t) -> p c t", c=8))

    def phase_R(self, srcT, nch, W, gpost, hsrc, final=False):
        kb = self.kb
        kb.phase()
        Wb = kb.sb("Wb", [128, nch, D], BF16)
        stages = [kb.sb("wst", [128, 2048], F32) for _ in range(2)]
        self.load_w(Wb, W, nch, D, None, stages)
        g = kb.sb("g", [128, D], F32)
        kb.dma(g.ap, gpost.partition_broadcast(128))
        Sb = [kb.sb("Sb", [128, nch, 512], BF16) for _ in range(2)]
        hts = [kb.sb("ht", [128, D], F32) for _ in range(3)]
        fn = [kb.sb("fn", [128, D], F32) for _ in range(2)]
        ssb = [kb.sb("ss", [128, 8], F32) for _ in range(2)]
        junk = kb.sb("junk", [128, 512], BF16)
        PS = [[kb.ps("psa", [128, 512], F32), kb.ps("psb", [128, 512], F32)] for _ in range(2)]
        srcv = srcT.rearrange("(c p) t -> p c t", p=128)
        blocks = self.blocks()
        kb.dma(Sb[0][:, :, 0:blocks[0][1] * 128], srcv[:, :, 0:blocks[0][1] * 128])
        ti = 0
        for bi, (t0, n) in enumerate(blocks):
            if bi + 1 < len(blocks):
                t1, n1 = blocks[bi + 1]
                kb.dma(Sb[(bi + 1) % 2][:, :, 0:n1 * 128], srcv[:, :, t1 * 128:(t1 + n1) * 128])
            S = Sb[bi % 2]
            for j in range(n):
                tile = t0 + j
                ht = hts[ti % 3]
                kb.dma(ht.ap, hsrc[tile * 128:(tile + 1) * 128, :])
                pa, pb = PS[ti % 2]
                ss = ssb[ti % 2]
                f = fn[ti % 2]
                for c in range(nch):
                    kb.mm(pa.ap, S[:, c, j * 128:(j + 1) * 128], Wb[:, c, 0:512], start=(c == 0), stop=(c == nch - 1))
                for c in range(nch):
                    kb.mm(pb.ap, S[:, c, j * 128:(j + 1) * 128], Wb[:, c, 512:1024], start=(c == 0), stop=(c == nch - 1))
                kb.memset("dve", ss[:, 0:2], 0.0)
                kb.act(junk.ap, pa.ap, AF.Square, accum=ss[:, 0:1])
                kb.act(junk.ap, pb.ap, AF.Square, accum=ss[:, 1:2])
                kb.tt("dve", ss[:, 2:3], ss[:, 0:1], ss[:, 1:2], ALU.add)
                kb.act(ss[:, 3:4], ss[:, 2:3], AF.Sqrt, bias=EPS, scale=1.0 / D)
                kb.recip(ss[:, 4:5], ss[:, 3:4])
                kb.stt("dve", f[:, 0:512], pa.ap, ss[:, 4:5], g[:, 0:512], ALU.mult, ALU.mult)
                kb.stt("dve", f[:, 512:1024], pb.ap, ss[:, 4:5], g[:, 512:1024], ALU.mult, ALU.mult)
                kb.tt("pool", f.ap, f.ap, ht.ap, ALU.add)
                if final:
                    if tile >= 1:
                        kb.dma(self.y[(tile - 1) * 128:tile * 128, :], f.ap, q="pool")
                else:
                    kb.dma(self.H[tile * 128:(tile + 1) * 128, :], f.ap, q="pool")
                ti += 1
        kb.end_phase()

    def phase_F1(self, l):
        kb = self.kb
        kb.phase()
        Wup = kb.sb("Wup", [128, 8, 2 * FFN], BF16)
        stages = [kb.sb("wst", [128, 2048], F32) for _ in range(2)]
        cst = kb.sb("cst", [128, 128], F32)
        gcol = kb.sb("gcol", [128, 8], F32)
        cw = kb.sb("cw", [128, 3, 44], F32)
        pst = kb.ps("pst", [128, 1024], BF16)
        psc = kb.ps("psc", [128, 128], F32)
        self.load_cols(gcol.ap, self.inp["norm_ffn_pre"][l].rearrange("(c p) -> c p", p=128), 8, cst, psc)
        for k in range(3):
            self.load_cols(cw[:, k, :], self.inp["ffn_conv"][l, k].rearrange("(c p) -> c p", p=128), 44, cst, psc)
        self.load_w(Wup, self.inp["ffn_w_up"][l], 8, 2 * FFN, gcol, stages)
        halo = kb.sb("halo", [128, 44, 2], F32)
        kb.memset("pool", halo.ap, 0.0)
        hts = [kb.sb("ht", [128, D], F32) for _ in range(2)]
        abf = [kb.sb("abf", [128, D], BF16) for _ in range(2)]
        ssb = [kb.sb("ss", [128, 8], F32) for _ in range(2)]
        junk = kb.sb("junk", [128, D], BF16)
        aT = [kb.sb("aT", [128, 8, 512], BF16) for _ in range(2)]
        Hb = [kb.sb("Hb", [128, 22, 512], BF16) for _ in range(2)]
        Ug = [kb.sb("Ug", [128, 514], F32) for _ in range(2)]
        Uu = [kb.sb("Uu", [128, 514], F32) for _ in range(2)]
        yg = [kb.sb("yg", [128, 512], F32) for _ in range(2)]
        yu = [kb.sb("yu", [128, 512], F32) for _ in range(2)]
        gg = [kb.sb("gg", [128, 512], F32) for _ in range(2)]
        ptmp = [kb.sb("ptmp", [128, 512], F32) for _ in range(2)]
        PSg = [kb.ps("psg", [128, 512], F32) for _ in range(2)]
        PSu = [kb.ps("psu", [128, 512], F32) for _ in range(2)]
        HTv = self.HT.rearrange("(c p) t -> p c t", p=128)
        hsrc = self.H
        ti = 0
        it = 0
        for bi, (t0, n) in enumerate(self.blocks()):
            TW = n * 128
            A = aT[bi % 2]
            for j in range(n):
                tile = t0 + j
                ht = hts[ti % 2]
                kb.dma(ht.ap, hsrc[tile * 128:(tile + 1) * 128, :])
                self.prenormT(ht, A[:, :, j * 128:(j + 1) * 128], abf[ti % 2], ssb[ti % 2], pst, junk, ti)
                ti += 1
            HB = Hb[bi % 2]
            for fc in range(22):
                pg = PSg[it % 2]; pu = PSu[it % 2]
                ug = Ug[it % 2]; uu = Uu[it % 2]
                for c in range(8):
                    kb.mm(pg[:, 0:TW], Wup[:, c, fc * 128:(fc + 1) * 128], A[:, c, 0:TW], start=(c == 0), stop=(c == 7))
                for c in range(8):
                    kb.mm(pu[:, 0:TW], Wup[:, c, FFN + fc * 128:FFN + (fc + 1) * 128], A[:, c, 0:TW], start=(c == 0), stop=(c == 7))
                a = yg[it % 2]; b = yu[it % 2]
                for (u_, p_, y_, hc) in ((ug, pg, a, fc), (uu, pu, b, 22 + fc)):
                    kb.copy("act", u_[:, 2:2 + TW], p_[:, 0:TW])
                    kb.amul(y_[:, 0:TW], p_[:, 0:TW], cw[:, 2, hc:hc + 1])
                    kb.copy("dve", u_[:, 0:2], halo[:, hc, :])
                    kb.copy("dve", halo[:, hc, :], u_[:, TW:TW + 2])
                    kb.stt("dve", y_[:, 0:TW], u_[:, 1:1 + TW], cw[:, 1, hc:hc + 1], y_[:, 0:TW], ALU.mult, ALU.add)
                    kb.stt("dve", y_[:, 0:TW], u_[:, 0:TW], cw[:, 0, hc:hc + 1], y_[:, 0:TW], ALU.mult, ALU.add)
                kb.act(gg[it % 2][:, 0:TW], a[:, 0:TW], AF.Gelu_apprx_tanh)
                kb.tt("dve", HB[:, fc, 0:TW], gg[it % 2][:, 0:TW], b[:, 0:TW], ALU.mult)
                it += 1
            kb.dma(HTv[:, :, t0 * 128:t0 * 128 + TW], HB[:, :, 0:TW], q="pool")
        kb.end_phase()

    def phase_init(self):
        kb = self.kb
        kb.phase()
        hts = [kb.sb("ht", [128, D], F32) for _ in range(4)]
        for t in range(self.NT):
            ht = hts[t % 4]
            kb.dma(ht.ap, self.inp["xpad"][t * 128:(t + 1) * 128, :])
            kb.dma(self.H[t * 128:(t + 1) * 128, :], ht.ap, q="pool")
        kb.end_phase()

    def ffn(self, l, final):
        self.phase_F1(l)
        self.phase_R(self.HT, 22, self.inp["ffn_w_down"][l], self.inp["norm_ffn_post"][l], self.H, final=final)

    def finish(self):
        self.const_stack.close()
        self.kb.es.close()


def phase_A1(self, l):
    kb = self.kb
    j = l // 2
    NT = self.NT
    kb.phase()
    Win = kb.sb("Win", [128, 8, 4112], BF16)
    stages = [kb.sb("wst", [128, 2048], F32) for _ in range(2)]
    cst = kb.sb("cst", [128, 128], F32)
    gcol = kb.sb("gcol", [128, 8], F32)
    pst = kb.ps("pst", [128, 1024], BF16)
    PSq = [kb.ps("psq", [128, 512], F32) for _ in range(2)]
    PSs = kb.ps("pss", [128, 512], F32)
    PSv = kb.ps("psv", [128, 1024], F32)
    PSm = kb.ps("psm", [128, 512], F32)
    PSr = kb.ps("psr", [16, 512], BF16)
    self.load_cols(gcol.ap, self.inp["norm_mix_pre"][l].rearrange("(c p) -> c p", p=128), 8, cst, PSs)
    self.load_w(Win, self.inp["attn_w_in"][j], 8, 4112, gcol, stages)
    qg = kb.sb("qg", [128, 2], F32)
    for half in range(2):
        kb.dma(qg[half * 64:(half + 1) * 64, 0:1], self.inp["attn_q_norm"][j].rearrange("(p o) -> p o", o=1))
        kb.dma(qg[half * 64:(half + 1) * 64, 1:2], self.inp["attn_k_norm"][j].rearrange("(p o) -> p o", o=1))
    kb.ts("dve", qg[:, 0:1], qg[:, 0:1], 0.125, ALU.mult)
    bfg = kb.sb("bfg", [128, 16], F32)
    kb.dma(bfg.ap, self.inp["attn_b_forget"][j].partition_broadcast(128))
    BD = kb.sb("BD", [128, 128], BF16)
    kb.memset("pool", BD.ap, 1.0)
    kb.asel(BD[:, 0:64], BD[:, 0:64], [[0, 64]], ALU.is_ge, 0.0, 63, -1)
    kb.asel(BD[:, 64:128], BD[:, 64:128], [[0, 64]], ALU.is_ge, 0.0, -64, 1)
    tri = kb.sb("tri", [128, 128], F32)
    kb.memset("pool", tri.ap, 1.0)
    kb.asel(tri.ap, tri.ap, [[1, 128]], ALU.is_ge, 0.0, 0, -1)
    onesf = kb.sb("onesf", [128, 128], F32)
    kb.memset("pool", onesf.ap, 1.0)
    onesb = kb.sb("onesb", [16, 512], BF16)
    kb.memset("pool", onesb.ap, 1.0)
    Ctok = kb.sb("Ctok", [128, NT, 16], F32)
    CAR = kb.sb("CAR", [128, NT + 1, 16], F32)
    kb.memset("dve", CAR[:, 0, :], 0.0)
    hts = [kb.sb("ht", [128, D], F32) for _ in range(2)]
    abf = [kb.sb("abf", [128, D], BF16) for _ in range(2)]
    ssb = [kb.sb("ss", [128, 8], F32) for _ in range(2)]
    junk = kb.sb("junk", [128, D], BF16)
    aT = [kb.sb("aT", [128, 8, 512], BF16) for _ in range(2)]
    sq = [kb.sb("sq", [128, 512], BF16) for _ in range(2)]
    rms = [kb.sb("rms", [128, 512], F32) for _ in range(2)]
    qn = [kb.sb("qn", [128, 512], BF16) for _ in range(3)]
    Va = [kb.sb("Va", [128, 16, 65], BF16) for _ in range(2)]
    V0 = kb.sb("V0", [128, 16, 65], BF16)
    for v in Va + [V0]:
        kb.memset("pool", v.ap, 1.0)
    kb.memset("pool", V0[0:NPAD, :, 64:65], 0.0)
    xs = [kb.sb("xs", [128, 48], F32) for _ in range(2)]
    rb = [kb.sb("rb", [128, 16], BF16) for _ in range(2)]
    rT = [kb.sb("rT", [16, 512], BF16) for _ in range(2)]
    ti = 0
    it = 0
    for bi, (t0, n) in enumerate(self.blocks()):
        TW = n * 128
        c0 = t0 * 128
        A = aT[bi % 2]
        for jj in range(n):
            tile = t0 + jj
            ht = hts[ti % 2]
            kb.dma(ht.ap, self.H[tile * 128:(tile + 1) * 128, :])
            self.prenormT(ht, A[:, :, jj * 128:(jj + 1) * 128], abf[ti % 2], ssb[ti % 2], pst, junk, ti)
            ti += 1
        for which in range(2):
            dst = self.QT if which == 0 else self.KT
            for ch in range(8):
                pq = PSq[it % 2]
                col = which * 1024 + ch * 128
                for c in range(8):
                    kb.mm(pq[:, 0:TW], Win[:, c, col:col + 128], A[:, c, 0:TW], start=(c == 0), stop=(c == 7))
                s_ = sq[it % 2]
                kb.act(s_[:, 0:TW], pq[:, 0:TW], AF.Square)
                kb.mm(PSs[:, 0:TW], BD.ap, s_[:, 0:TW])
                r_ = rms[it % 2]
                kb.act(r_[:, 0:TW], PSs[:, 0:TW], AF.Sqrt, bias=EPS, scale=1.0 / 64)
                kb.recip(r_[:, 0:TW], r_[:, 0:TW])
                q_ = qn[it % 3]
                kb.stt("dve", q_[:, 0:TW], pq[:, 0:TW], qg[:, which:which + 1], r_[:, 0:TW], ALU.mult, ALU.mult)
                kb.dma(dst[2 * ch, 0:64, c0:c0 + TW], q_[0:64, 0:TW], q="pool")
                kb.dma(dst[2 * ch + 1, 0:64, c0:c0 + TW], q_[64:128, 0:TW], q="pool")
                it += 1
        for ch in range(8):
            pq = PSq[it % 2]
            col = 3072 + ch * 128
            for c in range(8):
                kb.mm(pq[:, 0:TW], Win[:, c, col:col + 128], A[:, c, 0:TW], start=(c == 0), stop=(c == 7))
            q_ = qn[it % 3]
            kb.act(q_[:, 0:TW], pq[:, 0:TW], AF.Sigmoid)
            kb.dma(self.SG[ch * 128:(ch + 1) * 128, c0:c0 + TW], q_[:, 0:TW], q="pool")
            it += 1
        kb.dma(self.KT[:, 64, c0:c0 + TW], onesb[:, 0:TW], q="pool")
        for jj in range(n):
            tile = t0 + jj
            tc_ = slice(jj * 128, (jj + 1) * 128)
            for half in range(2):
                for c in range(8):
                    kb.mm(PSv[:, half * 512:(half + 1) * 512], A[:, c, tc_], Win[:, c, 2048 + half * 512:2048 + (half + 1) * 512],
                          start=(c == 0), stop=(c == 7))
            V = V0 if tile == 0 else Va[tile % 2]
            kb.copy("act", V[:, :, 0:64], PSv.ap.re("p (h d) -> p h d", h=16))
            kb.dma(self.VA[tile * 128:(tile + 1) * 128, :, :], V.ap, q="pool")
            for c in range(8):
                kb.mm(PSm[:, 0:16], A[:, c, tc_], Win[:, c, 4096:4112], start=(c == 0), stop=(c == 7))
            x = xs[tile % 2]
            kb.tt("dve", x[:, 0:16], PSm[:, 0:16], bfg.ap, ALU.add)
            kb.act(x[:, 16:32], x[:, 0:16], AF.Exp, scale=-1.0)
            kb.act(x[:, 32:48], x[:, 16:32], AF.Ln, bias=1.0)
            kb.mm(PSm[:, 16:32], tri.ap, x[:, 32:48])
            kb.mm(PSm[:, 32:48], onesf.ap, x[:, 32:48])
            kb.tt("dve", Ctok[:, tile, :], PSm[:, 16:32], CAR[:, tile, :], ALU.add)
            kb.tt("dve", CAR[:, tile + 1, :], PSm[:, 32:48], CAR[:, tile, :], ALU.add)
        R_ = rT[bi % 2]
        for jj in range(n):
            tile = t0 + jj
            r2 = rb[tile % 2]
            kb.tt("dve", r2.ap, CAR[:, t0 + n, :], Ctok[:, tile, :], ALU.subtract)
            kb.tr(PSr[:, jj * 128:(jj + 1) * 128], r2.ap, self.idb.ap)
        kb.copy("dve", R_[:, 0:TW], PSr[:, 0:TW])
        kb.dma(self.QT[:, 64, c0:c0 + TW], R_[:, 0:TW], q="pool")
    kb.dma(self.CT, Ctok.ap, q="pool")
    kb.dma(self.CAR, CAR.ap, q="pool")
    kb.end_phase()


def phase_A2(self, l):
    kb = self.kb
    NT = self.NT
    LP = self.LP
    kb.phase()
    Ctok = kb.sb("Ctok", [128, NT, 16], F32)
    CAR = kb.sb("CAR", [128, NT + 1, 16], F32)
    kb.dma(Ctok.ap, self.CT)
    kb.dma(CAR.ap, self.CAR)
    masks = []
    for jj in range(4):
        m = kb.sb("mask", [128, 512], BF16)
        kb.memset("pool", m.ap, 0.0)
        kb.asel(m.ap, m.ap, [[1, 512]], ALU.is_ge, NEGBIG, -128 * jj, -1)
        masks.append(m)
    onesf = kb.sb("onesf", [128, 64], F32)
    kb.memset("pool", onesf.ap, 1.0)
    KTh = [kb.sb("KTh", [65, LP], BF16) for _ in range(2)]
    VAh = [kb.sb("VAh", [128, NT, 65], BF16) for _ in range(2)]
    QTb = [kb.sb("QTb", [65, 512], BF16) for _ in range(2)]
    SGb = [kb.sb("SGb", [64, 512], BF16) for _ in range(2)]
    bias = [kb.sb("bias", [128, NT], F32) for _ in range(2)]
    P = [kb.sb("P", [128, 512], BF16) for _ in range(3)]
    oacc = [kb.sb("oacc", [65, 512], F32) for _ in range(2)]
    rden = [kb.sb("rden", [65, 512], F32) for _ in range(2)]
    tmp = [kb.sb("tmp", [64, 512], F32) for _ in range(2)]
    og = [kb.sb("og", [64, 512], BF16) for _ in range(2)]
    PSs = [kb.ps("pss", [128, 512], F32) for _ in range(3)]
    PSo = [kb.ps("pso", [65, 512], F32) for _ in range(2)]
    PSb = [kb.ps("psb", [64, 512], F32) for _ in range(2)]
    VAv = self.VA.rearrange("(t p) h e -> p t h e", p=128)
    blocks = self.blocks()
    it = 0
    ip = 0
    def load_head(hh):
        kb.dma(KTh[hh % 2].ap, self.KT[hh])
        for a in range(0, NT, 16):
            b = min(NT, a + 16)
            kb.dma(VAh[hh % 2][:, a:b, :], VAv[:, a:b, hh, :])
    load_head(0)
    for h in range(16):
        if h + 1 < 16:
            load_head(h + 1)
        K_ = KTh[h % 2]
        V_ = VAh[h % 2]
        for bi, (t0, n) in enumerate(blocks):
            TW = n * 128
            c0 = t0 * 128
            Q_ = QTb[it % 2]
            S_ = SGb[it % 2]
            kb.dma(Q_[:, 0:TW], self.QT[h, :, c0:c0 + TW])
            kb.dma(S_[:, 0:TW], self.SG[h * 64:(h + 1) * 64, c0:c0 + TW])
            nkt = t0 + n
            b_ = bias[it % 2]
            kb.ts("dve", b_[:, 0:nkt], Ctok[:, 0:nkt, h], CAR[:, t0 + n, h:h + 1], ALU.subtract)
            po = PSo[it % 2]
            def emit_s(kt, ipk):
                ps = PSs[ipk % 3]
                diag = kt >= t0
                kb.mm(ps[:, 0:TW], K_[:, kt * 128:(kt + 1) * 128], Q_[:, 0:TW], start=True, stop=not diag)
                if diag:
                    kb.mm(ps[:, 0:TW], self.idb.ap, masks[kt - t0][:, 0:TW], start=False, stop=True)
                p_ = P[ipk % 3]
                kb.act(p_[:, 0:TW], ps[:, 0:TW], AF.Exp, bias=b_[:, kt:kt + 1])
            emit_s(0, ip)
            for kt in range(nkt):
                if kt + 1 < nkt:
                    emit_s(kt + 1, ip + 1)
                kb.mm(po[:, 0:TW], V_[:, kt, :], P[ip % 3][:, 0:TW], start=(kt == 0), stop=(kt == nkt - 1))
                ip += 1
            oa = oacc[it % 2]
            rd = rden[it % 2]
            kb.copy("act", oa[:, 0:TW], po[:, 0:TW])
            kb.ts("dve", rd[64:65, 0:TW], oa[64:65, 0:TW], 1e-30, ALU.max)
            kb.recip(rd[64:65, 0:TW], rd[64:65, 0:TW])
            pb = PSb[it % 2]
            kb.mm(pb[:, 0:TW], onesf[64:65, 0:64], rd[64:65, 0:TW])
            t_ = tmp[it % 2]
            kb.tt("dve", t_[:, 0:TW], oa[0:64, 0:TW], pb[:, 0:TW], ALU.mult)
            o_ = og[it % 2]
            kb.tt("pool", o_[:, 0:TW], t_[:, 0:TW], S_[:, 0:TW], ALU.mult)
            kb.dma(self.OGT[h * 64:(h + 1) * 64, c0:c0 + TW], o_[:, 0:TW], q="pool")
            it += 1
    kb.end_phase()


Prog.phase_A1 = phase_A1
Prog.phase_A2 = phase_A2


def attn_layer(self, l):
    self.phase_A1(l)
    self.phase_A2(l)
    self.phase_R(self.OGT, 8, self.inp["attn_w_out"][l // 2], self.inp["norm_mix_post"][l], self.H)


Prog.attn_layer = attn_layer


def phase_D1(self, l):
    kb = self.kb
    j = l // 2
    NT = self.NT
    kb.phase()
    Win = kb.sb("Win", [128, 8, 4112], BF16)
    stages = [kb.sb("wst", [128, 2048], F32) for _ in range(2)]
    cst = kb.sb("cst", [128, 128], F32)
    gcol = kb.sb("gcol", [128, 8], F32)
    cw = kb.sb("cw", [128, 4, 24], F32)
    pst = kb.ps("pst", [128, 1024], BF16)
    PSq = [kb.ps("psq", [128, 512], F32) for _ in range(2)]
    PSs = kb.ps("pss", [128, 512], F32)
    PSv = kb.ps("psv", [128, 1024], F32)
    PSm = kb.ps("psm", [128, 512], F32)
    self.load_cols(gcol.ap, self.inp["norm_mix_pre"][l].rearrange("(c p) -> c p", p=128), 8, cst, PSs)
    for k in range(4):
        self.load_cols(cw[:, k, :], self.inp["dn_conv"][j, k].rearrange("(c p) -> c p", p=128), 24, cst, PSs)
    self.load_w(Win, self.inp["dn_w_in"][j], 8, 4112, gcol, stages)
    dtb = kb.sb("dtb", [128, 8], F32)
    nA = kb.sb("nA", [128, 8], F32)
    kb.dma(dtb.ap, self.inp["dn_dt_bias"][j].partition_broadcast(128))
    kb.dma(nA.ap, self.inp["dn_a_log"][j].partition_broadcast(128))
    kb.act(nA.ap, nA.ap, AF.Exp)
    kb.ts("dve", nA.ap, nA.ap, -1.0, ALU.mult)
    tri = kb.sb("tri", [128, 128], F32)
    kb.memset("pool", tri.ap, 1.0)
    kb.asel(tri.ap, tri.ap, [[1, 128]], ALU.is_ge, 0.0, 0, -1)
    onesb = kb.sb("onesb", [128, 128], BF16)
    kb.memset("pool", onesb.ap, 1.0)
    halo = kb.sb("halo", [128, 24, 3], F32)
    kb.memset("pool", halo.ap, 0.0)
    GB = kb.sb("GB", [128, NT, 16], F32)
    hts = [kb.sb("ht", [128, D], F32) for _ in range(2)]
    abf = [kb.sb("abf", [128, D], BF16) for _ in range(2)]
    ssb = [kb.sb("ss", [128, 8], F32) for _ in range(2)]
    junk = kb.sb("junk", [128, D], BF16)
    aT = [kb.sb("aT", [128, 8, 512], BF16) for _ in range(2)]
    U = [kb.sb("U", [128, 515], F32) for _ in range(2)]
    yv = [kb.sb("yv", [128, 512], F32) for _ in range(2)]
    z = [kb.sb("z", [128, 512], F32) for _ in range(3)]
    sq = [kb.sb("sq", [128, 512], BF16) for _ in range(2)]
    rms = [kb.sb("rms", [128, 512], F32) for _ in range(2)]
    sg = [kb.sb("sg", [128, D], BF16) for _ in range(2)]
    xs = [kb.sb("xs", [128, 32], F32) for _ in range(2)]
    ti = 0
    it = 0
    for bi, (t0, n) in enumerate(self.blocks()):
        TW = n * 128
        c0 = t0 * 128
        A = aT[bi % 2]
        for jj in range(n):
            tile = t0 + jj
            ht = hts[ti % 2]
            kb.dma(ht.ap, self.H[tile * 128:(tile + 1) * 128, :])
            self.prenormT(ht, A[:, :, jj * 128:(jj + 1) * 128], abf[ti % 2], ssb[ti % 2], pst, junk, ti)
            ti += 1
        for fc in range(24):
            pq = PSq[it % 2]
            for c in range(8):
                kb.mm(pq[:, 0:TW], Win[:, c, fc * 128:(fc + 1) * 128], A[:, c, 0:TW], start=(c == 0), stop=(c == 7))
            u = U[it % 2]
            kb.copy("dve", u[:, 0:3], halo[:, fc, :])
            kb.copy("act", u[:, 3:3 + TW], pq[:, 0:TW])
            kb.copy("dve", halo[:, fc, :], u[:, TW:TW + 3])
            y = yv[it % 2]
            kb.ts("dve", y[:, 0:TW], u[:, 3:3 + TW], cw[:, 3, fc:fc + 1], ALU.mult)
            for k in (2, 1, 0):
                kb.stt("dve", y[:, 0:TW], u[:, k:k + TW], cw[:, k, fc:fc + 1], y[:, 0:TW], ALU.mult, ALU.add)
            z_ = z[it % 3]
            kb.act(z_[:, 0:TW], y[:, 0:TW], AF.Silu)
            if fc < 16:
                s_ = sq[it % 2]
                kb.act(s_[:, 0:TW], z_[:, 0:TW], AF.Square)
                kb.mm(PSs[:, 0:TW], onesb.ap, s_[:, 0:TW])
                r_ = rms[it % 2]
                kb.act(r_[:, 0:TW], PSs[:, 0:TW], AF.Sqrt, bias=EPS)
                kb.recip(r_[:, 0:TW], r_[:, 0:TW])
                if fc < 8:
                    kb.stt("dve", z_[:, 0:TW], z_[:, 0:TW], float(HD_D ** -0.5), r_[:, 0:TW], ALU.mult, ALU.mult)
                else:
                    kb.tt("dve", z_[:, 0:TW], z_[:, 0:TW], r_[:, 0:TW], ALU.mult)
            kb.dma(self.QKVD[fc, :, c0:c0 + TW], z_[:, 0:TW], q="pool")
            it += 1
        for jj in range(n):
            tile = t0 + jj
            tc_ = slice(jj * 128, (jj + 1) * 128)
            for half in range(2):
                for c in range(8):
                    kb.mm(PSv[:, half * 512:(half + 1) * 512], A[:, c, tc_], Win[:, c, 3072 + half * 512:3072 + (half + 1) * 512],
                          start=(c == 0), stop=(c == 7))
            s2 = sg[tile % 2]
            kb.act(s2.ap, PSv.ap, AF.Silu)
            kb.dma(self.SGD[tile * 128:(tile + 1) * 128, :], s2.ap, q="pool")
            for c in range(8):
                kb.mm(PSm[:, 0:16], A[:, c, tc_], Win[:, c, 4096:4112], start=(c == 0), stop=(c == 7))
            x = xs[tile % 2]
            kb.act(GB[:, tile, 0:8], PSm[:, 0:8], AF.Sigmoid)
            kb.tt("dve", x[:, 0:8], PSm[:, 8:16], dtb.ap, ALU.add)
            kb.act(x[:, 8:16], x[:, 0:8], AF.Exp)
            kb.act(x[:, 16:24], x[:, 8:16], AF.Ln, bias=1.0)
            kb.tt("dve", x[:, 24:32], x[:, 16:24], nA.ap, ALU.mult)
            kb.mm(PSm[:, 16:24], tri.ap, x[:, 24:32])
            kb.copy("dve", GB[:, tile, 8:16], PSm[:, 16:24])
    kb.dma(self.GBD, GB.ap, q="pool")
    kb.end_phase()


def phase_D2(self, l, G=2):
    kb = self.kb
    j = l // 2
    NT = self.NT
    kb.phase()
    GB = kb.sb("GB", [128, NT, 16], F32)
    kb.dma(GB.ap, self.GBD)
    onrm = kb.sb("onrm", [128, 128], F32)
    kb.dma(onrm.ap, self.inp["dn_o_norm"][j].partition_broadcast(128))
    onesf = kb.sb("onesf", [128, 128], F32)
    kb.memset("pool", onesf.ap, 1.0)
    idf = self.idf
    NMsT = kb.sb("NMsT", [128, 128], F32)
    kb.memset("pool", NMsT.ap, 0.0)
    kb.asel(NMsT.ap, NMsT.ap, [[1, 128]], ALU.is_gt, NEGBIG, 0, -1)
    NPs = kb.sb("NPs", [128, 128], F32)
    kb.memset("pool", NPs.ap, 0.0)
    kb.asel(NPs.ap, NPs.ap, [[-1, 128]], ALU.is_gt, -NEGBIG, 0, 1)
    BDm = kb.sb("BDm", [128, 128], F32)
    kb.memset("pool", BDm.ap, 1.0)
    kb.asel(BDm[:, 0:64], BDm[:, 0:64], [[0, 64]], ALU.is_ge, 0.0, 63, -1)
    kb.asel(BDm[:, 64:128], BDm[:, 64:128], [[0, 64]], ALU.is_ge, 0.0, -64, 1)
    ST = [kb.sb("ST", [128, 128], F32) for _ in range(8)]
    for s_ in ST:
        kb.memset("dve", s_.ap, 0.0)
    X24 = [kb.sb("X24", [128, 24, 128], F32) for _ in range(2)]
    SGt = [kb.sb("SGt", [128, D], BF16) for _ in range(2)]
    Og = [kb.sb("Og", [128, D], BF16) for _ in range(2)]
    OgT = [kb.sb("OgT", [128, 8, 128], BF16) for _ in range(2)]
    tl = [kb.sb("tl", [128, 32], F32) for _ in range(2)]
    PSot = kb.ps("psot", [128, 1024], BF16)
    sets = []
    names = ["DG", "Gs", "DTs", "DTi", "Dn", "A", "B", "Ad", "Ao", "Bd", "Bo", "P", "Pt", "Xa", "Xb", "Xta", "Xtb",
             "Zs", "R", "QKT", "ktok", "kbg", "kd", "vb", "WT", "U0s", "Us", "qdT", "EGbc", "tmp", "on"]
    for g in range(G):
        d = {}
        for nm in names:
            w = 256 if nm in ("DG", "Gs") else 128
            d[nm] = kb.sb(nm, [128, w], F32)
        d["cols"] = kb.sb("cols", [128, 8], F32)
        bA = kb.ps("bA", [128, 512], F32)
        bC = kb.ps("bC", [128, 512], F32)
        bD = kb.ps("bD", [128, 512], F32)
        d["PSG"], d["PSK"], d["PSQ"] = kb.carve(bA, [(0, 256), (256, 384), (384, 512)])
        d["C0"], d["C1"], d["C2"], d["C3"] = kb.carve(bC, [(0, 128), (128, 256), (256, 384), (384, 512)])
        d["D0"], d["D1"], d["D2"], d["D3"] = kb.carve(bD, [(0, 128), (128, 256), (256, 384), (384, 512)])
        sets.append(d)
    Xv = self.QKVD.rearrange("f d t -> d f t")
    OGTv = self.OGT.rearrange("(c p) t -> p c t", p=128)

    def head(n, hd, d, X, sgt, og, t_):
        qT = X[:, hd, :]
        kT = X[:, 8 + hd, :]
        vT = X[:, 16 + hd, :]
        beta = GB[:, n, hd:hd + 1]
        gc = GB[:, n, 8 + hd:9 + hd]
        ngc = t_[:, hd:hd + 1]
        bg = t_[:, 16 + hd:17 + hd]
        cols = d["cols"]
        import os
        sub = int(os.environ.get("DBG_SUB", "99"))
        kb.ts("dve", d["DG"][:, 0:128], idf.ap, gc, ALU.mult)
        kb.ts("dve", d["DG"][:, 128:256], idf.ap, beta, ALU.mult)
        if sub >= 2:
            kb.mm(d["PSG"].ap, onesf.ap, d["DG"].ap)
        if sub >= 3:
            kb.copy("act", d["Gs"].ap, d["PSG"].ap)
        if sub >= 4:
            kb.mm(d["PSK"].ap, kT, kT)
        if sub >= 5:
            kb.mm(d["PSQ"].ap, kT, qT)
        yield
        G_ = d["Gs"][:, 0:128]
        kb.tt("dve", d["tmp"].ap, G_, NMsT.ap, ALU.add)
        kb.act(d["DTs"].ap, d["tmp"].ap, AF.Exp, bias=ngc)
        kb.tt("pool", d["DTi"].ap, d["DTs"].ap, idf.ap, ALU.add)
        kb.tt("dve", d["Dn"].ap, G_, NPs.ap, ALU.add)
        kb.act(d["Dn"].ap, d["Dn"].ap, AF.Exp, bias=gc, scale=-1.0)
        kb.act(d["EGbc"].ap, G_, AF.Exp)
        kb.act(cols[:, 0:1], gc, AF.Exp, bias=d["Gs"][:, 127:128], scale=-1.0)
        kb.act(cols[:, 1:2], d["Gs"][:, 127:128], AF.Exp)
        yield
        kb.stt("dve", d["A"].ap, d["PSK"].ap, beta, d["Dn"].ap, ALU.mult, ALU.mult)
        kb.tt("dve", d["B"].ap, d["PSK"].ap, d["DTs"].ap, ALU.mult)
        kb.tt("pool", d["B"].ap, d["B"].ap, d["Gs"][:, 128:256], ALU.mult)
        kb.tt("dve", d["QKT"].ap, d["PSQ"].ap, d["DTi"].ap, ALU.mult)
        kb.tt("pool", d["Ad"].ap, d["A"].ap, BDm.ap, ALU.mult)
        kb.tt("pool", d["Ao"].ap, d["A"].ap, d["Ad"].ap, ALU.subtract)
        kb.tt("pool", d["Bd"].ap, d["B"].ap, BDm.ap, ALU.mult)
        kb.tt("pool", d["Bo"].ap, d["B"].ap, d["Bd"].ap, ALU.subtract)
        kb.tt("dve", d["P"].ap, idf.ap, d["Ad"].ap, ALU.subtract)
        kb.tt("dve", d["Pt"].ap, idf.ap, d["Bd"].ap, ALU.subtract)
        yield
        Xc, Xtc = d["Ad"], d["Bd"]
        for k in range(1, 6):
            Xn = d["Xa"] if k % 2 else d["Xb"]
            Xtn = d["Xta"] if k % 2 else d["Xtb"]
            kb.mm(d["C0"].ap, Xtc.ap, Xc.ap)
            if k < 5:
                kb.mm(d["C1"].ap, Xc.ap, Xtc.ap)
            kb.copy("act", Xn.ap, d["C0"].ap)
            if k < 5:
                kb.copy("act", Xtn.ap, d["C1"].ap)
            kb.mm(d["C2"].ap, d["Pt"].ap, Xn.ap)
            kb.mm(d["C3"].ap, Xn.ap, d["Pt"].ap)
            kb.tt("dve", d["P"].ap, d["P"].ap, d["C2"].ap, ALU.add)
            kb.tt("dve", d["Pt"].ap, d["Pt"].ap, d["C3"].ap, ALU.add)
            Xc, Xtc = Xn, Xtn
            yield
        kb.mm(d["C0"].ap, d["Ao"].ap, d["Pt"].ap)
        kb.copy("act", d["Zs"].ap, d["C0"].ap)
        kb.mm(d["C1"].ap, d["P"].ap, d["Zs"].ap)
        kb.tt("dve", d["R"].ap, d["Pt"].ap, d["C1"].ap, ALU.subtract)
        kb.tr(d["C2"].ap, kT, idf.ap)
        kb.tr(d["C3"].ap, vT, idf.ap)
        kb.ts("dve", d["kbg"].ap, d["C2"].ap, bg, ALU.mult)
        kb.ts("dve", d["kd"].ap, d["C2"].ap, cols[:, 0:1], ALU.mult)
        kb.ts("dve", d["vb"].ap, d["C3"].ap, beta, ALU.mult)
        kb.tt("pool", d["qdT"].ap, qT, d["EGbc"].ap, ALU.mult)
        yield
        kb.mm(d["D0"].ap, d["kbg"].ap, d["R"].ap)
        kb.mm(d["D1"].ap, d["R"].ap, d["vb"].ap)
        kb.copy("act", d["WT"].ap, d["D0"].ap)
        kb.copy("act", d["U0s"].ap, d["D1"].ap)
        yield
        S_ = ST[hd]
        kb.mm(d["D2"].ap, d["WT"].ap, S_.ap)
        kb.tt("dve", d["Us"].ap, d["U0s"].ap, d["D2"].ap, ALU.subtract)
        kb.mm(d["D3"].ap, d["qdT"].ap, S_.ap, start=True, stop=False)
        kb.mm(d["D3"].ap, d["QKT"].ap, d["Us"].ap, start=False, stop=True)
        kb.mm(d["C0"].ap, d["kd"].ap, d["Us"].ap)
        kb.stt("dve", S_.ap, S_.ap, cols[:, 1:2], d["C0"].ap, ALU.mult, ALU.add)
        yield
        kb.memset("dve", cols[:, 2:3], 0.0)
        kb.act(d["tmp"].ap, d["D3"].ap, AF.Square, accum=cols[:, 2:3])
        kb.act(cols[:, 3:4], cols[:, 2:3], AF.Sqrt, bias=EPS, scale=1.0 / HD_D)
        kb.recip(cols[:, 4:5], cols[:, 3:4])
        kb.stt("dve", d["on"].ap, d["D3"].ap, cols[:, 4:5], onrm.ap, ALU.mult, ALU.mult)
        kb.tt("pool", og[:, hd * 128:(hd + 1) * 128], d["on"].ap, sgt[:, hd * 128:(hd + 1) * 128], ALU.mult)
        yield

    kb.dma(X24[0].ap, Xv[:, :, 0:128])
    for n in range(NT):
        if n + 1 < NT:
            kb.dma(X24[(n + 1) % 2].ap, Xv[:, :, (n + 1) * 128:(n + 2) * 128])
        X = X24[n % 2]
        sgt = SGt[n % 2]
        kb.dma(sgt.ap, self.SGD[n * 128:(n + 1) * 128, :])
        t_ = tl[n % 2]
        kb.ts("dve", t_[:, 0:8], GB[:, n, 8:16], -1.0, ALU.mult)
        kb.act(t_[:, 8:16], GB[:, n, 8:16], AF.Exp)
        kb.tt("dve", t_[:, 16:24], t_[:, 8:16], GB[:, n, 0:8], ALU.mult)
        og = Og[n % 2]
        for h0 in range(0, 8, G):
            gens = [head(n, h0 + g, sets[g], X, sgt, og, t_) for g in range(G)]
            alive = list(gens)
            import os
            nst = int(os.environ.get("DBG_STAGE", "99"))
            stg = 0
            while alive and stg < nst:
                stg += 1
                nxt = []
                for g_ in alive:
                    try:
                        next(g_)
                        nxt.append(g_)
                    except StopIteration:
                        pass
                alive = nxt
        for c in range(8):
            kb.tr(PSot[:, c * 128:(c + 1) * 128], og[:, c * 128:(c + 1) * 128], self.idb.ap)
        ot = OgT[n % 2]
        kb.copy("act", ot.ap, PSot.ap.re("p (c t) -> p c t", c=8))
        kb.dma(OGTv[:, :, n * 128:(n + 1) * 128], ot.ap, q="pool")
    kb.end_phase()


Prog.phase_D1 = phase_D1
Prog.phase_D2 = phase_D2


def dn_layer(self, l):
    self.phase_D1(l)
    self.phase_D2(l)
    self.phase_R(self.OGT, 8, self.inp["dn_w_out"][l // 2], self.inp["norm_mix_post"][l], self.H)


Prog.dn_layer = dn_layer


_SEQ = 8192
_NT = (_SEQ + NMETA + NPAD) // 128


def build_full(NT):
    p = Prog(NT)
    p.consts()
    p.phase_init()
    for l in range(4):
        if l % 2 == 0:
            p.attn_layer(l)
        else:
            p.dn_layer(l)
        p.ffn(l, final=(l == 3))
    p.finish()
    return p


def kernel(**inputs):
    x = np.asarray(inputs["x"], dtype=np.float32)
    B = x.shape[0]
    meta = np.asarray(inputs["meta_tokens"], dtype=np.float32)
    p = build_full(_NT)
    shared = {k: np.ascontiguousarray(np.asarray(inputs[k], dtype=np.float32)) for k in p.inp if k != "xpad"}
    in_maps = []
    for c in range(8):
        b = c % B
        xpad = np.concatenate([np.zeros((NPAD, D), np.float32), meta, x[b]], axis=0)
        m = dict(shared)
        m["xpad"] = np.ascontiguousarray(xpad)
        in_maps.append(m)
    res = run_bass_kernel_spmd(p.nc, in_maps, core_ids=list(range(8)))
    out = np.stack([np.asarray(res.results[b]["y"], dtype=np.float32) for b in range(B)], axis=0)
    return out
```

```python
import numpy as np
from contextlib import ExitStack
import concourse.bass as bass
import concourse.mybir as mybir
from concourse.bass_utils import run_bass_kernel_spmd

F32 = mybir.dt.float32
BF16 = mybir.dt.bfloat16
AF = mybir.ActivationFunctionType
ALU = mybir.AluOpType
AX = mybir.AxisListType

D = 1024
NH_A = 16
HD_A = 64
NH_D = 8
HD_D = 128
FFN = 2816
EPS = 1e-6
NPAD = 112
NMETA = 16
NEGBIG = -30000.0


class Buf:
    def __init__(self, t, name, space):
        self.t = t
        self.name = name
        self.space = space
        self.w = {}
        self.r = {}
        self.sem = None

    def __getitem__(self, k):
        return View(self, self.t[k])

    @property
    def ap(self):
        return View(self, self.t[:])


class View:
    def __init__(self, buf, ap):
        self.buf = buf
        self.ap = ap

    def __getitem__(self, k):
        return View(self.buf, self.ap[k])

    def re(self, s, **kw):
        return View(self.buf, self.ap.rearrange(s, **kw))

    def bc(self, shape):
        return View(self.buf, self.ap.to_broadcast(shape))


def _ap(x):
    return x.ap if isinstance(x, View) else x


class KB:
    def __init__(self):
        self.nc = bass.Bass("TRN2", target_bir_lowering=False)
        nc = self.nc
        self.es = ExitStack()
        self.eng = {"pe": nc.tensor, "dve": nc.vector, "act": nc.scalar, "pool": nc.gpsimd, "sp": nc.sync}
        self.cnt = {}
        self.esem = {}
        for e in ("pe", "dve", "act", "pool"):
            self.esem[e] = self.es.enter_context(nc.semaphore("cnt_" + e))
            self.cnt[e] = 0
        self.seen = {e: {} for e in self.eng}
        self.sem_pool = []
        self.live_sems = []
        self.phase_stack = None
        self.phase_bufs = []
        self.uid = 0
        self.dq = 0

    def phase(self):
        self.phase_stack = ExitStack()
        self.phase_bufs = []

    def end_phase(self):
        self.barrier()
        for b in self.phase_bufs:
            if b.sem is not None:
                self.sem_pool.append(b.sem)
                b.sem = None
        self.phase_stack.close()
        self.phase_stack = None
        self.phase_bufs = []

    def carve(self, buf, slices):
        out = []
        for (a, b) in slices:
            sbuf = Buf(buf.t[:, a:b], buf.name + "_%d" % a, buf.space)
            sbuf.w = buf.w
            sbuf.r = buf.r
            self.phase_bufs.append(sbuf)
            out.append(sbuf)
        return out

    def sb(self, name, shape, dt):
        self.uid += 1
        t = self.phase_stack.enter_context(self.nc.sbuf_tensor("%s_%d" % (name, self.uid), list(shape), dt))
        b = Buf(t, name, "sb")
        self.phase_bufs.append(b)
        return b

    def ps(self, name, shape, dt):
        self.uid += 1
        t = self.phase_stack.enter_context(self.nc.psum_tensor("%s_%d" % (name, self.uid), list(shape), dt))
        b = Buf(t, name, "ps")
        self.phase_bufs.append(b)
        return b

    def _getsem(self, b):
        if b.sem is None:
            if self.sem_pool:
                b.sem = self.sem_pool.pop()
            else:
                s = self.es.enter_context(self.nc.semaphore("dsem%d" % len(self.live_sems)))
                b.sem = [s, 0]
                self.live_sems.append(b.sem)
        return b.sem

    def _wait(self, e, deps):
        seen = self.seen[e]
        for sid, (sem, val) in deps.items():
            if seen.get(sid, 0) < val:
                self.eng[e].wait_ge(sem, val)
                seen[sid] = val

    def _deps(self, outs, ins):
        deps = {}

        def add(d):
            for sid, (sem, val) in d.items():
                if sid not in deps or deps[sid][1] < val:
                    deps[sid] = (sem, val)
        for v in ins:
            if isinstance(v, View):
                add(v.buf.w)
        for v in outs:
            if isinstance(v, View):
                add(v.buf.w)
                add(v.buf.r)
        return deps

    def _record(self, outs, ins, sem, val):
        sid = id(sem)
        for v in ins:
            if isinstance(v, View):
                v.buf.r[sid] = (sem, val)
        for v in outs:
            if isinstance(v, View):
                v.buf.w[sid] = (sem, val)

    def op(self, e, fn, outs, ins, pe_acc=False):
        deps = self._deps(outs, ins)
        if e == "pe":
            deps.pop(id(self.esem["pe"]), None)
        self._wait(e, deps)
        ins_ = fn()
        self.cnt[e] += 1
        ins_.then_inc(self.esem[e], 1)
        self._record(outs, ins, self.esem[e], self.cnt[e])
        return ins_

    def dma(self, out, in_, q=None, **kw):
        if q is None:
            q = "sp"
        sbv = out if isinstance(out, View) else in_
        assert isinstance(sbv, View)
        outs = [out] if isinstance(out, View) else []
        ins = [in_] if isinstance(in_, View) else []
        deps = self._deps(outs, ins)
        self._wait(q, deps)
        se = self._getsem(sbv.buf)
        self.eng[q].dma_start(out=_ap(out), in_=_ap(in_), **kw).then_inc(se[0], 16)
        se[1] += 16
        self._record(outs, ins, se[0], se[1])

    def barrier(self):
        allsems = {}
        for e in ("pe", "dve", "act", "pool"):
            allsems[id(self.esem[e])] = (self.esem[e], self.cnt[e])
        for se in self.live_sems:
            allsems[id(se[0])] = (se[0], se[1])
        for e in self.eng:
            self._wait(e, {k: v for k, v in allsems.items() if v[1] > 0})

    def mm(self, out, lhsT, rhs, start=True, stop=True):
        return self.op("pe", lambda: self.nc.tensor.matmul(_ap(out), lhsT=_ap(lhsT), rhs=_ap(rhs), start=start, stop=stop),
                       [out], [lhsT, rhs])

    def tr(self, out, in_, ident):
        return self.op("pe", lambda: self.nc.tensor.transpose(_ap(out), _ap(in_), _ap(ident)), [out], [in_, ident])

    def act(self, out, in_, func, bias=None, scale=None, accum=None, e="act"):
        kw = {}
        ins = [in_]
        outs = [out]
        if bias is not None:
            kw["bias"] = _ap(bias)
            ins.append(bias)
        if scale is not None:
            kw["scale"] = _ap(scale)
            ins.append(scale)
        if accum is not None:
            kw["accum_out"] = _ap(accum)
            outs.append(accum)
        return self.op("act", lambda: self.nc.scalar.activation(out=_ap(out), in_=_ap(in_), func=func, **kw), outs, ins)

    def tt(self, e, out, a, b, op):
        return self.op(e, lambda: self.eng[e].tensor_tensor(out=_ap(out), in0=_ap(a), in1=_ap(b), op=op), [out], [a, b])

    def ts(self, e, out, a, s1, op0, s2=None, op1=None, accum=None):
        kw = {}
        outs = [out]
        if op1 is not None:
            kw["op1"] = op1
        if accum is not None:
            kw["accum_out"] = _ap(accum)
            outs.append(accum)
        return self.op(e, lambda: self.eng[e].tensor_scalar(out=_ap(out), in0=_ap(a), scalar1=_ap(s1), scalar2=_ap(s2), op0=op0, **kw),
                       outs, [a, s1, s2])

    def stt(self, e, out, a, s, b, op0, op1):
        return self.op(e, lambda: self.eng[e].scalar_tensor_tensor(out=_ap(out), in0=_ap(a), scalar=_ap(s), in1=_ap(b), op0=op0, op1=op1),
                       [out], [a, s, b])

    def copy(self, e, out, in_):
        if e == "act":
            return self.op("act", lambda: self.nc.scalar.copy(out=_ap(out), in_=_ap(in_)), [out], [in_])
        return self.op(e, lambda: self.eng[e].tensor_copy(out=_ap(out), in_=_ap(in_)), [out], [in_])

    def amul(self, out, in_, mul):
        return self.op("act", lambda: self.nc.scalar.mul(out=_ap(out), in_=_ap(in_), mul=_ap(mul)), [out], [in_, mul])

    def memset(self, e, out, val):
        return self.op(e, lambda: self.eng[e].memset(_ap(out), val), [out], [])

    def recip(self, out, in_):
        return self.op("dve", lambda: self.nc.vector.reciprocal(out=_ap(out), in_=_ap(in_)), [out], [in_])

    def asel(self, out, in_, pattern, cmp, fill, base, cm):
        return self.op("pool", lambda: self.nc.gpsimd.affine_select(out=_ap(out), in_=_ap(in_), pattern=pattern, compare_op=cmp,
                                                                      fill=fill, base=base, channel_multiplier=cm), [out], [in_])


class Prog:
    def __init__(self, NT, layers=(0, 1, 2, 3), final_out=True):
        self.NT = NT
        self.LP = NT * 128
        self.kb = KB()
        self.nc = self.kb.nc
        self.layers = layers
        nc = self.nc
        LP = self.LP
        S = LP - 128
        shapes = {"xpad": [LP, D], "norm_mix_pre": [4, D], "norm_mix_post": [4, D], "norm_ffn_pre": [4, D], "norm_ffn_post": [4, D],
                  "attn_w_in": [2, D, 4112], "attn_b_forget": [2, 16], "attn_q_norm": [2, 64], "attn_k_norm": [2, 64],
                  "attn_w_out": [2, D, D], "dn_w_in": [2, D, 4112], "dn_conv": [2, 4, 3072], "dn_a_log": [2, 8], "dn_dt_bias": [2, 8],
                  "dn_o_norm": [2, 128], "dn_w_out": [2, D, D], "ffn_w_up": [4, D, 2 * FFN], "ffn_conv": [4, 3, 2 * FFN],
                  "ffn_w_down": [4, FFN, D]}

        class Lazy(dict):
            def __missing__(d, name):
                d[name] = nc.dram_tensor(name, list(shapes[name]), F32, kind="ExternalInput").ap()
                return d[name]
        self.inp = Lazy()
        self.y = nc.dram_tensor("y", [S, D], F32, kind="ExternalOutput").ap()

        def dscr(name, shape, dt):
            return nc.dram_tensor(name, list(shape), dt, kind="Internal").ap()
        self.H = dscr("H", [LP, D], F32)
        self.HT = dscr("HT", [FFN, LP], BF16)
        self.OGT = dscr("OGT", [D, LP], BF16)
        self.SG = dscr("SG", [D, LP], BF16)
        self.QT = dscr("QT", [16, 65, LP], BF16)
        self.KT = dscr("KT", [16, 65, LP], BF16)
        self.VA = dscr("VA", [LP, 16, 65], BF16)
        self.CT = dscr("CT", [128, NT, 16], F32)
        self.QKVD = dscr("QKVD", [24, 128, LP], F32)
        self.SGD = dscr("SGD", [LP, D], BF16)
        self.GBD = dscr("GBD", [128, NT, 16], F32)
        self.CAR = dscr("CAR", [128, NT + 1, 16], F32)

    def consts(self):
        kb = self.kb
        kb.phase()
        es = kb.phase_stack
        self.idf = kb.sb("idf", [128, 128], F32)
        self.idb = kb.sb("idb", [128, 128], BF16)
        kb.memset("pool", self.idf.ap, 1.0)
        kb.asel(self.idf.ap, self.idf.ap, [[-1, 128]], ALU.is_equal, 0.0, 0, 1)
        kb.copy("dve", self.idb.ap, self.idf.ap)
        self.const_stack = es
        kb.phase_stack = None
        kb.phase_bufs = []

    def blocks(self):
        out = []
        t = 0
        while t < self.NT:
            n = min(4, self.NT - t)
            out.append((t, n))
            t += n
        return out

    def load_cols(self, dst, src2d, n, stage, pst):
        kb = self.kb
        kb.dma(stage[0:n, :], src2d)
        kb.tr(pst[:, 0:n], stage[0:n, :], self.idf[0:n, 0:n])
        kb.copy("dve", dst, pst[:, 0:n])

    def load_w(self, dst, W, nch, ncols, gcol, stages, col0=0):
        kb = self.kb
        SEG = 2048
        k = 0
        for c in range(nch):
            for s0 in range(0, ncols, SEG):
                w = min(SEG, ncols - s0)
                st = stages[k % len(stages)]
                e = ("dve", "pool")[k % 2]
                k += 1
                kb.dma(st[:, 0:w], W[c * 128:(c + 1) * 128, s0:s0 + w])
                if gcol is not None:
                    kb.ts(e, dst[:, c, col0 + s0:col0 + s0 + w], st[:, 0:w], gcol[:, c:c + 1], ALU.mult)
                else:
                    kb.copy(e, dst[:, c, col0 + s0:col0 + s0 + w], st[:, 0:w])

    def prenormT(self, ht, aT_dst, abf, ss, pst, junk, k):
        kb = self.kb
        kb.memset("dve", ss[:, 0:1], 0.0)
        kb.act(junk.ap, ht.ap, AF.Square, accum=ss[:, 0:1])
        kb.act(ss[:, 1:2], ss[:, 0:1], AF.Sqrt, bias=EPS, scale=1.0 / D)
        kb.recip(ss[:, 2:3], ss[:, 1:2])
        kb.ts("dve", abf.ap, ht.ap, ss[:, 2:3], ALU.mult)
        for c in range(8):
            kb.tr(pst[:, c * 128:(c + 1) * 128], abf[:, c * 128:(c + 1) * 128], self.idb.ap)
        kb.copy(("act", "dve")[k % 2], aT_dst, pst.ap.re("p (c t) -> p c t", c=8))

    def phase_R(self, srcT, nch, W, gpost, hsrc, final=False):
        kb = self.kb
        kb.phase()
        Wb = kb.sb("Wb", [128, nch, D], BF16)
        stages = [kb.sb("wst", [128, 2048], F32) for _ in range(2)]
        self.load_w(Wb, W, nch, D, None, stages)
        g = kb.sb("g", [128, D], F32)
        kb.dma(g.ap, gpost.partition_broadcast(128))
        Sb = [kb.sb("Sb", [128, nch, 512], BF16) for _ in range(2)]
        hts = [kb.sb("ht", [128, D], F32) for _ in range(3)]
        fn = [kb.sb("fn", [128, D], F32) for _ in range(2)]
        ssb = [kb.sb("ss", [128, 8], F32) for _ in range(2)]
        junk = kb.sb("junk", [128, 512], BF16)
        PS = [[kb.ps("psa", [128, 512], F32), kb.ps("psb", [128, 512], F32)] for _ in range(2)]
        srcv = srcT.rearrange("(c p) t -> p c t", p=128)
        blocks = self.blocks()
        kb.dma(Sb[0][:, :, 0:blocks[0][1] * 128], srcv[:, :, 0:blocks[0][1] * 128])
        ti = 0
        for bi, (t0, n) in enumerate(blocks):
            if bi + 1 < len(blocks):
                t1, n1 = blocks[bi + 1]
                kb.dma(Sb[(bi + 1) % 2][:, :, 0:n1 * 128], srcv[:, :, t1 * 128:(t1 + n1) * 128])
            S = Sb[bi % 2]
            for j in range(n):
                tile = t0 + j
                ht = hts[ti % 3]
                kb.dma(ht.ap, hsrc[tile * 128:(tile + 1) * 128, :])
                pa, pb = PS[ti % 2]
                ss = ssb[ti % 2]
                f = fn[ti % 2]
                for c in range(nch):
                    kb.mm(pa.ap, S[:, c, j * 128:(j + 1) * 128], Wb[:, c, 0:512], start=(c == 0), stop=(c == nch - 1))
                for c in range(nch):
                    kb.mm(pb.ap, S[:, c, j * 128:(j + 1) * 128], Wb[:, c, 512:1024], start=(c == 0), stop=(c == nch - 1))
                kb.memset("dve", ss[:, 0:2], 0.0)
                kb.act(junk.ap, pa.ap, AF.Square, accum=ss[:, 0:1])
                kb.act(junk.ap, pb.ap, AF.Square, accum=ss[:, 1:2])
                kb.tt("dve", ss[:, 2:3], ss[:, 0:1], ss[:, 1:2], ALU.add)
                kb.act(ss[:, 3:4], ss[:, 2:3], AF.Sqrt, bias=EPS, scale=1.0 / D)
                kb.recip(ss[:, 4:5], ss[:, 3:4])
                kb.stt("dve", f[:, 0:512], pa.ap, ss[:, 4:5], g[:, 0:512], ALU.mult, ALU.mult)
                kb.stt("dve", f[:, 512:1024], pb.ap, ss[:, 4:5], g[:, 512:1024], ALU.mult, ALU.mult)
                kb.tt("pool", f.ap, f.ap, ht.ap, ALU.add)
                if final:
                    if tile >= 1:
                        kb.dma(self.y[(tile - 1) * 128:tile * 128, :], f.ap, q="pool")
                else:
                    kb.dma(self.H[tile * 128:(tile + 1) * 128, :], f.ap, q="pool")
                ti += 1
        kb.end_phase()

    def phase_F1(self, l):
        kb = self.kb
        kb.phase()
        Wup = kb.sb("Wup", [128, 8, 2 * FFN], BF16)
        stages = [kb.sb("wst", [128, 2048], F32) for _ in range(2)]
        cst = kb.sb("cst", [128, 128], F32)
        gcol = kb.sb("gcol", [128, 8], F32)
        cw = kb.sb("cw", [128, 3, 44], F32)
        pst = kb.ps("pst", [128, 1024], BF16)
        psc = kb.ps("psc", [128, 128], F32)
        self.load_cols(gcol.ap, self.inp["norm_ffn_pre"][l].rearrange("(c p) -> c p", p=128), 8, cst, psc)
        for k in range(3):
            self.load_cols(cw[:, k, :], self.inp["ffn_conv"][l, k].rearrange("(c p) -> c p", p=128), 44, cst, psc)
        self.load_w(Wup, self.inp["ffn_w_up"][l], 8, 2 * FFN, gcol, stages)
        halo = kb.sb("halo", [128, 44, 2], F32)
        kb.memset("pool", halo.ap, 0.0)
        hts = [kb.sb("ht", [128, D], F32) for _ in range(2)]
        abf = [kb.sb("abf", [128, D], BF16) for _ in range(2)]
        ssb = [kb.sb("ss", [128, 8], F32) for _ in range(2)]
        junk = kb.sb("junk", [128, D], BF16)
        aT = [kb.sb("aT", [128, 8, 512], BF16) for _ in range(2)]
        Hb = [kb.sb("Hb", [128, 22, 512], BF16) for _ in range(2)]
        Ug = [kb.sb("Ug", [128, 514], F32) for _ in range(2)]
        Uu = [kb.sb("Uu", [128, 514], F32) for _ in range(2)]
        yg = [kb.sb("yg", [128, 512], F32) for _ in range(2)]
        yu = [kb.sb("yu", [128, 512], F32) for _ in range(2)]
        gg = [kb.sb("gg", [128, 512], F32) for _ in range(2)]
        ptmp = [kb.sb("ptmp", [128, 512], F32) for _ in range(2)]
        PSg = [kb.ps("psg", [128, 512], F32) for _ in range(2)]
        PSu = [kb.ps("psu", [128, 512], F32) for _ in range(2)]
        HTv = self.HT.rearrange("(c p) t -> p c t", p=128)
        hsrc = self.H
        ti = 0
        it = 0
        for bi, (t0, n) in enumerate(self.blocks()):
            TW = n * 128
            A = aT[bi % 2]
            for j in range(n):
                tile = t0 + j
                ht = hts[ti % 2]
                kb.dma(ht.ap, hsrc[tile * 128:(tile + 1) * 128, :])
                self.prenormT(ht, A[:, :, j * 128:(j + 1) * 128], abf[ti % 2], ssb[ti % 2], pst, junk, ti)
                ti += 1
            HB = Hb[bi % 2]
            for fc in range(22):
                pg = PSg[it % 2]; pu = PSu[it % 2]
                ug = Ug[it % 2]; uu = Uu[it % 2]
                for c in range(8):
                    kb.mm(pg[:, 0:TW], Wup[:, c, fc * 128:(fc + 1) * 128], A[:, c, 0:TW], start=(c == 0), stop=(c == 7))
                for c in range(8):
                    kb.mm(pu[:, 0:TW], Wup[:, c, FFN + fc * 128:FFN + (fc + 1) * 128], A[:, c, 0:TW], start=(c == 0), stop=(c == 7))
                a = yg[it % 2]; b = yu[it % 2]
                for (u_, p_, y_, hc) in ((ug, pg, a, fc), (uu, pu, b, 22 + fc)):
                    kb.copy("act", u_[:, 2:2 + TW], p_[:, 0:TW])
                    kb.amul(y_[:, 0:TW], p_[:, 0:TW], cw[:, 2, hc:hc + 1])
                    kb.copy("dve", u_[:, 0:2], halo[:, hc, :])
                    kb.copy("dve", halo[:, hc, :], u_[:, TW:TW + 2])
                    kb.stt("dve", y_[:, 0:TW], u_[:, 1:1 + TW], cw[:, 1, hc:hc + 1], y_[:, 0:TW], ALU.mult, ALU.add)
                    kb.stt("dve", y_[:, 0:TW], u_[:, 0:TW], cw[:, 0, hc:hc + 1], y_[:, 0:TW], ALU.mult, ALU.add)
                kb.act(gg[it % 2][:, 0:TW], a[:, 0:TW], AF.Gelu_apprx_tanh)
                kb.tt("dve", HB[:, fc, 0:TW], gg[it % 2][:, 0:TW], b[:, 0:TW], ALU.mult)
                it += 1
            kb.dma(HTv[:, :, t0 * 128:t0 * 128 + TW], HB[:, :, 0:TW], q="pool")
        kb.end_phase()

    def phase_init(self):
        kb = self.kb
        kb.phase()
        hts = [kb.sb("ht", [128, D], F32) for _ in range(4)]
        for t in range(self.NT):
            ht = hts[t % 4]
            kb.dma(ht.ap, self.inp["xpad"][t * 128:(t + 1) * 128, :])
            kb.dma(self.H[t * 128:(t + 1) * 128, :], ht.ap, q="pool")
        kb.end_phase()

    def ffn(self, l, final):
        self.phase_F1(l)
        self.phase_R(self.HT, 22, self.inp["ffn_w_down"][l], self.inp["norm_ffn_post"][l], self.H, final=final)

    def finish(self):
        self.const_stack.close()
        self.kb.es.close()


def phase_A1(self, l):
    kb = self.kb
    j = l // 2
    NT = self.NT
    kb.phase()
    Win = kb.sb("Win", [128, 8, 4112], BF16)
    stages = [kb.sb("wst", [128, 2048], F32) for _ in range(2)]
    cst = kb.sb("cst", [128, 128], F32)
    gcol = kb.sb("gcol", [128, 8], F32)
    pst = kb.ps("pst", [128, 1024], BF16)
    PSq = [kb.ps("psq", [128, 512], F32) for _ in range(2)]
    PSs = kb.ps("pss", [128, 512], F32)
    PSv = kb.ps("psv", [128, 1024], F32)
    PSm = kb.ps("psm", [128, 512], F32)
    PSr = kb.ps("psr", [16, 512], BF16)
    self.load_cols(gcol.ap, self.inp["norm_mix_pre"][l].rearrange("(c p) -> c p", p=128), 8, cst, PSs)
    self.load_w(Win, self.inp["attn_w_in"][j], 8, 4112, gcol, stages)
    qg = kb.sb("qg", [128, 2], F32)
    for half in range(2):
        kb.dma(qg[half * 64:(half + 1) * 64, 0:1], self.inp["attn_q_norm"][j].rearrange("(p o) -> p o", o=1))
        kb.dma(qg[half * 64:(half + 1) * 64, 1:2], self.inp["attn_k_norm"][j].rearrange("(p o) -> p o", o=1))
    kb.ts("dve", qg[:, 0:1], qg[:, 0:1], 0.125, ALU.mult)
    bfg = kb.sb("bfg", [128, 16], F32)
    kb.dma(bfg.ap, self.inp["attn_b_forget"][j].partition_broadcast(128))
    BD = kb.sb("BD", [128, 128], BF16)
    kb.memset("pool", BD.ap, 1.0)
    kb.asel(BD[:, 0:64], BD[:, 0:64], [[0, 64]], ALU.is_ge, 0.0, 63, -1)
    kb.asel(BD[:, 64:128], BD[:, 64:128], [[0, 64]], ALU.is_ge, 0.0, -64, 1)
    tri = kb.sb("tri", [128, 128], F32)
    kb.memset("pool", tri.ap, 1.0)
    kb.asel(tri.ap, tri.ap, [[1, 128]], ALU.is_ge, 0.0, 0, -1)
    onesf = kb.sb("onesf", [128, 128], F32)
    kb.memset("pool", onesf.ap, 1.0)
    onesb = kb.sb("onesb", [16, 512], BF16)
    kb.memset("pool", onesb.ap, 1.0)
    Ctok = kb.sb("Ctok", [128, NT, 16], F32)
    CAR = kb.sb("CAR", [128, NT + 1, 16], F32)
    kb.memset("dve", CAR[:, 0, :], 0.0)
    hts = [kb.sb("ht", [128, D], F32) for _ in range(2)]
    abf = [kb.sb("abf", [128, D], BF16) for _ in range(2)]
    ssb = [kb.sb("ss", [128, 8], F32) for _ in range(2)]
    junk = kb.sb("junk", [128, D], BF16)
    aT = [kb.sb("aT", [128, 8, 512], BF16) for _ in range(2)]
    sq = [kb.sb("sq", [128, 512], BF16) for _ in range(2)]
    rms = [kb.sb("rms", [128, 512], F32) for _ in range(2)]
    qn = [kb.sb("qn", [128, 512], BF16) for _ in range(3)]
    Va = [kb.sb("Va", [128, 16, 65], BF16) for _ in range(2)]
    V0 = kb.sb("V0", [128, 16, 65], BF16)
    for v in Va + [V0]:
        kb.memset("pool", v.ap, 1.0)
    kb.memset("pool", V0[0:NPAD, :, 64:65], 0.0)
    xs = [kb.sb("xs", [128, 48], F32) for _ in range(2)]
    rb = [kb.sb("rb", [128, 16], BF16) for _ in range(2)]
    rT = [kb.sb("rT", [16, 512], BF16) for _ in range(2)]
    ti = 0
    it = 0
    for bi, (t0, n) in enumerate(self.blocks()):
        TW = n * 128
        c0 = t0 * 128
        A = aT[bi % 2]
        for jj in range(n):
            tile = t0 + jj
            ht = hts[ti % 2]
            kb.dma(ht.ap, self.H[tile * 128:(tile + 1) * 128, :])
            self.prenormT(ht, A[:, :, jj * 128:(jj + 1) * 128], abf[ti % 2], ssb[ti % 2], pst, junk, ti)
            ti += 1
        for which in range(2):
            dst = self.QT if which == 0 else self.KT
            for ch in range(8):
                pq = PSq[it % 2]
                col = which * 1024 + ch * 128
                for c in range(8):
                    kb.mm(pq[:, 0:TW], Win[:, c, col:col + 128], A[:, c, 0:TW], start=(c == 0), stop=(c == 7))
                s_ = sq[it % 2]
                kb.act(s_[:, 0:TW], pq[:, 0:TW], AF.Square)
                kb.mm(PSs[:, 0:TW], BD.ap, s_[:, 0:TW])
                r_ = rms[it % 2]
                kb.act(r_[:, 0:TW], PSs[:, 0:TW], AF.Sqrt, bias=EPS, scale=1.0 / 64)
                kb.recip(r_[:, 0:TW], r_[:, 0:TW])
                q_ = qn[it % 3]
                kb.stt("dve", q_[:, 0:TW], pq[:, 0:TW], qg[:, which:which + 1], r_[:, 0:TW], ALU.mult, ALU.mult)
                kb.dma(dst[2 * ch, 0:64, c0:c0 + TW], q_[0:64, 0:TW], q="pool")
                kb.dma(dst[2 * ch + 1, 0:64, c0:c0 + TW], q_[64:128, 0:TW], q="pool")
                it += 1
        for ch in range(8):
            pq = PSq[it % 2]
            col = 3072 + ch * 128
            for c in range(8):
                kb.mm(pq[:, 0:TW], Win[:, c, col:col + 128], A[:, c, 0:TW], start=(c == 0), stop=(c == 7))
            q_ = qn[it % 3]
            kb.act(q_[:, 0:TW], pq[:, 0:TW], AF.Sigmoid)
            kb.dma(self.SG[ch * 128:(ch + 1) * 128, c0:c0 + TW], q_[:, 0:TW], q="pool")
            it += 1
        kb.dma(self.KT[:, 64, c0:c0 + TW], onesb[:, 0:TW], q="pool")
        for jj in range(n):
            tile = t0 + jj
            tc_ = slice(jj * 128, (jj + 1) * 128)
            for half in range(2):
                for c in range(8):
                    kb.mm(PSv[:, half * 512:(half + 1) * 512], A[:, c, tc_], Win[:, c, 2048 + half * 512:2048 + (half + 1) * 512],
                          start=(c == 0), stop=(c == 7))
            V = V0 if tile == 0 else Va[tile % 2]
            kb.copy("act", V[:, :, 0:64], PSv.ap.re("p (h d) -> p h d", h=16))
            kb.dma(self.VA[tile * 128:(tile + 1) * 128, :, :], V.ap, q="pool")
            for c in range(8):
                kb.mm(PSm[:, 0:16], A[:, c, tc_], Win[:, c, 4096:4112], start=(c == 0), stop=(c == 7))
            x = xs[tile % 2]
            kb.tt("dve", x[:, 0:16], PSm[:, 0:16], bfg.ap, ALU.add)
            kb.act(x[:, 16:32], x[:, 0:16], AF.Exp, scale=-1.0)
            kb.act(x[:, 32:48], x[:, 16:32], AF.Ln, bias=1.0)
            kb.mm(PSm[:, 16:32], tri.ap, x[:, 32:48])
            kb.mm(PSm[:, 32:48], onesf.ap, x[:, 32:48])
            kb.tt("dve", Ctok[:, tile, :], PSm[:, 16:32], CAR[:, tile, :], ALU.add)
            kb.tt("dve", CAR[:, tile + 1, :], PSm[:, 32:48], CAR[:, tile, :], ALU.add)
        R_ = rT[bi % 2]
        for jj in range(n):
            tile = t0 + jj
            r2 = rb[tile % 2]
            kb.tt("dve", r2.ap, CAR[:, t0 + n, :], Ctok[:, tile, :], ALU.subtract)
            kb.tr(PSr[:, jj * 128:(jj + 1) * 128], r2.ap, self.idb.ap)
        kb.copy("dve", R_[:, 0:TW], PSr[:, 0:TW])
        kb.dma(self.QT[:, 64, c0:c0 + TW], R_[:, 0:TW], q="pool")
    kb.dma(self.CT, Ctok.ap, q="pool")
    kb.dma(self.CAR, CAR.ap, q="pool")
    kb.end_phase()


def phase_A2(self, l):
    kb = self.kb
    NT = self.NT
    LP = self.LP
    kb.phase()
    Ctok = kb.sb("Ctok", [128, NT, 16], F32)
    CAR = kb.sb("CAR", [128, NT + 1, 16], F32)
    kb.dma(Ctok.ap, self.CT)
    kb.dma(CAR.ap, self.CAR)
    masks = []
    for jj in range(4):
        m = kb.sb("mask", [128, 512], BF16)
        kb.memset("pool", m.ap, 0.0)
        kb.asel(m.ap, m.ap, [[1, 512]], ALU.is_ge, NEGBIG, -128 * jj, -1)
        masks.append(m)
    onesf = kb.sb("onesf", [128, 64], F32)
    kb.memset("pool", onesf.ap, 1.0)
    KTh = [kb.sb("KTh", [65, LP], BF16) for _ in range(2)]
    VAh = [kb.sb("VAh", [128, NT, 65], BF16) for _ in range(2)]
    QTb = [kb.sb("QTb", [65, 512], BF16) for _ in range(2)]
    SGb = [kb.sb("SGb", [64, 512], BF16) for _ in range(2)]
    bias = [kb.sb("bias", [128, NT], F32) for _ in range(2)]
    P = [kb.sb("P", [128, 512], BF16) for _ in range(3)]
    oacc = [kb.sb("oacc", [65, 512], F32) for _ in range(2)]
    rden = [kb.sb("rden", [65, 512], F32) for _ in range(2)]
    tmp = [kb.sb("tmp", [64, 512], F32) for _ in range(2)]
    og = [kb.sb("og", [64, 512], BF16) for _ in range(2)]
    PSs = [kb.ps("pss", [128, 512], F32) for _ in range(3)]
    PSo = [kb.ps("pso", [65, 512], F32) for _ in range(2)]
    PSb = [kb.ps("psb", [64, 512], F32) for _ in range(2)]
    VAv = self.VA.rearrange("(t p) h e -> p t h e", p=128)
    blocks = self.blocks()
    it = 0
    ip = 0
    def load_head(hh):
        kb.dma(KTh[hh % 2].ap, self.KT[hh])
        for a in range(0, NT, 16):
            b = min(NT, a + 16)
            kb.dma(VAh[hh % 2][:, a:b, :], VAv[:, a:b, hh, :])
    load_head(0)
    for h in range(16):
        if h + 1 < 16:
            load_head(h + 1)
        K_ = KTh[h % 2]
        V_ = VAh[h % 2]
        for bi, (t0, n) in enumerate(blocks):
            TW = n * 128
            c0 = t0 * 128
            Q_ = QTb[it % 2]
            S_ = SGb[it % 2]
            kb.dma(Q_[:, 0:TW], self.QT[h, :, c0:c0 + TW])
            kb.dma(S_[:, 0:TW], self.SG[h * 64:(h + 1) * 64, c0:c0 + TW])
            nkt = t0 + n
            b_ = bias[it % 2]
            kb.ts("dve", b_[:, 0:nkt], Ctok[:, 0:nkt, h], CAR[:, t0 + n, h:h + 1], ALU.subtract)
            po = PSo[it % 2]
            def emit_s(kt, ipk):
                ps = PSs[ipk % 3]
                diag = kt >= t0
                kb.mm(ps[:, 0:TW], K_[:, kt * 128:(kt + 1) * 128], Q_[:, 0:TW], start=True, stop=not diag)
                if diag:
                    kb.mm(ps[:, 0:TW], self.idb.ap, masks[kt - t0][:, 0:TW], start=False, stop=True)
                p_ = P[ipk % 3]
                kb.act(p_[:, 0:TW], ps[:, 0:TW], AF.Exp, bias=b_[:, kt:kt + 1])
            emit_s(0, ip)
            for kt in range(nkt):
                if kt + 1 < nkt:
                    emit_s(kt + 1, ip + 1)
                kb.mm(po[:, 0:TW], V_[:, kt, :], P[ip % 3][:, 0:TW], start=(kt == 0), stop=(kt == nkt - 1))
                ip += 1
            oa = oacc[it % 2]
            rd = rden[it % 2]
            kb.copy("act", oa[:, 0:TW], po[:, 0:TW])
            kb.ts("dve", rd[64:65, 0:TW], oa[64:65, 0:TW], 1e-30, ALU.max)
            kb.recip(rd[64:65, 0:TW], rd[64:65, 0:TW])
            pb = PSb[it % 2]
            kb.mm(pb[:, 0:TW], onesf[64:65, 0:64], rd[64:65, 0:TW])
            t_ = tmp[it % 2]
            kb.tt("dve", t_[:, 0:TW], oa[0:64, 0:TW], pb[:, 0:TW], ALU.mult)
            o_ = og[it % 2]
            kb.tt("pool", o_[:, 0:TW], t_[:, 0:TW], S_[:, 0:TW], ALU.mult)
            kb.dma(self.OGT[h * 64:(h + 1) * 64, c0:c0 + TW], o_[:, 0:TW], q="pool")
            it += 1
    kb.end_phase()


Prog.phase_A1 = phase_A1
Prog.phase_A2 = phase_A2


def attn_layer(self, l):
    self.phase_A1(l)
    self.phase_A2(l)
    self.phase_R(self.OGT, 8, self.inp["attn_w_out"][l // 2], self.inp["norm_mix_post"][l], self.H)


Prog.attn_layer = attn_layer


def phase_D1(self, l):
    kb = self.kb
    j = l // 2
    NT = self.NT
    kb.phase()
    Win = kb.sb("Win", [128, 8, 4112], BF16)
    stages = [kb.sb("wst", [128, 2048], F32) for _ in range(2)]
    cst = kb.sb("cst", [128, 128], F32)
    gcol = kb.sb("gcol", [128, 8], F32)
    cw = kb.sb("cw", [128, 4, 24], F32)
    pst = kb.ps("pst", [128, 1024], BF16)
    PSq = [kb.ps("psq", [128, 512], F32) for _ in range(2)]
    PSs = kb.ps("pss", [128, 512], F32)
    PSv = kb.ps("psv", [128, 1024], F32)
    PSm = kb.ps("psm", [128, 512], F32)
    self.load_cols(gcol.ap, self.inp["norm_mix_pre"][l].rearrange("(c p) -> c p", p=128), 8, cst, PSs)
    for k in range(4):
        self.load_cols(cw[:, k, :], self.inp["dn_conv"][j, k].rearrange("(c p) -> c p", p=128), 24, cst, PSs)
    self.load_w(Win, self.inp["dn_w_in"][j], 8, 4112, gcol, stages)
    dtb = kb.sb("dtb", [128, 8], F32)
    nA = kb.sb("nA", [128, 8], F32)
    kb.dma(dtb.ap, self.inp["dn_dt_bias"][j].partition_broadcast(128))
    kb.dma(nA.ap, self.inp["dn_a_log"][j].partition_broadcast(128))
    kb.act(nA.ap, nA.ap, AF.Exp)
    kb.ts("dve", nA.ap, nA.ap, -1.0, ALU.mult)
    tri = kb.sb("tri", [128, 128], F32)
    kb.memset("pool", tri.ap, 1.0)
    kb.asel(tri.ap, tri.ap, [[1, 128]], ALU.is_ge, 0.0, 0, -1)
    onesb = kb.sb("onesb", [128, 128], BF16)
    kb.memset("pool", onesb.ap, 1.0)
    halo = kb.sb("halo", [128, 24, 3], F32)
    kb.memset("pool", halo.ap, 0.0)
    GB = kb.sb("GB", [128, NT, 16], F32)
    hts = [kb.sb("ht", [128, D], F32) for _ in range(2)]
    abf = [kb.sb("abf", [128, D], BF16) for _ in range(2)]
    ssb = [kb.sb("ss", [128, 8], F32) for _ in range(2)]
    junk = kb.sb("junk", [128, D], BF16)
    aT = [kb.sb("aT", [128, 8, 512], BF16) for _ in range(2)]
    U = [kb.sb("U", [128, 515], F32) for _ in range(2)]
    yv = [kb.sb("yv", [128, 512], F32) for _ in range(2)]
    z = [kb.sb("z", [128, 512], F32) for _ in range(3)]
    sq = [kb.sb("sq", [128, 512], BF16) for _ in range(2)]
    rms = [kb.sb("rms", [128, 512], F32) for _ in range(2)]
    sg = [kb.sb("sg", [128, D], BF16) for _ in range(2)]
    xs = [kb.sb("xs", [128, 32], F32) for _ in range(2)]
    ti = 0
    it = 0
    for bi, (t0, n) in enumerate(self.blocks()):
        TW = n * 128
        c0 = t0 * 128
        A = aT[bi % 2]
        for jj in range(n):
            tile = t0 + jj
            ht = hts[ti % 2]
            kb.dma(ht.ap, self.H[tile * 128:(tile + 1) * 128, :])
            self.prenormT(ht, A[:, :, jj * 128:(jj + 1) * 128], abf[ti % 2], ssb[ti % 2], pst, junk, ti)
            ti += 1
        for fc in range(24):
            pq = PSq[it % 2]
            for c in range(8):
                kb.mm(pq[:, 0:TW], Win[:, c, fc * 128:(fc + 1) * 128], A[:, c, 0:TW], start=(c == 0), stop=(c == 7))
            u = U[it % 2]
            kb.copy("dve", u[:, 0:3], halo[:, fc, :])
            kb.copy("act", u[:, 3:3 + TW], pq[:, 0:TW])
            kb.copy("dve", halo[:, fc, :], u[:, TW:TW + 3])
            y = yv[it % 2]
            kb.ts("dve", y[:, 0:TW], u[:, 3:3 + TW], cw[:, 3, fc:fc + 1], ALU.mult)
            for k in (2, 1, 0):
                kb.stt("dve", y[:, 0:TW], u[:, k:k + TW], cw[:, k, fc:fc + 1], y[:, 0:TW], ALU.mult, ALU.add)
            z_ = z[it % 3]
            kb.act(z_[:, 0:TW], y[:, 0:TW], AF.Silu)
            if fc < 16:
                s_ = sq[it % 2]
                kb.act(s_[:, 0:TW], z_[:, 0:TW], AF.Square)
                kb.mm(PSs[:, 0:TW], onesb.ap, s_[:, 0:TW])
                r_ = rms[it % 2]
                kb.act(r_[:, 0:TW], PSs[:, 0:TW], AF.Sqrt, bias=EPS)
                kb.recip(r_[:, 0:TW], r_[:, 0:TW])
                if fc < 8:
                    kb.stt("dve", z_[:, 0:TW], z_[:, 0:TW], float(HD_D ** -0.5), r_[:, 0:TW], ALU.mult, ALU.mult)
                else:
                    kb.tt("dve", z_[:, 0:TW], z_[:, 0:TW], r_[:, 0:TW], ALU.mult)
            kb.dma(self.QKVD[fc, :, c0:c0 + TW], z_[:, 0:TW], q="pool")
            it += 1
        for jj in range(n):
            tile = t0 + jj
            tc_ = slice(jj * 128, (jj + 1) * 128)
            for half in range(2):
                for c in range(8):
                    kb.mm(PSv[:, half * 512:(half + 1) * 512], A[:, c, tc_], Win[:, c, 3072 + half * 512:3072 + (half + 1) * 512],
                          start=(c == 0), stop=(c == 7))
            s2 = sg[tile % 2]
            kb.act(s2.ap, PSv.ap, AF.Silu)
            kb.dma(self.SGD[tile * 128:(tile + 1) * 128, :], s2.ap, q="pool")
            for c in range(8):
                kb.mm(PSm[:, 0:16], A[:, c, tc_], Win[:, c, 4096:4112], start=(c == 0), stop=(c == 7))
            x = xs[tile % 2]
            kb.act(GB[:, tile, 0:8], PSm[:, 0:8], AF.Sigmoid)
            kb.tt("dve", x[:, 0:8], PSm[:, 8:16], dtb.ap, ALU.add)
            kb.act(x[:, 8:16], x[:, 0:8], AF.Exp)
            kb.act(x[:, 16:24], x[:, 8:16], AF.Ln, bias=1.0)
            kb.tt("dve", x[:, 24:32], x[:, 16:24], nA.ap, ALU.mult)
            kb.mm(PSm[:, 16:24], tri.ap, x[:, 24:32])
            kb.copy("dve", GB[:, tile, 8:16], PSm[:, 16:24])
    kb.dma(self.GBD, GB.ap, q="pool")
    kb.end_phase()


def phase_D2(self, l, G=4):
    kb = self.kb
    j = l // 2
    NT = self.NT
    kb.phase()
    GB = kb.sb("GB", [128, NT, 16], F32)
    kb.dma(GB.ap, self.GBD)
    onrm = kb.sb("onrm", [128, 128], F32)
    kb.dma(onrm.ap, self.inp["dn_o_norm"][j].partition_broadcast(128))
    onesf = kb.sb("onesf", [128, 128], F32)
    kb.memset("pool", onesf.ap, 1.0)
    idf = self.idf
    idb = self.idb
    NMsT = kb.sb("NMsT", [128, 128], F32)
    kb.memset("pool", NMsT.ap, 0.0)
    kb.asel(NMsT.ap, NMsT.ap, [[1, 128]], ALU.is_gt, NEGBIG, 0, -1)
    NPs = kb.sb("NPs", [128, 128], F32)
    kb.memset("pool", NPs.ap, 0.0)
    kb.asel(NPs.ap, NPs.ap, [[-1, 128]], ALU.is_gt, -NEGBIG, 0, 1)
    BDm = kb.sb("BDm", [128, 128], F32)
    kb.memset("pool", BDm.ap, 1.0)
    kb.asel(BDm[:, 0:64], BDm[:, 0:64], [[0, 64]], ALU.is_ge, 0.0, 63, -1)
    kb.asel(BDm[:, 64:128], BDm[:, 64:128], [[0, 64]], ALU.is_ge, 0.0, -64, 1)
    ST = [kb.sb("ST", [128, 128], F32) for _ in range(8)]
    STb = [kb.sb("STb", [128, 128], BF16) for _ in range(8)]
    for s_ in ST + STb:
        kb.memset("dve", s_.ap, 0.0)
    X24 = [kb.sb("X24", [128, 24, 128], F32) for _ in range(2)]
    X24b = [kb.sb("X24b", [128, 24, 128], BF16) for _ in range(2)]
    SGt = [kb.sb("SGt", [128, D], BF16) for _ in range(2)]
    Og = [kb.sb("Og", [128, D], BF16) for _ in range(2)]
    OgT = [kb.sb("OgT", [128, 8, 128], BF16) for _ in range(2)]
    tl = [kb.sb("tl", [128, 32], F32) for _ in range(2)]
    sets = []
    f32names = ["DG", "Gs", "DTs", "DTi", "Dn", "A", "B", "Ad", "Ao", "Bd", "Bo", "P", "Pt", "Xa", "Xb", "Xta", "Xtb",
                "Zs", "U0s", "EGbc", "tmp", "on"]
    bf16names = ["R", "QKT", "kbg", "kd", "vb", "WT", "Us", "qdT"]
    for g in range(G):
        d = {}
        for nm in f32names:
            w = 256 if nm in ("DG", "Gs") else 128
            d[nm] = kb.sb(nm, [128, w], F32)
        for nm in bf16names:
            d[nm] = kb.sb(nm, [128, 128], BF16)
        d["cols"] = kb.sb("cols", [128, 8], F32)
        bA = kb.ps("bA", [128, 512], F32)
        bB = kb.ps("bB", [128, 512], F32)
        d["bA"] = bA
        d["PSG"], d["PSK"], d["PSQ"] = kb.carve(bA, [(0, 256), (256, 384), (384, 512)])
        d["C0"], d["C1"], d["C2"], d["C3"] = kb.carve(bB, [(0, 128), (128, 256), (256, 384), (384, 512)])
        d["D0"], d["D1"], d["D2"], d["D3"] = d["C0"], d["C1"], d["C2"], d["C3"]
        d["C2b"] = View(d["C2"], d["C2"].t.bitcast(BF16)[:, 0:128])
        d["C3b"] = View(d["C3"], d["C3"].t.bitcast(BF16)[:, 0:128])
        sets.append(d)
    PSot = View(sets[0]["bA"], sets[0]["bA"].t[:].bitcast(BF16))
    Xv = self.QKVD.rearrange("f d t -> d f t")
    OGTv = self.OGT.rearrange("(c p) t -> p c t", p=128)

    def head(n, hd, d, X, Xb, sgt, og, t_):
        qT = X[:, hd, :]
        qTb = Xb[:, hd, :]
        kTb = Xb[:, 8 + hd, :]
        vTb = Xb[:, 16 + hd, :]
        beta = GB[:, n, hd:hd + 1]
        gc = GB[:, n, 8 + hd:9 + hd]
        ngc = t_[:, hd:hd + 1]
        bg = t_[:, 16 + hd:17 + hd]
        cols = d["cols"]
        kb.ts("dve", d["DG"][:, 0:128], idf.ap, gc, ALU.mult)
        kb.ts("dve", d["DG"][:, 128:256], idf.ap, beta, ALU.mult)
        kb.mm(d["PSG"].ap, onesf.ap, d["DG"].ap)
        kb.copy("act", d["Gs"].ap, d["PSG"].ap)
        kb.mm(d["PSK"].ap, kTb, kTb)
        kb.mm(d["PSQ"].ap, kTb, qTb)
        yield
        G_ = d["Gs"][:, 0:128]
        kb.tt("dve", d["tmp"].ap, G_, NMsT.ap, ALU.add)
        kb.act(d["DTs"].ap, d["tmp"].ap, AF.Exp, bias=ngc)
        kb.tt("pool", d["DTi"].ap, d["DTs"].ap, idf.ap, ALU.add)
        kb.tt("dve", d["Dn"].ap, G_, NPs.ap, ALU.add)
        kb.act(d["Dn"].ap, d["Dn"].ap, AF.Exp, bias=gc, scale=-1.0)
        kb.act(d["EGbc"].ap, G_, AF.Exp)
        kb.act(cols[:, 0:1], gc, AF.Exp, bias=d["Gs"][:, 127:128], scale=-1.0)
        kb.act(cols[:, 1:2], d["Gs"][:, 127:128], AF.Exp)
        yield
        kb.stt("dve", d["A"].ap, d["PSK"].ap, beta, d["Dn"].ap, ALU.mult, ALU.mult)
        kb.tt("dve", d["B"].ap, d["PSK"].ap, d["DTs"].ap, ALU.mult)
        kb.tt("pool", d["B"].ap, d["B"].ap, d["Gs"][:, 128:256], ALU.mult)
        kb.tt("dve", d["QKT"].ap, d["PSQ"].ap, d["DTi"].ap, ALU.mult)
        kb.tt("pool", d["Ad"].ap, d["A"].ap, BDm.ap, ALU.mult)
        kb.tt("pool", d["Ao"].ap, d["A"].ap, d["Ad"].ap, ALU.subtract)
        kb.tt("pool", d["Bd"].ap, d["B"].ap, BDm.ap, ALU.mult)
        kb.tt("pool", d["Bo"].ap, d["B"].ap, d["Bd"].ap, ALU.subtract)
        kb.tt("dve", d["P"].ap, idf.ap, d["Ad"].ap, ALU.subtract)
        kb.tt("dve", d["Pt"].ap, idf.ap, d["Bd"].ap, ALU.subtract)
        yield
        Xc, Xtc = d["Ad"], d["Bd"]
        for k in range(1, 6):
            Xn = d["Xa"] if k % 2 else d["Xb"]
            Xtn = d["Xta"] if k % 2 else d["Xtb"]
            kb.mm(d["C0"].ap, Xtc.ap, Xc.ap)
            if k < 5:
                kb.mm(d["C1"].ap, Xc.ap, Xtc.ap)
            kb.copy("act", Xn.ap, d["C0"].ap)
            if k < 5:
                kb.copy("act", Xtn.ap, d["C1"].ap)
            kb.mm(d["C2"].ap, d["Pt"].ap, Xn.ap)
            kb.mm(d["C3"].ap, Xn.ap, d["Pt"].ap)
            kb.tt("dve", d["P"].ap, d["P"].ap, d["C2"].ap, ALU.add)
            kb.tt("dve", d["Pt"].ap, d["Pt"].ap, d["C3"].ap, ALU.add)
            Xc, Xtc = Xn, Xtn
            yield
        kb.mm(d["C0"].ap, d["Ao"].ap, d["Pt"].ap)
        kb.copy("act", d["Zs"].ap, d["C0"].ap)
        kb.mm(d["C1"].ap, d["P"].ap, d["Zs"].ap)
        kb.tt("dve", d["R"].ap, d["Pt"].ap, d["C1"].ap, ALU.subtract)
        kb.tr(d["C2b"], kTb, idb.ap)
        kb.tr(d["C3b"], vTb, idb.ap)
        kb.ts("dve", d["kbg"].ap, d["C2b"], bg, ALU.mult)
        kb.ts("dve", d["kd"].ap, d["C2b"], cols[:, 0:1], ALU.mult)
        kb.ts("dve", d["vb"].ap, d["C3b"], beta, ALU.mult)
        kb.tt("pool", d["qdT"].ap, qT, d["EGbc"].ap, ALU.mult)
        yield
        kb.mm(d["D0"].ap, d["kbg"].ap, d["R"].ap)
        kb.mm(d["D1"].ap, d["R"].ap, d["vb"].ap)
        kb.copy("act", d["WT"].ap, d["D0"].ap)
        kb.copy("act", d["U0s"].ap, d["D1"].ap)
        yield
        S_ = ST[hd]
        Sb_ = STb[hd]
        kb.mm(d["D2"].ap, d["WT"].ap, Sb_.ap)
        kb.tt("dve", d["Us"].ap, d["U0s"].ap, d["D2"].ap, ALU.subtract)
        kb.mm(d["D3"].ap, d["qdT"].ap, Sb_.ap, start=True, stop=False)
        kb.mm(d["D3"].ap, d["QKT"].ap, d["Us"].ap, start=False, stop=True)
        kb.memset("dve", cols[:, 2:3], 0.0)
        kb.act(d["tmp"].ap, d["D3"].ap, AF.Square, accum=cols[:, 2:3])
        kb.act(cols[:, 3:4], cols[:, 2:3], AF.Sqrt, bias=EPS, scale=1.0 / HD_D)
        kb.recip(cols[:, 4:5], cols[:, 3:4])
        kb.stt("dve", d["on"].ap, d["D3"].ap, cols[:, 4:5], onrm.ap, ALU.mult, ALU.mult)
        kb.mm(d["C0"].ap, d["kd"].ap, d["Us"].ap)
        kb.stt("dve", S_.ap, S_.ap, cols[:, 1:2], d["C0"].ap, ALU.mult, ALU.add)
        kb.copy("act", Sb_.ap, S_.ap)
        kb.tt("pool", og[:, hd * 128:(hd + 1) * 128], d["on"].ap, sgt[:, hd * 128:(hd + 1) * 128], ALU.mult)
        yield

    kb.dma(X24[0].ap, Xv[:, :, 0:128])
    for n in range(NT):
        if n + 1 < NT:
            kb.dma(X24[(n + 1) % 2].ap, Xv[:, :, (n + 1) * 128:(n + 2) * 128])
        X = X24[n % 2]
        Xb = X24b[n % 2]
        kb.copy("act", Xb[:, 0:12, :], X[:, 0:12, :])
        kb.copy("dve", Xb[:, 12:24, :], X[:, 12:24, :])
        sgt = SGt[n % 2]
        kb.dma(sgt.ap, self.SGD[n * 128:(n + 1) * 128, :])
        t_ = tl[n % 2]
        kb.ts("dve", t_[:, 0:8], GB[:, n, 8:16], -1.0, ALU.mult)
        kb.act(t_[:, 8:16], GB[:, n, 8:16], AF.Exp)
        kb.tt("dve", t_[:, 16:24], t_[:, 8:16], GB[:, n, 0:8], ALU.mult)
        og = Og[n % 2]
        for h0 in range(0, 8, G):
            gens = [head(n, h0 + g, sets[g], X, Xb, sgt, og, t_) for g in range(G)]
            alive = list(gens)
            while alive:
                nxt = []
                for g_ in alive:
                    try:
                        next(g_)
                        nxt.append(g_)
                    except StopIteration:
                        pass
                alive = nxt
        for c in range(8):
            kb.tr(PSot[:, c * 128:(c + 1) * 128], og[:, c * 128:(c + 1) * 128], idb.ap)
        ot = OgT[n % 2]
        kb.copy("act", ot.ap, PSot.re("p (c t) -> p c t", c=8))
        kb.dma(OGTv[:, :, n * 128:(n + 1) * 128], ot.ap, q="pool")
    kb.end_phase()


Prog.phase_D1 = phase_D1
Prog.phase_D2 = phase_D2


def dn_layer(self, l):
    self.phase_D1(l)
    self.phase_D2(l)
    self.phase_R(self.OGT, 8, self.inp["dn_w_out"][l // 2], self.inp["norm_mix_post"][l], self.H)


Prog.dn_layer = dn_layer


_SEQ = 8192
_NT = (_SEQ + NMETA + NPAD) // 128


def build_full(NT):
    p = Prog(NT)
    p.consts()
    p.phase_init()
    for l in range(4):
        if l % 2 == 0:
            p.attn_layer(l)
        else:
            p.dn_layer(l)
        p.ffn(l, final=(l == 3))
    p.finish()
    return p


def kernel(**inputs):
    x = np.asarray(inputs["x"], dtype=np.float32)
    B = x.shape[0]
    meta = np.asarray(inputs["meta_tokens"], dtype=np.float32)
    p = build_full(_NT)
    shared = {k: np.ascontiguousarray(np.asarray(inputs[k], dtype=np.float32)) for k in p.inp if k != "xpad"}
    real = {0: 0, 1: 1, 4: 2, 5: 3}
    zshared = {k: np.zeros_like(v) for k, v in shared.items()}
    zx = np.zeros((NPAD + NMETA + x.shape[1], D), np.float32)
    in_maps = []
    for c in range(8):
        if c in real and real[c] < B:
            b = real[c]
            xpad = np.concatenate([np.zeros((NPAD, D), np.float32), meta, x[b]], axis=0)
            m = dict(shared)
            m["xpad"] = np.ascontiguousarray(xpad)
        else:
            m = dict(zshared)
            m["xpad"] = zx
        in_maps.append(m)
    res = run_bass_kernel_spmd(p.nc, in_maps, core_ids=list(range(8)))
    core_of = {b: c for c, b in real.items()}
    out = np.stack([np.asarray(res.results[core_of[b]]["y"], dtype=np.float32) for b in range(B)], axis=0)
    return out
```

```python
import numpy as np
from contextlib import ExitStack
import concourse.bass as bass
import concourse.mybir as mybir
from concourse.bass_utils import run_bass_kernel_spmd

F32 = mybir.dt.float32
BF16 = mybir.dt.bfloat16
AF = mybir.ActivationFunctionType
ALU = mybir.AluOpType
AX = mybir.AxisListType

D = 1024
NH_A = 16
HD_A = 64
NH_D = 8
HD_D = 128
FFN = 2816
EPS = 1e-6
NPAD = 112
NMETA = 16
NEGBIG = -30000.0


class Buf:
    def __init__(self, t, name, space):
        self.t = t
        self.name = name
        self.space = space
        self.w = {}
        self.r = {}
        self.sem = None

    def __getitem__(self, k):
        return View(self, self.t[k])

    @property
    def ap(self):
        return View(self, self.t[:])


class View:
    def __init__(self, buf, ap):
        self.buf = buf
        self.ap = ap

    def __getitem__(self, k):
        return View(self.buf, self.ap[k])

    def re(self, s, **kw):
        return View(self.buf, self.ap.rearrange(s, **kw))

    def bc(self, shape):
        return View(self.buf, self.ap.to_broadcast(shape))


def _ap(x):
    return x.ap if isinstance(x, View) else x


class KB:
    def __init__(self):
        self.nc = bass.Bass("TRN2", target_bir_lowering=False)
        nc = self.nc
        self.es = ExitStack()
        self.eng = {"pe": nc.tensor, "dve": nc.vector, "act": nc.scalar, "pool": nc.gpsimd, "sp": nc.sync}
        self.cnt = {}
        self.esem = {}
        for e in ("pe", "dve", "act", "pool"):
            self.esem[e] = self.es.enter_context(nc.semaphore("cnt_" + e))
            self.cnt[e] = 0
        self.seen = {e: {} for e in self.eng}
        self.sem_pool = []
        self.live_sems = []
        self.phase_stack = None
        self.phase_bufs = []
        self.uid = 0
        self.dq = 0

    def phase(self):
        self.phase_stack = ExitStack()
        self.phase_bufs = []

    def end_phase(self):
        self.barrier()
        for b in self.phase_bufs:
            if b.sem is not None:
                self.sem_pool.append(b.sem)
                b.sem = None
        self.phase_stack.close()
        self.phase_stack = None
        self.phase_bufs = []

    def carve(self, buf, slices):
        out = []
        for (a, b) in slices:
            sbuf = Buf(buf.t[:, a:b], buf.name + "_%d" % a, buf.space)
            sbuf.w = buf.w
            sbuf.r = buf.r
            self.phase_bufs.append(sbuf)
            out.append(sbuf)
        return out

    def sb(self, name, shape, dt):
        self.uid += 1
        t = self.phase_stack.enter_context(self.nc.sbuf_tensor("%s_%d" % (name, self.uid), list(shape), dt))
        b = Buf(t, name, "sb")
        self.phase_bufs.append(b)
        return b

    def ps(self, name, shape, dt):
        self.uid += 1
        t = self.phase_stack.enter_context(self.nc.psum_tensor("%s_%d" % (name, self.uid), list(shape), dt))
        b = Buf(t, name, "ps")
        self.phase_bufs.append(b)
        return b

    def _getsem(self, b):
        if b.sem is None:
            if self.sem_pool:
                b.sem = self.sem_pool.pop()
            else:
                s = self.es.enter_context(self.nc.semaphore("dsem%d" % len(self.live_sems)))
                b.sem = [s, 0]
                self.live_sems.append(b.sem)
        return b.sem

    def _wait(self, e, deps):
        seen = self.seen[e]
        for sid, (sem, val) in deps.items():
            if seen.get(sid, 0) < val:
                self.eng[e].wait_ge(sem, val)
                seen[sid] = val

    def _deps(self, outs, ins):
        deps = {}

        def add(d):
            for sid, (sem, val) in d.items():
                if sid not in deps or deps[sid][1] < val:
                    deps[sid] = (sem, val)
        for v in ins:
            if isinstance(v, View):
                add(v.buf.w)
        for v in outs:
            if isinstance(v, View):
                add(v.buf.w)
                add(v.buf.r)
        return deps

    def _record(self, outs, ins, sem, val):
        sid = id(sem)
        for v in ins:
            if isinstance(v, View):
                v.buf.r[sid] = (sem, val)
        for v in outs:
            if isinstance(v, View):
                v.buf.w[sid] = (sem, val)

    def op(self, e, fn, outs, ins, pe_acc=False):
        deps = self._deps(outs, ins)
        if e == "pe":
            deps.pop(id(self.esem["pe"]), None)
        self._wait(e, deps)
        ins_ = fn()
        self.cnt[e] += 1
        ins_.then_inc(self.esem[e], 1)
        self._record(outs, ins, self.esem[e], self.cnt[e])
        return ins_

    def dma(self, out, in_, q=None, **kw):
        if q is None:
            q = "sp"
        sbv = out if isinstance(out, View) else in_
        assert isinstance(sbv, View)
        outs = [out] if isinstance(out, View) else []
        ins = [in_] if isinstance(in_, View) else []
        deps = self._deps(outs, ins)
        self._wait(q, deps)
        se = self._getsem(sbv.buf)
        self.eng[q].dma_start(out=_ap(out), in_=_ap(in_), **kw).then_inc(se[0], 16)
        se[1] += 16
        self._record(outs, ins, se[0], se[1])

    def barrier(self):
        allsems = {}
        for e in ("pe", "dve", "act", "pool"):
            allsems[id(self.esem[e])] = (self.esem[e], self.cnt[e])
        for se in self.live_sems:
            allsems[id(se[0])] = (se[0], se[1])
        for e in self.eng:
            self._wait(e, {k: v for k, v in allsems.items() if v[1] > 0})

    def mm(self, out, lhsT, rhs, start=True, stop=True):
        return self.op("pe", lambda: self.nc.tensor.matmul(_ap(out), lhsT=_ap(lhsT), rhs=_ap(rhs), start=start, stop=stop),
                       [out], [lhsT, rhs])

    def tr(self, out, in_, ident):
        return self.op("pe", lambda: self.nc.tensor.transpose(_ap(out), _ap(in_), _ap(ident)), [out], [in_, ident])

    def act(self, out, in_, func, bias=None, scale=None, accum=None, e="act"):
        kw = {}
        ins = [in_]
        outs = [out]
        if bias is not None:
            kw["bias"] = _ap(bias)
            ins.append(bias)
        if scale is not None:
            kw["scale"] = _ap(scale)
            ins.append(scale)
        if accum is not None:
            kw["accum_out"] = _ap(accum)
            outs.append(accum)
        return self.op("act", lambda: self.nc.scalar.activation(out=_ap(out), in_=_ap(in_), func=func, **kw), outs, ins)

    def tt(self, e, out, a, b, op):
        return self.op(e, lambda: self.eng[e].tensor_tensor(out=_ap(out), in0=_ap(a), in1=_ap(b), op=op), [out], [a, b])

    def ts(self, e, out, a, s1, op0, s2=None, op1=None, accum=None):
        kw = {}
        outs = [out]
        if op1 is not None:
            kw["op1"] = op1
        if accum is not None:
            kw["accum_out"] = _ap(accum)
            outs.append(accum)
        return self.op(e, lambda: self.eng[e].tensor_scalar(out=_ap(out), in0=_ap(a), scalar1=_ap(s1), scalar2=_ap(s2), op0=op0, **kw),
                       outs, [a, s1, s2])

    def stt(self, e, out, a, s, b, op0, op1):
        return self.op(e, lambda: self.eng[e].scalar_tensor_tensor(out=_ap(out), in0=_ap(a), scalar=_ap(s), in1=_ap(b), op0=op0, op1=op1),
                       [out], [a, s, b])

    def copy(self, e, out, in_):
        if e == "act":
            return self.op("act", lambda: self.nc.scalar.copy(out=_ap(out), in_=_ap(in_)), [out], [in_])
        return self.op(e, lambda: self.eng[e].tensor_copy(out=_ap(out), in_=_ap(in_)), [out], [in_])

    def amul(self, out, in_, mul):
        return self.op("act", lambda: self.nc.scalar.mul(out=_ap(out), in_=_ap(in_), mul=_ap(mul)), [out], [in_, mul])

    def memset(self, e, out, val):
        return self.op(e, lambda: self.eng[e].memset(_ap(out), val), [out], [])

    def recip(self, out, in_):
        return self.op("dve", lambda: self.nc.vector.reciprocal(out=_ap(out), in_=_ap(in_)), [out], [in_])

    def asel(self, out, in_, pattern, cmp, fill, base, cm):
        return self.op("pool", lambda: self.nc.gpsimd.affine_select(out=_ap(out), in_=_ap(in_), pattern=pattern, compare_op=cmp,
                                                                      fill=fill, base=base, channel_multiplier=cm), [out], [in_])


class Prog:
    def __init__(self, NT, layers=(0, 1, 2, 3), final_out=True):
        self.NT = NT
        self.LP = NT * 128
        self.kb = KB()
        self.nc = self.kb.nc
        self.layers = layers
        nc = self.nc
        LP = self.LP
        S = LP - 128
        shapes = {"xpad": [LP, D], "norm_mix_pre": [4, D], "norm_mix_post": [4, D], "norm_ffn_pre": [4, D], "norm_ffn_post": [4, D],
                  "attn_w_in": [2, D, 4112], "attn_b_forget": [2, 16], "attn_q_norm": [2, 64], "attn_k_norm": [2, 64],
                  "attn_w_out": [2, D, D], "dn_w_in": [2, D, 4112], "dn_conv": [2, 4, 3072], "dn_a_log": [2, 8], "dn_dt_bias": [2, 8],
                  "dn_o_norm": [2, 128], "dn_w_out": [2, D, D], "ffn_w_up": [4, D, 2 * FFN], "ffn_conv": [4, 3, 2 * FFN],
                  "ffn_w_down": [4, FFN, D]}

        class Lazy(dict):
            def __missing__(d, name):
                d[name] = nc.dram_tensor(name, list(shapes[name]), F32, kind="ExternalInput").ap()
                return d[name]
        self.inp = Lazy()
        self.y = nc.dram_tensor("y", [S, D], F32, kind="ExternalOutput").ap()

        def dscr(name, shape, dt):
            return nc.dram_tensor(name, list(shape), dt, kind="Internal").ap()
        self.H = dscr("H", [LP, D], F32)
        self.HT = dscr("HT", [FFN, LP], BF16)
        self.OGT = dscr("OGT", [D, LP], BF16)
        self.SG = dscr("SG", [D, LP], BF16)
        self.QT = dscr("QT", [16, 65, LP], BF16)
        self.KT = dscr("KT", [16, 65, LP], BF16)
        self.VA = dscr("VA", [LP, 16, 65], BF16)
        self.CT = dscr("CT", [128, NT, 16], F32)
        self.QKVD = dscr("QKVD", [24, 128, LP], F32)
        self.SGD = dscr("SGD", [LP, D], BF16)
        self.GBD = dscr("GBD", [128, NT, 16], F32)
        self.CAR = dscr("CAR", [128, NT + 1, 16], F32)

    def consts(self):
        kb = self.kb
        kb.phase()
        es = kb.phase_stack
        self.idf = kb.sb("idf", [128, 128], F32)
        self.idb = kb.sb("idb", [128, 128], BF16)
        kb.memset("pool", self.idf.ap, 1.0)
        kb.asel(self.idf.ap, self.idf.ap, [[-1, 128]], ALU.is_equal, 0.0, 0, 1)
        kb.copy("dve", self.idb.ap, self.idf.ap)
        self.const_stack = es
        kb.phase_stack = None
        kb.phase_bufs = []

    def blocks(self):
        out = []
        t = 0
        while t < self.NT:
            n = min(4, self.NT - t)
            out.append((t, n))
            t += n
        return out

    def load_cols(self, dst, src2d, n, stage, pst):
        kb = self.kb
        kb.dma(stage[0:n, :], src2d)
        kb.tr(pst[:, 0:n], stage[0:n, :], self.idf[0:n, 0:n])
        kb.copy("dve", dst, pst[:, 0:n])

    def load_w(self, dst, W, nch, ncols, gcol, stages, col0=0):
        kb = self.kb
        SEG = 2048
        k = 0
        for c in range(nch):
            for s0 in range(0, ncols, SEG):
                w = min(SEG, ncols - s0)
                st = stages[k % len(stages)]
                e = ("dve", "pool")[k % 2]
                k += 1
                kb.dma(st[:, 0:w], W[c * 128:(c + 1) * 128, s0:s0 + w])
                if gcol is not None:
                    kb.ts(e, dst[:, c, col0 + s0:col0 + s0 + w], st[:, 0:w], gcol[:, c:c + 1], ALU.mult)
                else:
                    kb.copy(e, dst[:, c, col0 + s0:col0 + s0 + w], st[:, 0:w])

    def prenormT(self, ht, aT_dst, abf, ss, pst, junk, k):
        kb = self.kb
        kb.memset("dve", ss[:, 0:1], 0.0)
        kb.act(junk.ap, ht.ap, AF.Square, accum=ss[:, 0:1])
        kb.act(ss[:, 1:2], ss[:, 0:1], AF.Sqrt, bias=EPS, scale=1.0 / D)
        kb.recip(ss[:, 2:3], ss[:, 1:2])
        kb.ts("dve", abf.ap, ht.ap, ss[:, 2:3], ALU.mult)
        for c in range(8):
            kb.tr(pst[:, c * 128:(c + 1) * 128], abf[:, c * 128:(c + 1) * 128], self.idb.ap)
        kb.copy(("act", "dve")[k % 2], aT_dst, pst.ap.re("p (c t) -> p c t", c=8))

    def phase_R(self, srcT, nch, W, gpost, hsrc, final=False):
        kb = self.kb
        kb.phase()
        Wb = kb.sb("Wb", [128, nch, D], BF16)
        stages = [kb.sb("wst", [128, 2048], F32) for _ in range(2)]
        self.load_w(Wb, W, nch, D, None, stages)
        g = kb.sb("g", [128, D], F32)
        kb.dma(g.ap, gpost.partition_broadcast(128))
        Sb = [kb.sb("Sb", [128, nch, 512], BF16) for _ in range(2)]
        hts = [kb.sb("ht", [128, D], F32) for _ in range(3)]
        fn = [kb.sb("fn", [128, D], F32) for _ in range(2)]
        ssb = [kb.sb("ss", [128, 8], F32) for _ in range(2)]
        junk = kb.sb("junk", [128, 512], BF16)
        PS = [[kb.ps("psa", [128, 512], F32), kb.ps("psb", [128, 512], F32)] for _ in range(2)]
        srcv = srcT.rearrange("(c p) t -> p c t", p=128)
        blocks = self.blocks()
        kb.dma(Sb[0][:, :, 0:blocks[0][1] * 128], srcv[:, :, 0:blocks[0][1] * 128])
        ti = 0
        for bi, (t0, n) in enumerate(blocks):
            if bi + 1 < len(blocks):
                t1, n1 = blocks[bi + 1]
                kb.dma(Sb[(bi + 1) % 2][:, :, 0:n1 * 128], srcv[:, :, t1 * 128:(t1 + n1) * 128])
            S = Sb[bi % 2]
            for j in range(n):
                tile = t0 + j
                ht = hts[ti % 3]
                kb.dma(ht.ap, hsrc[tile * 128:(tile + 1) * 128, :])
                pa, pb = PS[ti % 2]
                ss = ssb[ti % 2]
                f = fn[ti % 2]
                for c in range(nch):
                    kb.mm(pa.ap, S[:, c, j * 128:(j + 1) * 128], Wb[:, c, 0:512], start=(c == 0), stop=(c == nch - 1))
                for c in range(nch):
                    kb.mm(pb.ap, S[:, c, j * 128:(j + 1) * 128], Wb[:, c, 512:1024], start=(c == 0), stop=(c == nch - 1))
                kb.memset("dve", ss[:, 0:2], 0.0)
                kb.act(junk.ap, pa.ap, AF.Square, accum=ss[:, 0:1])
                kb.act(junk.ap, pb.ap, AF.Square, accum=ss[:, 1:2])
                kb.tt("dve", ss[:, 2:3], ss[:, 0:1], ss[:, 1:2], ALU.add)
                kb.act(ss[:, 3:4], ss[:, 2:3], AF.Sqrt, bias=EPS, scale=1.0 / D)
                kb.recip(ss[:, 4:5], ss[:, 3:4])
                kb.stt("dve", f[:, 0:512], pa.ap, ss[:, 4:5], g[:, 0:512], ALU.mult, ALU.mult)
                kb.stt("dve", f[:, 512:1024], pb.ap, ss[:, 4:5], g[:, 512:1024], ALU.mult, ALU.mult)
                kb.tt("pool", f.ap, f.ap, ht.ap, ALU.add)
                if final:
                    if tile >= 1:
                        kb.dma(self.y[(tile - 1) * 128:tile * 128, :], f.ap, q="pool")
                else:
                    kb.dma(self.H[tile * 128:(tile + 1) * 128, :], f.ap, q="pool")
                ti += 1
        kb.end_phase()

    def phase_F1(self, l):
        kb = self.kb
        kb.phase()
        Wup = kb.sb("Wup", [128, 8, 2 * FFN], BF16)
        stages = [kb.sb("wst", [128, 2048], F32) for _ in range(2)]
        cst = kb.sb("cst", [128, 128], F32)
        gcol = kb.sb("gcol", [128, 8], F32)
        cw = kb.sb("cw", [128, 3, 44], F32)
        pst = kb.ps("pst", [128, 1024], BF16)
        psc = kb.ps("psc", [128, 128], F32)
        self.load_cols(gcol.ap, self.inp["norm_ffn_pre"][l].rearrange("(c p) -> c p", p=128), 8, cst, psc)
        for k in range(3):
            self.load_cols(cw[:, k, :], self.inp["ffn_conv"][l, k].rearrange("(c p) -> c p", p=128), 44, cst, psc)
        self.load_w(Wup, self.inp["ffn_w_up"][l], 8, 2 * FFN, gcol, stages)
        halo = kb.sb("halo", [128, 44, 2], F32)
        kb.memset("pool", halo.ap, 0.0)
        hts = [kb.sb("ht", [128, D], F32) for _ in range(2)]
        abf = [kb.sb("abf", [128, D], BF16) for _ in range(2)]
        ssb = [kb.sb("ss", [128, 8], F32) for _ in range(2)]
        junk = kb.sb("junk", [128, D], BF16)
        aT = [kb.sb("aT", [128, 8, 512], BF16) for _ in range(2)]
        Hb = [kb.sb("Hb", [128, 22, 512], BF16) for _ in range(2)]
        Ug = [kb.sb("Ug", [128, 514], F32) for _ in range(2)]
        Uu = [kb.sb("Uu", [128, 514], F32) for _ in range(2)]
        yg = [kb.sb("yg", [128, 512], F32) for _ in range(2)]
        yu = [kb.sb("yu", [128, 512], F32) for _ in range(2)]
        gg = [kb.sb("gg", [128, 512], F32) for _ in range(2)]
        ptmp = [kb.sb("ptmp", [128, 512], F32) for _ in range(2)]
        PSg = [kb.ps("psg", [128, 512], F32) for _ in range(2)]
        PSu = [kb.ps("psu", [128, 512], F32) for _ in range(2)]
        HTv = self.HT.rearrange("(c p) t -> p c t", p=128)
        hsrc = self.H
        ti = 0
        it = 0
        for bi, (t0, n) in enumerate(self.blocks()):
            TW = n * 128
            A = aT[bi % 2]
            for j in range(n):
                tile = t0 + j
                ht = hts[ti % 2]
                kb.dma(ht.ap, hsrc[tile * 128:(tile + 1) * 128, :])
                self.prenormT(ht, A[:, :, j * 128:(j + 1) * 128], abf[ti % 2], ssb[ti % 2], pst, junk, ti)
                ti += 1
            HB = Hb[bi % 2]
            for fc in range(22):
                pg = PSg[it % 2]; pu = PSu[it % 2]
                ug = Ug[it % 2]; uu = Uu[it % 2]
                for c in range(8):
                    kb.mm(pg[:, 0:TW], Wup[:, c, fc * 128:(fc + 1) * 128], A[:, c, 0:TW], start=(c == 0), stop=(c == 7))
                for c in range(8):
                    kb.mm(pu[:, 0:TW], Wup[:, c, FFN + fc * 128:FFN + (fc + 1) * 128], A[:, c, 0:TW], start=(c == 0), stop=(c == 7))
                a = yg[it % 2]; b = yu[it % 2]
                for (u_, p_, y_, hc) in ((ug, pg, a, fc), (uu, pu, b, 22 + fc)):
                    kb.copy("act", u_[:, 2:2 + TW], p_[:, 0:TW])
                    kb.amul(y_[:, 0:TW], p_[:, 0:TW], cw[:, 2, hc:hc + 1])
                    kb.copy("dve", u_[:, 0:2], halo[:, hc, :])
                    kb.copy("dve", halo[:, hc, :], u_[:, TW:TW + 2])
                    kb.stt("dve", y_[:, 0:TW], u_[:, 1:1 + TW], cw[:, 1, hc:hc + 1], y_[:, 0:TW], ALU.mult, ALU.add)
                    kb.stt("dve", y_[:, 0:TW], u_[:, 0:TW], cw[:, 0, hc:hc + 1], y_[:, 0:TW], ALU.mult, ALU.add)
                kb.act(gg[it % 2][:, 0:TW], a[:, 0:TW], AF.Gelu_apprx_tanh)
                kb.tt("dve", HB[:, fc, 0:TW], gg[it % 2][:, 0:TW], b[:, 0:TW], ALU.mult)
                it += 1
            kb.dma(HTv[:, :, t0 * 128:t0 * 128 + TW], HB[:, :, 0:TW], q="pool")
        kb.end_phase()

    def phase_init(self):
        kb = self.kb
        kb.phase()
        hts = [kb.sb("ht", [128, D], F32) for _ in range(4)]
        for t in range(self.NT):
            ht = hts[t % 4]
            kb.dma(ht.ap, self.inp["xpad"][t * 128:(t + 1) * 128, :])
            kb.dma(self.H[t * 128:(t + 1) * 128, :], ht.ap, q="pool")
        kb.end_phase()

    def ffn(self, l, final):
        self.phase_F1(l)
        self.phase_R(self.HT, 22, self.inp["ffn_w_down"][l], self.inp["norm_ffn_post"][l], self.H, final=final)

    def finish(self):
        self.const_stack.close()
        self.kb.es.close()


def phase_A1(self, l):
    kb = self.kb
    j = l // 2
    NT = self.NT
    kb.phase()
    Win = kb.sb("Win", [128, 8, 4112], BF16)
    stages = [kb.sb("wst", [128, 2048], F32) for _ in range(2)]
    cst = kb.sb("cst", [128, 128], F32)
    gcol = kb.sb("gcol", [128, 8], F32)
    pst = kb.ps("pst", [128, 1024], BF16)
    PSq = [kb.ps("psq", [128, 512], F32) for _ in range(2)]
    PSs = kb.ps("pss", [128, 512], F32)
    PSv = kb.ps("psv", [128, 1024], F32)
    PSm = kb.ps("psm", [128, 512], F32)
    PSr = kb.ps("psr", [16, 512], BF16)
    self.load_cols(gcol.ap, self.inp["norm_mix_pre"][l].rearrange("(c p) -> c p", p=128), 8, cst, PSs)
    self.load_w(Win, self.inp["attn_w_in"][j], 8, 4112, gcol, stages)
    qg = kb.sb("qg", [128, 2], F32)
    for half in range(2):
        kb.dma(qg[half * 64:(half + 1) * 64, 0:1], self.inp["attn_q_norm"][j].rearrange("(p o) -> p o", o=1))
        kb.dma(qg[half * 64:(half + 1) * 64, 1:2], self.inp["attn_k_norm"][j].rearrange("(p o) -> p o", o=1))
    kb.ts("dve", qg[:, 0:1], qg[:, 0:1], 0.125, ALU.mult)
    bfg = kb.sb("bfg", [128, 16], F32)
    kb.dma(bfg.ap, self.inp["attn_b_forget"][j].partition_broadcast(128))
    BD = kb.sb("BD", [128, 128], BF16)
    kb.memset("pool", BD.ap, 1.0)
    kb.asel(BD[:, 0:64], BD[:, 0:64], [[0, 64]], ALU.is_ge, 0.0, 63, -1)
    kb.asel(BD[:, 64:128], BD[:, 64:128], [[0, 64]], ALU.is_ge, 0.0, -64, 1)
    tri = kb.sb("tri", [128, 128], F32)
    kb.memset("pool", tri.ap, 1.0)
    kb.asel(tri.ap, tri.ap, [[1, 128]], ALU.is_ge, 0.0, 0, -1)
    onesf = kb.sb("onesf", [128, 128], F32)
    kb.memset("pool", onesf.ap, 1.0)
    onesb = kb.sb("onesb", [16, 512], BF16)
    kb.memset("pool", onesb.ap, 1.0)
    Ctok = kb.sb("Ctok", [128, NT, 16], F32)
    CAR = kb.sb("CAR", [128, NT + 1, 16], F32)
    kb.memset("dve", CAR[:, 0, :], 0.0)
    hts = [kb.sb("ht", [128, D], F32) for _ in range(2)]
    abf = [kb.sb("abf", [128, D], BF16) for _ in range(2)]
    ssb = [kb.sb("ss", [128, 8], F32) for _ in range(2)]
    junk = kb.sb("junk", [128, D], BF16)
    aT = [kb.sb("aT", [128, 8, 512], BF16) for _ in range(2)]
    sq = [kb.sb("sq", [128, 512], BF16) for _ in range(2)]
    rms = [kb.sb("rms", [128, 512], F32) for _ in range(2)]
    qn = [kb.sb("qn", [128, 512], BF16) for _ in range(3)]
    Va = [kb.sb("Va", [128, 16, 65], BF16) for _ in range(2)]
    V0 = kb.sb("V0", [128, 16, 65], BF16)
    for v in Va + [V0]:
        kb.memset("pool", v.ap, 1.0)
    kb.memset("pool", V0[0:NPAD, :, 64:65], 0.0)
    xs = [kb.sb("xs", [128, 48], F32) for _ in range(2)]
    rb = [kb.sb("rb", [128, 16], BF16) for _ in range(2)]
    rT = [kb.sb("rT", [16, 512], BF16) for _ in range(2)]
    ti = 0
    it = 0
    for bi, (t0, n) in enumerate(self.blocks()):
        TW = n * 128
        c0 = t0 * 128
        A = aT[bi % 2]
        for jj in range(n):
            tile = t0 + jj
            ht = hts[ti % 2]
            kb.dma(ht.ap, self.H[tile * 128:(tile + 1) * 128, :])
            self.prenormT(ht, A[:, :, jj * 128:(jj + 1) * 128], abf[ti % 2], ssb[ti % 2], pst, junk, ti)
            ti += 1
        for which in range(2):
            dst = self.QT if which == 0 else self.KT
            for ch in range(8):
                pq = PSq[it % 2]
                col = which * 1024 + ch * 128
                for c in range(8):
                    kb.mm(pq[:, 0:TW], Win[:, c, col:col + 128], A[:, c, 0:TW], start=(c == 0), stop=(c == 7))
                s_ = sq[it % 2]
                kb.act(s_[:, 0:TW], pq[:, 0:TW], AF.Square)
                kb.mm(PSs[:, 0:TW], BD.ap, s_[:, 0:TW])
                r_ = rms[it % 2]
                kb.act(r_[:, 0:TW], PSs[:, 0:TW], AF.Sqrt, bias=EPS, scale=1.0 / 64)
                kb.recip(r_[:, 0:TW], r_[:, 0:TW])
                q_ = qn[it % 3]
                kb.stt("dve", q_[:, 0:TW], pq[:, 0:TW], qg[:, which:which + 1], r_[:, 0:TW], ALU.mult, ALU.mult)
                kb.dma(dst[2 * ch, 0:64, c0:c0 + TW], q_[0:64, 0:TW], q="pool")
                kb.dma(dst[2 * ch + 1, 0:64, c0:c0 + TW], q_[64:128, 0:TW], q="pool")
                it += 1
        for ch in range(8):
            pq = PSq[it % 2]
            col = 3072 + ch * 128
            for c in range(8):
                kb.mm(pq[:, 0:TW], Win[:, c, col:col + 128], A[:, c, 0:TW], start=(c == 0), stop=(c == 7))
            q_ = qn[it % 3]
            kb.act(q_[:, 0:TW], pq[:, 0:TW], AF.Sigmoid)
            kb.dma(self.SG[ch * 128:(ch + 1) * 128, c0:c0 + TW], q_[:, 0:TW], q="pool")
            it += 1
        kb.dma(self.KT[:, 64, c0:c0 + TW], onesb[:, 0:TW], q="pool")
        for jj in range(n):
            tile = t0 + jj
            tc_ = slice(jj * 128, (jj + 1) * 128)
            for half in range(2):
                for c in range(8):
                    kb.mm(PSv[:, half * 512:(half + 1) * 512], A[:, c, tc_], Win[:, c, 2048 + half * 512:2048 + (half + 1) * 512],
                          start=(c == 0), stop=(c == 7))
            V = V0 if tile == 0 else Va[tile % 2]
            kb.copy("act", V[:, :, 0:64], PSv.ap.re("p (h d) -> p h d", h=16))
            kb.dma(self.VA[tile * 128:(tile + 1) * 128, :, :], V.ap, q="pool")
            for c in range(8):
                kb.mm(PSm[:, 0:16], A[:, c, tc_], Win[:, c, 4096:4112], start=(c == 0), stop=(c == 7))
            x = xs[tile % 2]
            kb.tt("dve", x[:, 0:16], PSm[:, 0:16], bfg.ap, ALU.add)
            kb.act(x[:, 16:32], x[:, 0:16], AF.Exp, scale=-1.0)
            kb.act(x[:, 32:48], x[:, 16:32], AF.Ln, bias=1.0)
            kb.mm(PSm[:, 16:32], tri.ap, x[:, 32:48])
            kb.mm(PSm[:, 32:48], onesf.ap, x[:, 32:48])
            kb.tt("dve", Ctok[:, tile, :], PSm[:, 16:32], CAR[:, tile, :], ALU.add)
            kb.tt("dve", CAR[:, tile + 1, :], PSm[:, 32:48], CAR[:, tile, :], ALU.add)
        R_ = rT[bi % 2]
        for jj in range(n):
            tile = t0 + jj
            r2 = rb[tile % 2]
            kb.tt("dve", r2.ap, CAR[:, t0 + n, :], Ctok[:, tile, :], ALU.subtract)
            kb.tr(PSr[:, jj * 128:(jj + 1) * 128], r2.ap, self.idb.ap)
        kb.copy("dve", R_[:, 0:TW], PSr[:, 0:TW])
        kb.dma(self.QT[:, 64, c0:c0 + TW], R_[:, 0:TW], q="pool")
    kb.dma(self.CT, Ctok.ap, q="pool")
    kb.dma(self.CAR, CAR.ap, q="pool")
    kb.end_phase()


def phase_A2(self, l):
    kb = self.kb
    NT = self.NT
    LP = self.LP
    kb.phase()
    Ctok = kb.sb("Ctok", [128, NT, 16], F32)
    CAR = kb.sb("CAR", [128, NT + 1, 16], F32)
    kb.dma(Ctok.ap, self.CT)
    kb.dma(CAR.ap, self.CAR)
    masks = []
    for jj in range(4):
        m = kb.sb("mask", [128, 512], BF16)
        kb.memset("pool", m.ap, 0.0)
        kb.asel(m.ap, m.ap, [[1, 512]], ALU.is_ge, NEGBIG, -128 * jj, -1)
        masks.append(m)
    onesf = kb.sb("onesf", [128, 64], F32)
    kb.memset("pool", onesf.ap, 1.0)
    KTh = [kb.sb("KTh", [128, LP], BF16) for _ in range(2)]
    VAh = [kb.sb("VAh", [128, NT, 65], BF16) for _ in range(2)]
    QTb = [kb.sb("QTb", [128, 512], BF16) for _ in range(2)]
    for b_ in KTh + QTb:
        kb.memset("pool", b_[64:128, :], 0.0)
    SGb = [kb.sb("SGb", [64, 512], BF16) for _ in range(2)]
    bias = [kb.sb("bias", [128, NT], F32) for _ in range(2)]
    P = [kb.sb("P", [128, 512], BF16) for _ in range(3)]
    oacc = [kb.sb("oacc", [65, 512], F32) for _ in range(2)]
    rden = [kb.sb("rden", [128, 512], F32) for _ in range(2)]
    for b_ in rden:
        kb.memset("pool", b_.ap, 0.0)
    sel = kb.sb("sel", [128, 64], F32)
    kb.memset("pool", sel.ap, 0.0)
    kb.memset("pool", sel[64:96, :], 1.0)
    kb.asel(sel[64:96, :], sel[64:96, :], [[0, 64]], ALU.is_ge, 0.0, 0, -1)
    tmp = [kb.sb("tmp", [64, 512], F32) for _ in range(2)]
    og = [kb.sb("og", [64, 512], BF16) for _ in range(2)]
    PSs = [kb.ps("pss", [128, 512], F32) for _ in range(3)]
    PSo = [kb.ps("pso", [65, 512], F32) for _ in range(2)]
    PSb = [kb.ps("psb", [64, 512], F32) for _ in range(2)]
    VAv = self.VA.rearrange("(t p) h e -> p t h e", p=128)
    blocks = self.blocks()
    it = 0
    ip = 0
    def load_head(hh):
        kb.dma(KTh[hh % 2][0:65, :], self.KT[hh])
        for a in range(0, NT, 16):
            b = min(NT, a + 16)
            kb.dma(VAh[hh % 2][:, a:b, :], VAv[:, a:b, hh, :])
    load_head(0)
    for h in range(16):
        if h + 1 < 16:
            load_head(h + 1)
        K_ = KTh[h % 2]
        V_ = VAh[h % 2]
        for bi, (t0, n) in enumerate(blocks):
            TW = n * 128
            c0 = t0 * 128
            Q_ = QTb[it % 2]
            S_ = SGb[it % 2]
            kb.dma(Q_[0:65, 0:TW], self.QT[h, :, c0:c0 + TW])
            kb.dma(S_[:, 0:TW], self.SG[h * 64:(h + 1) * 64, c0:c0 + TW])
            nkt = t0 + n
            b_ = bias[it % 2]
            kb.ts("dve", b_[:, 0:nkt], Ctok[:, 0:nkt, h], CAR[:, t0 + n, h:h + 1], ALU.subtract)
            po = PSo[it % 2]
            def emit_s(kt, ipk):
                ps = PSs[ipk % 3]
                diag = kt >= t0
                kb.mm(ps[:, 0:TW], K_[:, kt * 128:(kt + 1) * 128], Q_[:, 0:TW], start=True, stop=not diag)
                if diag:
                    kb.mm(ps[:, 0:TW], self.idb.ap, masks[kt - t0][:, 0:TW], start=False, stop=True)
                p_ = P[ipk % 3]
                kb.act(p_[:, 0:TW], ps[:, 0:TW], AF.Exp, bias=b_[:, kt:kt + 1])
            emit_s(0, ip)
            for kt in range(nkt):
                if kt + 1 < nkt:
                    emit_s(kt + 1, ip + 1)
                kb.mm(po[:, 0:TW], V_[:, kt, :], P[ip % 3][:, 0:TW], start=(kt == 0), stop=(kt == nkt - 1))
                ip += 1
            oa = oacc[it % 2]
            rd = rden[it % 2]
            kb.copy("act", oa[:, 0:TW], po[:, 0:TW])
            kb.ts("dve", rd[64:65, 0:TW], oa[64:65, 0:TW], 1e-30, ALU.max)
            kb.recip(rd[64:65, 0:TW], rd[64:65, 0:TW])
            pb = PSb[it % 2]
            kb.mm(pb[:, 0:TW], sel.ap, rd[:, 0:TW])
            t_ = tmp[it % 2]
            kb.tt("dve", t_[:, 0:TW], oa[0:64, 0:TW], pb[:, 0:TW], ALU.mult)
            o_ = og[it % 2]
            kb.tt("pool", o_[:, 0:TW], t_[:, 0:TW], S_[:, 0:TW], ALU.mult)
            kb.dma(self.OGT[h * 64:(h + 1) * 64, c0:c0 + TW], o_[:, 0:TW], q="pool")
            it += 1
    kb.end_phase()


Prog.phase_A1 = phase_A1
Prog.phase_A2 = phase_A2


def attn_layer(self, l):
    self.phase_A1(l)
    self.phase_A2(l)
    self.phase_R(self.OGT, 8, self.inp["attn_w_out"][l // 2], self.inp["norm_mix_post"][l], self.H)


Prog.attn_layer = attn_layer


def phase_D1(self, l):
    kb = self.kb
    j = l // 2
    NT = self.NT
    kb.phase()
    Win = kb.sb("Win", [128, 8, 4112], BF16)
    stages = [kb.sb("wst", [128, 2048], F32) for _ in range(2)]
    cst = kb.sb("cst", [128, 128], F32)
    gcol = kb.sb("gcol", [128, 8], F32)
    cw = kb.sb("cw", [128, 4, 24], F32)
    pst = kb.ps("pst", [128, 1024], BF16)
    PSq = [kb.ps("psq", [128, 512], F32) for _ in range(2)]
    PSs = kb.ps("pss", [128, 512], F32)
    PSv = kb.ps("psv", [128, 1024], F32)
    PSm = kb.ps("psm", [128, 512], F32)
    self.load_cols(gcol.ap, self.inp["norm_mix_pre"][l].rearrange("(c p) -> c p", p=128), 8, cst, PSs)
    for k in range(4):
        self.load_cols(cw[:, k, :], self.inp["dn_conv"][j, k].rearrange("(c p) -> c p", p=128), 24, cst, PSs)
    self.load_w(Win, self.inp["dn_w_in"][j], 8, 4112, gcol, stages)
    dtb = kb.sb("dtb", [128, 8], F32)
    nA = kb.sb("nA", [128, 8], F32)
    kb.dma(dtb.ap, self.inp["dn_dt_bias"][j].partition_broadcast(128))
    kb.dma(nA.ap, self.inp["dn_a_log"][j].partition_broadcast(128))
    kb.act(nA.ap, nA.ap, AF.Exp)
    kb.ts("dve", nA.ap, nA.ap, -1.0, ALU.mult)
    tri = kb.sb("tri", [128, 128], F32)
    kb.memset("pool", tri.ap, 1.0)
    kb.asel(tri.ap, tri.ap, [[1, 128]], ALU.is_ge, 0.0, 0, -1)
    onesb = kb.sb("onesb", [128, 128], BF16)
    kb.memset("pool", onesb.ap, 1.0)
    halo = kb.sb("halo", [128, 24, 3], F32)
    kb.memset("pool", halo.ap, 0.0)
    GB = kb.sb("GB", [128, NT, 16], F32)
    hts = [kb.sb("ht", [128, D], F32) for _ in range(2)]
    abf = [kb.sb("abf", [128, D], BF16) for _ in range(2)]
    ssb = [kb.sb("ss", [128, 8], F32) for _ in range(2)]
    junk = kb.sb("junk", [128, D], BF16)
    aT = [kb.sb("aT", [128, 8, 512], BF16) for _ in range(2)]
    U = [kb.sb("U", [128, 515], F32) for _ in range(2)]
    yv = [kb.sb("yv", [128, 512], F32) for _ in range(2)]
    z = [kb.sb("z", [128, 512], F32) for _ in range(3)]
    sq = [kb.sb("sq", [128, 512], BF16) for _ in range(2)]
    rms = [kb.sb("rms", [128, 512], F32) for _ in range(2)]
    sg = [kb.sb("sg", [128, D], BF16) for _ in range(2)]
    xs = [kb.sb("xs", [128, 32], F32) for _ in range(2)]
    ti = 0
    it = 0
    for bi, (t0, n) in enumerate(self.blocks()):
        TW = n * 128
        c0 = t0 * 128
        A = aT[bi % 2]
        for jj in range(n):
            tile = t0 + jj
            ht = hts[ti % 2]
            kb.dma(ht.ap, self.H[tile * 128:(tile + 1) * 128, :])
            self.prenormT(ht, A[:, :, jj * 128:(jj + 1) * 128], abf[ti % 2], ssb[ti % 2], pst, junk, ti)
            ti += 1
        for fc in range(24):
            pq = PSq[it % 2]
            for c in range(8):
                kb.mm(pq[:, 0:TW], Win[:, c, fc * 128:(fc + 1) * 128], A[:, c, 0:TW], start=(c == 0), stop=(c == 7))
            u = U[it % 2]
            kb.copy("dve", u[:, 0:3], halo[:, fc, :])
            kb.copy("act", u[:, 3:3 + TW], pq[:, 0:TW])
            kb.copy("dve", halo[:, fc, :], u[:, TW:TW + 3])
            y = yv[it % 2]
            kb.ts("dve", y[:, 0:TW], u[:, 3:3 + TW], cw[:, 3, fc:fc + 1], ALU.mult)
            for k in (2, 1, 0):
                kb.stt("dve", y[:, 0:TW], u[:, k:k + TW], cw[:, k, fc:fc + 1], y[:, 0:TW], ALU.mult, ALU.add)
            z_ = z[it % 3]
            kb.act(z_[:, 0:TW], y[:, 0:TW], AF.Silu)
            if fc < 16:
                s_ = sq[it % 2]
                kb.act(s_[:, 0:TW], z_[:, 0:TW], AF.Square)
                kb.mm(PSs[:, 0:TW], onesb.ap, s_[:, 0:TW])
                r_ = rms[it % 2]
                kb.act(r_[:, 0:TW], PSs[:, 0:TW], AF.Sqrt, bias=EPS)
                kb.recip(r_[:, 0:TW], r_[:, 0:TW])
                if fc < 8:
                    kb.stt("dve", z_[:, 0:TW], z_[:, 0:TW], float(HD_D ** -0.5), r_[:, 0:TW], ALU.mult, ALU.mult)
                else:
                    kb.tt("dve", z_[:, 0:TW], z_[:, 0:TW], r_[:, 0:TW], ALU.mult)
            kb.dma(self.QKVD[fc, :, c0:c0 + TW], z_[:, 0:TW], q="pool")
            it += 1
        for jj in range(n):
            tile = t0 + jj
            tc_ = slice(jj * 128, (jj + 1) * 128)
            for half in range(2):
                for c in range(8):
                    kb.mm(PSv[:, half * 512:(half + 1) * 512], A[:, c, tc_], Win[:, c, 3072 + half * 512:3072 + (half + 1) * 512],
                          start=(c == 0), stop=(c == 7))
            s2 = sg[tile % 2]
            kb.act(s2.ap, PSv.ap, AF.Silu)
            kb.dma(self.SGD[tile * 128:(tile + 1) * 128, :], s2.ap, q="pool")
            for c in range(8):
                kb.mm(PSm[:, 0:16], A[:, c, tc_], Win[:, c, 4096:4112], start=(c == 0), stop=(c == 7))
            x = xs[tile % 2]
            kb.act(GB[:, tile, 0:8], PSm[:, 0:8], AF.Sigmoid)
            kb.tt("dve", x[:, 0:8], PSm[:, 8:16], dtb.ap, ALU.add)
            kb.act(x[:, 8:16], x[:, 0:8], AF.Exp)
            kb.act(x[:, 16:24], x[:, 8:16], AF.Ln, bias=1.0)
            kb.tt("dve", x[:, 24:32], x[:, 16:24], nA.ap, ALU.mult)
            kb.mm(PSm[:, 16:24], tri.ap, x[:, 24:32])
            kb.copy("dve", GB[:, tile, 8:16], PSm[:, 16:24])
    kb.dma(self.GBD, GB.ap, q="pool")
    kb.end_phase()


def phase_D2(self, l, G=4):
    kb = self.kb
    j = l // 2
    NT = self.NT
    kb.phase()
    GB = kb.sb("GB", [128, NT, 16], F32)
    kb.dma(GB.ap, self.GBD)
    onrm = kb.sb("onrm", [128, 128], F32)
    kb.dma(onrm.ap, self.inp["dn_o_norm"][j].partition_broadcast(128))
    onesf = kb.sb("onesf", [128, 128], F32)
    kb.memset("pool", onesf.ap, 1.0)
    idf = self.idf
    idb = self.idb
    NMsT = kb.sb("NMsT", [128, 128], F32)
    kb.memset("pool", NMsT.ap, 0.0)
    kb.asel(NMsT.ap, NMsT.ap, [[1, 128]], ALU.is_gt, NEGBIG, 0, -1)
    NPs = kb.sb("NPs", [128, 128], F32)
    kb.memset("pool", NPs.ap, 0.0)
    kb.asel(NPs.ap, NPs.ap, [[-1, 128]], ALU.is_gt, -NEGBIG, 0, 1)
    BDm = kb.sb("BDm", [128, 128], F32)
    kb.memset("pool", BDm.ap, 1.0)
    kb.asel(BDm[:, 0:64], BDm[:, 0:64], [[0, 64]], ALU.is_ge, 0.0, 63, -1)
    kb.asel(BDm[:, 64:128], BDm[:, 64:128], [[0, 64]], ALU.is_ge, 0.0, -64, 1)
    ST = [kb.sb("ST", [128, 128], F32) for _ in range(8)]
    STb = [kb.sb("STb", [128, 128], BF16) for _ in range(8)]
    for s_ in ST + STb:
        kb.memset("dve", s_.ap, 0.0)
    X24 = [kb.sb("X24", [128, 24, 128], F32) for _ in range(2)]
    X24b = [kb.sb("X24b", [128, 24, 128], BF16) for _ in range(2)]
    SGt = [kb.sb("SGt", [128, D], BF16) for _ in range(2)]
    Og = [kb.sb("Og", [128, D], BF16) for _ in range(2)]
    OgT = [kb.sb("OgT", [128, 8, 128], BF16) for _ in range(2)]
    tl = [kb.sb("tl", [128, 32], F32) for _ in range(2)]
    sets = []
    f32names = ["DG", "Gs", "DTs", "DTi", "Dn", "A", "B", "Ad", "Ao", "Bd", "Bo", "P", "Pt",
                "U0s", "EGbc", "tmp", "on"]
    bf16names = ["R", "QKT", "kbg", "kd", "vb", "WT", "Us", "qdT",
                 "Adh", "Adl", "Bdh", "Bdl", "Xah", "Xal", "Xbh", "Xbl", "Xtah", "Xtal", "Xtbh", "Xtbl",
                 "Pth", "Ptl", "Ph", "Pl", "Aoh", "Aol", "Zh", "Zl"]
    for g in range(G):
        d = {}
        for nm in f32names:
            w = 256 if nm in ("DG", "Gs") else 128
            d[nm] = kb.sb(nm, [128, w], F32)
        for nm in bf16names:
            d[nm] = kb.sb(nm, [128, 128], BF16)
        d["cols"] = kb.sb("cols", [128, 8], F32)
        bA = kb.ps("bA", [128, 512], F32)
        bB = kb.ps("bB", [128, 512], F32)
        d["bA"] = bA
        d["PSG"], d["PSK"], d["PSQ"] = kb.carve(bA, [(0, 256), (256, 384), (384, 512)])
        d["C0"], d["C1"], d["C2"], d["C3"] = kb.carve(bB, [(0, 128), (128, 256), (256, 384), (384, 512)])
        d["D0"], d["D1"], d["D2"], d["D3"] = d["C0"], d["C1"], d["C2"], d["C3"]
        d["C2b"] = View(d["C2"], d["C2"].t.bitcast(BF16)[:, 0:128])
        d["C3b"] = View(d["C3"], d["C3"].t.bitcast(BF16)[:, 0:128])
        sets.append(d)
    PSot = View(sets[0]["bA"], sets[0]["bA"].t[:].bitcast(BF16))
    Xv = self.QKVD.rearrange("f d t -> d f t")
    OGTv = self.OGT.rearrange("(c p) t -> p c t", p=128)

    def head(n, hd, d, X, Xb, sgt, og, t_):
        qT = X[:, hd, :]
        qTb = Xb[:, hd, :]
        kTb = Xb[:, 8 + hd, :]
        vTb = Xb[:, 16 + hd, :]
        beta = GB[:, n, hd:hd + 1]
        gc = GB[:, n, 8 + hd:9 + hd]
        ngc = t_[:, hd:hd + 1]
        bg = t_[:, 16 + hd:17 + hd]
        cols = d["cols"]
        kb.ts("dve", d["DG"][:, 0:128], idf.ap, gc, ALU.mult)
        kb.ts("dve", d["DG"][:, 128:256], idf.ap, beta, ALU.mult)
        kb.mm(d["PSG"].ap, onesf.ap, d["DG"].ap)
        kb.copy("act", d["Gs"].ap, d["PSG"].ap)
        kb.mm(d["PSK"].ap, kTb, kTb)
        kb.mm(d["PSQ"].ap, kTb, qTb)
        yield
        G_ = d["Gs"][:, 0:128]
        kb.tt("dve", d["tmp"].ap, G_, NMsT.ap, ALU.add)
        kb.act(d["DTs"].ap, d["tmp"].ap, AF.Exp, bias=ngc)
        kb.tt("pool", d["DTi"].ap, d["DTs"].ap, idf.ap, ALU.add)
        kb.tt("dve", d["Dn"].ap, G_, NPs.ap, ALU.add)
        kb.act(d["Dn"].ap, d["Dn"].ap, AF.Exp, bias=gc, scale=-1.0)
        kb.act(d["EGbc"].ap, G_, AF.Exp)
        kb.act(cols[:, 0:1], gc, AF.Exp, bias=d["Gs"][:, 127:128], scale=-1.0)
        kb.act(cols[:, 1:2], d["Gs"][:, 127:128], AF.Exp)
        yield
        kb.stt("dve", d["A"].ap, d["PSK"].ap, beta, d["Dn"].ap, ALU.mult, ALU.mult)
        kb.tt("dve", d["B"].ap, d["PSK"].ap, d["DTs"].ap, ALU.mult)
        kb.tt("pool", d["B"].ap, d["B"].ap, d["Gs"][:, 128:256], ALU.mult)
        kb.tt("dve", d["QKT"].ap, d["PSQ"].ap, d["DTi"].ap, ALU.mult)
        kb.tt("pool", d["Ad"].ap, d["A"].ap, BDm.ap, ALU.mult)
        kb.tt("pool", d["Ao"].ap, d["A"].ap, d["Ad"].ap, ALU.subtract)
        kb.tt("pool", d["Bd"].ap, d["B"].ap, BDm.ap, ALU.mult)
        kb.tt("pool", d["Bo"].ap, d["B"].ap, d["Bd"].ap, ALU.subtract)
        kb.tt("dve", d["P"].ap, idf.ap, d["Ad"].ap, ALU.subtract)
        kb.tt("dve", d["Pt"].ap, idf.ap, d["Bd"].ap, ALU.subtract)
        yield
        def split(src, hi, lo, e="dve"):
            kb.copy("act", d[hi].ap, src)
            kb.tt(e, d[lo].ap, src, d[hi].ap, ALU.subtract)

        def mm3(out, A_, B_):
            kb.mm(out, d[A_[0]].ap, d[B_[0]].ap, start=True, stop=False)
            kb.mm(out, d[A_[0]].ap, d[B_[1]].ap, start=False, stop=False)
            kb.mm(out, d[A_[1]].ap, d[B_[0]].ap, start=False, stop=True)
        split(d["Ad"].ap, "Adh", "Adl", "pool")
        split(d["Bd"].ap, "Bdh", "Bdl", "pool")
        split(d["Pt"].ap, "Pth", "Ptl")
        split(d["Ao"].ap, "Aoh", "Aol", "pool")
        yield
        Xc, Xtc = ("Adh", "Adl"), ("Bdh", "Bdl")
        PT = ("Pth", "Ptl")
        for k in range(1, 6):
            Xn = ("Xah", "Xal") if k % 2 else ("Xbh", "Xbl")
            Xtn = ("Xtah", "Xtal") if k % 2 else ("Xtbh", "Xtbl")
            mm3(d["C0"].ap, Xtc, Xc)
            if k < 5:
                mm3(d["C1"].ap, Xc, Xtc)
            yield
            split(d["C0"].ap, Xn[0], Xn[1])
            if k < 5:
                split(d["C1"].ap, Xtn[0], Xtn[1])
            yield
            mm3(d["C2"].ap, PT, Xn)
            mm3(d["C3"].ap, Xn, PT)
            yield
            kb.tt("dve", d["P"].ap, d["P"].ap, d["C2"].ap, ALU.add)
            kb.tt("dve", d["Pt"].ap, d["Pt"].ap, d["C3"].ap, ALU.add)
            split(d["Pt"].ap, "Pth", "Ptl")
            Xc, Xtc = Xn, Xtn
            yield
        split(d["P"].ap, "Ph", "Pl", "pool")
        mm3(d["C0"].ap, ("Aoh", "Aol"), PT)
        yield
        split(d["C0"].ap, "Zh", "Zl")
        yield
        mm3(d["C1"].ap, ("Ph", "Pl"), ("Zh", "Zl"))
        kb.tt("dve", d["R"].ap, d["Pt"].ap, d["C1"].ap, ALU.subtract)
        kb.tr(d["C2b"], kTb, idb.ap)
        kb.tr(d["C3b"], vTb, idb.ap)
        kb.ts("dve", d["kbg"].ap, d["C2b"], bg, ALU.mult)
        kb.ts("dve", d["kd"].ap, d["C2b"], cols[:, 0:1], ALU.mult)
        kb.ts("dve", d["vb"].ap, d["C3b"], beta, ALU.mult)
        kb.tt("pool", d["qdT"].ap, qT, d["EGbc"].ap, ALU.mult)
        yield
        kb.mm(d["D0"].ap, d["kbg"].ap, d["R"].ap)
        kb.mm(d["D1"].ap, d["R"].ap, d["vb"].ap)
        kb.copy("act", d["WT"].ap, d["D0"].ap)
        kb.copy("act", d["U0s"].ap, d["D1"].ap)
        yield
        S_ = ST[hd]
        Sb_ = STb[hd]
        kb.mm(d["D2"].ap, d["WT"].ap, Sb_.ap)
        kb.tt("dve", d["Us"].ap, d["U0s"].ap, d["D2"].ap, ALU.subtract)
        kb.mm(d["D3"].ap, d["qdT"].ap, Sb_.ap, start=True, stop=False)
        kb.mm(d["D3"].ap, d["QKT"].ap, d["Us"].ap, start=False, stop=True)
        kb.memset("dve", cols[:, 2:3], 0.0)
        kb.act(d["tmp"].ap, d["D3"].ap, AF.Square, accum=cols[:, 2:3])
        kb.act(cols[:, 3:4], cols[:, 2:3], AF.Sqrt, bias=EPS, scale=1.0 / HD_D)
        kb.recip(cols[:, 4:5], cols[:, 3:4])
        kb.stt("dve", d["on"].ap, d["D3"].ap, cols[:, 4:5], onrm.ap, ALU.mult, ALU.mult)
        kb.mm(d["C0"].ap, d["kd"].ap, d["Us"].ap)
        kb.stt("dve", S_.ap, S_.ap, cols[:, 1:2], d["C0"].ap, ALU.mult, ALU.add)
        kb.copy("act", Sb_.ap, S_.ap)
        kb.tt("pool", og[:, hd * 128:(hd + 1) * 128], d["on"].ap, sgt[:, hd * 128:(hd + 1) * 128], ALU.mult)
        yield

    kb.dma(X24[0].ap, Xv[:, :, 0:128])
    for n in range(NT):
        if n + 1 < NT:
            kb.dma(X24[(n + 1) % 2].ap, Xv[:, :, (n + 1) * 128:(n + 2) * 128])
        X = X24[n % 2]
        Xb = X24b[n % 2]
        kb.copy("act", Xb[:, 0:12, :], X[:, 0:12, :])
        kb.copy("dve", Xb[:, 12:24, :], X[:, 12:24, :])
        sgt = SGt[n % 2]
        kb.dma(sgt.ap, self.SGD[n * 128:(n + 1) * 128, :])
        t_ = tl[n % 2]
        kb.ts("dve", t_[:, 0:8], GB[:, n, 8:16], -1.0, ALU.mult)
        kb.act(t_[:, 8:16], GB[:, n, 8:16], AF.Exp)
        kb.tt("dve", t_[:, 16:24], t_[:, 8:16], GB[:, n, 0:8], ALU.mult)
        og = Og[n % 2]
        for h0 in range(0, 8, G):
            gens = [head(n, h0 + g, sets[g], X, Xb, sgt, og, t_) for g in range(G)]
            alive = list(gens)
            while alive:
                nxt = []
                for g_ in alive:
                    try:
                        next(g_)
                        nxt.append(g_)
                    except StopIteration:
                        pass
                alive = nxt
        for c in range(8):
            kb.tr(PSot[:, c * 128:(c + 1) * 128], og[:, c * 128:(c + 1) * 128], idb.ap)
        ot = OgT[n % 2]
        kb.copy("act", ot.ap, PSot.re("p (c t) -> p c t", c=8))
        kb.dma(OGTv[:, :, n * 128:(n + 1) * 128], ot.ap, q="pool")
    kb.end_phase()


Prog.phase_D1 = phase_D1
Prog.phase_D2 = phase_D2


def dn_layer(self, l):
    self.phase_D1(l)
    self.phase_D2(l)
    self.phase_R(self.OGT, 8, self.inp["dn_w_out"][l // 2], self.inp["norm_mix_post"][l], self.H)


Prog.dn_layer = dn_layer


_SEQ = 8192
_NT = (_SEQ + NMETA + NPAD) // 128


def build_full(NT):
    p = Prog(NT)
    p.consts()
    p.phase_init()
    for l in range(4):
        if l % 2 == 0:
            p.attn_layer(l)
        else:
            p.dn_layer(l)
        p.ffn(l, final=(l == 3))
    p.finish()
    return p


def kernel(**inputs):
    x = np.asarray(inputs["x"], dtype=np.float32)
    B = x.shape[0]
    meta = np.asarray(inputs["meta_tokens"], dtype=np.float32)
    p = build_full(_NT)
    shared = {k: np.ascontiguousarray(np.asarray(inputs[k], dtype=np.float32)) for k in p.inp if k != "xpad"}
    real = {0: 0, 1: 1, 4: 2, 5: 3}
    zshared = {k: np.zeros_like(v) for k, v in shared.items()}
    zx = np.zeros((NPAD + NMETA + x.shape[1], D), np.float32)
    in_maps = []
    for c in range(8):
        if c in real and real[c] < B:
            b = real[c]
            xpad = np.concatenate([np.zeros((NPAD, D), np.float32), meta, x[b]], axis=0)
            m = dict(shared)
            m["xpad"] = np.ascontiguousarray(xpad)
        else:
            m = dict(zshared)
            m["xpad"] = zx
        in_maps.append(m)
    res = run_bass_kernel_spmd(p.nc, in_maps, core_ids=list(range(8)))
    core_of = {b: c for c, b in real.items()}
    out = np.stack([np.asarray(res.results[core_of[b]]["y"], dtype=np.float32) for b in range(B)], axis=0)
    return out
```

```python
import numpy as np
from contextlib import ExitStack
import concourse.bass as bass
import concourse.mybir as mybir
from concourse.bass_utils import run_bass_kernel_spmd

F32 = mybir.dt.float32
BF16 = mybir.dt.bfloat16
AF = mybir.ActivationFunctionType
ALU = mybir.AluOpType
AX = mybir.AxisListType

D = 1024
NH_A = 16
HD_A = 64
NH_D = 8
HD_D = 128
FFN = 2816
EPS = 1e-6
NPAD = 112
NMETA = 16
NEGBIG = -30000.0


class Buf:
    def __init__(self, t, name, space):
        self.t = t
        self.name = name
        self.space = space
        self.w = {}
        self.r = {}
        self.sem = None

    def __getitem__(self, k):
        return View(self, self.t[k])

    @property
    def ap(self):
        return View(self, self.t[:])


class View:
    def __init__(self, buf, ap):
        self.buf = buf
        self.ap = ap

    def __getitem__(self, k):
        return View(self.buf, self.ap[k])

    def re(self, s, **kw):
        return View(self.buf, self.ap.rearrange(s, **kw))

    def bc(self, shape):
        return View(self.buf, self.ap.to_broadcast(shape))


def _ap(x):
    return x.ap if isinstance(x, View) else x


class KB:
    def __init__(self):
        self.nc = bass.Bass("TRN2", target_bir_lowering=False)
        nc = self.nc
        self.es = ExitStack()
        self.eng = {"pe": nc.tensor, "dve": nc.vector, "act": nc.scalar, "pool": nc.gpsimd, "sp": nc.sync}
        self.cnt = {}
        self.esem = {}
        for e in ("pe", "dve", "act", "pool"):
            self.esem[e] = self.es.enter_context(nc.semaphore("cnt_" + e))
            self.cnt[e] = 0
        self.seen = {e: {} for e in self.eng}
        self.sem_pool = []
        self.live_sems = []
        self.phase_stack = None
        self.phase_bufs = []
        self.uid = 0
        self.dq = 0

    def phase(self):
        self.phase_stack = ExitStack()
        self.phase_bufs = []

    def end_phase(self):
        self.barrier()
        for b in self.phase_bufs:
            if b.sem is not None:
                self.sem_pool.append(b.sem)
                b.sem = None
        self.phase_stack.close()
        self.phase_stack = None
        self.phase_bufs = []

    def carve(self, buf, slices):
        out = []
        for (a, b) in slices:
            sbuf = Buf(buf.t[:, a:b], buf.name + "_%d" % a, buf.space)
            sbuf.w = buf.w
            sbuf.r = buf.r
            self.phase_bufs.append(sbuf)
            out.append(sbuf)
        return out

    def sb(self, name, shape, dt):
        self.uid += 1
        t = self.phase_stack.enter_context(self.nc.sbuf_tensor("%s_%d" % (name, self.uid), list(shape), dt))
        b = Buf(t, name, "sb")
        self.phase_bufs.append(b)
        return b

    def ps(self, name, shape, dt):
        self.uid += 1
        t = self.phase_stack.enter_context(self.nc.psum_tensor("%s_%d" % (name, self.uid), list(shape), dt))
        b = Buf(t, name, "ps")
        self.phase_bufs.append(b)
        return b

    def _getsem(self, b):
        if b.sem is None:
            if self.sem_pool:
                b.sem = self.sem_pool.pop()
            else:
                s = self.es.enter_context(self.nc.semaphore("dsem%d" % len(self.live_sems)))
                b.sem = [s, 0]
                self.live_sems.append(b.sem)
        return b.sem

    def _wait(self, e, deps):
        seen = self.seen[e]
        for sid, (sem, val) in deps.items():
            if seen.get(sid, 0) < val:
                self.eng[e].wait_ge(sem, val)
                seen[sid] = val

    def _deps(self, outs, ins):
        deps = {}

        def add(d):
            for sid, (sem, val) in d.items():
                if sid not in deps or deps[sid][1] < val:
                    deps[sid] = (sem, val)
        for v in ins:
            if isinstance(v, View):
                add(v.buf.w)
        for v in outs:
            if isinstance(v, View):
                add(v.buf.w)
                add(v.buf.r)
        return deps

    def _record(self, outs, ins, sem, val):
        sid = id(sem)
        for v in ins:
            if isinstance(v, View):
                v.buf.r[sid] = (sem, val)
        for v in outs:
            if isinstance(v, View):
                v.buf.w[sid] = (sem, val)

    def op(self, e, fn, outs, ins, pe_acc=False):
        deps = self._deps(outs, ins)
        if e == "pe":
            deps.pop(id(self.esem["pe"]), None)
        self._wait(e, deps)
        ins_ = fn()
        self.cnt[e] += 1
        ins_.then_inc(self.esem[e], 1)
        self._record(outs, ins, self.esem[e], self.cnt[e])
        return ins_

    def dma(self, out, in_, q=None, **kw):
        if q is None:
            q = "sp"
        sbv = out if isinstance(out, View) else in_
        assert isinstance(sbv, View)
        outs = [out] if isinstance(out, View) else []
        ins = [in_] if isinstance(in_, View) else []
        deps = self._deps(outs, ins)
        self._wait(q, deps)
        se = self._getsem(sbv.buf)
        self.eng[q].dma_start(out=_ap(out), in_=_ap(in_), **kw).then_inc(se[0], 16)
        se[1] += 16
        self._record(outs, ins, se[0], se[1])

    def barrier(self):
        allsems = {}
        for e in ("pe", "dve", "act", "pool"):
            allsems[id(self.esem[e])] = (self.esem[e], self.cnt[e])
        for se in self.live_sems:
            allsems[id(se[0])] = (se[0], se[1])
        for e in self.eng:
            self._wait(e, {k: v for k, v in allsems.items() if v[1] > 0})

    def mm(self, out, lhsT, rhs, start=True, stop=True):
        return self.op("pe", lambda: self.nc.tensor.matmul(_ap(out), lhsT=_ap(lhsT), rhs=_ap(rhs), start=start, stop=stop),
                       [out], [lhsT, rhs])

    def tr(self, out, in_, ident):
        return self.op("pe", lambda: self.nc.tensor.transpose(_ap(out), _ap(in_), _ap(ident)), [out], [in_, ident])

    def act(self, out, in_, func, bias=None, scale=None, accum=None, e="act"):
        kw = {}
        ins = [in_]
        outs = [out]
        if bias is not None:
            kw["bias"] = _ap(bias)
            ins.append(bias)
        if scale is not None:
            kw["scale"] = _ap(scale)
            ins.append(scale)
        if accum is not None:
            kw["accum_out"] = _ap(accum)
            outs.append(accum)
        return self.op("act", lambda: self.nc.scalar.activation(out=_ap(out), in_=_ap(in_), func=func, **kw), outs, ins)

    def tt(self, e, out, a, b, op):
        return self.op(e, lambda: self.eng[e].tensor_tensor(out=_ap(out), in0=_ap(a), in1=_ap(b), op=op), [out], [a, b])

    def ts(self, e, out, a, s1, op0, s2=None, op1=None, accum=None):
        kw = {}
        outs = [out]
        if op1 is not None:
            kw["op1"] = op1
        if accum is not None:
            kw["accum_out"] = _ap(accum)
            outs.append(accum)
        return self.op(e, lambda: self.eng[e].tensor_scalar(out=_ap(out), in0=_ap(a), scalar1=_ap(s1), scalar2=_ap(s2), op0=op0, **kw),
                       outs, [a, s1, s2])

    def stt(self, e, out, a, s, b, op0, op1):
        return self.op(e, lambda: self.eng[e].scalar_tensor_tensor(out=_ap(out), in0=_ap(a), scalar=_ap(s), in1=_ap(b), op0=op0, op1=op1),
                       [out], [a, s, b])

    def copy(self, e, out, in_):
        if e == "act":
            return self.op("act", lambda: self.nc.scalar.copy(out=_ap(out), in_=_ap(in_)), [out], [in_])
        return self.op(e, lambda: self.eng[e].tensor_copy(out=_ap(out), in_=_ap(in_)), [out], [in_])

    def amul(self, out, in_, mul):
        return self.op("act", lambda: self.nc.scalar.mul(out=_ap(out), in_=_ap(in_), mul=_ap(mul)), [out], [in_, mul])

    def memset(self, e, out, val):
        return self.op(e, lambda: self.eng[e].memset(_ap(out), val), [out], [])

    def recip(self, out, in_):
        return self.op("dve", lambda: self.nc.vector.reciprocal(out=_ap(out), in_=_ap(in_)), [out], [in_])

    def asel(self, out, in_, pattern, cmp, fill, base, cm):
        return self.op("pool", lambda: self.nc.gpsimd.affine_select(out=_ap(out), in_=_ap(in_), pattern=pattern, compare_op=cmp,
                                                                      fill=fill, base=base, channel_multiplier=cm), [out], [in_])


class Prog:
    def __init__(self, NT, layers=(0, 1, 2, 3), final_out=True):
        self.NT = NT
        self.LP = NT * 128
        self.kb = KB()
        self.nc = self.kb.nc
        self.layers = layers
        nc = self.nc
        LP = self.LP
        S = LP - 128
        shapes = {"xpad": [LP, D], "norm_mix_pre": [4, D], "norm_mix_post": [4, D], "norm_ffn_pre": [4, D], "norm_ffn_post": [4, D],
                  "attn_w_in": [2, D, 4112], "attn_b_forget": [2, 16], "attn_q_norm": [2, 64], "attn_k_norm": [2, 64],
                  "attn_w_out": [2, D, D], "dn_w_in": [2, D, 4112], "dn_conv": [2, 4, 3072], "dn_a_log": [2, 8], "dn_dt_bias": [2, 8],
                  "dn_o_norm": [2, 128], "dn_w_out": [2, D, D], "ffn_w_up": [4, D, 2 * FFN], "ffn_conv": [4, 3, 2 * FFN],
                  "ffn_w_down": [4, FFN, D]}

        class Lazy(dict):
            def __missing__(d, name):
                d[name] = nc.dram_tensor(name, list(shapes[name]), F32, kind="ExternalInput").ap()
                return d[name]
        self.inp = Lazy()
        self.y = nc.dram_tensor("y", [S, D], F32, kind="ExternalOutput").ap()

        def dscr(name, shape, dt):
            return nc.dram_tensor(name, list(shape), dt, kind="Internal").ap()
        self.H = dscr("H", [LP, D], F32)
        self.HT = dscr("HT", [FFN, LP], BF16)
        self.OGT = dscr("OGT", [D, LP], BF16)
        self.SG = dscr("SG", [D, LP], BF16)
        self.QT = dscr("QT", [16, 65, LP], BF16)
        self.KT = dscr("KT", [16, 65, LP], BF16)
        self.VA = dscr("VA", [LP, 16, 65], BF16)
        self.CT = dscr("CT", [128, NT, 16], F32)
        self.QKVD = dscr("QKVD", [24, 128, LP], F32)
        self.SGD = dscr("SGD", [LP, D], BF16)
        self.GBD = dscr("GBD", [128, NT, 16], F32)
        self.CAR = dscr("CAR", [128, NT + 1, 16], F32)

    def consts(self):
        kb = self.kb
        kb.phase()
        es = kb.phase_stack
        self.idf = kb.sb("idf", [128, 128], F32)
        self.idb = kb.sb("idb", [128, 128], BF16)
        kb.memset("pool", self.idf.ap, 1.0)
        kb.asel(self.idf.ap, self.idf.ap, [[-1, 128]], ALU.is_equal, 0.0, 0, 1)
        kb.copy("dve", self.idb.ap, self.idf.ap)
        self.const_stack = es
        kb.phase_stack = None
        kb.phase_bufs = []

    def blocks(self):
        out = []
        t = 0
        while t < self.NT:
            n = min(4, self.NT - t)
            out.append((t, n))
            t += n
        return out

    def load_cols(self, dst, src2d, n, stage, pst):
        kb = self.kb
        kb.dma(stage[0:n, :], src2d)
        kb.tr(pst[:, 0:n], stage[0:n, :], self.idf[0:n, 0:n])
        kb.copy("dve", dst, pst[:, 0:n])

    def load_w(self, dst, W, nch, ncols, gcol, stages, col0=0):
        kb = self.kb
        SEG = 2048
        k = 0
        for c in range(nch):
            for s0 in range(0, ncols, SEG):
                w = min(SEG, ncols - s0)
                st = stages[k % len(stages)]
                e = ("dve", "pool")[k % 2]
                k += 1
                kb.dma(st[:, 0:w], W[c * 128:(c + 1) * 128, s0:s0 + w])
                if gcol is not None:
                    kb.ts(e, dst[:, c, col0 + s0:col0 + s0 + w], st[:, 0:w], gcol[:, c:c + 1], ALU.mult)
                else:
                    kb.copy(e, dst[:, c, col0 + s0:col0 + s0 + w], st[:, 0:w])

    def prenormT(self, ht, aT_dst, abf, ss, pst, junk, k):
        kb = self.kb
        kb.memset("dve", ss[:, 0:1], 0.0)
        kb.act(junk.ap, ht.ap, AF.Square, accum=ss[:, 0:1])
        kb.act(ss[:, 1:2], ss[:, 0:1], AF.Sqrt, bias=EPS, scale=1.0 / D)
        kb.recip(ss[:, 2:3], ss[:, 1:2])
        kb.ts("dve", abf.ap, ht.ap, ss[:, 2:3], ALU.mult)
        for c in range(8):
            kb.tr(pst[:, c * 128:(c + 1) * 128], abf[:, c * 128:(c + 1) * 128], self.idb.ap)
        kb.copy(("act", "dve")[k % 2], aT_dst, pst.ap.re("p (c t) -> p c t", c=8))

    def phase_R(self, srcT, nch, W, gpost, hsrc, final=False):
        kb = self.kb
        kb.phase()
        Wb = kb.sb("Wb", [128, nch, D], BF16)
        stages = [kb.sb("wst", [128, 2048], F32) for _ in range(2)]
        self.load_w(Wb, W, nch, D, None, stages)
        g = kb.sb("g", [128, D], F32)
        kb.dma(g.ap, gpost.partition_broadcast(128))
        Sb = [kb.sb("Sb", [128, nch, 512], BF16) for _ in range(2)]
        hts = [kb.sb("ht", [128, D], F32) for _ in range(3)]
        fn = [kb.sb("fn", [128, D], F32) for _ in range(2)]
        ssb = [kb.sb("ss", [128, 8], F32) for _ in range(2)]
        junk = kb.sb("junk", [128, 512], BF16)
        PS = [[kb.ps("psa", [128, 512], F32), kb.ps("psb", [128, 512], F32)] for _ in range(2)]
        srcv = srcT.rearrange("(c p) t -> p c t", p=128)
        blocks = self.blocks()
        kb.dma(Sb[0][:, :, 0:blocks[0][1] * 128], srcv[:, :, 0:blocks[0][1] * 128])
        ti = 0
        for bi, (t0, n) in enumerate(blocks):
            if bi + 1 < len(blocks):
                t1, n1 = blocks[bi + 1]
                kb.dma(Sb[(bi + 1) % 2][:, :, 0:n1 * 128], srcv[:, :, t1 * 128:(t1 + n1) * 128])
            S = Sb[bi % 2]
            for j in range(n):
                tile = t0 + j
                ht = hts[ti % 3]
                kb.dma(ht.ap, hsrc[tile * 128:(tile + 1) * 128, :])
                pa, pb = PS[ti % 2]
                ss = ssb[ti % 2]
                f = fn[ti % 2]
                for c in range(nch):
                    kb.mm(pa.ap, S[:, c, j * 128:(j + 1) * 128], Wb[:, c, 0:512], start=(c == 0), stop=(c == nch - 1))
                for c in range(nch):
                    kb.mm(pb.ap, S[:, c, j * 128:(j + 1) * 128], Wb[:, c, 512:1024], start=(c == 0), stop=(c == nch - 1))
                kb.memset("dve", ss[:, 0:2], 0.0)
                kb.act(junk.ap, pa.ap, AF.Square, accum=ss[:, 0:1])
                kb.act(junk.ap, pb.ap, AF.Square, accum=ss[:, 1:2])
                kb.tt("dve", ss[:, 2:3], ss[:, 0:1], ss[:, 1:2], ALU.add)
                kb.act(ss[:, 3:4], ss[:, 2:3], AF.Sqrt, bias=EPS, scale=1.0 / D)
                kb.recip(ss[:, 4:5], ss[:, 3:4])
                kb.stt("dve", f[:, 0:512], pa.ap, ss[:, 4:5], g[:, 0:512], ALU.mult, ALU.mult)
                kb.stt("dve", f[:, 512:1024], pb.ap, ss[:, 4:5], g[:, 512:1024], ALU.mult, ALU.mult)
                kb.tt("pool", f.ap, f.ap, ht.ap, ALU.add)
                if final:
                    if tile >= 1:
                        kb.dma(self.y[(tile - 1) * 128:tile * 128, :], f.ap, q="pool")
                else:
                    kb.dma(self.H[tile * 128:(tile + 1) * 128, :], f.ap, q="pool")
                ti += 1
        kb.end_phase()

    def phase_F1(self, l):
        kb = self.kb
        kb.phase()
        Wup = kb.sb("Wup", [128, 8, 2 * FFN], BF16)
        stages = [kb.sb("wst", [128, 2048], F32) for _ in range(2)]
        cst = kb.sb("cst", [128, 128], F32)
        gcol = kb.sb("gcol", [128, 8], F32)
        cw = kb.sb("cw", [128, 3, 44], F32)
        pst = kb.ps("pst", [128, 1024], BF16)
        psc = kb.ps("psc", [128, 128], F32)
        self.load_cols(gcol.ap, self.inp["norm_ffn_pre"][l].rearrange("(c p) -> c p", p=128), 8, cst, psc)
        for k in range(3):
            self.load_cols(cw[:, k, :], self.inp["ffn_conv"][l, k].rearrange("(c p) -> c p", p=128), 44, cst, psc)
        self.load_w(Wup, self.inp["ffn_w_up"][l], 8, 2 * FFN, gcol, stages)
        halo = kb.sb("halo", [128, 44, 2], F32)
        kb.memset("pool", halo.ap, 0.0)
        hts = [kb.sb("ht", [128, D], F32) for _ in range(2)]
        abf = [kb.sb("abf", [128, D], BF16) for _ in range(2)]
        ssb = [kb.sb("ss", [128, 8], F32) for _ in range(2)]
        junk = kb.sb("junk", [128, D], BF16)
        aT = [kb.sb("aT", [128, 8, 512], BF16) for _ in range(2)]
        Hb = [kb.sb("Hb", [128, 22, 512], BF16) for _ in range(2)]
        Ug = [kb.sb("Ug", [128, 514], F32) for _ in range(2)]
        Uu = [kb.sb("Uu", [128, 514], F32) for _ in range(2)]
        yg = [kb.sb("yg", [128, 512], F32) for _ in range(2)]
        yu = [kb.sb("yu", [128, 512], F32) for _ in range(2)]
        gg = [kb.sb("gg", [128, 512], F32) for _ in range(2)]
        ptmp = [kb.sb("ptmp", [128, 512], F32) for _ in range(2)]
        PSg = [kb.ps("psg", [128, 512], F32) for _ in range(2)]
        PSu = [kb.ps("psu", [128, 512], F32) for _ in range(2)]
        HTv = self.HT.rearrange("(c p) t -> p c t", p=128)
        hsrc = self.H
        ti = 0
        it = 0
        for bi, (t0, n) in enumerate(self.blocks()):
            TW = n * 128
            A = aT[bi % 2]
            for j in range(n):
                tile = t0 + j
                ht = hts[ti % 2]
                kb.dma(ht.ap, hsrc[tile * 128:(tile + 1) * 128, :])
                self.prenormT(ht, A[:, :, j * 128:(j + 1) * 128], abf[ti % 2], ssb[ti % 2], pst, junk, ti)
                ti += 1
            HB = Hb[bi % 2]
            for fc in range(22):
                pg = PSg[it % 2]; pu = PSu[it % 2]
                ug = Ug[it % 2]; uu = Uu[it % 2]
                for c in range(8):
                    kb.mm(pg[:, 0:TW], Wup[:, c, fc * 128:(fc + 1) * 128], A[:, c, 0:TW], start=(c == 0), stop=(c == 7))
                for c in range(8):
                    kb.mm(pu[:, 0:TW], Wup[:, c, FFN + fc * 128:FFN + (fc + 1) * 128], A[:, c, 0:TW], start=(c == 0), stop=(c == 7))
                a = yg[it % 2]; b = yu[it % 2]
                for (u_, p_, y_, hc) in ((ug, pg, a, fc), (uu, pu, b, 22 + fc)):
                    kb.copy("act", u_[:, 2:2 + TW], p_[:, 0:TW])
                    kb.amul(y_[:, 0:TW], p_[:, 0:TW], cw[:, 2, hc:hc + 1])
                    kb.copy("dve", u_[:, 0:2], halo[:, hc, :])
                    kb.copy("dve", halo[:, hc, :], u_[:, TW:TW + 2])
                    kb.stt("dve", y_[:, 0:TW], u_[:, 1:1 + TW], cw[:, 1, hc:hc + 1], y_[:, 0:TW], ALU.mult, ALU.add)
                    kb.stt("dve", y_[:, 0:TW], u_[:, 0:TW], cw[:, 0, hc:hc + 1], y_[:, 0:TW], ALU.mult, ALU.add)
                kb.act(gg[it % 2][:, 0:TW], a[:, 0:TW], AF.Gelu_apprx_tanh)
                kb.tt("dve", HB[:, fc, 0:TW], gg[it % 2][:, 0:TW], b[:, 0:TW], ALU.mult)
                it += 1
            kb.dma(HTv[:, :, t0 * 128:t0 * 128 + TW], HB[:, :, 0:TW], q="pool")
        kb.end_phase()

    def phase_init(self):
        kb = self.kb
        kb.phase()
        hts = [kb.sb("ht", [128, D], F32) for _ in range(4)]
        for t in range(self.NT):
            ht = hts[t % 4]
            kb.dma(ht.ap, self.inp["xpad"][t * 128:(t + 1) * 128, :])
            kb.dma(self.H[t * 128:(t + 1) * 128, :], ht.ap, q="pool")
        kb.end_phase()

    def ffn(self, l, final):
        self.phase_F1(l)
        self.phase_R(self.HT, 22, self.inp["ffn_w_down"][l], self.inp["norm_ffn_post"][l], self.H, final=final)

    def finish(self):
        self.const_stack.close()
        self.kb.es.close()


def phase_A1(self, l):
    kb = self.kb
    j = l // 2
    NT = self.NT
    kb.phase()
    Win = kb.sb("Win", [128, 8, 4112], BF16)
    stages = [kb.sb("wst", [128, 2048], F32) for _ in range(2)]
    cst = kb.sb("cst", [128, 128], F32)
    gcol = kb.sb("gcol", [128, 8], F32)
    pst = kb.ps("pst", [128, 1024], BF16)
    PSq = [kb.ps("psq", [128, 512], F32) for _ in range(2)]
    PSs = kb.ps("pss", [128, 512], F32)
    PSv = kb.ps("psv", [128, 1024], F32)
    PSm = kb.ps("psm", [128, 512], F32)
    PSr = kb.ps("psr", [16, 512], BF16)
    self.load_cols(gcol.ap, self.inp["norm_mix_pre"][l].rearrange("(c p) -> c p", p=128), 8, cst, PSs)
    self.load_w(Win, self.inp["attn_w_in"][j], 8, 4112, gcol, stages)
    qg = kb.sb("qg", [128, 2], F32)
    for half in range(2):
        kb.dma(qg[half * 64:(half + 1) * 64, 0:1], self.inp["attn_q_norm"][j].rearrange("(p o) -> p o", o=1))
        kb.dma(qg[half * 64:(half + 1) * 64, 1:2], self.inp["attn_k_norm"][j].rearrange("(p o) -> p o", o=1))
    kb.ts("dve", qg[:, 0:1], qg[:, 0:1], 0.125, ALU.mult)
    bfg = kb.sb("bfg", [128, 16], F32)
    kb.dma(bfg.ap, self.inp["attn_b_forget"][j].partition_broadcast(128))
    BD = kb.sb("BD", [128, 128], BF16)
    kb.memset("pool", BD.ap, 1.0)
    kb.asel(BD[:, 0:64], BD[:, 0:64], [[0, 64]], ALU.is_ge, 0.0, 63, -1)
    kb.asel(BD[:, 64:128], BD[:, 64:128], [[0, 64]], ALU.is_ge, 0.0, -64, 1)
    tri = kb.sb("tri", [128, 128], F32)
    kb.memset("pool", tri.ap, 1.0)
    kb.asel(tri.ap, tri.ap, [[1, 128]], ALU.is_ge, 0.0, 0, -1)
    onesf = kb.sb("onesf", [128, 128], F32)
    kb.memset("pool", onesf.ap, 1.0)
    onesb = kb.sb("onesb", [16, 512], BF16)
    kb.memset("pool", onesb.ap, 1.0)
    Ctok = kb.sb("Ctok", [128, NT, 16], F32)
    CAR = kb.sb("CAR", [128, NT + 1, 16], F32)
    kb.memset("dve", CAR[:, 0, :], 0.0)
    hts = [kb.sb("ht", [128, D], F32) for _ in range(2)]
    abf = [kb.sb("abf", [128, D], BF16) for _ in range(2)]
    ssb = [kb.sb("ss", [128, 8], F32) for _ in range(2)]
    junk = kb.sb("junk", [128, D], BF16)
    aT = [kb.sb("aT", [128, 8, 512], BF16) for _ in range(2)]
    sq = [kb.sb("sq", [128, 512], BF16) for _ in range(2)]
    rms = [kb.sb("rms", [128, 512], F32) for _ in range(2)]
    qn = [kb.sb("qn", [128, 512], BF16) for _ in range(3)]
    Va = [kb.sb("Va", [128, 16, 65], BF16) for _ in range(2)]
    V0 = kb.sb("V0", [128, 16, 65], BF16)
    for v in Va + [V0]:
        kb.memset("pool", v.ap, 1.0)
    kb.memset("pool", V0[0:NPAD, :, 64:65], 0.0)
    xs = [kb.sb("xs", [128, 48], F32) for _ in range(2)]
    rb = [kb.sb("rb", [128, 16], BF16) for _ in range(2)]
    rT = [kb.sb("rT", [16, 512], BF16) for _ in range(2)]
    ti = 0
    it = 0
    for bi, (t0, n) in enumerate(self.blocks()):
        TW = n * 128
        c0 = t0 * 128
        A = aT[bi % 2]
        for jj in range(n):
            tile = t0 + jj
            ht = hts[ti % 2]
            kb.dma(ht.ap, self.H[tile * 128:(tile + 1) * 128, :])
            self.prenormT(ht, A[:, :, jj * 128:(jj + 1) * 128], abf[ti % 2], ssb[ti % 2], pst, junk, ti)
            ti += 1
        def projA(idx, it_):
            pq_ = PSq[it_ % 2]
            col_ = (idx // 8) * 1024 + (idx % 8) * 128
            for c in range(8):
                kb.mm(pq_[:, 0:TW], Win[:, c, col_:col_ + 128], A[:, c, 0:TW], start=(c == 0), stop=(c == 7))
        projA(0, it)
        for which in range(2):
            dst = self.QT if which == 0 else self.KT
            for ch in range(8):
                pq = PSq[it % 2]
                s_ = sq[it % 2]
                kb.act(s_[:, 0:TW], pq[:, 0:TW], AF.Square)
                if which * 8 + ch + 1 < 16:
                    projA(which * 8 + ch + 1, it + 1)
                kb.mm(PSs[:, 0:TW], BD.ap, s_[:, 0:TW])
                r_ = rms[it % 2]
                kb.act(r_[:, 0:TW], PSs[:, 0:TW], AF.Sqrt, bias=EPS, scale=1.0 / 64)
                kb.recip(r_[:, 0:TW], r_[:, 0:TW])
                q_ = qn[it % 3]
                kb.stt("dve", q_[:, 0:TW], pq[:, 0:TW], qg[:, which:which + 1], r_[:, 0:TW], ALU.mult, ALU.mult)
                kb.dma(dst[2 * ch, 0:64, c0:c0 + TW], q_[0:64, 0:TW], q="pool")
                kb.dma(dst[2 * ch + 1, 0:64, c0:c0 + TW], q_[64:128, 0:TW], q="pool")
                it += 1
        for ch in range(8):
            pq = PSq[it % 2]
            col = 3072 + ch * 128
            for c in range(8):
                kb.mm(pq[:, 0:TW], Win[:, c, col:col + 128], A[:, c, 0:TW], start=(c == 0), stop=(c == 7))
            q_ = qn[it % 3]
            kb.act(q_[:, 0:TW], pq[:, 0:TW], AF.Sigmoid)
            kb.dma(self.SG[ch * 128:(ch + 1) * 128, c0:c0 + TW], q_[:, 0:TW], q="pool")
            it += 1
        kb.dma(self.KT[:, 64, c0:c0 + TW], onesb[:, 0:TW], q="pool")
        for jj in range(n):
            tile = t0 + jj
            tc_ = slice(jj * 128, (jj + 1) * 128)
            for half in range(2):
                for c in range(8):
                    kb.mm(PSv[:, half * 512:(half + 1) * 512], A[:, c, tc_], Win[:, c, 2048 + half * 512:2048 + (half + 1) * 512],
                          start=(c == 0), stop=(c == 7))
            V = V0 if tile == 0 else Va[tile % 2]
            kb.copy("act", V[:, :, 0:64], PSv.ap.re("p (h d) -> p h d", h=16))
            kb.dma(self.VA[tile * 128:(tile + 1) * 128, :, :], V.ap, q="pool")
            for c in range(8):
                kb.mm(PSm[:, 0:16], A[:, c, tc_], Win[:, c, 4096:4112], start=(c == 0), stop=(c == 7))
            x = xs[tile % 2]
            kb.tt("dve", x[:, 0:16], PSm[:, 0:16], bfg.ap, ALU.add)
            kb.act(x[:, 16:32], x[:, 0:16], AF.Exp, scale=-1.0)
            kb.act(x[:, 32:48], x[:, 16:32], AF.Ln, bias=1.0)
            kb.mm(PSm[:, 16:32], tri.ap, x[:, 32:48])
            kb.mm(PSm[:, 32:48], onesf.ap, x[:, 32:48])
            kb.tt("dve", Ctok[:, tile, :], PSm[:, 16:32], CAR[:, tile, :], ALU.add)
            kb.tt("dve", CAR[:, tile + 1, :], PSm[:, 32:48], CAR[:, tile, :], ALU.add)
        R_ = rT[bi % 2]
        for jj in range(n):
            tile = t0 + jj
            r2 = rb[tile % 2]
            kb.tt("dve", r2.ap, CAR[:, t0 + n, :], Ctok[:, tile, :], ALU.subtract)
            kb.tr(PSr[:, jj * 128:(jj + 1) * 128], r2.ap, self.idb.ap)
        kb.copy("dve", R_[:, 0:TW], PSr[:, 0:TW])
        kb.dma(self.QT[:, 64, c0:c0 + TW], R_[:, 0:TW], q="pool")
    kb.dma(self.CT, Ctok.ap, q="pool")
    kb.dma(self.CAR, CAR.ap, q="pool")
    kb.end_phase()


def phase_A2(self, l):
    kb = self.kb
    NT = self.NT
    LP = self.LP
    kb.phase()
    Ctok = kb.sb("Ctok", [128, NT, 16], F32)
    CAR = kb.sb("CAR", [128, NT + 1, 16], F32)
    kb.dma(Ctok.ap, self.CT)
    kb.dma(CAR.ap, self.CAR)
    masks = []
    for jj in range(4):
        m = kb.sb("mask", [128, 512], BF16)
        kb.memset("pool", m.ap, 0.0)
        kb.asel(m.ap, m.ap, [[1, 512]], ALU.is_ge, NEGBIG, -128 * jj, -1)
        masks.append(m)
    onesf = kb.sb("onesf", [128, 64], F32)
    kb.memset("pool", onesf.ap, 1.0)
    KTh = [kb.sb("KTh", [128, LP], BF16) for _ in range(2)]
    VAh = [kb.sb("VAh", [128, NT, 65], BF16) for _ in range(2)]
    QTb = [kb.sb("QTb", [128, 512], BF16) for _ in range(2)]
    for b_ in KTh + QTb:
        kb.memset("pool", b_[64:128, :], 0.0)
    SGb = [kb.sb("SGb", [64, 512], BF16) for _ in range(2)]
    bias = [kb.sb("bias", [128, NT], F32) for _ in range(2)]
    P = [kb.sb("P", [128, 512], BF16) for _ in range(4)]
    oacc = [kb.sb("oacc", [65, 512], F32) for _ in range(2)]
    rden = [kb.sb("rden", [128, 512], F32) for _ in range(2)]
    for b_ in rden:
        kb.memset("pool", b_.ap, 0.0)
    sel = kb.sb("sel", [128, 64], F32)
    kb.memset("pool", sel.ap, 0.0)
    kb.memset("pool", sel[64:96, :], 1.0)
    kb.asel(sel[64:96, :], sel[64:96, :], [[0, 64]], ALU.is_ge, 0.0, 0, -1)
    tmp = [kb.sb("tmp", [64, 512], F32) for _ in range(2)]
    og = [kb.sb("og", [64, 512], BF16) for _ in range(2)]
    PSs = [kb.ps("pss", [128, 512], F32) for _ in range(4)]
    PSo = [kb.ps("pso", [65, 512], F32) for _ in range(2)]
    PSb = [kb.ps("psb", [64, 512], F32) for _ in range(2)]
    VAv = self.VA.rearrange("(t p) h e -> p t h e", p=128)
    blocks = self.blocks()
    it = 0
    ip = 0
    def load_head(hh):
        kb.dma(KTh[hh % 2][0:65, :], self.KT[hh])
        for a in range(0, NT, 16):
            b = min(NT, a + 16)
            kb.dma(VAh[hh % 2][:, a:b, :], VAv[:, a:b, hh, :])
    load_head(0)
    for h in range(16):
        if h + 1 < 16:
            load_head(h + 1)
        K_ = KTh[h % 2]
        V_ = VAh[h % 2]
        for bi, (t0, n) in enumerate(blocks):
            TW = n * 128
            c0 = t0 * 128
            Q_ = QTb[it % 2]
            S_ = SGb[it % 2]
            kb.dma(Q_[0:65, 0:TW], self.QT[h, :, c0:c0 + TW])
            kb.dma(S_[:, 0:TW], self.SG[h * 64:(h + 1) * 64, c0:c0 + TW])
            nkt = t0 + n
            b_ = bias[it % 2]
            kb.ts("dve", b_[:, 0:nkt], Ctok[:, 0:nkt, h], CAR[:, t0 + n, h:h + 1], ALU.subtract)
            po = PSo[it % 2]
            def emit_s(kt, ipk):
                ps = PSs[ipk % 4]
                diag = kt >= t0
                kb.mm(ps[:, 0:TW], K_[:, kt * 128:(kt + 1) * 128], Q_[:, 0:TW], start=True, stop=not diag)
                if diag:
                    kb.mm(ps[:, 0:TW], self.idb.ap, masks[kt - t0][:, 0:TW], start=False, stop=True)
                p_ = P[ipk % 4]
                kb.act(p_[:, 0:TW], ps[:, 0:TW], AF.Exp, bias=b_[:, kt:kt + 1])
            emit_s(0, ip)
            if nkt > 1:
                emit_s(1, ip + 1)
            for kt in range(nkt):
                if kt + 2 < nkt:
                    emit_s(kt + 2, ip + 2)
                kb.mm(po[:, 0:TW], V_[:, kt, :], P[ip % 4][:, 0:TW], start=(kt == 0), stop=(kt == nkt - 1))
                ip += 1
            oa = oacc[it % 2]
            rd = rden[it % 2]
            kb.copy("act", oa[:, 0:TW], po[:, 0:TW])
            kb.ts("dve", rd[64:65, 0:TW], oa[64:65, 0:TW], 1e-30, ALU.max)
            kb.recip(rd[64:65, 0:TW], rd[64:65, 0:TW])
            pb = PSb[it % 2]
            kb.mm(pb[:, 0:TW], sel.ap, rd[:, 0:TW])
            t_ = tmp[it % 2]
            kb.tt("dve", t_[:, 0:TW], oa[0:64, 0:TW], pb[:, 0:TW], ALU.mult)
            o_ = og[it % 2]
            kb.tt("pool", o_[:, 0:TW], t_[:, 0:TW], S_[:, 0:TW], ALU.mult)
            kb.dma(self.OGT[h * 64:(h + 1) * 64, c0:c0 + TW], o_[:, 0:TW], q="pool")
            it += 1
    kb.end_phase()


Prog.phase_A1 = phase_A1
Prog.phase_A2 = phase_A2


def attn_layer(self, l):
    self.phase_A1(l)
    self.phase_A2(l)
    self.phase_R(self.OGT, 8, self.inp["attn_w_out"][l // 2], self.inp["norm_mix_post"][l], self.H)


Prog.attn_layer = attn_layer


def phase_D1(self, l):
    kb = self.kb
    j = l // 2
    NT = self.NT
    kb.phase()
    Win = kb.sb("Win", [128, 8, 4112], BF16)
    stages = [kb.sb("wst", [128, 2048], F32) for _ in range(2)]
    cst = kb.sb("cst", [128, 128], F32)
    gcol = kb.sb("gcol", [128, 8], F32)
    cw = kb.sb("cw", [128, 4, 24], F32)
    pst = kb.ps("pst", [128, 1024], BF16)
    PSq = [kb.ps("psq", [128, 512], F32) for _ in range(2)]
    PSs = kb.ps("pss", [128, 512], F32)
    PSv = kb.ps("psv", [128, 1024], F32)
    PSm = kb.ps("psm", [128, 512], F32)
    self.load_cols(gcol.ap, self.inp["norm_mix_pre"][l].rearrange("(c p) -> c p", p=128), 8, cst, PSs)
    for k in range(4):
        self.load_cols(cw[:, k, :], self.inp["dn_conv"][j, k].rearrange("(c p) -> c p", p=128), 24, cst, PSs)
    self.load_w(Win, self.inp["dn_w_in"][j], 8, 4112, gcol, stages)
    dtb = kb.sb("dtb", [128, 8], F32)
    nA = kb.sb("nA", [128, 8], F32)
    kb.dma(dtb.ap, self.inp["dn_dt_bias"][j].partition_broadcast(128))
    kb.dma(nA.ap, self.inp["dn_a_log"][j].partition_broadcast(128))
    kb.act(nA.ap, nA.ap, AF.Exp)
    kb.ts("dve", nA.ap, nA.ap, -1.0, ALU.mult)
    tri = kb.sb("tri", [128, 128], F32)
    kb.memset("pool", tri.ap, 1.0)
    kb.asel(tri.ap, tri.ap, [[1, 128]], ALU.is_ge, 0.0, 0, -1)
    onesb = kb.sb("onesb", [128, 128], BF16)
    kb.memset("pool", onesb.ap, 1.0)
    halo = kb.sb("halo", [128, 24, 3], F32)
    kb.memset("pool", halo.ap, 0.0)
    GB = kb.sb("GB", [128, NT, 16], F32)
    hts = [kb.sb("ht", [128, D], F32) for _ in range(2)]
    abf = [kb.sb("abf", [128, D], BF16) for _ in range(2)]
    ssb = [kb.sb("ss", [128, 8], F32) for _ in range(2)]
    junk = kb.sb("junk", [128, D], BF16)
    aT = [kb.sb("aT", [128, 8, 512], BF16) for _ in range(2)]
    U = [kb.sb("U", [128, 515], F32) for _ in range(2)]
    yv = [kb.sb("yv", [128, 512], F32) for _ in range(2)]
    z = [kb.sb("z", [128, 512], F32) for _ in range(3)]
    sq = [kb.sb("sq", [128, 512], BF16) for _ in range(2)]
    rms = [kb.sb("rms", [128, 512], F32) for _ in range(2)]
    sg = [kb.sb("sg", [128, D], BF16) for _ in range(2)]
    xs = [kb.sb("xs", [128, 32], F32) for _ in range(2)]
    ti = 0
    it = 0
    for bi, (t0, n) in enumerate(self.blocks()):
        TW = n * 128
        c0 = t0 * 128
        A = aT[bi % 2]
        for jj in range(n):
            tile = t0 + jj
            ht = hts[ti % 2]
            kb.dma(ht.ap, self.H[tile * 128:(tile + 1) * 128, :])
            self.prenormT(ht, A[:, :, jj * 128:(jj + 1) * 128], abf[ti % 2], ssb[ti % 2], pst, junk, ti)
            ti += 1
        def proj(fc_, it_):
            pq_ = PSq[it_ % 2]
            for c in range(8):
                kb.mm(pq_[:, 0:TW], Win[:, c, fc_ * 128:(fc_ + 1) * 128], A[:, c, 0:TW], start=(c == 0), stop=(c == 7))
        proj(0, it)
        for fc in range(24):
            pq = PSq[it % 2]
            u = U[it % 2]
            kb.copy("dve", u[:, 0:3], halo[:, fc, :])
            kb.copy("act", u[:, 3:3 + TW], pq[:, 0:TW])
            kb.copy("dve", halo[:, fc, :], u[:, TW:TW + 3])
            y = yv[it % 2]
            kb.ts("dve", y[:, 0:TW], u[:, 3:3 + TW], cw[:, 3, fc:fc + 1], ALU.mult)
            for k in (2, 1, 0):
                kb.stt("dve", y[:, 0:TW], u[:, k:k + TW], cw[:, k, fc:fc + 1], y[:, 0:TW], ALU.mult, ALU.add)
            z_ = z[it % 3]
            kb.act(z_[:, 0:TW], y[:, 0:TW], AF.Silu)
            if fc + 1 < 24:
                proj(fc + 1, it + 1)
            if fc < 16:
                s_ = sq[it % 2]
                kb.act(s_[:, 0:TW], z_[:, 0:TW], AF.Square)
                kb.mm(PSs[:, 0:TW], onesb.ap, s_[:, 0:TW])
                r_ = rms[it % 2]
                kb.act(r_[:, 0:TW], PSs[:, 0:TW], AF.Sqrt, bias=EPS)
                kb.recip(r_[:, 0:TW], r_[:, 0:TW])
                if fc < 8:
                    kb.stt("dve", z_[:, 0:TW], z_[:, 0:TW], float(HD_D ** -0.5), r_[:, 0:TW], ALU.mult, ALU.mult)
                else:
                    kb.tt("dve", z_[:, 0:TW], z_[:, 0:TW], r_[:, 0:TW], ALU.mult)
            kb.dma(self.QKVD[fc, :, c0:c0 + TW], z_[:, 0:TW], q="pool")
            it += 1
        for jj in range(n):
            tile = t0 + jj
            tc_ = slice(jj * 128, (jj + 1) * 128)
            for half in range(2):
                for c in range(8):
                    kb.mm(PSv[:, half * 512:(half + 1) * 512], A[:, c, tc_], Win[:, c, 3072 + half * 512:3072 + (half + 1) * 512],
                          start=(c == 0), stop=(c == 7))
            s2 = sg[tile % 2]
            kb.act(s2.ap, PSv.ap, AF.Silu)
            kb.dma(self.SGD[tile * 128:(tile + 1) * 128, :], s2.ap, q="pool")
            for c in range(8):
                kb.mm(PSm[:, 0:16], A[:, c, tc_], Win[:, c, 4096:4112], start=(c == 0), stop=(c == 7))
            x = xs[tile % 2]
            kb.act(GB[:, tile, 0:8], PSm[:, 0:8], AF.Sigmoid)
            kb.tt("dve", x[:, 0:8], PSm[:, 8:16], dtb.ap, ALU.add)
            kb.act(x[:, 8:16], x[:, 0:8], AF.Exp)
            kb.act(x[:, 16:24], x[:, 8:16], AF.Ln, bias=1.0)
            kb.tt("dve", x[:, 24:32], x[:, 16:24], nA.ap, ALU.mult)
            kb.mm(PSm[:, 16:24], tri.ap, x[:, 24:32])
            kb.copy("dve", GB[:, tile, 8:16], PSm[:, 16:24])
    kb.dma(self.GBD, GB.ap, q="pool")
    kb.end_phase()


def phase_D2(self, l, G=8):
    kb = self.kb
    j = l // 2
    NT = self.NT
    kb.phase()
    GB = kb.sb("GB", [128, NT, 16], F32)
    kb.dma(GB.ap, self.GBD)
    onrm = kb.sb("onrm", [128, 128], F32)
    kb.dma(onrm.ap, self.inp["dn_o_norm"][j].partition_broadcast(128))
    onesf = kb.sb("onesf", [128, 128], F32)
    kb.memset("pool", onesf.ap, 1.0)
    idf = self.idf
    idb = self.idb
    NMsT = kb.sb("NMsT", [128, 128], F32)
    kb.memset("pool", NMsT.ap, 0.0)
    kb.asel(NMsT.ap, NMsT.ap, [[1, 128]], ALU.is_gt, NEGBIG, 0, -1)
    NPs = kb.sb("NPs", [128, 128], F32)
    kb.memset("pool", NPs.ap, 0.0)
    kb.asel(NPs.ap, NPs.ap, [[-1, 128]], ALU.is_gt, -NEGBIG, 0, 1)
    BDm = kb.sb("BDm", [128, 128], F32)
    kb.memset("pool", BDm.ap, 1.0)
    kb.asel(BDm[:, 0:64], BDm[:, 0:64], [[0, 64]], ALU.is_ge, 0.0, 63, -1)
    kb.asel(BDm[:, 64:128], BDm[:, 64:128], [[0, 64]], ALU.is_ge, 0.0, -64, 1)
    ST = [kb.sb("ST", [128, 128], F32) for _ in range(8)]
    STb = [kb.sb("STb", [128, 128], BF16) for _ in range(8)]
    for s_ in ST + STb:
        kb.memset("dve", s_.ap, 0.0)
    X24 = [kb.sb("X24", [128, 24, 128], F32) for _ in range(2)]
    X24b = [kb.sb("X24b", [128, 24, 128], BF16) for _ in range(2)]
    SGt = [kb.sb("SGt", [128, D], BF16) for _ in range(2)]
    Og = [kb.sb("Og", [128, D], BF16) for _ in range(2)]
    OgT = [kb.sb("OgT", [128, 8, 128], BF16) for _ in range(2)]
    tl = [kb.sb("tl", [128, 32], F32) for _ in range(2)]
    sets = []
    f32names = ["DG", "Gs", "DTs", "DTi", "Dn", "A", "B", "Ad", "Ao", "Bd", "Bo", "P", "Pt",
                "U0s", "EGbc", "tmp", "on"]
    bf16names = ["R", "QKT", "kbg", "kd", "vb", "WT", "Us", "qdT",
                 "Adh", "Adl", "Bdh", "Bdl", "Xah", "Xal", "Xbh", "Xbl", "Xtah", "Xtal", "Xtbh", "Xtbl",
                 "Pth", "Ptl", "Ph", "Pl", "Aoh", "Aol", "Zh", "Zl"]
    for g in range(G):
        d = {}
        for nm in f32names:
            w = 256 if nm in ("DG", "Gs") else 128
            d[nm] = kb.sb(nm, [128, w], F32)
        for nm in bf16names:
            d[nm] = kb.sb(nm, [128, 128], BF16)
        d["cols"] = kb.sb("cols", [128, 8], F32)
        bA = kb.ps("bA", [128, 512], F32)
        d["bA"] = bA
        d["PSG"], d["PSK"], d["PSQ"] = kb.carve(bA, [(0, 256), (256, 384), (384, 512)])
        d["C0"], d["C1"], d["C2"], d["C3"] = kb.carve(bA, [(0, 128), (128, 256), (256, 384), (384, 512)])
        d["D0"], d["D1"], d["D2"], d["D3"] = d["C0"], d["C1"], d["C2"], d["C3"]
        d["C2b"] = View(d["C2"], d["C2"].t.bitcast(BF16)[:, 0:128])
        d["C3b"] = View(d["C3"], d["C3"].t.bitcast(BF16)[:, 0:128])
        sets.append(d)
    PSot = View(sets[0]["bA"], sets[0]["bA"].t[:].bitcast(BF16))
    Xv = self.QKVD.rearrange("f d t -> d f t")
    OGTv = self.OGT.rearrange("(c p) t -> p c t", p=128)

    def head(n, hd, d, X, Xb, sgt, og, t_):
        qT = X[:, hd, :]
        qTb = Xb[:, hd, :]
        kTb = Xb[:, 8 + hd, :]
        vTb = Xb[:, 16 + hd, :]
        beta = GB[:, n, hd:hd + 1]
        gc = GB[:, n, 8 + hd:9 + hd]
        ngc = t_[:, hd:hd + 1]
        bg = t_[:, 16 + hd:17 + hd]
        cols = d["cols"]
        kb.ts("dve", d["DG"][:, 0:128], idf.ap, gc, ALU.mult)
        kb.ts("dve", d["DG"][:, 128:256], idf.ap, beta, ALU.mult)
        kb.mm(d["PSG"].ap, onesf.ap, d["DG"].ap)
        kb.copy("act", d["Gs"].ap, d["PSG"].ap)
        kb.mm(d["PSK"].ap, kTb, kTb)
        kb.mm(d["PSQ"].ap, kTb, qTb)
        yield
        G_ = d["Gs"][:, 0:128]
        kb.tt("dve", d["tmp"].ap, G_, NMsT.ap, ALU.add)
        kb.act(d["DTs"].ap, d["tmp"].ap, AF.Exp, bias=ngc)
        kb.tt("pool", d["DTi"].ap, d["DTs"].ap, idf.ap, ALU.add)
        kb.tt("dve", d["Dn"].ap, G_, NPs.ap, ALU.add)
        kb.act(d["Dn"].ap, d["Dn"].ap, AF.Exp, bias=gc, scale=-1.0)
        kb.act(d["EGbc"].ap, G_, AF.Exp)
        kb.act(cols[:, 0:1], gc, AF.Exp, bias=d["Gs"][:, 127:128], scale=-1.0)
        kb.act(cols[:, 1:2], d["Gs"][:, 127:128], AF.Exp)
        yield
        kb.stt("dve", d["A"].ap, d["PSK"].ap, beta, d["Dn"].ap, ALU.mult, ALU.mult)
        kb.tt("dve", d["B"].ap, d["PSK"].ap, d["DTs"].ap, ALU.mult)
        kb.tt("pool", d["B"].ap, d["B"].ap, d["Gs"][:, 128:256], ALU.mult)
        kb.tt("dve", d["QKT"].ap, d["PSQ"].ap, d["DTi"].ap, ALU.mult)
        kb.tt("pool", d["Ad"].ap, d["A"].ap, BDm.ap, ALU.mult)
        kb.tt("pool", d["Ao"].ap, d["A"].ap, d["Ad"].ap, ALU.subtract)
        kb.tt("pool", d["Bd"].ap, d["B"].ap, BDm.ap, ALU.mult)
        kb.tt("pool", d["Bo"].ap, d["B"].ap, d["Bd"].ap, ALU.subtract)
        kb.tt("dve", d["P"].ap, idf.ap, d["Ad"].ap, ALU.subtract)
        kb.tt("dve", d["Pt"].ap, idf.ap, d["Bd"].ap, ALU.subtract)
        yield
        def split(src, hi, lo, e="dve"):
            kb.copy("act", d[hi].ap, src)
            kb.tt(e, d[lo].ap, src, d[hi].ap, ALU.subtract)

        def mm3(out, A_, B_):
            kb.mm(out, d[A_[0]].ap, d[B_[0]].ap, start=True, stop=False)
            kb.mm(out, d[A_[0]].ap, d[B_[1]].ap, start=False, stop=False)
            kb.mm(out, d[A_[1]].ap, d[B_[0]].ap, start=False, stop=True)
        split(d["Ad"].ap, "Adh", "Adl", "pool")
        split(d["Bd"].ap, "Bdh", "Bdl", "pool")
        split(d["Pt"].ap, "Pth", "Ptl")
        split(d["Ao"].ap, "Aoh", "Aol", "pool")
        yield
        Xc, Xtc = ("Adh", "Adl"), ("Bdh", "Bdl")
        PT = ("Pth", "Ptl")
        for k in range(1, 6):
            Xn = ("Xah", "Xal") if k % 2 else ("Xbh", "Xbl")
            Xtn = ("Xtah", "Xtal") if k % 2 else ("Xtbh", "Xtbl")
            mm3(d["C0"].ap, Xtc, Xc)
            if k < 5:
                mm3(d["C1"].ap, Xc, Xtc)
            yield
            split(d["C0"].ap, Xn[0], Xn[1])
            if k < 5:
                split(d["C1"].ap, Xtn[0], Xtn[1])
            yield
            mm3(d["C2"].ap, PT, Xn)
            mm3(d["C3"].ap, Xn, PT)
            yield
            kb.tt("dve", d["P"].ap, d["P"].ap, d["C2"].ap, ALU.add)
            kb.tt("dve", d["Pt"].ap, d["Pt"].ap, d["C3"].ap, ALU.add)
            split(d["Pt"].ap, "Pth", "Ptl")
            Xc, Xtc = Xn, Xtn
            yield
        split(d["P"].ap, "Ph", "Pl", "pool")
        mm3(d["C0"].ap, ("Aoh", "Aol"), PT)
        yield
        split(d["C0"].ap, "Zh", "Zl")
        yield
        mm3(d["C1"].ap, ("Ph", "Pl"), ("Zh", "Zl"))
        kb.tt("dve", d["R"].ap, d["Pt"].ap, d["C1"].ap, ALU.subtract)
        kb.tr(d["C2b"], kTb, idb.ap)
        kb.tr(d["C3b"], vTb, idb.ap)
        kb.ts("dve", d["kbg"].ap, d["C2b"], bg, ALU.mult)
        kb.ts("dve", d["kd"].ap, d["C2b"], cols[:, 0:1], ALU.mult)
        kb.ts("dve", d["vb"].ap, d["C3b"], beta, ALU.mult)
        kb.tt("pool", d["qdT"].ap, qT, d["EGbc"].ap, ALU.mult)
        yield
        kb.mm(d["D0"].ap, d["kbg"].ap, d["R"].ap)
        kb.mm(d["D1"].ap, d["R"].ap, d["vb"].ap)
        kb.copy("act", d["WT"].ap, d["D0"].ap)
        kb.copy("act", d["U0s"].ap, d["D1"].ap)
        yield
        S_ = ST[hd]
        Sb_ = STb[hd]
        kb.mm(d["D2"].ap, d["WT"].ap, Sb_.ap)
        kb.tt("dve", d["Us"].ap, d["U0s"].ap, d["D2"].ap, ALU.subtract)
        kb.mm(d["D3"].ap, d["qdT"].ap, Sb_.ap, start=True, stop=False)
        kb.mm(d["D3"].ap, d["QKT"].ap, d["Us"].ap, start=False, stop=True)
        kb.memset("dve", cols[:, 2:3], 0.0)
        kb.act(d["tmp"].ap, d["D3"].ap, AF.Square, accum=cols[:, 2:3])
        kb.act(cols[:, 3:4], cols[:, 2:3], AF.Sqrt, bias=EPS, scale=1.0 / HD_D)
        kb.recip(cols[:, 4:5], cols[:, 3:4])
        kb.stt("dve", d["on"].ap, d["D3"].ap, cols[:, 4:5], onrm.ap, ALU.mult, ALU.mult)
        kb.mm(d["C0"].ap, d["kd"].ap, d["Us"].ap)
        kb.stt("dve", S_.ap, S_.ap, cols[:, 1:2], d["C0"].ap, ALU.mult, ALU.add)
        kb.copy("act", Sb_.ap, S_.ap)
        kb.tt("pool", og[:, hd * 128:(hd + 1) * 128], d["on"].ap, sgt[:, hd * 128:(hd + 1) * 128], ALU.mult)
        yield

    kb.dma(X24[0].ap, Xv[:, :, 0:128])
    for n in range(NT):
        if n + 1 < NT:
            kb.dma(X24[(n + 1) % 2].ap, Xv[:, :, (n + 1) * 128:(n + 2) * 128])
        X = X24[n % 2]
        Xb = X24b[n % 2]
        kb.copy("act", Xb[:, 0:12, :], X[:, 0:12, :])
        kb.copy("dve", Xb[:, 12:24, :], X[:, 12:24, :])
        sgt = SGt[n % 2]
        kb.dma(sgt.ap, self.SGD[n * 128:(n + 1) * 128, :])
        t_ = tl[n % 2]
        kb.ts("dve", t_[:, 0:8], GB[:, n, 8:16], -1.0, ALU.mult)
        kb.act(t_[:, 8:16], GB[:, n, 8:16], AF.Exp)
        kb.tt("dve", t_[:, 16:24], t_[:, 8:16], GB[:, n, 0:8], ALU.mult)
        og = Og[n % 2]
        for h0 in range(0, 8, G):
            gens = [head(n, h0 + g, sets[g], X, Xb, sgt, og, t_) for g in range(G)]
            alive = list(gens)
            while alive:
                nxt = []
                for g_ in alive:
                    try:
                        next(g_)
                        nxt.append(g_)
                    except StopIteration:
                        pass
                alive = nxt
        for c in range(8):
            kb.tr(PSot[:, c * 128:(c + 1) * 128], og[:, c * 128:(c + 1) * 128], idb.ap)
        ot = OgT[n % 2]
        kb.copy("act", ot.ap, PSot.re("p (c t) -> p c t", c=8))
        kb.dma(OGTv[:, :, n * 128:(n + 1) * 128], ot.ap, q="pool")
    kb.end_phase()


Prog.phase_D1 = phase_D1
Prog.phase_D2 = phase_D2


def dn_layer(self, l):
    self.phase_D1(l)
    self.phase_D2(l)
    self.phase_R(self.OGT, 8, self.inp["dn_w_out"][l // 2], self.inp["norm_mix_post"][l], self.H)


Prog.dn_layer = dn_layer


_SEQ = 8192
_NT = (_SEQ + NMETA + NPAD) // 128


def build_full(NT):
    p = Prog(NT)
    p.consts()
    p.phase_init()
    for l in range(4):
        if l % 2 == 0:
            p.attn_layer(l)
        else:
            p.dn_layer(l)
        p.ffn(l, final=(l == 3))
    p.finish()
    return p


def kernel(**inputs):
    x = np.asarray(inputs["x"], dtype=np.float32)
    B = x.shape[0]
    meta = np.asarray(inputs["meta_tokens"], dtype=np.float32)
    p = build_full(_NT)
    shared = {k: np.ascontiguousarray(np.asarray(inputs[k], dtype=np.float32)) for k in p.inp if k != "xpad"}
    real = {0: 0, 1: 1, 4: 2, 5: 3}
    zshared = {k: np.zeros_like(v) for k, v in shared.items()}
    zx = np.zeros((NPAD + NMETA + x.shape[1], D), np.float32)
    in_maps = []
    for c in range(8):
        if c in real and real[c] < B:
            b = real[c]
            xpad = np.concatenate([np.zeros((NPAD, D), np.float32), meta, x[b]], axis=0)
            m = dict(shared)
            m["xpad"] = np.ascontiguousarray(xpad)
        else:
            m = dict(zshared)
            m["xpad"] = zx
        in_maps.append(m)
    res = run_bass_kernel_spmd(p.nc, in_maps, core_ids=list(range(8)))
    core_of = {b: c for c, b in real.items()}
    out = np.stack([np.asarray(res.results[core_of[b]]["y"], dtype=np.float32) for b in range(B)], axis=0)
    return out
```

```python
import numpy as np
from contextlib import ExitStack
import concourse.bass as bass
import concourse.mybir as mybir
from concourse.bass_utils import run_bass_kernel_spmd

F32 = mybir.dt.float32
BF16 = mybir.dt.bfloat16
AF = mybir.ActivationFunctionType
ALU = mybir.AluOpType
AX = mybir.AxisListType

D = 1024
NH_A = 16
HD_A = 64
NH_D = 8
HD_D = 128
FFN = 2816
EPS = 1e-6
NPAD = 112
NMETA = 16
NEGBIG = -30000.0


class Buf:
    def __init__(self, t, name, space):
        self.t = t
        self.name = name
        self.space = space
        self.w = {}
        self.r = {}
        self.sem = None

    def __getitem__(self, k):
        return View(self, self.t[k])

    @property
    def ap(self):
        return View(self, self.t[:])


class View:
    def __init__(self, buf, ap):
        self.buf = buf
        self.ap = ap

    def __getitem__(self, k):
        return View(self.buf, self.ap[k])

    def re(self, s, **kw):
        return View(self.buf, self.ap.rearrange(s, **kw))

    def bc(self, shape):
        return View(self.buf, self.ap.to_broadcast(shape))


def _ap(x):
    return x.ap if isinstance(x, View) else x


class KB:
    def __init__(self):
        self.nc = bass.Bass("TRN2", target_bir_lowering=False)
        nc = self.nc
        self.es = ExitStack()
        self.eng = {"pe": nc.tensor, "dve": nc.vector, "act": nc.scalar, "pool": nc.gpsimd, "sp": nc.sync}
        self.cnt = {}
        self.esem = {}
        for e in ("pe", "dve", "act", "pool"):
            self.esem[e] = self.es.enter_context(nc.semaphore("cnt_" + e))
            self.cnt[e] = 0
        self.seen = {e: {} for e in self.eng}
        self.sem_pool = []
        self.live_sems = []
        self.phase_stack = None
        self.phase_bufs = []
        self.uid = 0
        self.dq = 0

    def phase(self):
        self.phase_stack = ExitStack()
        self.phase_bufs = []

    def end_phase(self):
        self.barrier()
        for b in self.phase_bufs:
            if b.sem is not None:
                self.sem_pool.append(b.sem)
                b.sem = None
        self.phase_stack.close()
        self.phase_stack = None
        self.phase_bufs = []

    def carve(self, buf, slices):
        out = []
        for (a, b) in slices:
            sbuf = Buf(buf.t[:, a:b], buf.name + "_%d" % a, buf.space)
            sbuf.w = buf.w
            sbuf.r = buf.r
            self.phase_bufs.append(sbuf)
            out.append(sbuf)
        return out

    def sb(self, name, shape, dt):
        self.uid += 1
        t = self.phase_stack.enter_context(self.nc.sbuf_tensor("%s_%d" % (name, self.uid), list(shape), dt))
        b = Buf(t, name, "sb")
        self.phase_bufs.append(b)
        return b

    def ps(self, name, shape, dt):
        self.uid += 1
        t = self.phase_stack.enter_context(self.nc.psum_tensor("%s_%d" % (name, self.uid), list(shape), dt))
        b = Buf(t, name, "ps")
        self.phase_bufs.append(b)
        return b

    def _getsem(self, b):
        if b.sem is None:
            if self.sem_pool:
                b.sem = self.sem_pool.pop()
            else:
                s = self.es.enter_context(self.nc.semaphore("dsem%d" % len(self.live_sems)))
                b.sem = [s, 0]
                self.live_sems.append(b.sem)
        return b.sem

    def _wait(self, e, deps):
        seen = self.seen[e]
        for sid, (sem, val) in deps.items():
            if seen.get(sid, 0) < val:
                self.eng[e].wait_ge(sem, val)
                seen[sid] = val

    def _deps(self, outs, ins):
        deps = {}

        def add(d):
            for sid, (sem, val) in d.items():
                if sid not in deps or deps[sid][1] < val:
                    deps[sid] = (sem, val)
        for v in ins:
            if isinstance(v, View):
                add(v.buf.w)
        for v in outs:
            if isinstance(v, View):
                add(v.buf.w)
                add(v.buf.r)
        return deps

    def _record(self, outs, ins, sem, val):
        sid = id(sem)
        for v in ins:
            if isinstance(v, View):
                v.buf.r[sid] = (sem, val)
        for v in outs:
            if isinstance(v, View):
                v.buf.w[sid] = (sem, val)

    def op(self, e, fn, outs, ins, pe_acc=False):
        deps = self._deps(outs, ins)
        if e == "pe":
            deps.pop(id(self.esem["pe"]), None)
        self._wait(e, deps)
        ins_ = fn()
        self.cnt[e] += 1
        ins_.then_inc(self.esem[e], 1)
        self._record(outs, ins, self.esem[e], self.cnt[e])
        return ins_

    def dma(self, out, in_, q=None, **kw):
        if q is None:
            q = "sp"
        sbv = out if isinstance(out, View) else in_
        assert isinstance(sbv, View)
        outs = [out] if isinstance(out, View) else []
        ins = [in_] if isinstance(in_, View) else []
        deps = self._deps(outs, ins)
        self._wait(q, deps)
        se = self._getsem(sbv.buf)
        self.eng[q].dma_start(out=_ap(out), in_=_ap(in_), **kw).then_inc(se[0], 16)
        se[1] += 16
        self._record(outs, ins, se[0], se[1])

    def barrier(self):
        allsems = {}
        for e in ("pe", "dve", "act", "pool"):
            allsems[id(self.esem[e])] = (self.esem[e], self.cnt[e])
        for se in self.live_sems:
            allsems[id(se[0])] = (se[0], se[1])
        for e in self.eng:
            self._wait(e, {k: v for k, v in allsems.items() if v[1] > 0})

    def mm(self, out, lhsT, rhs, start=True, stop=True):
        return self.op("pe", lambda: self.nc.tensor.matmul(_ap(out), lhsT=_ap(lhsT), rhs=_ap(rhs), start=start, stop=stop),
                       [out], [lhsT, rhs])

    def tr(self, out, in_, ident):
        return self.op("pe", lambda: self.nc.tensor.transpose(_ap(out), _ap(in_), _ap(ident)), [out], [in_, ident])

    def act(self, out, in_, func, bias=None, scale=None, accum=None, e="act"):
        kw = {}
        ins = [in_]
        outs = [out]
        if bias is not None:
            kw["bias"] = _ap(bias)
            ins.append(bias)
        if scale is not None:
            kw["scale"] = _ap(scale)
            ins.append(scale)
        if accum is not None:
            kw["accum_out"] = _ap(accum)
            outs.append(accum)
        return self.op("act", lambda: self.nc.scalar.activation(out=_ap(out), in_=_ap(in_), func=func, **kw), outs, ins)

    def tt(self, e, out, a, b, op):
        return self.op(e, lambda: self.eng[e].tensor_tensor(out=_ap(out), in0=_ap(a), in1=_ap(b), op=op), [out], [a, b])

    def ts(self, e, out, a, s1, op0, s2=None, op1=None, accum=None):
        kw = {}
        outs = [out]
        if op1 is not None:
            kw["op1"] = op1
        if accum is not None:
            kw["accum_out"] = _ap(accum)
            outs.append(accum)
        return self.op(e, lambda: self.eng[e].tensor_scalar(out=_ap(out), in0=_ap(a), scalar1=_ap(s1), scalar2=_ap(s2), op0=op0, **kw),
                       outs, [a, s1, s2])

    def stt(self, e, out, a, s, b, op0, op1):
        return self.op(e, lambda: self.eng[e].scalar_tensor_tensor(out=_ap(out), in0=_ap(a), scalar=_ap(s), in1=_ap(b), op0=op0, op1=op1),
                       [out], [a, s, b])

    def copy(self, e, out, in_):
        if e == "act":
            return self.op("act", lambda: self.nc.scalar.copy(out=_ap(out), in_=_ap(in_)), [out], [in_])
        return self.op(e, lambda: self.eng[e].tensor_copy(out=_ap(out), in_=_ap(in_)), [out], [in_])

    def amul(self, out, in_, mul):
        return self.op("act", lambda: self.nc.scalar.mul(out=_ap(out), in_=_ap(in_), mul=_ap(mul)), [out], [in_, mul])

    def memset(self, e, out, val):
        return self.op(e, lambda: self.eng[e].memset(_ap(out), val), [out], [])

    def recip(self, out, in_):
        return self.op("dve", lambda: self.nc.vector.reciprocal(out=_ap(out), in_=_ap(in_)), [out], [in_])

    def asel(self, out, in_, pattern, cmp, fill, base, cm):
        return self.op("pool", lambda: self.nc.gpsimd.affine_select(out=_ap(out), in_=_ap(in_), pattern=pattern, compare_op=cmp,
                                                                      fill=fill, base=base, channel_multiplier=cm), [out], [in_])


class Prog:
    def __init__(self, NT, layers=(0, 1, 2, 3), final_out=True):
        self.NT = NT
        self.LP = NT * 128
        self.kb = KB()
        self.nc = self.kb.nc
        self.layers = layers
        nc = self.nc
        LP = self.LP
        S = LP - 128
        shapes = {"xpad": [LP, D], "norm_mix_pre": [4, D], "norm_mix_post": [4, D], "norm_ffn_pre": [4, D], "norm_ffn_post": [4, D],
                  "attn_w_in": [2, D, 4112], "attn_b_forget": [2, 16], "attn_q_norm": [2, 64], "attn_k_norm": [2, 64],
                  "attn_w_out": [2, D, D], "dn_w_in": [2, D, 4112], "dn_conv": [2, 4, 3072], "dn_a_log": [2, 8], "dn_dt_bias": [2, 8],
                  "dn_o_norm": [2, 128], "dn_w_out": [2, D, D], "ffn_w_up": [4, D, 2 * FFN], "ffn_conv": [4, 3, 2 * FFN],
                  "ffn_w_down": [4, FFN, D]}

        class Lazy(dict):
            def __missing__(d, name):
                d[name] = nc.dram_tensor(name, list(shapes[name]), F32, kind="ExternalInput").ap()
                return d[name]
        self.inp = Lazy()
        self.y = nc.dram_tensor("y", [S, D], F32, kind="ExternalOutput").ap()

        def dscr(name, shape, dt):
            return nc.dram_tensor(name, list(shape), dt, kind="Internal").ap()
        self.H = dscr("H", [LP, D], F32)
        self.HT = dscr("HT", [FFN, LP], BF16)
        self.OGT = dscr("OGT", [D, LP], BF16)
        self.SG = dscr("SG", [D, LP], BF16)
        self.QT = dscr("QT", [16, 65, LP], BF16)
        self.KT = dscr("KT", [16, 65, LP], BF16)
        self.VA = dscr("VA", [LP, 16, 65], BF16)
        self.CT = dscr("CT", [128, NT, 16], F32)
        self.QKVD = dscr("QKVD", [24, 128, LP], F32)
        self.SGD = dscr("SGD", [LP, D], BF16)
        self.GBD = dscr("GBD", [128, NT, 16], F32)
        self.CAR = dscr("CAR", [128, NT + 1, 16], F32)

    def consts(self):
        kb = self.kb
        kb.phase()
        es = kb.phase_stack
        self.idf = kb.sb("idf", [128, 128], F32)
        self.idb = kb.sb("idb", [128, 128], BF16)
        kb.memset("pool", self.idf.ap, 1.0)
        kb.asel(self.idf.ap, self.idf.ap, [[-1, 128]], ALU.is_equal, 0.0, 0, 1)
        kb.copy("dve", self.idb.ap, self.idf.ap)
        self.const_stack = es
        kb.phase_stack = None
        kb.phase_bufs = []

    def blocks(self):
        out = []
        t = 0
        while t < self.NT:
            n = min(4, self.NT - t)
            out.append((t, n))
            t += n
        return out

    def load_cols(self, dst, src2d, n, stage, pst):
        kb = self.kb
        kb.dma(stage[0:n, :], src2d)
        kb.tr(pst[:, 0:n], stage[0:n, :], self.idf[0:n, 0:n])
        kb.copy("dve", dst, pst[:, 0:n])

    def load_w(self, dst, W, nch, ncols, gcol, stages, col0=0):
        kb = self.kb
        SEG = 2048
        k = 0
        for c in range(nch):
            for s0 in range(0, ncols, SEG):
                w = min(SEG, ncols - s0)
                st = stages[k % len(stages)]
                e = ("dve", "pool")[k % 2]
                k += 1
                kb.dma(st[:, 0:w], W[c * 128:(c + 1) * 128, s0:s0 + w])
                if gcol is not None:
                    kb.ts(e, dst[:, c, col0 + s0:col0 + s0 + w], st[:, 0:w], gcol[:, c:c + 1], ALU.mult)
                else:
                    kb.copy(e, dst[:, c, col0 + s0:col0 + s0 + w], st[:, 0:w])

    def prenormT(self, ht, aT_dst, abf, ss, pst, junk, k):
        kb = self.kb
        kb.memset("dve", ss[:, 0:1], 0.0)
        kb.act(junk.ap, ht.ap, AF.Square, accum=ss[:, 0:1])
        kb.act(ss[:, 1:2], ss[:, 0:1], AF.Sqrt, bias=EPS, scale=1.0 / D)
        kb.recip(ss[:, 2:3], ss[:, 1:2])
        kb.ts("dve", abf.ap, ht.ap, ss[:, 2:3], ALU.mult)
        for c in range(8):
            kb.tr(pst[:, c * 128:(c + 1) * 128], abf[:, c * 128:(c + 1) * 128], self.idb.ap)
        kb.copy(("act", "dve")[k % 2], aT_dst, pst.ap.re("p (c t) -> p c t", c=8))

    def phase_R(self, srcT, nch, W, gpost, hsrc, final=False):
        kb = self.kb
        kb.phase()
        Wb = kb.sb("Wb", [128, nch, D], BF16)
        stages = [kb.sb("wst", [128, 2048], F32) for _ in range(2)]
        self.load_w(Wb, W, nch, D, None, stages)
        g = kb.sb("g", [128, D], F32)
        kb.dma(g.ap, gpost.partition_broadcast(128))
        Sb = [kb.sb("Sb", [128, nch, 512], BF16) for _ in range(2)]
        hts = [kb.sb("ht", [128, D], F32) for _ in range(3)]
        fn = [kb.sb("fn", [128, D], F32) for _ in range(2)]
        ssb = [kb.sb("ss", [128, 8], F32) for _ in range(2)]
        junk = kb.sb("junk", [128, 512], BF16)
        PS = [[kb.ps("psa", [128, 512], F32), kb.ps("psb", [128, 512], F32)] for _ in range(2)]
        srcv = srcT.rearrange("(c p) t -> p c t", p=128)
        blocks = self.blocks()
        kb.dma(Sb[0][:, :, 0:blocks[0][1] * 128], srcv[:, :, 0:blocks[0][1] * 128])
        ti = 0
        for bi, (t0, n) in enumerate(blocks):
            if bi + 1 < len(blocks):
                t1, n1 = blocks[bi + 1]
                kb.dma(Sb[(bi + 1) % 2][:, :, 0:n1 * 128], srcv[:, :, t1 * 128:(t1 + n1) * 128])
            S = Sb[bi % 2]
            for j in range(n):
                tile = t0 + j
                ht = hts[ti % 3]
                kb.dma(ht.ap, hsrc[tile * 128:(tile + 1) * 128, :])
                pa, pb = PS[ti % 2]
                ss = ssb[ti % 2]
                f = fn[ti % 2]
                for c in range(nch):
                    kb.mm(pa.ap, S[:, c, j * 128:(j + 1) * 128], Wb[:, c, 0:512], start=(c == 0), stop=(c == nch - 1))
                for c in range(nch):
                    kb.mm(pb.ap, S[:, c, j * 128:(j + 1) * 128], Wb[:, c, 512:1024], start=(c == 0), stop=(c == nch - 1))
                kb.memset("dve", ss[:, 0:2], 0.0)
                kb.act(junk.ap, pa.ap, AF.Square, accum=ss[:, 0:1])
                kb.act(junk.ap, pb.ap, AF.Square, accum=ss[:, 1:2])
                kb.tt("dve", ss[:, 2:3], ss[:, 0:1], ss[:, 1:2], ALU.add)
                kb.act(ss[:, 3:4], ss[:, 2:3], AF.Sqrt, bias=EPS, scale=1.0 / D)
                kb.recip(ss[:, 4:5], ss[:, 3:4])
                kb.stt("dve", f[:, 0:512], pa.ap, ss[:, 4:5], g[:, 0:512], ALU.mult, ALU.mult)
                kb.stt("dve", f[:, 512:1024], pb.ap, ss[:, 4:5], g[:, 512:1024], ALU.mult, ALU.mult)
                kb.tt("pool", f.ap, f.ap, ht.ap, ALU.add)
                if final:
                    if tile >= 1:
                        kb.dma(self.y[(tile - 1) * 128:tile * 128, :], f.ap, q="pool")
                else:
                    kb.dma(self.H[tile * 128:(tile + 1) * 128, :], f.ap, q="pool")
                ti += 1
        kb.end_phase()

    def phase_F1(self, l):
        kb = self.kb
        kb.phase()
        Wup = kb.sb("Wup", [128, 8, 2 * FFN], BF16)
        stages = [kb.sb("wst", [128, 2048], F32) for _ in range(2)]
        cst = kb.sb("cst", [128, 128], F32)
        gcol = kb.sb("gcol", [128, 8], F32)
        cw = kb.sb("cw", [128, 3, 44], F32)
        pst = kb.ps("pst", [128, 1024], BF16)
        psc = kb.ps("psc", [128, 128], F32)
        self.load_cols(gcol.ap, self.inp["norm_ffn_pre"][l].rearrange("(c p) -> c p", p=128), 8, cst, psc)
        for k in range(3):
            self.load_cols(cw[:, k, :], self.inp["ffn_conv"][l, k].rearrange("(c p) -> c p", p=128), 44, cst, psc)
        self.load_w(Wup, self.inp["ffn_w_up"][l], 8, 2 * FFN, gcol, stages)
        halo = kb.sb("halo", [128, 44, 2], F32)
        kb.memset("pool", halo.ap, 0.0)
        hts = [kb.sb("ht", [128, D], F32) for _ in range(2)]
        abf = [kb.sb("abf", [128, D], BF16) for _ in range(2)]
        ssb = [kb.sb("ss", [128, 8], F32) for _ in range(2)]
        junk = kb.sb("junk", [128, D], BF16)
        aT = [kb.sb("aT", [128, 8, 512], BF16) for _ in range(2)]
        Hb = [kb.sb("Hb", [128, 22, 512], BF16) for _ in range(2)]
        Ug = [kb.sb("Ug", [128, 514], F32) for _ in range(2)]
        Uu = [kb.sb("Uu", [128, 514], F32) for _ in range(2)]
        yg = [kb.sb("yg", [128, 512], F32) for _ in range(2)]
        yu = [kb.sb("yu", [128, 512], F32) for _ in range(2)]
        gg = [kb.sb("gg", [128, 512], F32) for _ in range(2)]
        ptmp = [kb.sb("ptmp", [128, 512], F32) for _ in range(2)]
        PSg = [kb.ps("psg", [128, 512], F32) for _ in range(2)]
        PSu = [kb.ps("psu", [128, 512], F32) for _ in range(2)]
        HTv = self.HT.rearrange("(c p) t -> p c t", p=128)
        hsrc = self.H
        ti = 0
        it = 0
        for bi, (t0, n) in enumerate(self.blocks()):
            TW = n * 128
            A = aT[bi % 2]
            for j in range(n):
                tile = t0 + j
                ht = hts[ti % 2]
                kb.dma(ht.ap, hsrc[tile * 128:(tile + 1) * 128, :])
                self.prenormT(ht, A[:, :, j * 128:(j + 1) * 128], abf[ti % 2], ssb[ti % 2], pst, junk, ti)
                ti += 1
            HB = Hb[bi % 2]
            for fc in range(22):
                pg = PSg[it % 2]; pu = PSu[it % 2]
                ug = Ug[it % 2]; uu = Uu[it % 2]
                for c in range(8):
                    kb.mm(pg[:, 0:TW], Wup[:, c, fc * 128:(fc + 1) * 128], A[:, c, 0:TW], start=(c == 0), stop=(c == 7))
                for c in range(8):
                    kb.mm(pu[:, 0:TW], Wup[:, c, FFN + fc * 128:FFN + (fc + 1) * 128], A[:, c, 0:TW], start=(c == 0), stop=(c == 7))
                a = yg[it % 2]; b = yu[it % 2]
                for (u_, p_, y_, hc) in ((ug, pg, a, fc), (uu, pu, b, 22 + fc)):
                    kb.copy("act", u_[:, 2:2 + TW], p_[:, 0:TW])
                    kb.amul(y_[:, 0:TW], p_[:, 0:TW], cw[:, 2, hc:hc + 1])
                    kb.copy("dve", u_[:, 0:2], halo[:, hc, :])
                    kb.copy("dve", halo[:, hc, :], u_[:, TW:TW + 2])
                    kb.stt("dve", y_[:, 0:TW], u_[:, 1:1 + TW], cw[:, 1, hc:hc + 1], y_[:, 0:TW], ALU.mult, ALU.add)
                    kb.stt("dve", y_[:, 0:TW], u_[:, 0:TW], cw[:, 0, hc:hc + 1], y_[:, 0:TW], ALU.mult, ALU.add)
                kb.act(gg[it % 2][:, 0:TW], a[:, 0:TW], AF.Gelu_apprx_tanh)
                kb.tt("dve", HB[:, fc, 0:TW], gg[it % 2][:, 0:TW], b[:, 0:TW], ALU.mult)
                it += 1
            kb.dma(HTv[:, :, t0 * 128:t0 * 128 + TW], HB[:, :, 0:TW], q="pool")
        kb.end_phase()

    def phase_init(self):
        kb = self.kb
        kb.phase()
        hts = [kb.sb("ht", [128, D], F32) for _ in range(4)]
        for t in range(self.NT):
            ht = hts[t % 4]
            kb.dma(ht.ap, self.inp["xpad"][t * 128:(t + 1) * 128, :])
            kb.dma(self.H[t * 128:(t + 1) * 128, :], ht.ap, q="pool")
        kb.end_phase()

    def ffn(self, l, final):
        self.phase_F1(l)
        self.phase_R(self.HT, 22, self.inp["ffn_w_down"][l], self.inp["norm_ffn_post"][l], self.H, final=final)

    def finish(self):
        self.const_stack.close()
        self.kb.es.close()


def phase_A1(self, l):
    kb = self.kb
    j = l // 2
    NT = self.NT
    kb.phase()
    Win = kb.sb("Win", [128, 8, 4112], BF16)
    stages = [kb.sb("wst", [128, 2048], F32) for _ in range(2)]
    cst = kb.sb("cst", [128, 128], F32)
    gcol = kb.sb("gcol", [128, 8], F32)
    pst = kb.ps("pst", [128, 1024], BF16)
    PSq = [kb.ps("psq", [128, 512], F32) for _ in range(2)]
    PSs = kb.ps("pss", [128, 512], F32)
    PSv = kb.ps("psv", [128, 1024], F32)
    PSm = kb.ps("psm", [128, 512], F32)
    PSr = kb.ps("psr", [16, 512], BF16)
    self.load_cols(gcol.ap, self.inp["norm_mix_pre"][l].rearrange("(c p) -> c p", p=128), 8, cst, PSs)
    self.load_w(Win, self.inp["attn_w_in"][j], 8, 4112, gcol, stages)
    qg = kb.sb("qg", [128, 2], F32)
    for half in range(2):
        kb.dma(qg[half * 64:(half + 1) * 64, 0:1], self.inp["attn_q_norm"][j].rearrange("(p o) -> p o", o=1))
        kb.dma(qg[half * 64:(half + 1) * 64, 1:2], self.inp["attn_k_norm"][j].rearrange("(p o) -> p o", o=1))
    kb.ts("dve", qg[:, 0:1], qg[:, 0:1], 0.125, ALU.mult)
    bfg = kb.sb("bfg", [128, 16], F32)
    kb.dma(bfg.ap, self.inp["attn_b_forget"][j].partition_broadcast(128))
    BD = kb.sb("BD", [128, 128], BF16)
    kb.memset("pool", BD.ap, 1.0)
    kb.asel(BD[:, 0:64], BD[:, 0:64], [[0, 64]], ALU.is_ge, 0.0, 63, -1)
    kb.asel(BD[:, 64:128], BD[:, 64:128], [[0, 64]], ALU.is_ge, 0.0, -64, 1)
    tri = kb.sb("tri", [128, 128], F32)
    kb.memset("pool", tri.ap, 1.0)
    kb.asel(tri.ap, tri.ap, [[1, 128]], ALU.is_ge, 0.0, 0, -1)
    onesf = kb.sb("onesf", [128, 128], F32)
    kb.memset("pool", onesf.ap, 1.0)
    onesb = kb.sb("onesb", [16, 512], BF16)
    kb.memset("pool", onesb.ap, 1.0)
    Ctok = kb.sb("Ctok", [128, NT, 16], F32)
    CAR = kb.sb("CAR", [128, NT + 1, 16], F32)
    kb.memset("dve", CAR[:, 0, :], 0.0)
    hts = [kb.sb("ht", [128, D], F32) for _ in range(2)]
    abf = [kb.sb("abf", [128, D], BF16) for _ in range(2)]
    ssb = [kb.sb("ss", [128, 8], F32) for _ in range(2)]
    junk = kb.sb("junk", [128, D], BF16)
    aT = [kb.sb("aT", [128, 8, 512], BF16) for _ in range(2)]
    sq = [kb.sb("sq", [128, 512], BF16) for _ in range(2)]
    rms = [kb.sb("rms", [128, 512], F32) for _ in range(2)]
    qn = [kb.sb("qn", [128, 512], BF16) for _ in range(3)]
    Va = [kb.sb("Va", [128, 16, 65], BF16) for _ in range(2)]
    V0 = kb.sb("V0", [128, 16, 65], BF16)
    for v in Va + [V0]:
        kb.memset("pool", v.ap, 1.0)
    kb.memset("pool", V0[0:NPAD, :, 64:65], 0.0)
    xs = [kb.sb("xs", [128, 48], F32) for _ in range(2)]
    rb = [kb.sb("rb", [128, 16], BF16) for _ in range(2)]
    rT = [kb.sb("rT", [16, 512], BF16) for _ in range(2)]
    ti = 0
    it = 0
    for bi, (t0, n) in enumerate(self.blocks()):
        TW = n * 128
        c0 = t0 * 128
        A = aT[bi % 2]
        for jj in range(n):
            tile = t0 + jj
            ht = hts[ti % 2]
            kb.dma(ht.ap, self.H[tile * 128:(tile + 1) * 128, :])
            self.prenormT(ht, A[:, :, jj * 128:(jj + 1) * 128], abf[ti % 2], ssb[ti % 2], pst, junk, ti)
            ti += 1
        def projA(idx, it_):
            pq_ = PSq[it_ % 2]
            col_ = (idx // 8) * 1024 + (idx % 8) * 128
            for c in range(8):
                kb.mm(pq_[:, 0:TW], Win[:, c, col_:col_ + 128], A[:, c, 0:TW], start=(c == 0), stop=(c == 7))
        projA(0, it)
        for which in range(2):
            dst = self.QT if which == 0 else self.KT
            for ch in range(8):
                pq = PSq[it % 2]
                s_ = sq[it % 2]
                kb.act(s_[:, 0:TW], pq[:, 0:TW], AF.Square)
                if which * 8 + ch + 1 < 16:
                    projA(which * 8 + ch + 1, it + 1)
                kb.mm(PSs[:, 0:TW], BD.ap, s_[:, 0:TW])
                r_ = rms[it % 2]
                kb.act(r_[:, 0:TW], PSs[:, 0:TW], AF.Sqrt, bias=EPS, scale=1.0 / 64)
                kb.recip(r_[:, 0:TW], r_[:, 0:TW])
                q_ = qn[it % 3]
                kb.stt("dve", q_[:, 0:TW], pq[:, 0:TW], qg[:, which:which + 1], r_[:, 0:TW], ALU.mult, ALU.mult)
                kb.dma(dst[2 * ch, 0:64, c0:c0 + TW], q_[0:64, 0:TW], q="pool")
                kb.dma(dst[2 * ch + 1, 0:64, c0:c0 + TW], q_[64:128, 0:TW], q="pool")
                it += 1
        for ch in range(8):
            pq = PSq[it % 2]
            col = 3072 + ch * 128
            for c in range(8):
                kb.mm(pq[:, 0:TW], Win[:, c, col:col + 128], A[:, c, 0:TW], start=(c == 0), stop=(c == 7))
            q_ = qn[it % 3]
            kb.act(q_[:, 0:TW], pq[:, 0:TW], AF.Sigmoid)
            kb.dma(self.SG[ch * 128:(ch + 1) * 128, c0:c0 + TW], q_[:, 0:TW], q="pool")
            it += 1
        kb.dma(self.KT[:, 64, c0:c0 + TW], onesb[:, 0:TW], q="pool")
        for jj in range(n):
            tile = t0 + jj
            tc_ = slice(jj * 128, (jj + 1) * 128)
            for half in range(2):
                for c in range(8):
                    kb.mm(PSv[:, half * 512:(half + 1) * 512], A[:, c, tc_], Win[:, c, 2048 + half * 512:2048 + (half + 1) * 512],
                          start=(c == 0), stop=(c == 7))
            V = V0 if tile == 0 else Va[tile % 2]
            kb.copy("act", V[:, :, 0:64], PSv.ap.re("p (h d) -> p h d", h=16))
            kb.dma(self.VA[tile * 128:(tile + 1) * 128, :, :], V.ap, q="pool")
            for c in range(8):
                kb.mm(PSm[:, 0:16], A[:, c, tc_], Win[:, c, 4096:4112], start=(c == 0), stop=(c == 7))
            x = xs[tile % 2]
            kb.tt("dve", x[:, 0:16], PSm[:, 0:16], bfg.ap, ALU.add)
            kb.act(x[:, 16:32], x[:, 0:16], AF.Exp, scale=-1.0)
            kb.act(x[:, 32:48], x[:, 16:32], AF.Ln, bias=1.0)
            kb.mm(PSm[:, 16:32], tri.ap, x[:, 32:48])
            kb.mm(PSm[:, 32:48], onesf.ap, x[:, 32:48])
            kb.tt("dve", Ctok[:, tile, :], PSm[:, 16:32], CAR[:, tile, :], ALU.add)
            kb.tt("dve", CAR[:, tile + 1, :], PSm[:, 32:48], CAR[:, tile, :], ALU.add)
        R_ = rT[bi % 2]
        for jj in range(n):
            tile = t0 + jj
            r2 = rb[tile % 2]
            kb.tt("dve", r2.ap, CAR[:, t0 + n, :], Ctok[:, tile, :], ALU.subtract)
            kb.tr(PSr[:, jj * 128:(jj + 1) * 128], r2.ap, self.idb.ap)
        kb.copy("dve", R_[:, 0:TW], PSr[:, 0:TW])
        kb.dma(self.QT[:, 64, c0:c0 + TW], R_[:, 0:TW], q="pool")
    kb.dma(self.CT, Ctok.ap, q="pool")
    kb.dma(self.CAR, CAR.ap, q="pool")
    kb.end_phase()


def phase_A2(self, l):
    kb = self.kb
    NT = self.NT
    LP = self.LP
    kb.phase()
    Ctok = kb.sb("Ctok", [128, NT, 16], F32)
    CAR = kb.sb("CAR", [128, NT + 1, 16], F32)
    kb.dma(Ctok.ap, self.CT)
    kb.dma(CAR.ap, self.CAR)
    masks = []
    for jj in range(4):
        m = kb.sb("mask", [128, 512], BF16)
        kb.memset("pool", m.ap, 0.0)
        kb.asel(m.ap, m.ap, [[1, 512]], ALU.is_ge, NEGBIG, -128 * jj, -1)
        masks.append(m)
    onesf = kb.sb("onesf", [128, 64], F32)
    kb.memset("pool", onesf.ap, 1.0)
    KTh = [kb.sb("KTh", [128, LP], BF16) for _ in range(2)]
    VAh = [kb.sb("VAh", [128, NT, 65], BF16) for _ in range(2)]
    QTb = [kb.sb("QTb", [128, 512], BF16) for _ in range(2)]
    for b_ in KTh + QTb:
        kb.memset("pool", b_[64:128, :], 0.0)
    SGb = [kb.sb("SGb", [64, 512], BF16) for _ in range(2)]
    bias = [kb.sb("bias", [128, NT], F32) for _ in range(2)]
    P = [kb.sb("P", [128, 512], BF16) for _ in range(4)]
    oacc = [kb.sb("oacc", [65, 512], F32) for _ in range(2)]
    rden = [kb.sb("rden", [128, 512], F32) for _ in range(2)]
    for b_ in rden:
        kb.memset("pool", b_.ap, 0.0)
    sel = kb.sb("sel", [128, 64], F32)
    kb.memset("pool", sel.ap, 0.0)
    kb.memset("pool", sel[64:96, :], 1.0)
    kb.asel(sel[64:96, :], sel[64:96, :], [[0, 64]], ALU.is_ge, 0.0, 0, -1)
    tmp = [kb.sb("tmp", [64, 512], F32) for _ in range(2)]
    og = [kb.sb("og", [64, 512], BF16) for _ in range(2)]
    PSs = [kb.ps("pss", [128, 512], F32) for _ in range(4)]
    PSo = [kb.ps("pso", [65, 512], F32) for _ in range(2)]
    PSb = [kb.ps("psb", [64, 512], F32) for _ in range(2)]
    VAv = self.VA.rearrange("(t p) h e -> p t h e", p=128)
    blocks = self.blocks()
    it = 0
    ip = 0
    def load_head(hh):
        kb.dma(KTh[hh % 2][0:65, :], self.KT[hh])
        for a in range(0, NT, 16):
            b = min(NT, a + 16)
            kb.dma(VAh[hh % 2][:, a:b, :], VAv[:, a:b, hh, :])
    load_head(0)
    for h in range(16):
        if h + 1 < 16:
            load_head(h + 1)
        K_ = KTh[h % 2]
        V_ = VAh[h % 2]
        for bi, (t0, n) in enumerate(blocks):
            TW = n * 128
            c0 = t0 * 128
            Q_ = QTb[it % 2]
            S_ = SGb[it % 2]
            kb.dma(Q_[0:65, 0:TW], self.QT[h, :, c0:c0 + TW])
            kb.dma(S_[:, 0:TW], self.SG[h * 64:(h + 1) * 64, c0:c0 + TW])
            nkt = t0 + n
            b_ = bias[it % 2]
            kb.ts("dve", b_[:, 0:nkt], Ctok[:, 0:nkt, h], CAR[:, t0 + n, h:h + 1], ALU.subtract)
            po = PSo[it % 2]
            def emit_s(kt, ipk):
                ps = PSs[ipk % 4]
                diag = kt >= t0
                kb.mm(ps[:, 0:TW], K_[:, kt * 128:(kt + 1) * 128], Q_[:, 0:TW], start=True, stop=not diag)
                if diag:
                    kb.mm(ps[:, 0:TW], self.idb.ap, masks[kt - t0][:, 0:TW], start=False, stop=True)
                p_ = P[ipk % 4]
                kb.act(p_[:, 0:TW], ps[:, 0:TW], AF.Exp, bias=b_[:, kt:kt + 1])
            emit_s(0, ip)
            if nkt > 1:
                emit_s(1, ip + 1)
            for kt in range(nkt):
                if kt + 2 < nkt:
                    emit_s(kt + 2, ip + 2)
                kb.mm(po[:, 0:TW], V_[:, kt, :], P[ip % 4][:, 0:TW], start=(kt == 0), stop=(kt == nkt - 1))
                ip += 1
            oa = oacc[it % 2]
            rd = rden[it % 2]
            kb.copy("act", oa[:, 0:TW], po[:, 0:TW])
            kb.ts("dve", rd[64:65, 0:TW], oa[64:65, 0:TW], 1e-30, ALU.max)
            kb.recip(rd[64:65, 0:TW], rd[64:65, 0:TW])
            pb = PSb[it % 2]
            kb.mm(pb[:, 0:TW], sel.ap, rd[:, 0:TW])
            t_ = tmp[it % 2]
            kb.tt("dve", t_[:, 0:TW], oa[0:64, 0:TW], pb[:, 0:TW], ALU.mult)
            o_ = og[it % 2]
            kb.tt("pool", o_[:, 0:TW], t_[:, 0:TW], S_[:, 0:TW], ALU.mult)
            kb.dma(self.OGT[h * 64:(h + 1) * 64, c0:c0 + TW], o_[:, 0:TW], q="pool")
            it += 1
    kb.end_phase()


Prog.phase_A1 = phase_A1
Prog.phase_A2 = phase_A2


def attn_layer(self, l):
    self.phase_A1(l)
    self.phase_A2(l)
    self.phase_R(self.OGT, 8, self.inp["attn_w_out"][l // 2], self.inp["norm_mix_post"][l], self.H)


Prog.attn_layer = attn_layer


def phase_D1(self, l):
    kb = self.kb
    j = l // 2
    NT = self.NT
    kb.phase()
    Win = kb.sb("Win", [128, 8, 4112], BF16)
    stages = [kb.sb("wst", [128, 2048], F32) for _ in range(2)]
    cst = kb.sb("cst", [128, 128], F32)
    gcol = kb.sb("gcol", [128, 8], F32)
    cw = kb.sb("cw", [128, 4, 24], F32)
    pst = kb.ps("pst", [128, 1024], BF16)
    PSq = [kb.ps("psq", [128, 512], F32) for _ in range(2)]
    PSs = kb.ps("pss", [128, 512], F32)
    PSv = kb.ps("psv", [128, 1024], F32)
    PSm = kb.ps("psm", [128, 512], F32)
    self.load_cols(gcol.ap, self.inp["norm_mix_pre"][l].rearrange("(c p) -> c p", p=128), 8, cst, PSs)
    for k in range(4):
        self.load_cols(cw[:, k, :], self.inp["dn_conv"][j, k].rearrange("(c p) -> c p", p=128), 24, cst, PSs)
    self.load_w(Win, self.inp["dn_w_in"][j], 8, 4112, gcol, stages)
    dtb = kb.sb("dtb", [128, 8], F32)
    nA = kb.sb("nA", [128, 8], F32)
    kb.dma(dtb.ap, self.inp["dn_dt_bias"][j].partition_broadcast(128))
    kb.dma(nA.ap, self.inp["dn_a_log"][j].partition_broadcast(128))
    kb.act(nA.ap, nA.ap, AF.Exp)
    kb.ts("dve", nA.ap, nA.ap, -1.0, ALU.mult)
    tri = kb.sb("tri", [128, 128], F32)
    kb.memset("pool", tri.ap, 1.0)
    kb.asel(tri.ap, tri.ap, [[1, 128]], ALU.is_ge, 0.0, 0, -1)
    onesb = kb.sb("onesb", [128, 128], BF16)
    kb.memset("pool", onesb.ap, 1.0)
    halo = kb.sb("halo", [128, 24, 3], F32)
    kb.memset("pool", halo.ap, 0.0)
    GB = kb.sb("GB", [128, NT, 16], F32)
    hts = [kb.sb("ht", [128, D], F32) for _ in range(2)]
    abf = [kb.sb("abf", [128, D], BF16) for _ in range(2)]
    ssb = [kb.sb("ss", [128, 8], F32) for _ in range(2)]
    junk = kb.sb("junk", [128, D], BF16)
    aT = [kb.sb("aT", [128, 8, 512], BF16) for _ in range(2)]
    U = [kb.sb("U", [128, 515], F32) for _ in range(2)]
    yv = [kb.sb("yv", [128, 512], F32) for _ in range(2)]
    z = [kb.sb("z", [128, 512], F32) for _ in range(3)]
    sq = [kb.sb("sq", [128, 512], BF16) for _ in range(2)]
    rms = [kb.sb("rms", [128, 512], F32) for _ in range(2)]
    sg = [kb.sb("sg", [128, D], BF16) for _ in range(2)]
    xs = [kb.sb("xs", [128, 32], F32) for _ in range(2)]
    ti = 0
    it = 0
    for bi, (t0, n) in enumerate(self.blocks()):
        TW = n * 128
        c0 = t0 * 128
        A = aT[bi % 2]
        for jj in range(n):
            tile = t0 + jj
            ht = hts[ti % 2]
            kb.dma(ht.ap, self.H[tile * 128:(tile + 1) * 128, :])
            self.prenormT(ht, A[:, :, jj * 128:(jj + 1) * 128], abf[ti % 2], ssb[ti % 2], pst, junk, ti)
            ti += 1
        def stageA(fc, it_):
            pq = PSq[it_ % 2]
            for c in range(8):
                kb.mm(pq[:, 0:TW], Win[:, c, fc * 128:(fc + 1) * 128], A[:, c, 0:TW], start=(c == 0), stop=(c == 7))
            u = U[it_ % 2]
            kb.copy("dve", u[:, 0:3], halo[:, fc, :])
            kb.copy("act", u[:, 3:3 + TW], pq[:, 0:TW])
            kb.copy("dve", halo[:, fc, :], u[:, TW:TW + 3])
            y = yv[it_ % 2]
            kb.ts("dve", y[:, 0:TW], u[:, 3:3 + TW], cw[:, 3, fc:fc + 1], ALU.mult)
            for k in (2, 1, 0):
                kb.stt("dve", y[:, 0:TW], u[:, k:k + TW], cw[:, k, fc:fc + 1], y[:, 0:TW], ALU.mult, ALU.add)

        def stageB(fc, it_):
            y = yv[it_ % 2]
            z_ = z[it_ % 3]
            kb.act(z_[:, 0:TW], y[:, 0:TW], AF.Silu)
            if fc < 16:
                s_ = sq[it_ % 2]
                kb.act(s_[:, 0:TW], z_[:, 0:TW], AF.Square)
                kb.mm(PSs[:, 0:TW], onesb.ap, s_[:, 0:TW])
                r_ = rms[it_ % 2]
                kb.act(r_[:, 0:TW], PSs[:, 0:TW], AF.Sqrt, bias=EPS)
                kb.recip(r_[:, 0:TW], r_[:, 0:TW])
                if fc < 8:
                    kb.stt("dve", z_[:, 0:TW], z_[:, 0:TW], float(HD_D ** -0.5), r_[:, 0:TW], ALU.mult, ALU.mult)
                else:
                    kb.tt("dve", z_[:, 0:TW], z_[:, 0:TW], r_[:, 0:TW], ALU.mult)
            kb.dma(self.QKVD[fc, :, c0:c0 + TW], z_[:, 0:TW], q="pool")

        stageA(0, it)
        for fc in range(24):
            if fc + 1 < 24:
                stageA(fc + 1, it + 1)
            stageB(fc, it)
            it += 1
        for jj in range(n):
            tile = t0 + jj
            tc_ = slice(jj * 128, (jj + 1) * 128)
            for half in range(2):
                for c in range(8):
                    kb.mm(PSv[:, half * 512:(half + 1) * 512], A[:, c, tc_], Win[:, c, 3072 + half * 512:3072 + (half + 1) * 512],
                          start=(c == 0), stop=(c == 7))
            s2 = sg[tile % 2]
            kb.act(s2.ap, PSv.ap, AF.Silu)
            kb.dma(self.SGD[tile * 128:(tile + 1) * 128, :], s2.ap, q="pool")
            for c in range(8):
                kb.mm(PSm[:, 0:16], A[:, c, tc_], Win[:, c, 4096:4112], start=(c == 0), stop=(c == 7))
            x = xs[tile % 2]
            kb.act(GB[:, tile, 0:8], PSm[:, 0:8], AF.Sigmoid)
            kb.tt("dve", x[:, 0:8], PSm[:, 8:16], dtb.ap, ALU.add)
            kb.act(x[:, 8:16], x[:, 0:8], AF.Exp)
            kb.act(x[:, 16:24], x[:, 8:16], AF.Ln, bias=1.0)
            kb.tt("dve", x[:, 24:32], x[:, 16:24], nA.ap, ALU.mult)
            kb.mm(PSm[:, 16:24], tri.ap, x[:, 24:32])
            kb.copy("dve", GB[:, tile, 8:16], PSm[:, 16:24])
    kb.dma(self.GBD, GB.ap, q="pool")
    kb.end_phase()


def phase_D2(self, l, G=8):
    kb = self.kb
    j = l // 2
    NT = self.NT
    kb.phase()
    GB = kb.sb("GB", [128, NT, 16], F32)
    kb.dma(GB.ap, self.GBD)
    onrm = kb.sb("onrm", [128, 128], F32)
    kb.dma(onrm.ap, self.inp["dn_o_norm"][j].partition_broadcast(128))
    onesf = kb.sb("onesf", [128, 128], F32)
    kb.memset("pool", onesf.ap, 1.0)
    idf = self.idf
    idb = self.idb
    NMsT = kb.sb("NMsT", [128, 128], F32)
    kb.memset("pool", NMsT.ap, 0.0)
    kb.asel(NMsT.ap, NMsT.ap, [[1, 128]], ALU.is_gt, NEGBIG, 0, -1)
    NPs = kb.sb("NPs", [128, 128], F32)
    kb.memset("pool", NPs.ap, 0.0)
    kb.asel(NPs.ap, NPs.ap, [[-1, 128]], ALU.is_gt, -NEGBIG, 0, 1)
    BDm = kb.sb("BDm", [128, 128], F32)
    kb.memset("pool", BDm.ap, 1.0)
    kb.asel(BDm[:, 0:64], BDm[:, 0:64], [[0, 64]], ALU.is_ge, 0.0, 63, -1)
    kb.asel(BDm[:, 64:128], BDm[:, 64:128], [[0, 64]], ALU.is_ge, 0.0, -64, 1)
    ST = [kb.sb("ST", [128, 128], F32) for _ in range(8)]
    STb = [kb.sb("STb", [128, 128], BF16) for _ in range(8)]
    for s_ in ST + STb:
        kb.memset("dve", s_.ap, 0.0)
    X24 = [kb.sb("X24", [128, 24, 128], F32) for _ in range(2)]
    X24b = [kb.sb("X24b", [128, 24, 128], BF16) for _ in range(2)]
    SGt = [kb.sb("SGt", [128, D], BF16) for _ in range(2)]
    Og = [kb.sb("Og", [128, D], BF16) for _ in range(2)]
    OgT = [kb.sb("OgT", [128, 8, 128], BF16) for _ in range(2)]
    tl = [kb.sb("tl", [128, 32], F32) for _ in range(2)]
    sets = []
    f32names = ["DG", "Gs", "DTs", "DTi", "Dn", "A", "B", "Ad", "Ao", "Bd", "Bo", "P", "Pt",
                "U0s", "EGbc", "tmp", "on"]
    bf16names = ["R", "QKT", "kbg", "kd", "vb", "WT", "Us", "qdT",
                 "Adh", "Adl", "Bdh", "Bdl", "Xah", "Xal", "Xbh", "Xbl", "Xtah", "Xtal", "Xtbh", "Xtbl",
                 "Pth", "Ptl", "Ph", "Pl", "Aoh", "Aol", "Zh", "Zl"]
    for g in range(G):
        d = {}
        for nm in f32names:
            w = 256 if nm in ("DG", "Gs") else 128
            d[nm] = kb.sb(nm, [128, w], F32)
        for nm in bf16names:
            d[nm] = kb.sb(nm, [128, 128], BF16)
        d["cols"] = kb.sb("cols", [128, 8], F32)
        bA = kb.ps("bA", [128, 512], F32)
        d["bA"] = bA
        d["PSG"], d["PSK"], d["PSQ"] = kb.carve(bA, [(0, 256), (256, 384), (384, 512)])
        d["C0"], d["C1"], d["C2"], d["C3"] = kb.carve(bA, [(0, 128), (128, 256), (256, 384), (384, 512)])
        d["D0"], d["D1"], d["D2"], d["D3"] = d["C0"], d["C1"], d["C2"], d["C3"]
        d["C2b"] = View(d["C2"], d["C2"].t.bitcast(BF16)[:, 0:128])
        d["C3b"] = View(d["C3"], d["C3"].t.bitcast(BF16)[:, 0:128])
        sets.append(d)
    Xv = self.QKVD.rearrange("f d t -> d f t")
    OGTv = self.OGT.rearrange("(c p) t -> p c t", p=128)

    def head(n, hd, d, X, Xb, sgt, og, t_):
        qT = X[:, hd, :]
        qTb = Xb[:, hd, :]
        kTb = Xb[:, 8 + hd, :]
        vTb = Xb[:, 16 + hd, :]
        beta = GB[:, n, hd:hd + 1]
        gc = GB[:, n, 8 + hd:9 + hd]
        ngc = t_[:, hd:hd + 1]
        bg = t_[:, 16 + hd:17 + hd]
        cols = d["cols"]
        kb.ts("dve", d["DG"][:, 0:128], idf.ap, gc, ALU.mult)
        kb.ts("dve", d["DG"][:, 128:256], idf.ap, beta, ALU.mult)
        kb.mm(d["PSG"].ap, onesf.ap, d["DG"].ap)
        kb.copy("act", d["Gs"].ap, d["PSG"].ap)
        kb.mm(d["PSK"].ap, kTb, kTb)
        kb.mm(d["PSQ"].ap, kTb, qTb)
        yield
        G_ = d["Gs"][:, 0:128]
        kb.tt("dve", d["tmp"].ap, G_, NMsT.ap, ALU.add)
        kb.act(d["DTs"].ap, d["tmp"].ap, AF.Exp, bias=ngc)
        kb.tt("pool", d["DTi"].ap, d["DTs"].ap, idf.ap, ALU.add)
        kb.tt("dve", d["Dn"].ap, G_, NPs.ap, ALU.add)
        kb.act(d["Dn"].ap, d["Dn"].ap, AF.Exp, bias=gc, scale=-1.0)
        kb.act(d["EGbc"].ap, G_, AF.Exp)
        kb.act(cols[:, 0:1], gc, AF.Exp, bias=d["Gs"][:, 127:128], scale=-1.0)
        kb.act(cols[:, 1:2], d["Gs"][:, 127:128], AF.Exp)
        yield
        kb.stt("dve", d["A"].ap, d["PSK"].ap, beta, d["Dn"].ap, ALU.mult, ALU.mult)
        kb.tt("dve", d["B"].ap, d["PSK"].ap, d["DTs"].ap, ALU.mult)
        kb.tt("pool", d["B"].ap, d["B"].ap, d["Gs"][:, 128:256], ALU.mult)
        kb.tt("dve", d["QKT"].ap, d["PSQ"].ap, d["DTi"].ap, ALU.mult)
        kb.tt("pool", d["Ad"].ap, d["A"].ap, BDm.ap, ALU.mult)
        kb.tt("pool", d["Ao"].ap, d["A"].ap, d["Ad"].ap, ALU.subtract)
        kb.tt("pool", d["Bd"].ap, d["B"].ap, BDm.ap, ALU.mult)
        kb.tt("pool", d["Bo"].ap, d["B"].ap, d["Bd"].ap, ALU.subtract)
        kb.tt("dve", d["P"].ap, idf.ap, d["Ad"].ap, ALU.subtract)
        kb.tt("dve", d["Pt"].ap, idf.ap, d["Bd"].ap, ALU.subtract)
        yield
        def split(src, hi, lo, e="dve"):
            kb.copy("act", d[hi].ap, src)
            kb.tt(e, d[lo].ap, src, d[hi].ap, ALU.subtract)

        def mm3(out, A_, B_):
            kb.mm(out, d[A_[0]].ap, d[B_[0]].ap, start=True, stop=False)
            kb.mm(out, d[A_[0]].ap, d[B_[1]].ap, start=False, stop=False)
            kb.mm(out, d[A_[1]].ap, d[B_[0]].ap, start=False, stop=True)
        split(d["Ad"].ap, "Adh", "Adl", "pool")
        split(d["Bd"].ap, "Bdh", "Bdl", "pool")
        split(d["Pt"].ap, "Pth", "Ptl")
        split(d["Ao"].ap, "Aoh", "Aol", "pool")
        yield
        Xc, Xtc = ("Adh", "Adl"), ("Bdh", "Bdl")
        PT = ("Pth", "Ptl")
        for k in range(1, 6):
            Xn = ("Xah", "Xal") if k % 2 else ("Xbh", "Xbl")
            Xtn = ("Xtah", "Xtal") if k % 2 else ("Xtbh", "Xtbl")
            mm3(d["C0"].ap, Xtc, Xc)
            if k < 5:
                mm3(d["C1"].ap, Xc, Xtc)
            yield
            split(d["C0"].ap, Xn[0], Xn[1])
            if k < 5:
                split(d["C1"].ap, Xtn[0], Xtn[1])
            yield
            mm3(d["C2"].ap, PT, Xn)
            mm3(d["C3"].ap, Xn, PT)
            yield
            kb.tt("dve", d["P"].ap, d["P"].ap, d["C2"].ap, ALU.add)
            kb.tt("dve", d["Pt"].ap, d["Pt"].ap, d["C3"].ap, ALU.add)
            split(d["Pt"].ap, "Pth", "Ptl")
            Xc, Xtc = Xn, Xtn
            yield
        split(d["P"].ap, "Ph", "Pl", "pool")
        mm3(d["C0"].ap, ("Aoh", "Aol"), PT)
        yield
        split(d["C0"].ap, "Zh", "Zl")
        yield
        mm3(d["C1"].ap, ("Ph", "Pl"), ("Zh", "Zl"))
        kb.tt("dve", d["R"].ap, d["Pt"].ap, d["C1"].ap, ALU.subtract)
        kb.tr(d["C2b"], kTb, idb.ap)
        kb.tr(d["C3b"], vTb, idb.ap)
        kb.ts("dve", d["kbg"].ap, d["C2b"], bg, ALU.mult)
        kb.ts("dve", d["kd"].ap, d["C2b"], cols[:, 0:1], ALU.mult)
        kb.ts("dve", d["vb"].ap, d["C3b"], beta, ALU.mult)
        kb.tt("pool", d["qdT"].ap, qT, d["EGbc"].ap, ALU.mult)
        yield
        kb.mm(d["D0"].ap, d["kbg"].ap, d["R"].ap)
        kb.mm(d["D1"].ap, d["R"].ap, d["vb"].ap)
        kb.copy("act", d["WT"].ap, d["D0"].ap)
        kb.copy("act", d["U0s"].ap, d["D1"].ap)
        yield
        S_ = ST[hd]
        Sb_ = STb[hd]
        kb.mm(d["D2"].ap, d["WT"].ap, Sb_.ap)
        kb.tt("dve", d["Us"].ap, d["U0s"].ap, d["D2"].ap, ALU.subtract)
        kb.mm(d["D3"].ap, d["qdT"].ap, Sb_.ap, start=True, stop=False)
        kb.mm(d["D3"].ap, d["QKT"].ap, d["Us"].ap, start=False, stop=True)
        kb.memset("dve", cols[:, 2:3], 0.0)
        kb.act(d["tmp"].ap, d["D3"].ap, AF.Square, accum=cols[:, 2:3])
        kb.act(cols[:, 3:4], cols[:, 2:3], AF.Sqrt, bias=EPS, scale=1.0 / HD_D)
        kb.recip(cols[:, 4:5], cols[:, 3:4])
        kb.stt("dve", d["on"].ap, d["D3"].ap, cols[:, 4:5], onrm.ap, ALU.mult, ALU.mult)
        kb.mm(d["C0"].ap, d["kd"].ap, d["Us"].ap)
        kb.stt("dve", S_.ap, S_.ap, cols[:, 1:2], d["C0"].ap, ALU.mult, ALU.add)
        kb.copy("act", Sb_.ap, S_.ap)
        kb.tt("pool", og[:, hd * 128:(hd + 1) * 128], d["on"].ap, sgt[:, hd * 128:(hd + 1) * 128], ALU.mult)
        yield

    PSot = View(sets[G - 1]["bA"], sets[G - 1]["bA"].t[:].bitcast(BF16))
    HG = G // 2
    import os
    SKEW = int(os.environ.get("DBG_SKEW", "22"))

    def prologue(n):
        X = X24[n % 2]
        Xb = X24b[n % 2]
        kb.copy("act", Xb[:, 0:12, :], X[:, 0:12, :])
        kb.copy("dve", Xb[:, 12:24, :], X[:, 12:24, :])
        kb.dma(SGt[n % 2].ap, self.SGD[n * 128:(n + 1) * 128, :])
        t_ = tl[n % 2]
        kb.ts("dve", t_[:, 0:8], GB[:, n, 8:16], -1.0, ALU.mult)
        kb.act(t_[:, 8:16], GB[:, n, 8:16], AF.Exp)
        kb.tt("dve", t_[:, 16:24], t_[:, 8:16], GB[:, n, 0:8], ALU.mult)

    def epilogue(n):
        og = Og[n % 2]
        for c in range(8):
            kb.tr(PSot[:, c * 128:(c + 1) * 128], og[:, c * 128:(c + 1) * 128], idb.ap)
        ot = OgT[n % 2]
        kb.copy("act", ot.ap, PSot.re("p (c t) -> p c t", c=8))
        kb.dma(OGTv[:, :, n * 128:(n + 1) * 128], ot.ap, q="pool")
        if n + 2 < NT:
            kb.dma(X24[n % 2].ap, Xv[:, :, (n + 2) * 128:(n + 3) * 128])

    def item(n, grp):
        if grp == 0:
            prologue(n)
        gens = [head(n, grp * HG + g, sets[grp * HG + g], X24[n % 2], X24b[n % 2], SGt[n % 2], Og[n % 2], tl[n % 2])
                for g in range(HG)]
        alive = list(gens)
        while alive:
            nxt = []
            for g_ in alive:
                try:
                    next(g_)
                    nxt.append(g_)
                except StopIteration:
                    pass
            alive = nxt
            yield
        if grp == 1:
            epilogue(n)

    kb.dma(X24[0].ap, Xv[:, :, 0:128])
    if NT > 1:
        kb.dma(X24[1].ap, Xv[:, :, 128:256])
    items = [item(n, grp) for n in range(NT) for grp in range(2)]
    active = []
    nxt_item = 0
    while nxt_item < len(items) or active:
        if nxt_item < len(items) and (not active or (len(active) < 2 and active[-1][1] >= SKEW)):
            active.append([items[nxt_item], 0])
            nxt_item += 1
        still = []
        for ent in active:
            try:
                next(ent[0])
                ent[1] += 1
                still.append(ent)
            except StopIteration:
                pass
        active = still
    kb.end_phase()


Prog.phase_D1 = phase_D1
Prog.phase_D2 = phase_D2


def dn_layer(self, l):
    self.phase_D1(l)
    self.phase_D2(l)
    self.phase_R(self.OGT, 8, self.inp["dn_w_out"][l // 2], self.inp["norm_mix_post"][l], self.H)


Prog.dn_layer = dn_layer


_SEQ = 8192
_NT = (_SEQ + NMETA + NPAD) // 128


def build_full(NT):
    p = Prog(NT)
    p.consts()
    p.phase_init()
    for l in range(4):
        if l % 2 == 0:
            p.attn_layer(l)
        else:
            p.dn_layer(l)
        p.ffn(l, final=(l == 3))
    p.finish()
    return p


def kernel(**inputs):
    x = np.asarray(inputs["x"], dtype=np.float32)
    B = x.shape[0]
    meta = np.asarray(inputs["meta_tokens"], dtype=np.float32)
    p = build_full(_NT)
    shared = {k: np.ascontiguousarray(np.asarray(inputs[k], dtype=np.float32)) for k in p.inp if k != "xpad"}
    real = {0: 0, 1: 1, 4: 2, 5: 3}
    zshared = {k: np.zeros_like(v) for k, v in shared.items()}
    zx = np.zeros((NPAD + NMETA + x.shape[1], D), np.float32)
    in_maps = []
    for c in range(8):
        if c in real and real[c] < B:
            b = real[c]
            xpad = np.concatenate([np.zeros((NPAD, D), np.float32), meta, x[b]], axis=0)
            m = dict(shared)
            m["xpad"] = np.ascontiguousarray(xpad)
        else:
            m = dict(zshared)
            m["xpad"] = zx
        in_maps.append(m)
    res = run_bass_kernel_spmd(p.nc, in_maps, core_ids=list(range(8)))
    core_of = {b: c for c, b in real.items()}
    out = np.stack([np.asarray(res.results[core_of[b]]["y"], dtype=np.float32) for b in range(B)], axis=0)
    return out
```

```python
import numpy as np
from contextlib import ExitStack
import concourse.bass as bass
import concourse.mybir as mybir
from concourse.bass_utils import run_bass_kernel_spmd

F32 = mybir.dt.float32
BF16 = mybir.dt.bfloat16
AF = mybir.ActivationFunctionType
ALU = mybir.AluOpType
AX = mybir.AxisListType

D = 1024
NH_A = 16
HD_A = 64
NH_D = 8
HD_D = 128
FFN = 2816
EPS = 1e-6
NPAD = 112
NMETA = 16
NEGBIG = -30000.0


class Buf:
    def __init__(self, t, name, space):
        self.t = t
        self.name = name
        self.space = space
        self.w = {}
        self.r = {}
        self.sm = {}
        self.sem = None

    def __getitem__(self, k):
        return View(self, self.t[k])

    @property
    def ap(self):
        return View(self, self.t[:])


class View:
    def __init__(self, buf, ap):
        self.buf = buf
        self.ap = ap

    def __getitem__(self, k):
        return View(self.buf, self.ap[k])

    def re(self, s, **kw):
        return View(self.buf, self.ap.rearrange(s, **kw))

    def bc(self, shape):
        return View(self.buf, self.ap.to_broadcast(shape))


def _ap(x):
    return x.ap if isinstance(x, View) else x


class KB:
    def __init__(self):
        self.nc = bass.Bass("TRN2", target_bir_lowering=False)
        nc = self.nc
        self.es = ExitStack()
        self.eng = {"pe": nc.tensor, "dve": nc.vector, "act": nc.scalar, "pool": nc.gpsimd, "sp": nc.sync}
        self.cnt = {}
        self.esem = {}
        for e in ("pe", "dve", "act", "pool"):
            self.esem[e] = self.es.enter_context(nc.semaphore("cnt_" + e))
            self.cnt[e] = 0
        self.seen = {e: {} for e in self.eng}
        self.sem_pool = []
        self.live_sems = []
        self.phase_stack = None
        self.phase_bufs = []
        self.uid = 0
        self.dq = 0
        self.noself = False
        import os
        self.fast_self = tuple(x for x in os.environ.get("FAST_SELF", "dve").split(",") if x)

    def phase(self):
        self.phase_stack = ExitStack()
        self.phase_bufs = []

    def end_phase(self):
        self.barrier()
        for b in self.phase_bufs:
            if b.sem is not None:
                self.sem_pool.append(b.sem)
                b.sem = None
        self.phase_stack.close()
        self.phase_stack = None
        self.phase_bufs = []

    def carve(self, buf, slices):
        out = []
        for (a, b) in slices:
            sbuf = Buf(buf.t[:, a:b], buf.name + "_%d" % a, buf.space)
            sbuf.w = buf.w
            sbuf.r = buf.r
            sbuf.sm = buf.sm
            self.phase_bufs.append(sbuf)
            out.append(sbuf)
        return out

    def sb(self, name, shape, dt):
        self.uid += 1
        t = self.phase_stack.enter_context(self.nc.sbuf_tensor("%s_%d" % (name, self.uid), list(shape), dt))
        b = Buf(t, name, "sb")
        self.phase_bufs.append(b)
        return b

    def ps(self, name, shape, dt):
        self.uid += 1
        t = self.phase_stack.enter_context(self.nc.psum_tensor("%s_%d" % (name, self.uid), list(shape), dt))
        b = Buf(t, name, "ps")
        self.phase_bufs.append(b)
        return b

    def _getsem(self, b):
        if b.sem is None:
            if self.sem_pool:
                b.sem = self.sem_pool.pop()
            else:
                s = self.es.enter_context(self.nc.semaphore("dsem%d" % len(self.live_sems)))
                b.sem = [s, 0]
                self.live_sems.append(b.sem)
        return b.sem

    def _wait(self, e, deps):
        seen = self.seen[e]
        for sid, (sem, val) in deps.items():
            if seen.get(sid, 0) < val:
                self.eng[e].wait_ge(sem, val)
                seen[sid] = val

    def _deps(self, outs, ins):
        deps = {}

        def add(d):
            for sid, (sem, val) in d.items():
                if sid not in deps or deps[sid][1] < val:
                    deps[sid] = (sem, val)
        for v in ins:
            if isinstance(v, View):
                add(v.buf.w)
        for v in outs:
            if isinstance(v, View):
                add(v.buf.w)
                add(v.buf.r)
        return deps

    def _record(self, outs, ins, sem, val):
        sid = id(sem)
        for v in ins:
            if isinstance(v, View):
                v.buf.r[sid] = (sem, val)
        for v in outs:
            if isinstance(v, View):
                v.buf.w[sid] = (sem, val)

    def op(self, e, fn, outs, ins, pe_acc=False):
        deps = self._deps(outs, ins)
        small = True
        if e == "pe":
            deps.pop(id(self.esem[e]), None)
        elif e in self.fast_self:
            sh = _ap(outs[0]).shape
            n = 1
            for v_ in sh[1:]:
                n *= v_
            small = n < 256
            sid = id(self.esem[e])
            if not small and sid in deps:
                need = 0
                for v in list(ins) + list(outs):
                    if isinstance(v, View):
                        need = max(need, v.buf.sm.get(sid, 0))
                if need > 0:
                    deps[sid] = (self.esem[e], need)
                else:
                    deps.pop(sid)
        self._wait(e, deps)
        ins_ = fn()
        self.cnt[e] += 1
        ins_.then_inc(self.esem[e], 1)
        self._record(outs, ins, self.esem[e], self.cnt[e])
        if small and e in self.fast_self:
            sid = id(self.esem[e])
            for v in list(ins) + list(outs):
                if isinstance(v, View):
                    v.buf.sm[sid] = self.cnt[e]
        return ins_

    def dma(self, out, in_, q=None, **kw):
        if q is None:
            q = "sp"
        sbv = out if isinstance(out, View) else in_
        assert isinstance(sbv, View)
        outs = [out] if isinstance(out, View) else []
        ins = [in_] if isinstance(in_, View) else []
        deps = self._deps(outs, ins)
        self._wait(q, deps)
        se = self._getsem(sbv.buf)
        self.eng[q].dma_start(out=_ap(out), in_=_ap(in_), **kw).then_inc(se[0], 16)
        se[1] += 16
        self._record(outs, ins, se[0], se[1])

    def barrier(self):
        allsems = {}
        for e in ("pe", "dve", "act", "pool"):
            allsems[id(self.esem[e])] = (self.esem[e], self.cnt[e])
        for se in self.live_sems:
            allsems[id(se[0])] = (se[0], se[1])
        for e in self.eng:
            self._wait(e, {k: v for k, v in allsems.items() if v[1] > 0})

    def mm(self, out, lhsT, rhs, start=True, stop=True):
        return self.op("pe", lambda: self.nc.tensor.matmul(_ap(out), lhsT=_ap(lhsT), rhs=_ap(rhs), start=start, stop=stop),
                       [out], [lhsT, rhs])

    def tr(self, out, in_, ident):
        return self.op("pe", lambda: self.nc.tensor.transpose(_ap(out), _ap(in_), _ap(ident)), [out], [in_, ident])

    def act(self, out, in_, func, bias=None, scale=None, accum=None, e="act"):
        kw = {}
        ins = [in_]
        outs = [out]
        if bias is not None:
            kw["bias"] = _ap(bias)
            ins.append(bias)
        if scale is not None:
            kw["scale"] = _ap(scale)
            ins.append(scale)
        if accum is not None:
            kw["accum_out"] = _ap(accum)
            outs.append(accum)
        return self.op("act", lambda: self.nc.scalar.activation(out=_ap(out), in_=_ap(in_), func=func, **kw), outs, ins)

    def tt(self, e, out, a, b, op):
        return self.op(e, lambda: self.eng[e].tensor_tensor(out=_ap(out), in0=_ap(a), in1=_ap(b), op=op), [out], [a, b])

    def ts(self, e, out, a, s1, op0, s2=None, op1=None, accum=None):
        kw = {}
        outs = [out]
        if op1 is not None:
            kw["op1"] = op1
        if accum is not None:
            kw["accum_out"] = _ap(accum)
            outs.append(accum)
        return self.op(e, lambda: self.eng[e].tensor_scalar(out=_ap(out), in0=_ap(a), scalar1=_ap(s1), scalar2=_ap(s2), op0=op0, **kw),
                       outs, [a, s1, s2])

    def stt(self, e, out, a, s, b, op0, op1):
        return self.op(e, lambda: self.eng[e].scalar_tensor_tensor(out=_ap(out), in0=_ap(a), scalar=_ap(s), in1=_ap(b), op0=op0, op1=op1),
                       [out], [a, s, b])

    def copy(self, e, out, in_):
        if e == "act":
            return self.op("act", lambda: self.nc.scalar.copy(out=_ap(out), in_=_ap(in_)), [out], [in_])
        return self.op(e, lambda: self.eng[e].tensor_copy(out=_ap(out), in_=_ap(in_)), [out], [in_])

    def amul(self, out, in_, mul):
        return self.op("act", lambda: self.nc.scalar.mul(out=_ap(out), in_=_ap(in_), mul=_ap(mul)), [out], [in_, mul])

    def memset(self, e, out, val):
        return self.op(e, lambda: self.eng[e].memset(_ap(out), val), [out], [])

    def recip(self, out, in_):
        return self.op("dve", lambda: self.nc.vector.reciprocal(out=_ap(out), in_=_ap(in_)), [out], [in_])

    def asel(self, out, in_, pattern, cmp, fill, base, cm):
        return self.op("pool", lambda: self.nc.gpsimd.affine_select(out=_ap(out), in_=_ap(in_), pattern=pattern, compare_op=cmp,
                                                                      fill=fill, base=base, channel_multiplier=cm), [out], [in_])


class Prog:
    def __init__(self, NT, layers=(0, 1, 2, 3), final_out=True):
        self.NT = NT
        self.LP = NT * 128
        self.kb = KB()
        self.nc = self.kb.nc
        self.layers = layers
        nc = self.nc
        LP = self.LP
        S = LP - 128
        shapes = {"xpad": [LP, D], "norm_mix_pre": [4, D], "norm_mix_post": [4, D], "norm_ffn_pre": [4, D], "norm_ffn_post": [4, D],
                  "attn_w_in": [2, D, 4112], "attn_b_forget": [2, 16], "attn_q_norm": [2, 64], "attn_k_norm": [2, 64],
                  "attn_w_out": [2, D, D], "dn_w_in": [2, D, 4112], "dn_conv": [2, 4, 3072], "dn_a_log": [2, 8], "dn_dt_bias": [2, 8],
                  "dn_o_norm": [2, 128], "dn_w_out": [2, D, D], "ffn_w_up": [4, D, 2 * FFN], "ffn_conv": [4, 3, 2 * FFN],
                  "ffn_w_down": [4, FFN, D]}

        class Lazy(dict):
            def __missing__(d, name):
                d[name] = nc.dram_tensor(name, list(shapes[name]), F32, kind="ExternalInput").ap()
                return d[name]
        self.inp = Lazy()
        self.y = nc.dram_tensor("y", [S, D], F32, kind="ExternalOutput").ap()

        def dscr(name, shape, dt):
            return nc.dram_tensor(name, list(shape), dt, kind="Internal").ap()
        self.H = dscr("H", [LP, D], F32)
        self.HT = dscr("HT", [FFN, LP], BF16)
        self.OGT = dscr("OGT", [D, LP], BF16)
        self.SG = dscr("SG", [D, LP], BF16)
        self.QT = dscr("QT", [16, 65, LP], BF16)
        self.KT = dscr("KT", [16, 65, LP], BF16)
        self.VA = dscr("VA", [LP, 16, 65], BF16)
        self.CT = dscr("CT", [128, NT, 16], F32)
        self.QKVD = dscr("QKVD", [24, 128, LP], F32)
        self.SGD = dscr("SGD", [LP, D], BF16)
        self.GBD = dscr("GBD", [128, NT, 16], F32)
        self.CAR = dscr("CAR", [128, NT + 1, 16], F32)

    def consts(self):
        kb = self.kb
        kb.phase()
        es = kb.phase_stack
        self.idf = kb.sb("idf", [128, 128], F32)
        self.idb = kb.sb("idb", [128, 128], BF16)
        kb.memset("pool", self.idf.ap, 1.0)
        kb.asel(self.idf.ap, self.idf.ap, [[-1, 128]], ALU.is_equal, 0.0, 0, 1)
        kb.copy("dve", self.idb.ap, self.idf.ap)
        self.const_stack = es
        kb.phase_stack = None
        kb.phase_bufs = []

    def blocks(self):
        out = []
        t = 0
        while t < self.NT:
            n = min(4, self.NT - t)
            out.append((t, n))
            t += n
        return out

    def load_cols(self, dst, src2d, n, stage, pst):
        kb = self.kb
        kb.dma(stage[0:n, :], src2d)
        kb.tr(pst[:, 0:n], stage[0:n, :], self.idf[0:n, 0:n])
        kb.copy("dve", dst, pst[:, 0:n])

    def load_w(self, dst, W, nch, ncols, gcol, stages, col0=0):
        kb = self.kb
        SEG = 2048
        k = 0
        for c in range(nch):
            for s0 in range(0, ncols, SEG):
                w = min(SEG, ncols - s0)
                st = stages[k % len(stages)]
                e = ("dve", "act")[k % 2]
                k += 1
                kb.dma(st[:, 0:w], W[c * 128:(c + 1) * 128, s0:s0 + w])
                if gcol is not None:
                    if e == "act":
                        kb.amul(dst[:, c, col0 + s0:col0 + s0 + w], st[:, 0:w], gcol[:, c:c + 1])
                    else:
                        kb.ts(e, dst[:, c, col0 + s0:col0 + s0 + w], st[:, 0:w], gcol[:, c:c + 1], ALU.mult)
                else:
                    kb.copy(e, dst[:, c, col0 + s0:col0 + s0 + w], st[:, 0:w])

    def prenormT(self, ht, aT_dst, abf, ss, pst, junk, k):
        kb = self.kb
        kb.memset("dve", ss[:, 0:1], 0.0)
        kb.act(junk.ap, ht.ap, AF.Square, accum=ss[:, 0:1])
        kb.act(ss[:, 1:2], ss[:, 0:1], AF.Sqrt, bias=EPS, scale=1.0 / D)
        kb.recip(ss[:, 2:3], ss[:, 1:2])
        kb.ts("dve", abf.ap, ht.ap, ss[:, 2:3], ALU.mult)
        for c in range(8):
            kb.tr(pst[:, c * 128:(c + 1) * 128], abf[:, c * 128:(c + 1) * 128], self.idb.ap)
        kb.copy(("act", "dve")[k % 2], aT_dst, pst.ap.re("p (c t) -> p c t", c=8))

    def phase_R(self, srcT, nch, W, gpost, hsrc, final=False):
        kb = self.kb
        kb.phase()
        Wb = kb.sb("Wb", [128, nch, D], BF16)
        stages = [kb.sb("wst", [128, 2048], F32) for _ in range(2)]
        self.load_w(Wb, W, nch, D, None, stages)
        g = kb.sb("g", [128, D], F32)
        kb.dma(g.ap, gpost.partition_broadcast(128))
        Sb = [kb.sb("Sb", [128, nch, 512], BF16) for _ in range(2)]
        hts = [kb.sb("ht", [128, D], F32) for _ in range(3)]
        fn = [kb.sb("fn", [128, D], F32) for _ in range(2)]
        ssb = [kb.sb("ss", [128, 8], F32) for _ in range(2)]
        junk = kb.sb("junk", [128, 512], BF16)
        PS = [[kb.ps("psa", [128, 512], F32), kb.ps("psb", [128, 512], F32)] for _ in range(2)]
        srcv = srcT.rearrange("(c p) t -> p c t", p=128)
        blocks = self.blocks()
        kb.dma(Sb[0][:, :, 0:blocks[0][1] * 128], srcv[:, :, 0:blocks[0][1] * 128])
        ti = 0
        for bi, (t0, n) in enumerate(blocks):
            if bi + 1 < len(blocks):
                t1, n1 = blocks[bi + 1]
                kb.dma(Sb[(bi + 1) % 2][:, :, 0:n1 * 128], srcv[:, :, t1 * 128:(t1 + n1) * 128])
            S = Sb[bi % 2]
            for j in range(n):
                tile = t0 + j
                ht = hts[ti % 3]
                kb.dma(ht.ap, hsrc[tile * 128:(tile + 1) * 128, :])
                pa, pb = PS[ti % 2]
                ss = ssb[ti % 2]
                f = fn[ti % 2]
                for c in range(nch):
                    kb.mm(pa.ap, S[:, c, j * 128:(j + 1) * 128], Wb[:, c, 0:512], start=(c == 0), stop=(c == nch - 1))
                for c in range(nch):
                    kb.mm(pb.ap, S[:, c, j * 128:(j + 1) * 128], Wb[:, c, 512:1024], start=(c == 0), stop=(c == nch - 1))
                kb.memset("dve", ss[:, 0:2], 0.0)
                kb.act(junk.ap, pa.ap, AF.Square, accum=ss[:, 0:1])
                kb.act(junk.ap, pb.ap, AF.Square, accum=ss[:, 1:2])
                kb.tt("dve", ss[:, 2:3], ss[:, 0:1], ss[:, 1:2], ALU.add)
                kb.act(ss[:, 3:4], ss[:, 2:3], AF.Sqrt, bias=EPS, scale=1.0 / D)
                kb.recip(ss[:, 4:5], ss[:, 3:4])
                kb.stt("dve", f[:, 0:512], pa.ap, ss[:, 4:5], g[:, 0:512], ALU.mult, ALU.mult)
                kb.stt("dve", f[:, 512:1024], pb.ap, ss[:, 4:5], g[:, 512:1024], ALU.mult, ALU.mult)
                kb.tt("pool", f.ap, f.ap, ht.ap, ALU.add)
                if final:
                    if tile >= 1:
                        kb.dma(self.y[(tile - 1) * 128:tile * 128, :], f.ap, q="pool")
                else:
                    kb.dma(self.H[tile * 128:(tile + 1) * 128, :], f.ap, q="pool")
                ti += 1
        kb.end_phase()

    def phase_F1(self, l):
        kb = self.kb
        kb.phase()
        Wup = kb.sb("Wup", [128, 8, 2 * FFN], BF16)
        stages = [kb.sb("wst", [128, 2048], F32) for _ in range(2)]
        cst = kb.sb("cst", [128, 128], F32)
        gcol = kb.sb("gcol", [128, 8], F32)
        cw = kb.sb("cw", [128, 3, 44], F32)
        pst = kb.ps("pst", [128, 1024], BF16)
        psc = kb.ps("psc", [128, 128], F32)
        self.load_cols(gcol.ap, self.inp["norm_ffn_pre"][l].rearrange("(c p) -> c p", p=128), 8, cst, psc)
        for k in range(3):
            self.load_cols(cw[:, k, :], self.inp["ffn_conv"][l, k].rearrange("(c p) -> c p", p=128), 44, cst, psc)
        self.load_w(Wup, self.inp["ffn_w_up"][l], 8, 2 * FFN, gcol, stages)
        halo = kb.sb("halo", [128, 44, 2], F32)
        kb.memset("pool", halo.ap, 0.0)
        hts = [kb.sb("ht", [128, D], F32) for _ in range(2)]
        abf = [kb.sb("abf", [128, D], BF16) for _ in range(2)]
        ssb = [kb.sb("ss", [128, 8], F32) for _ in range(2)]
        junk = kb.sb("junk", [128, D], BF16)
        aT = [kb.sb("aT", [128, 8, 512], BF16) for _ in range(2)]
        Hb = [kb.sb("Hb", [128, 22, 512], BF16) for _ in range(2)]
        Ug = [kb.sb("Ug", [128, 514], F32) for _ in range(2)]
        Uu = [kb.sb("Uu", [128, 514], F32) for _ in range(2)]
        yg = [kb.sb("yg", [128, 512], F32) for _ in range(2)]
        yu = [kb.sb("yu", [128, 512], F32) for _ in range(2)]
        gg = [kb.sb("gg", [128, 512], F32) for _ in range(2)]
        ptmp = [kb.sb("ptmp", [128, 512], F32) for _ in range(2)]
        PSg = [kb.ps("psg", [128, 512], F32) for _ in range(2)]
        PSu = [kb.ps("psu", [128, 512], F32) for _ in range(2)]
        HTv = self.HT.rearrange("(c p) t -> p c t", p=128)
        hsrc = self.H
        ti = 0
        it = 0
        for bi, (t0, n) in enumerate(self.blocks()):
            TW = n * 128
            A = aT[bi % 2]
            for j in range(n):
                tile = t0 + j
                ht = hts[ti % 2]
                kb.dma(ht.ap, hsrc[tile * 128:(tile + 1) * 128, :])
                self.prenormT(ht, A[:, :, j * 128:(j + 1) * 128], abf[ti % 2], ssb[ti % 2], pst, junk, ti)
                ti += 1
            HB = Hb[bi % 2]
            for fc in range(22):
                pg = PSg[it % 2]; pu = PSu[it % 2]
                ug = Ug[it % 2]; uu = Uu[it % 2]
                for c in range(8):
                    kb.mm(pg[:, 0:TW], Wup[:, c, fc * 128:(fc + 1) * 128], A[:, c, 0:TW], start=(c == 0), stop=(c == 7))
                for c in range(8):
                    kb.mm(pu[:, 0:TW], Wup[:, c, FFN + fc * 128:FFN + (fc + 1) * 128], A[:, c, 0:TW], start=(c == 0), stop=(c == 7))
                a = yg[it % 2]; b = yu[it % 2]
                for (u_, p_, y_, hc) in ((ug, pg, a, fc), (uu, pu, b, 22 + fc)):
                    kb.copy("act", u_[:, 2:2 + TW], p_[:, 0:TW])
                    kb.amul(y_[:, 0:TW], p_[:, 0:TW], cw[:, 2, hc:hc + 1])
                    kb.copy("dve", u_[:, 0:2], halo[:, hc, :])
                    kb.copy("dve", halo[:, hc, :], u_[:, TW:TW + 2])
                    kb.stt("dve", y_[:, 0:TW], u_[:, 1:1 + TW], cw[:, 1, hc:hc + 1], y_[:, 0:TW], ALU.mult, ALU.add)
                    kb.stt("dve", y_[:, 0:TW], u_[:, 0:TW], cw[:, 0, hc:hc + 1], y_[:, 0:TW], ALU.mult, ALU.add)
                kb.act(gg[it % 2][:, 0:TW], a[:, 0:TW], AF.Gelu_apprx_tanh)
                kb.tt("dve", HB[:, fc, 0:TW], gg[it % 2][:, 0:TW], b[:, 0:TW], ALU.mult)
                it += 1
            kb.dma(HTv[:, :, t0 * 128:t0 * 128 + TW], HB[:, :, 0:TW], q="pool")
        kb.end_phase()

    def phase_init(self):
        kb = self.kb
        kb.phase()
        hts = [kb.sb("ht", [128, D], F32) for _ in range(4)]
        for t in range(self.NT):
            ht = hts[t % 4]
            kb.dma(ht.ap, self.inp["xpad"][t * 128:(t + 1) * 128, :])
            kb.dma(self.H[t * 128:(t + 1) * 128, :], ht.ap, q="pool")
        kb.end_phase()

    def ffn(self, l, final):
        self.phase_F1(l)
        self.phase_R(self.HT, 22, self.inp["ffn_w_down"][l], self.inp["norm_ffn_post"][l], self.H, final=final)

    def finish(self):
        self.const_stack.close()
        self.kb.es.close()


def phase_A1(self, l):
    kb = self.kb
    j = l // 2
    NT = self.NT
    kb.phase()
    Win = kb.sb("Win", [128, 8, 4112], BF16)
    stages = [kb.sb("wst", [128, 2048], F32) for _ in range(2)]
    cst = kb.sb("cst", [128, 128], F32)
    gcol = kb.sb("gcol", [128, 8], F32)
    pst = kb.ps("pst", [128, 1024], BF16)
    PSq = [kb.ps("psq", [128, 512], F32) for _ in range(2)]
    PSs = kb.ps("pss", [128, 512], F32)
    PSv = kb.ps("psv", [128, 1024], F32)
    PSm = kb.ps("psm", [128, 512], F32)
    PSr = kb.ps("psr", [16, 512], BF16)
    self.load_cols(gcol.ap, self.inp["norm_mix_pre"][l].rearrange("(c p) -> c p", p=128), 8, cst, PSs)
    self.load_w(Win, self.inp["attn_w_in"][j], 8, 4112, gcol, stages)
    qg = kb.sb("qg", [128, 2], F32)
    for half in range(2):
        kb.dma(qg[half * 64:(half + 1) * 64, 0:1], self.inp["attn_q_norm"][j].rearrange("(p o) -> p o", o=1))
        kb.dma(qg[half * 64:(half + 1) * 64, 1:2], self.inp["attn_k_norm"][j].rearrange("(p o) -> p o", o=1))
    kb.ts("dve", qg[:, 0:1], qg[:, 0:1], 0.125, ALU.mult)
    bfg = kb.sb("bfg", [128, 16], F32)
    kb.dma(bfg.ap, self.inp["attn_b_forget"][j].partition_broadcast(128))
    BD = kb.sb("BD", [128, 128], BF16)
    kb.memset("pool", BD.ap, 1.0)
    kb.asel(BD[:, 0:64], BD[:, 0:64], [[0, 64]], ALU.is_ge, 0.0, 63, -1)
    kb.asel(BD[:, 64:128], BD[:, 64:128], [[0, 64]], ALU.is_ge, 0.0, -64, 1)
    tri = kb.sb("tri", [128, 128], F32)
    kb.memset("pool", tri.ap, 1.0)
    kb.asel(tri.ap, tri.ap, [[1, 128]], ALU.is_ge, 0.0, 0, -1)
    onesf = kb.sb("onesf", [128, 128], F32)
    kb.memset("pool", onesf.ap, 1.0)
    onesb = kb.sb("onesb", [16, 512], BF16)
    kb.memset("pool", onesb.ap, 1.0)
    Ctok = kb.sb("Ctok", [128, NT, 16], F32)
    CAR = kb.sb("CAR", [128, NT + 1, 16], F32)
    kb.memset("dve", CAR[:, 0, :], 0.0)
    hts = [kb.sb("ht", [128, D], F32) for _ in range(2)]
    abf = [kb.sb("abf", [128, D], BF16) for _ in range(2)]
    ssb = [kb.sb("ss", [128, 8], F32) for _ in range(2)]
    junk = kb.sb("junk", [128, D], BF16)
    aT = [kb.sb("aT", [128, 8, 512], BF16) for _ in range(2)]
    sq = [kb.sb("sq", [128, 512], BF16) for _ in range(2)]
    rms = [kb.sb("rms", [128, 512], F32) for _ in range(2)]
    qn = [kb.sb("qn", [128, 512], BF16) for _ in range(3)]
    Va = [kb.sb("Va", [128, 16, 65], BF16) for _ in range(2)]
    V0 = kb.sb("V0", [128, 16, 65], BF16)
    for v in Va + [V0]:
        kb.memset("pool", v.ap, 1.0)
    kb.memset("pool", V0[0:NPAD, :, 64:65], 0.0)
    xs = [kb.sb("xs", [128, 48], F32) for _ in range(2)]
    rb = [kb.sb("rb", [128, 16], BF16) for _ in range(2)]
    rT = [kb.sb("rT", [16, 512], BF16) for _ in range(2)]
    ti = 0
    it = 0
    for bi, (t0, n) in enumerate(self.blocks()):
        TW = n * 128
        c0 = t0 * 128
        A = aT[bi % 2]
        for jj in range(n):
            tile = t0 + jj
            ht = hts[ti % 2]
            kb.dma(ht.ap, self.H[tile * 128:(tile + 1) * 128, :])
            self.prenormT(ht, A[:, :, jj * 128:(jj + 1) * 128], abf[ti % 2], ssb[ti % 2], pst, junk, ti)
            ti += 1
        def projA(idx, it_):
            pq_ = PSq[it_ % 2]
            col_ = (idx // 8) * 1024 + (idx % 8) * 128
            for c in range(8):
                kb.mm(pq_[:, 0:TW], Win[:, c, col_:col_ + 128], A[:, c, 0:TW], start=(c == 0), stop=(c == 7))
        projA(0, it)
        for which in range(2):
            dst = self.QT if which == 0 else self.KT
            for ch in range(8):
                pq = PSq[it % 2]
                s_ = sq[it % 2]
                kb.act(s_[:, 0:TW], pq[:, 0:TW], AF.Square)
                if which * 8 + ch + 1 < 16:
                    projA(which * 8 + ch + 1, it + 1)
                kb.mm(PSs[:, 0:TW], BD.ap, s_[:, 0:TW])
                r_ = rms[it % 2]
                kb.act(r_[:, 0:TW], PSs[:, 0:TW], AF.Sqrt, bias=EPS, scale=1.0 / 64)
                kb.recip(r_[:, 0:TW], r_[:, 0:TW])
                q_ = qn[it % 3]
                kb.stt("dve", q_[:, 0:TW], pq[:, 0:TW], qg[:, which:which + 1], r_[:, 0:TW], ALU.mult, ALU.mult)
                kb.dma(dst[2 * ch, 0:64, c0:c0 + TW], q_[0:64, 0:TW], q="pool")
                kb.dma(dst[2 * ch + 1, 0:64, c0:c0 + TW], q_[64:128, 0:TW], q="pool")
                it += 1
        for ch in range(8):
            pq = PSq[it % 2]
            col = 3072 + ch * 128
            for c in range(8):
                kb.mm(pq[:, 0:TW], Win[:, c, col:col + 128], A[:, c, 0:TW], start=(c == 0), stop=(c == 7))
            q_ = qn[it % 3]
            kb.act(q_[:, 0:TW], pq[:, 0:TW], AF.Sigmoid)
            kb.dma(self.SG[ch * 128:(ch + 1) * 128, c0:c0 + TW], q_[:, 0:TW], q="pool")
            it += 1
        kb.dma(self.KT[:, 64, c0:c0 + TW], onesb[:, 0:TW], q="pool")
        for jj in range(n):
            tile = t0 + jj
            tc_ = slice(jj * 128, (jj + 1) * 128)
            for half in range(2):
                for c in range(8):
                    kb.mm(PSv[:, half * 512:(half + 1) * 512], A[:, c, tc_], Win[:, c, 2048 + half * 512:2048 + (half + 1) * 512],
                          start=(c == 0), stop=(c == 7))
            V = V0 if tile == 0 else Va[tile % 2]
            kb.copy("act", V[:, :, 0:64], PSv.ap.re("p (h d) -> p h d", h=16))
            kb.dma(self.VA[tile * 128:(tile + 1) * 128, :, :], V.ap, q="pool")
            for c in range(8):
                kb.mm(PSm[:, 0:16], A[:, c, tc_], Win[:, c, 4096:4112], start=(c == 0), stop=(c == 7))
            x = xs[tile % 2]
            kb.tt("dve", x[:, 0:16], PSm[:, 0:16], bfg.ap, ALU.add)
            kb.act(x[:, 16:32], x[:, 0:16], AF.Exp, scale=-1.0)
            kb.act(x[:, 32:48], x[:, 16:32], AF.Ln, bias=1.0)
            kb.mm(PSm[:, 16:32], tri.ap, x[:, 32:48])
            kb.mm(PSm[:, 32:48], onesf.ap, x[:, 32:48])
            kb.tt("dve", Ctok[:, tile, :], PSm[:, 16:32], CAR[:, tile, :], ALU.add)
            kb.tt("dve", CAR[:, tile + 1, :], PSm[:, 32:48], CAR[:, tile, :], ALU.add)
        R_ = rT[bi % 2]
        for jj in range(n):
            tile = t0 + jj
            r2 = rb[tile % 2]
            kb.tt("dve", r2.ap, CAR[:, t0 + n, :], Ctok[:, tile, :], ALU.subtract)
            kb.tr(PSr[:, jj * 128:(jj + 1) * 128], r2.ap, self.idb.ap)
        kb.copy("dve", R_[:, 0:TW], PSr[:, 0:TW])
        kb.dma(self.QT[:, 64, c0:c0 + TW], R_[:, 0:TW], q="pool")
    kb.dma(self.CT, Ctok.ap, q="pool")
    kb.dma(self.CAR, CAR.ap, q="pool")
    kb.end_phase()


def phase_A2(self, l):
    kb = self.kb
    NT = self.NT
    LP = self.LP
    kb.phase()
    Ctok = kb.sb("Ctok", [128, NT, 16], F32)
    CAR = kb.sb("CAR", [128, NT + 1, 16], F32)
    kb.dma(Ctok.ap, self.CT)
    kb.dma(CAR.ap, self.CAR)
    masks = []
    for jj in range(4):
        m = kb.sb("mask", [128, 512], BF16)
        kb.memset("pool", m.ap, 0.0)
        kb.asel(m.ap, m.ap, [[1, 512]], ALU.is_ge, NEGBIG, -128 * jj, -1)
        masks.append(m)
    onesf = kb.sb("onesf", [128, 64], F32)
    kb.memset("pool", onesf.ap, 1.0)
    KTh = [kb.sb("KTh", [128, LP], BF16) for _ in range(2)]
    VAh = [kb.sb("VAh", [128, NT, 65], BF16) for _ in range(2)]
    QTb = [kb.sb("QTb", [128, 512], BF16) for _ in range(2)]
    for b_ in KTh + QTb:
        kb.memset("pool", b_[64:128, :], 0.0)
    SGb = [kb.sb("SGb", [64, 512], BF16) for _ in range(2)]
    bias = [kb.sb("bias", [128, NT], F32) for _ in range(2)]
    P = [kb.sb("P", [128, 512], BF16) for _ in range(4)]
    oacc = [kb.sb("oacc", [65, 512], F32) for _ in range(2)]
    rden = [kb.sb("rden", [128, 512], F32) for _ in range(2)]
    for b_ in rden:
        kb.memset("pool", b_.ap, 0.0)
    sel = kb.sb("sel", [128, 64], F32)
    kb.memset("pool", sel.ap, 0.0)
    kb.memset("pool", sel[64:96, :], 1.0)
    kb.asel(sel[64:96, :], sel[64:96, :], [[0, 64]], ALU.is_ge, 0.0, 0, -1)
    tmp = [kb.sb("tmp", [64, 512], F32) for _ in range(2)]
    og = [kb.sb("og", [64, 512], BF16) for _ in range(2)]
    PSs = [kb.ps("pss", [128, 512], F32) for _ in range(4)]
    PSo = [kb.ps("pso", [65, 512], F32) for _ in range(2)]
    PSb = [kb.ps("psb", [64, 512], F32) for _ in range(2)]
    VAv = self.VA.rearrange("(t p) h e -> p t h e", p=128)
    blocks = self.blocks()
    it = 0
    ip = 0
    def load_head(hh):
        kb.dma(KTh[hh % 2][0:65, :], self.KT[hh])
        for a in range(0, NT, 16):
            b = min(NT, a + 16)
            kb.dma(VAh[hh % 2][:, a:b, :], VAv[:, a:b, hh, :])
    load_head(0)
    for h in range(16):
        if h + 1 < 16:
            load_head(h + 1)
        K_ = KTh[h % 2]
        V_ = VAh[h % 2]
        for bi, (t0, n) in enumerate(blocks):
            TW = n * 128
            c0 = t0 * 128
            Q_ = QTb[it % 2]
            S_ = SGb[it % 2]
            kb.dma(Q_[0:65, 0:TW], self.QT[h, :, c0:c0 + TW])
            kb.dma(S_[:, 0:TW], self.SG[h * 64:(h + 1) * 64, c0:c0 + TW])
            nkt = t0 + n
            b_ = bias[it % 2]
            kb.ts("dve", b_[:, 0:nkt], Ctok[:, 0:nkt, h], CAR[:, t0 + n, h:h + 1], ALU.subtract)
            po = PSo[it % 2]
            def emit_s(kt, ipk):
                ps = PSs[ipk % 4]
                diag = kt >= t0
                kb.mm(ps[:, 0:TW], K_[:, kt * 128:(kt + 1) * 128], Q_[:, 0:TW], start=True, stop=not diag)
                if diag:
                    kb.mm(ps[:, 0:TW], self.idb.ap, masks[kt - t0][:, 0:TW], start=False, stop=True)
                p_ = P[ipk % 4]
                kb.act(p_[:, 0:TW], ps[:, 0:TW], AF.Exp, bias=b_[:, kt:kt + 1])
            emit_s(0, ip)
            if nkt > 1:
                emit_s(1, ip + 1)
            for kt in range(nkt):
                if kt + 2 < nkt:
                    emit_s(kt + 2, ip + 2)
                kb.mm(po[:, 0:TW], V_[:, kt, :], P[ip % 4][:, 0:TW], start=(kt == 0), stop=(kt == nkt - 1))
                ip += 1
            oa = oacc[it % 2]
            rd = rden[it % 2]
            kb.copy("act", oa[:, 0:TW], po[:, 0:TW])
            kb.ts("dve", rd[64:65, 0:TW], oa[64:65, 0:TW], 1e-30, ALU.max)
            kb.recip(rd[64:65, 0:TW], rd[64:65, 0:TW])
            pb = PSb[it % 2]
            kb.mm(pb[:, 0:TW], sel.ap, rd[:, 0:TW])
            t_ = tmp[it % 2]
            kb.tt("dve", t_[:, 0:TW], oa[0:64, 0:TW], pb[:, 0:TW], ALU.mult)
            o_ = og[it % 2]
            kb.tt("pool", o_[:, 0:TW], t_[:, 0:TW], S_[:, 0:TW], ALU.mult)
            kb.dma(self.OGT[h * 64:(h + 1) * 64, c0:c0 + TW], o_[:, 0:TW], q="pool")
            it += 1
    kb.end_phase()


Prog.phase_A1 = phase_A1
Prog.phase_A2 = phase_A2


def attn_layer(self, l):
    self.phase_A1(l)
    self.phase_A2(l)
    self.phase_R(self.OGT, 8, self.inp["attn_w_out"][l // 2], self.inp["norm_mix_post"][l], self.H)


Prog.attn_layer = attn_layer


def phase_D1(self, l):
    kb = self.kb
    j = l // 2
    NT = self.NT
    kb.phase()
    Win = kb.sb("Win", [128, 8, 4112], BF16)
    stages = [kb.sb("wst", [128, 2048], F32) for _ in range(2)]
    cst = kb.sb("cst", [128, 128], F32)
    gcol = kb.sb("gcol", [128, 8], F32)
    cw = kb.sb("cw", [128, 4, 24], F32)
    pst = kb.ps("pst", [128, 1024], BF16)
    PSq = [kb.ps("psq", [128, 512], F32) for _ in range(2)]
    PSs = kb.ps("pss", [128, 512], F32)
    PSv = kb.ps("psv", [128, 1024], F32)
    PSm = kb.ps("psm", [128, 512], F32)
    self.load_cols(gcol.ap, self.inp["norm_mix_pre"][l].rearrange("(c p) -> c p", p=128), 8, cst, PSs)
    for k in range(4):
        self.load_cols(cw[:, k, :], self.inp["dn_conv"][j, k].rearrange("(c p) -> c p", p=128), 24, cst, PSs)
    self.load_w(Win, self.inp["dn_w_in"][j], 8, 4112, gcol, stages)
    dtb = kb.sb("dtb", [128, 8], F32)
    nA = kb.sb("nA", [128, 8], F32)
    kb.dma(dtb.ap, self.inp["dn_dt_bias"][j].partition_broadcast(128))
    kb.dma(nA.ap, self.inp["dn_a_log"][j].partition_broadcast(128))
    kb.act(nA.ap, nA.ap, AF.Exp)
    kb.ts("dve", nA.ap, nA.ap, -1.0, ALU.mult)
    tri = kb.sb("tri", [128, 128], F32)
    kb.memset("pool", tri.ap, 1.0)
    kb.asel(tri.ap, tri.ap, [[1, 128]], ALU.is_ge, 0.0, 0, -1)
    onesb = kb.sb("onesb", [128, 128], BF16)
    kb.memset("pool", onesb.ap, 1.0)
    halo = kb.sb("halo", [128, 24, 3], F32)
    kb.memset("pool", halo.ap, 0.0)
    GB = kb.sb("GB", [128, NT, 16], F32)
    hts = [kb.sb("ht", [128, D], F32) for _ in range(2)]
    abf = [kb.sb("abf", [128, D], BF16) for _ in range(2)]
    ssb = [kb.sb("ss", [128, 8], F32) for _ in range(2)]
    junk = kb.sb("junk", [128, D], BF16)
    aT = [kb.sb("aT", [128, 8, 512], BF16) for _ in range(2)]
    U = [kb.sb("U", [128, 515], F32) for _ in range(2)]
    yv = [kb.sb("yv", [128, 512], F32) for _ in range(2)]
    z = [kb.sb("z", [128, 512], F32) for _ in range(3)]
    sq = [kb.sb("sq", [128, 512], BF16) for _ in range(2)]
    rms = [kb.sb("rms", [128, 512], F32) for _ in range(2)]
    sg = [kb.sb("sg", [128, D], BF16) for _ in range(2)]
    xs = [kb.sb("xs", [128, 32], F32) for _ in range(2)]
    ti = 0
    it = 0
    for bi, (t0, n) in enumerate(self.blocks()):
        TW = n * 128
        c0 = t0 * 128
        A = aT[bi % 2]
        for jj in range(n):
            tile = t0 + jj
            ht = hts[ti % 2]
            kb.dma(ht.ap, self.H[tile * 128:(tile + 1) * 128, :])
            self.prenormT(ht, A[:, :, jj * 128:(jj + 1) * 128], abf[ti % 2], ssb[ti % 2], pst, junk, ti)
            ti += 1
        def stageA(fc, it_):
            pq = PSq[it_ % 2]
            for c in range(8):
                kb.mm(pq[:, 0:TW], Win[:, c, fc * 128:(fc + 1) * 128], A[:, c, 0:TW], start=(c == 0), stop=(c == 7))
            u = U[it_ % 2]
            kb.copy("dve", u[:, 0:3], halo[:, fc, :])
            kb.copy("act", u[:, 3:3 + TW], pq[:, 0:TW])
            kb.copy("dve", halo[:, fc, :], u[:, TW:TW + 3])
            y = yv[it_ % 2]
            kb.ts("dve", y[:, 0:TW], u[:, 3:3 + TW], cw[:, 3, fc:fc + 1], ALU.mult)
            for k in (2, 1, 0):
                kb.stt("dve", y[:, 0:TW], u[:, k:k + TW], cw[:, k, fc:fc + 1], y[:, 0:TW], ALU.mult, ALU.add)

        def stageB(fc, it_):
            y = yv[it_ % 2]
            z_ = z[it_ % 3]
            kb.act(z_[:, 0:TW], y[:, 0:TW], AF.Silu)
            if fc < 16:
                s_ = sq[it_ % 2]
                kb.act(s_[:, 0:TW], z_[:, 0:TW], AF.Square)
                kb.mm(PSs[:, 0:TW], onesb.ap, s_[:, 0:TW])
                r_ = rms[it_ % 2]
                kb.act(r_[:, 0:TW], PSs[:, 0:TW], AF.Sqrt, bias=EPS)
                kb.recip(r_[:, 0:TW], r_[:, 0:TW])
                if fc < 8:
                    kb.stt("dve", z_[:, 0:TW], z_[:, 0:TW], float(HD_D ** -0.5), r_[:, 0:TW], ALU.mult, ALU.mult)
                else:
                    kb.tt("dve", z_[:, 0:TW], z_[:, 0:TW], r_[:, 0:TW], ALU.mult)
            kb.dma(self.QKVD[fc, :, c0:c0 + TW], z_[:, 0:TW], q="pool")

        stageA(0, it)
        for fc in range(24):
            if fc + 1 < 24:
                stageA(fc + 1, it + 1)
            stageB(fc, it)
            it += 1
        for jj in range(n):
            tile = t0 + jj
            tc_ = slice(jj * 128, (jj + 1) * 128)
            for half in range(2):
                for c in range(8):
                    kb.mm(PSv[:, half * 512:(half + 1) * 512], A[:, c, tc_], Win[:, c, 3072 + half * 512:3072 + (half + 1) * 512],
                          start=(c == 0), stop=(c == 7))
            s2 = sg[tile % 2]
            kb.act(s2.ap, PSv.ap, AF.Silu)
            kb.dma(self.SGD[tile * 128:(tile + 1) * 128, :], s2.ap, q="pool")
            for c in range(8):
                kb.mm(PSm[:, 0:16], A[:, c, tc_], Win[:, c, 4096:4112], start=(c == 0), stop=(c == 7))
            x = xs[tile % 2]
            kb.act(GB[:, tile, 0:8], PSm[:, 0:8], AF.Sigmoid)
            kb.tt("dve", x[:, 0:8], PSm[:, 8:16], dtb.ap, ALU.add)
            kb.act(x[:, 8:16], x[:, 0:8], AF.Exp)
            kb.act(x[:, 16:24], x[:, 8:16], AF.Ln, bias=1.0)
            kb.tt("dve", x[:, 24:32], x[:, 16:24], nA.ap, ALU.mult)
            kb.mm(PSm[:, 16:24], tri.ap, x[:, 24:32])
            kb.copy("dve", GB[:, tile, 8:16], PSm[:, 16:24])
    kb.dma(self.GBD, GB.ap, q="pool")
    kb.end_phase()


def phase_D2(self, l, G=8):
    kb = self.kb
    j = l // 2
    NT = self.NT
    kb.phase()
    GB = kb.sb("GB", [128, NT, 16], F32)
    kb.dma(GB.ap, self.GBD)
    onrm = kb.sb("onrm", [128, 128], F32)
    kb.dma(onrm.ap, self.inp["dn_o_norm"][j].partition_broadcast(128))
    onesf = kb.sb("onesf", [128, 128], F32)
    kb.memset("pool", onesf.ap, 1.0)
    idf = self.idf
    idb = self.idb
    NMsT = kb.sb("NMsT", [128, 128], F32)
    kb.memset("pool", NMsT.ap, 0.0)
    kb.asel(NMsT.ap, NMsT.ap, [[1, 128]], ALU.is_gt, NEGBIG, 0, -1)
    NPs = kb.sb("NPs", [128, 128], F32)
    kb.memset("pool", NPs.ap, 0.0)
    kb.asel(NPs.ap, NPs.ap, [[-1, 128]], ALU.is_gt, -NEGBIG, 0, 1)
    BDm = kb.sb("BDm", [128, 128], F32)
    kb.memset("pool", BDm.ap, 1.0)
    kb.asel(BDm[:, 0:64], BDm[:, 0:64], [[0, 64]], ALU.is_ge, 0.0, 63, -1)
    kb.asel(BDm[:, 64:128], BDm[:, 64:128], [[0, 64]], ALU.is_ge, 0.0, -64, 1)
    ST = [kb.sb("ST", [128, 128], F32) for _ in range(8)]
    STb = [kb.sb("STb", [128, 128], BF16) for _ in range(8)]
    for s_ in ST + STb:
        kb.memset("dve", s_.ap, 0.0)
    X24 = [kb.sb("X24", [128, 24, 128], F32) for _ in range(2)]
    X24b = [kb.sb("X24b", [128, 24, 128], BF16) for _ in range(2)]
    SGt = [kb.sb("SGt", [128, D], BF16) for _ in range(2)]
    Og = [kb.sb("Og", [128, D], BF16) for _ in range(2)]
    OgT = [kb.sb("OgT", [128, 8, 128], BF16) for _ in range(2)]
    tl = [kb.sb("tl", [128, 32], F32) for _ in range(2)]
    sets = []
    f32names = ["DG", "Gs", "DTs", "DTi", "Dn", "A", "B", "Ad", "Ao", "Bd", "Bo", "P", "Pt",
                "U0s", "EGbc", "tmp", "on"]
    bf16names = ["R", "QKT", "kbg", "kd", "vb", "WT", "Us", "qdT",
                 "Adh", "Adl", "Bdh", "Bdl", "Xah", "Xal", "Xbh", "Xbl", "Xtah", "Xtal", "Xtbh", "Xtbl",
                 "Pth", "Ptl", "Ph", "Pl", "Aoh", "Aol", "Zh", "Zl"]
    for g in range(G):
        d = {}
        for nm in f32names:
            w = 256 if nm in ("DG", "Gs") else 128
            d[nm] = kb.sb(nm, [128, w], F32)
        for nm in bf16names:
            d[nm] = kb.sb(nm, [128, 128], BF16)
        d["cols"] = kb.sb("cols", [128, 8], F32)
        bA = kb.ps("bA", [128, 512], F32)
        d["bA"] = bA
        d["PSG"], d["PSK"], d["PSQ"] = kb.carve(bA, [(0, 256), (256, 384), (384, 512)])
        d["C0"], d["C1"], d["C2"], d["C3"] = kb.carve(bA, [(0, 128), (128, 256), (256, 384), (384, 512)])
        d["D0"], d["D1"], d["D2"], d["D3"] = d["C0"], d["C1"], d["C2"], d["C3"]
        d["C2b"] = View(d["C2"], d["C2"].t.bitcast(BF16)[:, 0:128])
        d["C3b"] = View(d["C3"], d["C3"].t.bitcast(BF16)[:, 0:128])
        sets.append(d)
    Xv = self.QKVD.rearrange("f d t -> d f t")
    OGTv = self.OGT.rearrange("(c p) t -> p c t", p=128)

    def head(n, hd, d, X, Xb, sgt, og, t_):
        qT = X[:, hd, :]
        qTb = Xb[:, hd, :]
        kTb = Xb[:, 8 + hd, :]
        vTb = Xb[:, 16 + hd, :]
        beta = GB[:, n, hd:hd + 1]
        gc = GB[:, n, 8 + hd:9 + hd]
        ngc = t_[:, hd:hd + 1]
        bg = t_[:, 16 + hd:17 + hd]
        cols = d["cols"]
        kb.ts("dve", d["DG"][:, 0:128], idf.ap, gc, ALU.mult)
        kb.ts("dve", d["DG"][:, 128:256], idf.ap, beta, ALU.mult)
        kb.mm(d["PSG"].ap, onesf.ap, d["DG"].ap)
        kb.copy("act", d["Gs"].ap, d["PSG"].ap)
        kb.mm(d["PSK"].ap, kTb, kTb)
        kb.mm(d["PSQ"].ap, kTb, qTb)
        yield
        G_ = d["Gs"][:, 0:128]
        kb.tt("dve", d["tmp"].ap, G_, NMsT.ap, ALU.add)
        kb.act(d["DTs"].ap, d["tmp"].ap, AF.Exp, bias=ngc)
        kb.tt("pool", d["DTi"].ap, d["DTs"].ap, idf.ap, ALU.add)
        kb.tt("dve", d["Dn"].ap, G_, NPs.ap, ALU.add)
        kb.act(d["Dn"].ap, d["Dn"].ap, AF.Exp, bias=gc, scale=-1.0)
        kb.act(d["EGbc"].ap, G_, AF.Exp)
        kb.act(cols[:, 0:1], gc, AF.Exp, bias=d["Gs"][:, 127:128], scale=-1.0)
        kb.act(cols[:, 1:2], d["Gs"][:, 127:128], AF.Exp)
        yield
        kb.stt("dve", d["A"].ap, d["PSK"].ap, beta, d["Dn"].ap, ALU.mult, ALU.mult)
        kb.tt("dve", d["B"].ap, d["PSK"].ap, d["DTs"].ap, ALU.mult)
        kb.tt("pool", d["B"].ap, d["B"].ap, d["Gs"][:, 128:256], ALU.mult)
        kb.tt("dve", d["QKT"].ap, d["PSQ"].ap, d["DTi"].ap, ALU.mult)
        kb.tt("pool", d["Ad"].ap, d["A"].ap, BDm.ap, ALU.mult)
        kb.tt("pool", d["Ao"].ap, d["A"].ap, d["Ad"].ap, ALU.subtract)
        kb.tt("pool", d["Bd"].ap, d["B"].ap, BDm.ap, ALU.mult)
        kb.tt("pool", d["Bo"].ap, d["B"].ap, d["Bd"].ap, ALU.subtract)
        kb.tt("dve", d["P"].ap, idf.ap, d["Ad"].ap, ALU.subtract)
        kb.tt("dve", d["Pt"].ap, idf.ap, d["Bd"].ap, ALU.subtract)
        yield
        def split(src, hi, lo, e="dve"):
            kb.copy("act", d[hi].ap, src)
            kb.tt(e, d[lo].ap, src, d[hi].ap, ALU.subtract)

        def mm3(out, A_, B_):
            kb.mm(out, d[A_[0]].ap, d[B_[0]].ap, start=True, stop=False)
            kb.mm(out, d[A_[0]].ap, d[B_[1]].ap, start=False, stop=False)
            kb.mm(out, d[A_[1]].ap, d[B_[0]].ap, start=False, stop=True)
        split(d["Ad"].ap, "Adh", "Adl", "pool")
        split(d["Bd"].ap, "Bdh", "Bdl", "pool")
        split(d["Pt"].ap, "Pth", "Ptl")
        split(d["Ao"].ap, "Aoh", "Aol", "pool")
        yield
        Xc, Xtc = ("Adh", "Adl"), ("Bdh", "Bdl")
        PT = ("Pth", "Ptl")
        for k in range(1, 6):
            Xn = ("Xah", "Xal") if k % 2 else ("Xbh", "Xbl")
            Xtn = ("Xtah", "Xtal") if k % 2 else ("Xtbh", "Xtbl")
            mm3(d["C0"].ap, Xtc, Xc)
            if k < 5:
                mm3(d["C1"].ap, Xc, Xtc)
            yield
            split(d["C0"].ap, Xn[0], Xn[1])
            if k < 5:
                split(d["C1"].ap, Xtn[0], Xtn[1])
            yield
            mm3(d["C2"].ap, PT, Xn)
            mm3(d["C3"].ap, Xn, PT)
            yield
            kb.tt("dve", d["P"].ap, d["P"].ap, d["C2"].ap, ALU.add)
            kb.tt("dve", d["Pt"].ap, d["Pt"].ap, d["C3"].ap, ALU.add)
            split(d["Pt"].ap, "Pth", "Ptl")
            Xc, Xtc = Xn, Xtn
            yield
        split(d["P"].ap, "Ph", "Pl", "pool")
        mm3(d["C0"].ap, ("Aoh", "Aol"), PT)
        yield
        split(d["C0"].ap, "Zh", "Zl")
        yield
        mm3(d["C1"].ap, ("Ph", "Pl"), ("Zh", "Zl"))
        kb.tt("dve", d["R"].ap, d["Pt"].ap, d["C1"].ap, ALU.subtract)
        kb.tr(d["C2b"], kTb, idb.ap)
        kb.tr(d["C3b"], vTb, idb.ap)
        kb.ts("dve", d["kbg"].ap, d["C2b"], bg, ALU.mult)
        kb.ts("dve", d["kd"].ap, d["C2b"], cols[:, 0:1], ALU.mult)
        kb.ts("dve", d["vb"].ap, d["C3b"], beta, ALU.mult)
        kb.tt("pool", d["qdT"].ap, qT, d["EGbc"].ap, ALU.mult)
        yield
        kb.mm(d["D0"].ap, d["kbg"].ap, d["R"].ap)
        kb.mm(d["D1"].ap, d["R"].ap, d["vb"].ap)
        kb.copy("act", d["WT"].ap, d["D0"].ap)
        kb.copy("act", d["U0s"].ap, d["D1"].ap)
        yield
        S_ = ST[hd]
        Sb_ = STb[hd]
        kb.mm(d["D2"].ap, d["WT"].ap, Sb_.ap)
        kb.tt("dve", d["Us"].ap, d["U0s"].ap, d["D2"].ap, ALU.subtract)
        kb.mm(d["D3"].ap, d["qdT"].ap, Sb_.ap, start=True, stop=False)
        kb.mm(d["D3"].ap, d["QKT"].ap, d["Us"].ap, start=False, stop=True)
        kb.memset("dve", cols[:, 2:3], 0.0)
        kb.act(d["tmp"].ap, d["D3"].ap, AF.Square, accum=cols[:, 2:3])
        kb.act(cols[:, 3:4], cols[:, 2:3], AF.Sqrt, bias=EPS, scale=1.0 / HD_D)
        kb.recip(cols[:, 4:5], cols[:, 3:4])
        kb.stt("dve", d["on"].ap, d["D3"].ap, cols[:, 4:5], onrm.ap, ALU.mult, ALU.mult)
        kb.mm(d["C0"].ap, d["kd"].ap, d["Us"].ap)
        kb.stt("dve", S_.ap, S_.ap, cols[:, 1:2], d["C0"].ap, ALU.mult, ALU.add)
        kb.copy("act", Sb_.ap, S_.ap)
        kb.tt("pool", og[:, hd * 128:(hd + 1) * 128], d["on"].ap, sgt[:, hd * 128:(hd + 1) * 128], ALU.mult)
        yield

    PSot = View(sets[G - 1]["bA"], sets[G - 1]["bA"].t[:].bitcast(BF16))
    HG = G // 2
    import os
    SKEW = int(os.environ.get("DBG_SKEW", "22"))

    def prologue(n):
        X = X24[n % 2]
        Xb = X24b[n % 2]
        kb.copy("act", Xb[:, 0:12, :], X[:, 0:12, :])
        kb.copy("dve", Xb[:, 12:24, :], X[:, 12:24, :])
        kb.dma(SGt[n % 2].ap, self.SGD[n * 128:(n + 1) * 128, :])
        t_ = tl[n % 2]
        kb.ts("dve", t_[:, 0:8], GB[:, n, 8:16], -1.0, ALU.mult)
        kb.act(t_[:, 8:16], GB[:, n, 8:16], AF.Exp)
        kb.tt("dve", t_[:, 16:24], t_[:, 8:16], GB[:, n, 0:8], ALU.mult)

    def epilogue(n):
        og = Og[n % 2]
        for c in range(8):
            kb.tr(PSot[:, c * 128:(c + 1) * 128], og[:, c * 128:(c + 1) * 128], idb.ap)
        ot = OgT[n % 2]
        kb.copy("act", ot.ap, PSot.re("p (c t) -> p c t", c=8))
        kb.dma(OGTv[:, :, n * 128:(n + 1) * 128], ot.ap, q="pool")
        if n + 2 < NT:
            kb.dma(X24[n % 2].ap, Xv[:, :, (n + 2) * 128:(n + 3) * 128])

    def item(n, grp):
        if grp == 0:
            prologue(n)
        gens = [head(n, grp * HG + g, sets[grp * HG + g], X24[n % 2], X24b[n % 2], SGt[n % 2], Og[n % 2], tl[n % 2])
                for g in range(HG)]
        alive = list(gens)
        while alive:
            nxt = []
            for g_ in alive:
                try:
                    next(g_)
                    nxt.append(g_)
                except StopIteration:
                    pass
            alive = nxt
            yield
        if grp == 1:
            epilogue(n)

    kb.dma(X24[0].ap, Xv[:, :, 0:128])
    if NT > 1:
        kb.dma(X24[1].ap, Xv[:, :, 128:256])
    items = [item(n, grp) for n in range(NT) for grp in range(2)]
    active = []
    nxt_item = 0
    while nxt_item < len(items) or active:
        if nxt_item < len(items) and (not active or (len(active) < 2 and active[-1][1] >= SKEW)):
            active.append([items[nxt_item], 0])
            nxt_item += 1
        still = []
        for ent in active:
            try:
                next(ent[0])
                ent[1] += 1
                still.append(ent)
            except StopIteration:
                pass
        active = still
    kb.end_phase()


Prog.phase_D1 = phase_D1
Prog.phase_D2 = phase_D2


def dn_layer(self, l):
    self.phase_D1(l)
    self.phase_D2(l)
    self.phase_R(self.OGT, 8, self.inp["dn_w_out"][l // 2], self.inp["norm_mix_post"][l], self.H)


Prog.dn_layer = dn_layer


_SEQ = 8192
_NT = (_SEQ + NMETA + NPAD) // 128


def build_full(NT):
    p = Prog(NT)
    p.consts()
    p.phase_init()
    for l in range(4):
        if l % 2 == 0:
            p.attn_layer(l)
        else:
            p.dn_layer(l)
        p.ffn(l, final=(l == 3))
    p.finish()
    return p


def kernel(**inputs):
    x = np.asarray(inputs["x"], dtype=np.float32)
    B = x.shape[0]
    meta = np.asarray(inputs["meta_tokens"], dtype=np.float32)
    p = build_full(_NT)
    shared = {k: np.ascontiguousarray(np.asarray(inputs[k], dtype=np.float32)) for k in p.inp if k != "xpad"}
    real = {0: 0, 1: 1, 4: 2, 5: 3}
    zshared = {k: np.zeros_like(v) for k, v in shared.items()}
    zx = np.zeros((NPAD + NMETA + x.shape[1], D), np.float32)
    in_maps = []
    for c in range(8):
        if c in real and real[c] < B:
            b = real[c]
            xpad = np.concatenate([np.zeros((NPAD, D), np.float32), meta, x[b]], axis=0)
            m = dict(shared)
            m["xpad"] = np.ascontiguousarray(xpad)
        else:
            m = dict(zshared)
            m["xpad"] = zx
        in_maps.append(m)
    res = run_bass_kernel_spmd(p.nc, in_maps, core_ids=list(range(8)))
    core_of = {b: c for c, b in real.items()}
    out = np.stack([np.asarray(res.results[core_of[b]]["y"], dtype=np.float32) for b in range(B)], axis=0)
    return out
```

```python
import numpy as np
from contextlib import ExitStack
import concourse.bass as bass
import concourse.mybir as mybir
from concourse.bass_utils import run_bass_kernel_spmd

F32 = mybir.dt.float32
BF16 = mybir.dt.bfloat16
AF = mybir.ActivationFunctionType
ALU = mybir.AluOpType
AX = mybir.AxisListType

D = 1024
NH_A = 16
HD_A = 64
NH_D = 8
HD_D = 128
FFN = 2816
EPS = 1e-6
NPAD = 112
NMETA = 16
NEGBIG = -30000.0


class Buf:
    def __init__(self, t, name, space):
        self.t = t
        self.name = name
        self.space = space
        self.w = {}
        self.r = {}
        self.sm = {}
        self.sem = None

    def __getitem__(self, k):
        return View(self, self.t[k])

    @property
    def ap(self):
        return View(self, self.t[:])


class View:
    def __init__(self, buf, ap):
        self.buf = buf
        self.ap = ap

    def __getitem__(self, k):
        return View(self.buf, self.ap[k])

    def re(self, s, **kw):
        return View(self.buf, self.ap.rearrange(s, **kw))

    def bc(self, shape):
        return View(self.buf, self.ap.to_broadcast(shape))


def _ap(x):
    return x.ap if isinstance(x, View) else x


class KB:
    def __init__(self):
        self.nc = bass.Bass("TRN2", target_bir_lowering=False)
        nc = self.nc
        self.es = ExitStack()
        self.eng = {"pe": nc.tensor, "dve": nc.vector, "act": nc.scalar, "pool": nc.gpsimd, "sp": nc.sync}
        self.cnt = {}
        self.esem = {}
        for e in ("pe", "dve", "act", "pool"):
            self.esem[e] = self.es.enter_context(nc.semaphore("cnt_" + e))
            self.cnt[e] = 0
        self.seen = {e: {} for e in self.eng}
        self.sem_pool = []
        self.live_sems = []
        self.phase_stack = None
        self.phase_bufs = []
        self.uid = 0
        self.dq = 0
        self.noself = False
        import os
        self.fast_self = tuple(x for x in os.environ.get("FAST_SELF", "dve").split(",") if x)

    def phase(self):
        self.phase_stack = ExitStack()
        self.phase_bufs = []

    def end_phase(self):
        self.barrier()
        for b in self.phase_bufs:
            if b.sem is not None:
                self.sem_pool.append(b.sem)
                b.sem = None
        self.phase_stack.close()
        self.phase_stack = None
        self.phase_bufs = []

    def carve(self, buf, slices):
        out = []
        for (a, b) in slices:
            sbuf = Buf(buf.t[:, a:b], buf.name + "_%d" % a, buf.space)
            sbuf.w = buf.w
            sbuf.r = buf.r
            sbuf.sm = buf.sm
            self.phase_bufs.append(sbuf)
            out.append(sbuf)
        return out

    def sb(self, name, shape, dt):
        self.uid += 1
        t = self.phase_stack.enter_context(self.nc.sbuf_tensor("%s_%d" % (name, self.uid), list(shape), dt))
        b = Buf(t, name, "sb")
        self.phase_bufs.append(b)
        return b

    def ps(self, name, shape, dt):
        self.uid += 1
        t = self.phase_stack.enter_context(self.nc.psum_tensor("%s_%d" % (name, self.uid), list(shape), dt))
        b = Buf(t, name, "ps")
        self.phase_bufs.append(b)
        return b

    def _getsem(self, b):
        if b.sem is None:
            if self.sem_pool:
                b.sem = self.sem_pool.pop()
            else:
                s = self.es.enter_context(self.nc.semaphore("dsem%d" % len(self.live_sems)))
                b.sem = [s, 0]
                self.live_sems.append(b.sem)
        return b.sem

    def _wait(self, e, deps):
        seen = self.seen[e]
        for sid, (sem, val) in deps.items():
            if seen.get(sid, 0) < val:
                self.eng[e].wait_ge(sem, val)
                seen[sid] = val

    def _deps(self, outs, ins):
        deps = {}

        def add(d):
            for sid, (sem, val) in d.items():
                if sid not in deps or deps[sid][1] < val:
                    deps[sid] = (sem, val)
        for v in ins:
            if isinstance(v, View):
                add(v.buf.w)
        for v in outs:
            if isinstance(v, View):
                add(v.buf.w)
                add(v.buf.r)
        return deps

    def _record(self, outs, ins, sem, val):
        sid = id(sem)
        for v in ins:
            if isinstance(v, View):
                v.buf.r[sid] = (sem, val)
        for v in outs:
            if isinstance(v, View):
                v.buf.w[sid] = (sem, val)

    def op(self, e, fn, outs, ins, pe_acc=False):
        deps = self._deps(outs, ins)
        small = True
        if e == "pe":
            deps.pop(id(self.esem[e]), None)
        elif e in self.fast_self:
            sh = _ap(outs[0]).shape
            n = 1
            for v_ in sh[1:]:
                n *= v_
            small = n < 256
            sid = id(self.esem[e])
            if not small and sid in deps:
                need = 0
                for v in list(ins) + list(outs):
                    if isinstance(v, View):
                        need = max(need, v.buf.sm.get(sid, 0))
                if need > 0:
                    deps[sid] = (self.esem[e], need)
                else:
                    deps.pop(sid)
        self._wait(e, deps)
        ins_ = fn()
        self.cnt[e] += 1
        ins_.then_inc(self.esem[e], 1)
        self._record(outs, ins, self.esem[e], self.cnt[e])
        if small and e in self.fast_self:
            sid = id(self.esem[e])
            for v in list(ins) + list(outs):
                if isinstance(v, View):
                    v.buf.sm[sid] = self.cnt[e]
        return ins_

    def dma(self, out, in_, q=None, **kw):
        if q is None:
            q = "sp"
        sbv = out if isinstance(out, View) else in_
        assert isinstance(sbv, View)
        outs = [out] if isinstance(out, View) else []
        ins = [in_] if isinstance(in_, View) else []
        deps = self._deps(outs, ins)
        self._wait(q, deps)
        se = self._getsem(sbv.buf)
        self.eng[q].dma_start(out=_ap(out), in_=_ap(in_), **kw).then_inc(se[0], 16)
        se[1] += 16
        self._record(outs, ins, se[0], se[1])

    def barrier(self):
        allsems = {}
        for e in ("pe", "dve", "act", "pool"):
            allsems[id(self.esem[e])] = (self.esem[e], self.cnt[e])
        for se in self.live_sems:
            allsems[id(se[0])] = (se[0], se[1])
        for e in self.eng:
            self._wait(e, {k: v for k, v in allsems.items() if v[1] > 0})

    def mm(self, out, lhsT, rhs, start=True, stop=True):
        return self.op("pe", lambda: self.nc.tensor.matmul(_ap(out), lhsT=_ap(lhsT), rhs=_ap(rhs), start=start, stop=stop),
                       [out], [lhsT, rhs])

    def tr(self, out, in_, ident):
        return self.op("pe", lambda: self.nc.tensor.transpose(_ap(out), _ap(in_), _ap(ident)), [out], [in_, ident])

    def act(self, out, in_, func, bias=None, scale=None, accum=None, e="act"):
        kw = {}
        ins = [in_]
        outs = [out]
        if bias is not None:
            kw["bias"] = _ap(bias)
            ins.append(bias)
        if scale is not None:
            kw["scale"] = _ap(scale)
            ins.append(scale)
        if accum is not None:
            kw["accum_out"] = _ap(accum)
            outs.append(accum)
        return self.op("act", lambda: self.nc.scalar.activation(out=_ap(out), in_=_ap(in_), func=func, **kw), outs, ins)

    def tt(self, e, out, a, b, op):
        return self.op(e, lambda: self.eng[e].tensor_tensor(out=_ap(out), in0=_ap(a), in1=_ap(b), op=op), [out], [a, b])

    def ts(self, e, out, a, s1, op0, s2=None, op1=None, accum=None):
        kw = {}
        outs = [out]
        if op1 is not None:
            kw["op1"] = op1
        if accum is not None:
            kw["accum_out"] = _ap(accum)
            outs.append(accum)
        return self.op(e, lambda: self.eng[e].tensor_scalar(out=_ap(out), in0=_ap(a), scalar1=_ap(s1), scalar2=_ap(s2), op0=op0, **kw),
                       outs, [a, s1, s2])

    def stt(self, e, out, a, s, b, op0, op1):
        return self.op(e, lambda: self.eng[e].scalar_tensor_tensor(out=_ap(out), in0=_ap(a), scalar=_ap(s), in1=_ap(b), op0=op0, op1=op1),
                       [out], [a, s, b])

    def copy(self, e, out, in_):
        if e == "act":
            return self.op("act", lambda: self.nc.scalar.copy(out=_ap(out), in_=_ap(in_)), [out], [in_])
        return self.op(e, lambda: self.eng[e].tensor_copy(out=_ap(out), in_=_ap(in_)), [out], [in_])

    def amul(self, out, in_, mul):
        return self.op("act", lambda: self.nc.scalar.mul(out=_ap(out), in_=_ap(in_), mul=_ap(mul)), [out], [in_, mul])

    def memset(self, e, out, val):
        return self.op(e, lambda: self.eng[e].memset(_ap(out), val), [out], [])

    def recip(self, out, in_):
        return self.op("dve", lambda: self.nc.vector.reciprocal(out=_ap(out), in_=_ap(in_)), [out], [in_])

    def asel(self, out, in_, pattern, cmp, fill, base, cm):
        return self.op("pool", lambda: self.nc.gpsimd.affine_select(out=_ap(out), in_=_ap(in_), pattern=pattern, compare_op=cmp,
                                                                      fill=fill, base=base, channel_multiplier=cm), [out], [in_])


class Prog:
    def __init__(self, NT, layers=(0, 1, 2, 3), final_out=True):
        self.NT = NT
        self.LP = NT * 128
        self.kb = KB()
        self.nc = self.kb.nc
        self.layers = layers
        nc = self.nc
        LP = self.LP
        S = LP - 128
        shapes = {"xpad": [LP, D], "norm_mix_pre": [4, D], "norm_mix_post": [4, D], "norm_ffn_pre": [4, D], "norm_ffn_post": [4, D],
                  "attn_w_in": [2, D, 4112], "attn_b_forget": [2, 16], "attn_q_norm": [2, 64], "attn_k_norm": [2, 64],
                  "attn_w_out": [2, D, D], "dn_w_in": [2, D, 4112], "dn_conv": [2, 4, 3072], "dn_a_log": [2, 8], "dn_dt_bias": [2, 8],
                  "dn_o_norm": [2, 128], "dn_w_out": [2, D, D], "ffn_w_up": [4, D, 2 * FFN], "ffn_conv": [4, 3, 2 * FFN],
                  "ffn_w_down": [4, FFN, D]}

        class Lazy(dict):
            def __missing__(d, name):
                d[name] = nc.dram_tensor(name, list(shapes[name]), F32, kind="ExternalInput").ap()
                return d[name]
        self.inp = Lazy()
        self.y = nc.dram_tensor("y", [S, D], F32, kind="ExternalOutput").ap()

        def dscr(name, shape, dt):
            return nc.dram_tensor(name, list(shape), dt, kind="Internal").ap()
        self.H = dscr("H", [LP, D], F32)
        self.HT = dscr("HT", [FFN, LP], BF16)
        self.OGT = dscr("OGT", [D, LP], BF16)
        self.SG = dscr("SG", [D, LP], BF16)
        self.QT = dscr("QT", [16, 65, LP], BF16)
        self.KT = dscr("KT", [16, 65, LP], BF16)
        self.VA = dscr("VA", [LP, 16, 65], BF16)
        self.CT = dscr("CT", [128, NT, 16], F32)
        self.QKVD = dscr("QKVD", [24, 128, LP], F32)
        self.SGD = dscr("SGD", [LP, D], BF16)
        self.GBD = dscr("GBD", [128, NT, 16], F32)
        self.CAR = dscr("CAR", [128, NT + 1, 16], F32)

    def consts(self):
        kb = self.kb
        kb.phase()
        es = kb.phase_stack
        self.idf = kb.sb("idf", [128, 128], F32)
        self.idb = kb.sb("idb", [128, 128], BF16)
        kb.memset("pool", self.idf.ap, 1.0)
        kb.asel(self.idf.ap, self.idf.ap, [[-1, 128]], ALU.is_equal, 0.0, 0, 1)
        kb.copy("dve", self.idb.ap, self.idf.ap)
        self.const_stack = es
        kb.phase_stack = None
        kb.phase_bufs = []

    def blocks(self):
        out = []
        t = 0
        while t < self.NT:
            n = min(4, self.NT - t)
            out.append((t, n))
            t += n
        return out

    def load_cols(self, dst, src2d, n, stage, pst):
        kb = self.kb
        kb.dma(stage[0:n, :], src2d)
        kb.tr(pst[:, 0:n], stage[0:n, :], self.idf[0:n, 0:n])
        kb.copy("dve", dst, pst[:, 0:n])

    def load_w(self, dst, W, nch, ncols, gcol, stages, col0=0):
        kb = self.kb
        SEG = 2048
        k = 0
        for c in range(nch):
            for s0 in range(0, ncols, SEG):
                w = min(SEG, ncols - s0)
                st = stages[k % len(stages)]
                e = ("dve", "act")[k % 2]
                k += 1
                kb.dma(st[:, 0:w], W[c * 128:(c + 1) * 128, s0:s0 + w])
                if gcol is not None:
                    if e == "act":
                        kb.amul(dst[:, c, col0 + s0:col0 + s0 + w], st[:, 0:w], gcol[:, c:c + 1])
                    else:
                        kb.ts(e, dst[:, c, col0 + s0:col0 + s0 + w], st[:, 0:w], gcol[:, c:c + 1], ALU.mult)
                else:
                    kb.copy(e, dst[:, c, col0 + s0:col0 + s0 + w], st[:, 0:w])

    def prenormT(self, ht, aT_dst, abf, ss, pst, junk, k):
        kb = self.kb
        kb.memset("dve", ss[:, 0:1], 0.0)
        kb.act(junk.ap, ht.ap, AF.Square, accum=ss[:, 0:1])
        kb.act(ss[:, 1:2], ss[:, 0:1], AF.Sqrt, bias=EPS, scale=1.0 / D)
        kb.recip(ss[:, 2:3], ss[:, 1:2])
        kb.ts("dve", abf.ap, ht.ap, ss[:, 2:3], ALU.mult)
        for c in range(8):
            kb.tr(pst[:, c * 128:(c + 1) * 128], abf[:, c * 128:(c + 1) * 128], self.idb.ap)
        kb.copy(("act", "dve")[k % 2], aT_dst, pst.ap.re("p (c t) -> p c t", c=8))

    def phase_R(self, srcT, nch, W, gpost, hsrc, final=False):
        kb = self.kb
        kb.phase()
        Wb = kb.sb("Wb", [128, nch, D], BF16)
        stages = [kb.sb("wst", [128, 2048], F32) for _ in range(2)]
        self.load_w(Wb, W, nch, D, None, stages)
        g = kb.sb("g", [128, D], F32)
        kb.dma(g.ap, gpost.partition_broadcast(128))
        Sb = [kb.sb("Sb", [128, nch, 512], BF16) for _ in range(2)]
        hts = [kb.sb("ht", [128, D], F32) for _ in range(3)]
        fn = [kb.sb("fn", [128, D], F32) for _ in range(2)]
        ssb = [kb.sb("ss", [128, 8], F32) for _ in range(2)]
        junk = kb.sb("junk", [128, 512], BF16)
        PS = [[kb.ps("psa", [128, 512], F32), kb.ps("psb", [128, 512], F32)] for _ in range(2)]
        srcv = srcT.rearrange("(c p) t -> p c t", p=128)
        blocks = self.blocks()
        kb.dma(Sb[0][:, :, 0:blocks[0][1] * 128], srcv[:, :, 0:blocks[0][1] * 128])
        ti = 0
        for bi, (t0, n) in enumerate(blocks):
            if bi + 1 < len(blocks):
                t1, n1 = blocks[bi + 1]
                kb.dma(Sb[(bi + 1) % 2][:, :, 0:n1 * 128], srcv[:, :, t1 * 128:(t1 + n1) * 128])
            S = Sb[bi % 2]
            for j in range(n):
                tile = t0 + j
                ht = hts[ti % 3]
                kb.dma(ht.ap, hsrc[tile * 128:(tile + 1) * 128, :])
                pa, pb = PS[ti % 2]
                ss = ssb[ti % 2]
                f = fn[ti % 2]
                for c in range(nch):
                    kb.mm(pa.ap, S[:, c, j * 128:(j + 1) * 128], Wb[:, c, 0:512], start=(c == 0), stop=(c == nch - 1))
                for c in range(nch):
                    kb.mm(pb.ap, S[:, c, j * 128:(j + 1) * 128], Wb[:, c, 512:1024], start=(c == 0), stop=(c == nch - 1))
                kb.memset("dve", ss[:, 0:2], 0.0)
                kb.act(junk.ap, pa.ap, AF.Square, accum=ss[:, 0:1])
                kb.act(junk.ap, pb.ap, AF.Square, accum=ss[:, 1:2])
                kb.tt("dve", ss[:, 2:3], ss[:, 0:1], ss[:, 1:2], ALU.add)
                kb.act(ss[:, 3:4], ss[:, 2:3], AF.Sqrt, bias=EPS, scale=1.0 / D)
                kb.recip(ss[:, 4:5], ss[:, 3:4])
                kb.stt("dve", f[:, 0:512], pa.ap, ss[:, 4:5], g[:, 0:512], ALU.mult, ALU.mult)
                kb.stt("dve", f[:, 512:1024], pb.ap, ss[:, 4:5], g[:, 512:1024], ALU.mult, ALU.mult)
                kb.tt("dve", f.ap, f.ap, ht.ap, ALU.add)
                if final:
                    if tile >= 1:
                        kb.dma(self.y[(tile - 1) * 128:tile * 128, :], f.ap, q="pool")
                else:
                    kb.dma(self.H[tile * 128:(tile + 1) * 128, :], f.ap, q="pool")
                ti += 1
        kb.end_phase()

    def phase_F1(self, l):
        kb = self.kb
        kb.phase()
        Wup = kb.sb("Wup", [128, 8, 2 * FFN], BF16)
        stages = [kb.sb("wst", [128, 2048], F32) for _ in range(2)]
        cst = kb.sb("cst", [128, 128], F32)
        gcol = kb.sb("gcol", [128, 8], F32)
        cw = kb.sb("cw", [128, 3, 44], F32)
        pst = kb.ps("pst", [128, 1024], BF16)
        psc = kb.ps("psc", [128, 128], F32)
        self.load_cols(gcol.ap, self.inp["norm_ffn_pre"][l].rearrange("(c p) -> c p", p=128), 8, cst, psc)
        for k in range(3):
            self.load_cols(cw[:, k, :], self.inp["ffn_conv"][l, k].rearrange("(c p) -> c p", p=128), 44, cst, psc)
        self.load_w(Wup, self.inp["ffn_w_up"][l], 8, 2 * FFN, gcol, stages)
        halo = kb.sb("halo", [128, 44, 2], F32)
        kb.memset("pool", halo.ap, 0.0)
        hts = [kb.sb("ht", [128, D], F32) for _ in range(2)]
        abf = [kb.sb("abf", [128, D], BF16) for _ in range(2)]
        ssb = [kb.sb("ss", [128, 8], F32) for _ in range(2)]
        junk = kb.sb("junk", [128, D], BF16)
        aT = [kb.sb("aT", [128, 8, 512], BF16) for _ in range(2)]
        Hb = [kb.sb("Hb", [128, 22, 512], BF16) for _ in range(2)]
        Ug = [kb.sb("Ug", [128, 514], F32) for _ in range(2)]
        Uu = [kb.sb("Uu", [128, 514], F32) for _ in range(2)]
        yg = [kb.sb("yg", [128, 512], F32) for _ in range(2)]
        yu = [kb.sb("yu", [128, 512], F32) for _ in range(2)]
        gg = [kb.sb("gg", [128, 512], F32) for _ in range(2)]
        ptmp = [kb.sb("ptmp", [128, 512], F32) for _ in range(2)]
        PSg = [kb.ps("psg", [128, 512], F32) for _ in range(2)]
        PSu = [kb.ps("psu", [128, 512], F32) for _ in range(2)]
        HTv = self.HT.rearrange("(c p) t -> p c t", p=128)
        hsrc = self.H
        ti = 0
        it = 0
        for bi, (t0, n) in enumerate(self.blocks()):
            TW = n * 128
            A = aT[bi % 2]
            for j in range(n):
                tile = t0 + j
                ht = hts[ti % 2]
                kb.dma(ht.ap, hsrc[tile * 128:(tile + 1) * 128, :])
                self.prenormT(ht, A[:, :, j * 128:(j + 1) * 128], abf[ti % 2], ssb[ti % 2], pst, junk, ti)
                ti += 1
            HB = Hb[bi % 2]
            for fc in range(22):
                pg = PSg[it % 2]; pu = PSu[it % 2]
                ug = Ug[it % 2]; uu = Uu[it % 2]
                for c in range(8):
                    kb.mm(pg[:, 0:TW], Wup[:, c, fc * 128:(fc + 1) * 128], A[:, c, 0:TW], start=(c == 0), stop=(c == 7))
                for c in range(8):
                    kb.mm(pu[:, 0:TW], Wup[:, c, FFN + fc * 128:FFN + (fc + 1) * 128], A[:, c, 0:TW], start=(c == 0), stop=(c == 7))
                a = yg[it % 2]; b = yu[it % 2]
                for (u_, p_, y_, hc) in ((ug, pg, a, fc), (uu, pu, b, 22 + fc)):
                    kb.copy("act", u_[:, 2:2 + TW], p_[:, 0:TW])
                    kb.amul(y_[:, 0:TW], p_[:, 0:TW], cw[:, 2, hc:hc + 1])
                    kb.copy("pool", u_[:, 0:2], halo[:, hc, :])
                    kb.copy("pool", halo[:, hc, :], u_[:, TW:TW + 2])
                    kb.stt("dve", y_[:, 0:TW], u_[:, 1:1 + TW], cw[:, 1, hc:hc + 1], y_[:, 0:TW], ALU.mult, ALU.add)
                    kb.stt("dve", y_[:, 0:TW], u_[:, 0:TW], cw[:, 0, hc:hc + 1], y_[:, 0:TW], ALU.mult, ALU.add)
                kb.act(gg[it % 2][:, 0:TW], a[:, 0:TW], AF.Gelu_apprx_tanh)
                kb.tt("dve", HB[:, fc, 0:TW], gg[it % 2][:, 0:TW], b[:, 0:TW], ALU.mult)
                it += 1
            kb.dma(HTv[:, :, t0 * 128:t0 * 128 + TW], HB[:, :, 0:TW], q="pool")
        kb.end_phase()

    def phase_init(self):
        kb = self.kb
        kb.phase()
        hts = [kb.sb("ht", [128, D], F32) for _ in range(4)]
        for t in range(self.NT):
            ht = hts[t % 4]
            kb.dma(ht.ap, self.inp["xpad"][t * 128:(t + 1) * 128, :])
            kb.dma(self.H[t * 128:(t + 1) * 128, :], ht.ap, q="pool")
        kb.end_phase()

    def ffn(self, l, final):
        self.phase_F1(l)
        self.phase_R(self.HT, 22, self.inp["ffn_w_down"][l], self.inp["norm_ffn_post"][l], self.H, final=final)

    def finish(self):
        self.const_stack.close()
        self.kb.es.close()


def phase_A1(self, l):
    kb = self.kb
    j = l // 2
    NT = self.NT
    kb.phase()
    Win = kb.sb("Win", [128, 8, 4112], BF16)
    stages = [kb.sb("wst", [128, 2048], F32) for _ in range(2)]
    cst = kb.sb("cst", [128, 128], F32)
    gcol = kb.sb("gcol", [128, 8], F32)
    pst = kb.ps("pst", [128, 1024], BF16)
    PSq = [kb.ps("psq", [128, 512], F32) for _ in range(2)]
    PSs = kb.ps("pss", [128, 512], F32)
    PSv = kb.ps("psv", [128, 1024], F32)
    PSm = kb.ps("psm", [128, 512], F32)
    PSr = kb.ps("psr", [16, 512], BF16)
    self.load_cols(gcol.ap, self.inp["norm_mix_pre"][l].rearrange("(c p) -> c p", p=128), 8, cst, PSs)
    self.load_w(Win, self.inp["attn_w_in"][j], 8, 4112, gcol, stages)
    qg = kb.sb("qg", [128, 2], F32)
    for half in range(2):
        kb.dma(qg[half * 64:(half + 1) * 64, 0:1], self.inp["attn_q_norm"][j].rearrange("(p o) -> p o", o=1))
        kb.dma(qg[half * 64:(half + 1) * 64, 1:2], self.inp["attn_k_norm"][j].rearrange("(p o) -> p o", o=1))
    kb.ts("dve", qg[:, 0:1], qg[:, 0:1], 0.125, ALU.mult)
    bfg = kb.sb("bfg", [128, 16], F32)
    kb.dma(bfg.ap, self.inp["attn_b_forget"][j].partition_broadcast(128))
    BD = kb.sb("BD", [128, 128], BF16)
    kb.memset("pool", BD.ap, 1.0)
    kb.asel(BD[:, 0:64], BD[:, 0:64], [[0, 64]], ALU.is_ge, 0.0, 63, -1)
    kb.asel(BD[:, 64:128], BD[:, 64:128], [[0, 64]], ALU.is_ge, 0.0, -64, 1)
    tri = kb.sb("tri", [128, 128], F32)
    kb.memset("pool", tri.ap, 1.0)
    kb.asel(tri.ap, tri.ap, [[1, 128]], ALU.is_ge, 0.0, 0, -1)
    onesf = kb.sb("onesf", [128, 128], F32)
    kb.memset("pool", onesf.ap, 1.0)
    onesb = kb.sb("onesb", [16, 512], BF16)
    kb.memset("pool", onesb.ap, 1.0)
    Ctok = kb.sb("Ctok", [128, NT, 16], F32)
    CAR = kb.sb("CAR", [128, NT + 1, 16], F32)
    kb.memset("dve", CAR[:, 0, :], 0.0)
    hts = [kb.sb("ht", [128, D], F32) for _ in range(2)]
    abf = [kb.sb("abf", [128, D], BF16) for _ in range(2)]
    ssb = [kb.sb("ss", [128, 8], F32) for _ in range(2)]
    junk = kb.sb("junk", [128, D], BF16)
    aT = [kb.sb("aT", [128, 8, 512], BF16) for _ in range(2)]
    sq = [kb.sb("sq", [128, 512], BF16) for _ in range(2)]
    rms = [kb.sb("rms", [128, 512], F32) for _ in range(2)]
    qn = [kb.sb("qn", [128, 512], BF16) for _ in range(3)]
    Va = [kb.sb("Va", [128, 16, 65], BF16) for _ in range(2)]
    V0 = kb.sb("V0", [128, 16, 65], BF16)
    for v in Va + [V0]:
        kb.memset("pool", v.ap, 1.0)
    kb.memset("pool", V0[0:NPAD, :, 64:65], 0.0)
    xs = [kb.sb("xs", [128, 48], F32) for _ in range(2)]
    rb = [kb.sb("rb", [128, 16], BF16) for _ in range(2)]
    rT = [kb.sb("rT", [16, 512], BF16) for _ in range(2)]
    ti = 0
    it = 0
    for bi, (t0, n) in enumerate(self.blocks()):
        TW = n * 128
        c0 = t0 * 128
        A = aT[bi % 2]
        for jj in range(n):
            tile = t0 + jj
            ht = hts[ti % 2]
            kb.dma(ht.ap, self.H[tile * 128:(tile + 1) * 128, :])
            self.prenormT(ht, A[:, :, jj * 128:(jj + 1) * 128], abf[ti % 2], ssb[ti % 2], pst, junk, ti)
            ti += 1
        def projA(idx, it_):
            pq_ = PSq[it_ % 2]
            col_ = (idx // 8) * 1024 + (idx % 8) * 128
            for c in range(8):
                kb.mm(pq_[:, 0:TW], Win[:, c, col_:col_ + 128], A[:, c, 0:TW], start=(c == 0), stop=(c == 7))
        projA(0, it)
        for which in range(2):
            dst = self.QT if which == 0 else self.KT
            for ch in range(8):
                pq = PSq[it % 2]
                s_ = sq[it % 2]
                kb.act(s_[:, 0:TW], pq[:, 0:TW], AF.Square)
                if which * 8 + ch + 1 < 16:
                    projA(which * 8 + ch + 1, it + 1)
                kb.mm(PSs[:, 0:TW], BD.ap, s_[:, 0:TW])
                r_ = rms[it % 2]
                kb.act(r_[:, 0:TW], PSs[:, 0:TW], AF.Sqrt, bias=EPS, scale=1.0 / 64)
                kb.recip(r_[:, 0:TW], r_[:, 0:TW])
                q_ = qn[it % 3]
                kb.stt("dve", q_[:, 0:TW], pq[:, 0:TW], qg[:, which:which + 1], r_[:, 0:TW], ALU.mult, ALU.mult)
                kb.dma(dst[2 * ch, 0:64, c0:c0 + TW], q_[0:64, 0:TW], q="pool")
                kb.dma(dst[2 * ch + 1, 0:64, c0:c0 + TW], q_[64:128, 0:TW], q="pool")
                it += 1
        for ch in range(8):
            pq = PSq[it % 2]
            col = 3072 + ch * 128
            for c in range(8):
                kb.mm(pq[:, 0:TW], Win[:, c, col:col + 128], A[:, c, 0:TW], start=(c == 0), stop=(c == 7))
            q_ = qn[it % 3]
            kb.act(q_[:, 0:TW], pq[:, 0:TW], AF.Sigmoid)
            kb.dma(self.SG[ch * 128:(ch + 1) * 128, c0:c0 + TW], q_[:, 0:TW], q="pool")
            it += 1
        kb.dma(self.KT[:, 64, c0:c0 + TW], onesb[:, 0:TW], q="pool")
        for jj in range(n):
            tile = t0 + jj
            tc_ = slice(jj * 128, (jj + 1) * 128)
            for half in range(2):
                for c in range(8):
                    kb.mm(PSv[:, half * 512:(half + 1) * 512], A[:, c, tc_], Win[:, c, 2048 + half * 512:2048 + (half + 1) * 512],
                          start=(c == 0), stop=(c == 7))
            V = V0 if tile == 0 else Va[tile % 2]
            kb.copy("act", V[:, :, 0:64], PSv.ap.re("p (h d) -> p h d", h=16))
            kb.dma(self.VA[tile * 128:(tile + 1) * 128, :, :], V.ap, q="pool")
            for c in range(8):
                kb.mm(PSm[:, 0:16], A[:, c, tc_], Win[:, c, 4096:4112], start=(c == 0), stop=(c == 7))
            x = xs[tile % 2]
            kb.tt("dve", x[:, 0:16], PSm[:, 0:16], bfg.ap, ALU.add)
            kb.act(x[:, 16:32], x[:, 0:16], AF.Exp, scale=-1.0)
            kb.act(x[:, 32:48], x[:, 16:32], AF.Ln, bias=1.0)
            kb.mm(PSm[:, 16:32], tri.ap, x[:, 32:48])
            kb.mm(PSm[:, 32:48], onesf.ap, x[:, 32:48])
            kb.tt("dve", Ctok[:, tile, :], PSm[:, 16:32], CAR[:, tile, :], ALU.add)
            kb.tt("dve", CAR[:, tile + 1, :], PSm[:, 32:48], CAR[:, tile, :], ALU.add)
        R_ = rT[bi % 2]
        for jj in range(n):
            tile = t0 + jj
            r2 = rb[tile % 2]
            kb.tt("dve", r2.ap, CAR[:, t0 + n, :], Ctok[:, tile, :], ALU.subtract)
            kb.tr(PSr[:, jj * 128:(jj + 1) * 128], r2.ap, self.idb.ap)
        kb.copy("dve", R_[:, 0:TW], PSr[:, 0:TW])
        kb.dma(self.QT[:, 64, c0:c0 + TW], R_[:, 0:TW], q="pool")
    kb.dma(self.CT, Ctok.ap, q="pool")
    kb.dma(self.CAR, CAR.ap, q="pool")
    kb.end_phase()


def phase_A2(self, l):
    kb = self.kb
    NT = self.NT
    LP = self.LP
    kb.phase()
    Ctok = kb.sb("Ctok", [128, NT, 16], F32)
    CAR = kb.sb("CAR", [128, NT + 1, 16], F32)
    kb.dma(Ctok.ap, self.CT)
    kb.dma(CAR.ap, self.CAR)
    masks = []
    for jj in range(4):
        m = kb.sb("mask", [128, 512], BF16)
        kb.memset("pool", m.ap, 0.0)
        kb.asel(m.ap, m.ap, [[1, 512]], ALU.is_ge, NEGBIG, -128 * jj, -1)
        masks.append(m)
    onesf = kb.sb("onesf", [128, 64], F32)
    kb.memset("pool", onesf.ap, 1.0)
    KTh = [kb.sb("KTh", [128, LP], BF16) for _ in range(2)]
    VAh = [kb.sb("VAh", [128, NT, 65], BF16) for _ in range(2)]
    QTb = [kb.sb("QTb", [128, 512], BF16) for _ in range(2)]
    for b_ in KTh + QTb:
        kb.memset("pool", b_[64:128, :], 0.0)
    SGb = [kb.sb("SGb", [64, 512], BF16) for _ in range(2)]
    bias = [kb.sb("bias", [128, NT], F32) for _ in range(2)]
    P = [kb.sb("P", [128, 512], BF16) for _ in range(4)]
    oacc = [kb.sb("oacc", [65, 512], F32) for _ in range(2)]
    rden = [kb.sb("rden", [128, 512], F32) for _ in range(2)]
    for b_ in rden:
        kb.memset("pool", b_.ap, 0.0)
    sel = kb.sb("sel", [128, 64], F32)
    kb.memset("pool", sel.ap, 0.0)
    kb.memset("pool", sel[64:96, :], 1.0)
    kb.asel(sel[64:96, :], sel[64:96, :], [[0, 64]], ALU.is_ge, 0.0, 0, -1)
    tmp = [kb.sb("tmp", [64, 512], F32) for _ in range(2)]
    og = [kb.sb("og", [64, 512], BF16) for _ in range(2)]
    PSs = [kb.ps("pss", [128, 512], F32) for _ in range(4)]
    PSo = [kb.ps("pso", [65, 512], F32) for _ in range(2)]
    PSb = [kb.ps("psb", [64, 512], F32) for _ in range(2)]
    VAv = self.VA.rearrange("(t p) h e -> p t h e", p=128)
    blocks = self.blocks()
    it = 0
    ip = 0
    def load_head(hh):
        kb.dma(KTh[hh % 2][0:65, :], self.KT[hh])
        for a in range(0, NT, 16):
            b = min(NT, a + 16)
            kb.dma(VAh[hh % 2][:, a:b, :], VAv[:, a:b, hh, :])
    load_head(0)
    for h in range(16):
        if h + 1 < 16:
            load_head(h + 1)
        K_ = KTh[h % 2]
        V_ = VAh[h % 2]
        for bi, (t0, n) in enumerate(blocks):
            TW = n * 128
            c0 = t0 * 128
            Q_ = QTb[it % 2]
            S_ = SGb[it % 2]
            kb.dma(Q_[0:65, 0:TW], self.QT[h, :, c0:c0 + TW])
            kb.dma(S_[:, 0:TW], self.SG[h * 64:(h + 1) * 64, c0:c0 + TW])
            nkt = t0 + n
            b_ = bias[it % 2]
            kb.ts("dve", b_[:, 0:nkt], Ctok[:, 0:nkt, h], CAR[:, t0 + n, h:h + 1], ALU.subtract)
            po = PSo[it % 2]
            def emit_s(kt, ipk):
                ps = PSs[ipk % 4]
                diag = kt >= t0
                kb.mm(ps[:, 0:TW], K_[:, kt * 128:(kt + 1) * 128], Q_[:, 0:TW], start=True, stop=not diag)
                if diag:
                    kb.mm(ps[:, 0:TW], self.idb.ap, masks[kt - t0][:, 0:TW], start=False, stop=True)
                p_ = P[ipk % 4]
                kb.act(p_[:, 0:TW], ps[:, 0:TW], AF.Exp, bias=b_[:, kt:kt + 1])
            emit_s(0, ip)
            if nkt > 1:
                emit_s(1, ip + 1)
            for kt in range(nkt):
                if kt + 2 < nkt:
                    emit_s(kt + 2, ip + 2)
                kb.mm(po[:, 0:TW], V_[:, kt, :], P[ip % 4][:, 0:TW], start=(kt == 0), stop=(kt == nkt - 1))
                ip += 1
            oa = oacc[it % 2]
            rd = rden[it % 2]
            kb.copy("act", oa[:, 0:TW], po[:, 0:TW])
            kb.ts("dve", rd[64:65, 0:TW], oa[64:65, 0:TW], 1e-30, ALU.max)
            kb.recip(rd[64:65, 0:TW], rd[64:65, 0:TW])
            pb = PSb[it % 2]
            kb.mm(pb[:, 0:TW], sel.ap, rd[:, 0:TW])
            t_ = tmp[it % 2]
            kb.tt("dve", t_[:, 0:TW], oa[0:64, 0:TW], pb[:, 0:TW], ALU.mult)
            o_ = og[it % 2]
            kb.tt("pool", o_[:, 0:TW], t_[:, 0:TW], S_[:, 0:TW], ALU.mult)
            kb.dma(self.OGT[h * 64:(h + 1) * 64, c0:c0 + TW], o_[:, 0:TW], q="pool")
            it += 1
    kb.end_phase()


Prog.phase_A1 = phase_A1
Prog.phase_A2 = phase_A2


def attn_layer(self, l):
    self.phase_A1(l)
    self.phase_A2(l)
    self.phase_R(self.OGT, 8, self.inp["attn_w_out"][l // 2], self.inp["norm_mix_post"][l], self.H)


Prog.attn_layer = attn_layer


def phase_D1(self, l):
    kb = self.kb
    j = l // 2
    NT = self.NT
    kb.phase()
    Win = kb.sb("Win", [128, 8, 4112], BF16)
    stages = [kb.sb("wst", [128, 2048], F32) for _ in range(2)]
    cst = kb.sb("cst", [128, 128], F32)
    gcol = kb.sb("gcol", [128, 8], F32)
    cw = kb.sb("cw", [128, 4, 24], F32)
    pst = kb.ps("pst", [128, 1024], BF16)
    PSq = [kb.ps("psq", [128, 512], F32) for _ in range(2)]
    PSs = kb.ps("pss", [128, 512], F32)
    PSv = kb.ps("psv", [128, 1024], F32)
    PSm = kb.ps("psm", [128, 512], F32)
    self.load_cols(gcol.ap, self.inp["norm_mix_pre"][l].rearrange("(c p) -> c p", p=128), 8, cst, PSs)
    for k in range(4):
        self.load_cols(cw[:, k, :], self.inp["dn_conv"][j, k].rearrange("(c p) -> c p", p=128), 24, cst, PSs)
    self.load_w(Win, self.inp["dn_w_in"][j], 8, 4112, gcol, stages)
    dtb = kb.sb("dtb", [128, 8], F32)
    nA = kb.sb("nA", [128, 8], F32)
    kb.dma(dtb.ap, self.inp["dn_dt_bias"][j].partition_broadcast(128))
    kb.dma(nA.ap, self.inp["dn_a_log"][j].partition_broadcast(128))
    kb.act(nA.ap, nA.ap, AF.Exp)
    kb.ts("dve", nA.ap, nA.ap, -1.0, ALU.mult)
    tri = kb.sb("tri", [128, 128], F32)
    kb.memset("pool", tri.ap, 1.0)
    kb.asel(tri.ap, tri.ap, [[1, 128]], ALU.is_ge, 0.0, 0, -1)
    onesb = kb.sb("onesb", [128, 128], BF16)
    kb.memset("pool", onesb.ap, 1.0)
    halo = kb.sb("halo", [128, 24, 3], F32)
    kb.memset("pool", halo.ap, 0.0)
    GB = kb.sb("GB", [128, NT, 16], F32)
    hts = [kb.sb("ht", [128, D], F32) for _ in range(2)]
    abf = [kb.sb("abf", [128, D], BF16) for _ in range(2)]
    ssb = [kb.sb("ss", [128, 8], F32) for _ in range(2)]
    junk = kb.sb("junk", [128, D], BF16)
    aT = [kb.sb("aT", [128, 8, 512], BF16) for _ in range(2)]
    U = [kb.sb("U", [128, 515], F32) for _ in range(2)]
    yv = [kb.sb("yv", [128, 512], F32) for _ in range(2)]
    z = [kb.sb("z", [128, 512], F32) for _ in range(3)]
    sq = [kb.sb("sq", [128, 512], BF16) for _ in range(2)]
    rms = [kb.sb("rms", [128, 512], F32) for _ in range(2)]
    sg = [kb.sb("sg", [128, D], BF16) for _ in range(2)]
    xs = [kb.sb("xs", [128, 32], F32) for _ in range(2)]
    ti = 0
    it = 0
    for bi, (t0, n) in enumerate(self.blocks()):
        TW = n * 128
        c0 = t0 * 128
        A = aT[bi % 2]
        for jj in range(n):
            tile = t0 + jj
            ht = hts[ti % 2]
            kb.dma(ht.ap, self.H[tile * 128:(tile + 1) * 128, :])
            self.prenormT(ht, A[:, :, jj * 128:(jj + 1) * 128], abf[ti % 2], ssb[ti % 2], pst, junk, ti)
            ti += 1
        def stageA(fc, it_):
            pq = PSq[it_ % 2]
            for c in range(8):
                kb.mm(pq[:, 0:TW], Win[:, c, fc * 128:(fc + 1) * 128], A[:, c, 0:TW], start=(c == 0), stop=(c == 7))
            u = U[it_ % 2]
            kb.copy("pool", u[:, 0:3], halo[:, fc, :])
            kb.copy("act", u[:, 3:3 + TW], pq[:, 0:TW])
            kb.copy("pool", halo[:, fc, :], u[:, TW:TW + 3])
            y = yv[it_ % 2]
            kb.ts("dve", y[:, 0:TW], u[:, 3:3 + TW], cw[:, 3, fc:fc + 1], ALU.mult)
            for k in (2, 1, 0):
                kb.stt("dve", y[:, 0:TW], u[:, k:k + TW], cw[:, k, fc:fc + 1], y[:, 0:TW], ALU.mult, ALU.add)

        def stageB(fc, it_):
            y = yv[it_ % 2]
            z_ = z[it_ % 3]
            kb.act(z_[:, 0:TW], y[:, 0:TW], AF.Silu)
            if fc < 16:
                s_ = sq[it_ % 2]
                kb.act(s_[:, 0:TW], z_[:, 0:TW], AF.Square)
                kb.mm(PSs[:, 0:TW], onesb.ap, s_[:, 0:TW])
                r_ = rms[it_ % 2]
                kb.act(r_[:, 0:TW], PSs[:, 0:TW], AF.Sqrt, bias=EPS)
                kb.recip(r_[:, 0:TW], r_[:, 0:TW])
                if fc < 8:
                    kb.stt("dve", z_[:, 0:TW], z_[:, 0:TW], float(HD_D ** -0.5), r_[:, 0:TW], ALU.mult, ALU.mult)
                else:
                    kb.tt("dve", z_[:, 0:TW], z_[:, 0:TW], r_[:, 0:TW], ALU.mult)
            kb.dma(self.QKVD[fc, :, c0:c0 + TW], z_[:, 0:TW], q="pool")

        stageA(0, it)
        for fc in range(24):
            if fc + 1 < 24:
                stageA(fc + 1, it + 1)
            stageB(fc, it)
            it += 1
        for jj in range(n):
            tile = t0 + jj
            tc_ = slice(jj * 128, (jj + 1) * 128)
            for half in range(2):
                for c in range(8):
                    kb.mm(PSv[:, half * 512:(half + 1) * 512], A[:, c, tc_], Win[:, c, 3072 + half * 512:3072 + (half + 1) * 512],
                          start=(c == 0), stop=(c == 7))
            s2 = sg[tile % 2]
            kb.act(s2.ap, PSv.ap, AF.Silu)
            kb.dma(self.SGD[tile * 128:(tile + 1) * 128, :], s2.ap, q="pool")
            for c in range(8):
                kb.mm(PSm[:, 0:16], A[:, c, tc_], Win[:, c, 4096:4112], start=(c == 0), stop=(c == 7))
            x = xs[tile % 2]
            kb.act(GB[:, tile, 0:8], PSm[:, 0:8], AF.Sigmoid)
            kb.tt("dve", x[:, 0:8], PSm[:, 8:16], dtb.ap, ALU.add)
            kb.act(x[:, 8:16], x[:, 0:8], AF.Exp)
            kb.act(x[:, 16:24], x[:, 8:16], AF.Ln, bias=1.0)
            kb.tt("dve", x[:, 24:32], x[:, 16:24], nA.ap, ALU.mult)
            kb.mm(PSm[:, 16:24], tri.ap, x[:, 24:32])
            kb.copy("dve", GB[:, tile, 8:16], PSm[:, 16:24])
    kb.dma(self.GBD, GB.ap, q="pool")
    kb.end_phase()


def phase_D2(self, l, G=8):
    kb = self.kb
    j = l // 2
    NT = self.NT
    kb.phase()
    GB = kb.sb("GB", [128, NT, 16], F32)
    kb.dma(GB.ap, self.GBD)
    onrm = kb.sb("onrm", [128, 128], F32)
    kb.dma(onrm.ap, self.inp["dn_o_norm"][j].partition_broadcast(128))
    onesf = kb.sb("onesf", [128, 128], F32)
    kb.memset("pool", onesf.ap, 1.0)
    idf = self.idf
    idb = self.idb
    NMsT = kb.sb("NMsT", [128, 128], F32)
    kb.memset("pool", NMsT.ap, 0.0)
    kb.asel(NMsT.ap, NMsT.ap, [[1, 128]], ALU.is_gt, NEGBIG, 0, -1)
    NPs = kb.sb("NPs", [128, 128], F32)
    kb.memset("pool", NPs.ap, 0.0)
    kb.asel(NPs.ap, NPs.ap, [[-1, 128]], ALU.is_gt, -NEGBIG, 0, 1)
    BDm = kb.sb("BDm", [128, 128], F32)
    kb.memset("pool", BDm.ap, 1.0)
    kb.asel(BDm[:, 0:64], BDm[:, 0:64], [[0, 64]], ALU.is_ge, 0.0, 63, -1)
    kb.asel(BDm[:, 64:128], BDm[:, 64:128], [[0, 64]], ALU.is_ge, 0.0, -64, 1)
    ST = [kb.sb("ST", [128, 128], F32) for _ in range(8)]
    STb = [kb.sb("STb", [128, 128], BF16) for _ in range(8)]
    for s_ in ST + STb:
        kb.memset("dve", s_.ap, 0.0)
    X24 = [kb.sb("X24", [128, 24, 128], F32) for _ in range(2)]
    X24b = [kb.sb("X24b", [128, 24, 128], BF16) for _ in range(2)]
    SGt = [kb.sb("SGt", [128, D], BF16) for _ in range(2)]
    Og = [kb.sb("Og", [128, D], BF16) for _ in range(2)]
    OgT = [kb.sb("OgT", [128, 8, 128], BF16) for _ in range(2)]
    tl = [kb.sb("tl", [128, 32], F32) for _ in range(2)]
    sets = []
    f32names = ["DG", "Gs", "DTs", "DTi", "Dn", "A", "B", "Ad", "Ao", "Bd", "Bo", "P", "Pt",
                "U0s", "EGbc", "tmp", "on"]
    bf16names = ["R", "QKT", "kbg", "kd", "vb", "WT", "Us", "qdT",
                 "Adh", "Adl", "Bdh", "Bdl", "Xah", "Xal", "Xbh", "Xbl", "Xtah", "Xtal", "Xtbh", "Xtbl",
                 "Pth", "Ptl", "Ph", "Pl", "Aoh", "Aol", "Zh", "Zl"]
    for g in range(G):
        d = {}
        for nm in f32names:
            w = 256 if nm in ("DG", "Gs") else 128
            d[nm] = kb.sb(nm, [128, w], F32)
        for nm in bf16names:
            d[nm] = kb.sb(nm, [128, 128], BF16)
        d["cols"] = kb.sb("cols", [128, 8], F32)
        bA = kb.ps("bA", [128, 512], F32)
        d["bA"] = bA
        d["PSG"], d["PSK"], d["PSQ"] = kb.carve(bA, [(0, 256), (256, 384), (384, 512)])
        d["C0"], d["C1"], d["C2"], d["C3"] = kb.carve(bA, [(0, 128), (128, 256), (256, 384), (384, 512)])
        d["D0"], d["D1"], d["D2"], d["D3"] = d["C0"], d["C1"], d["C2"], d["C3"]
        d["C2b"] = View(d["C2"], d["C2"].t.bitcast(BF16)[:, 0:128])
        d["C3b"] = View(d["C3"], d["C3"].t.bitcast(BF16)[:, 0:128])
        sets.append(d)
    Xv = self.QKVD.rearrange("f d t -> d f t")
    OGTv = self.OGT.rearrange("(c p) t -> p c t", p=128)

    def head(n, hd, d, X, Xb, sgt, og, t_):
        qT = X[:, hd, :]
        qTb = Xb[:, hd, :]
        kTb = Xb[:, 8 + hd, :]
        vTb = Xb[:, 16 + hd, :]
        beta = GB[:, n, hd:hd + 1]
        gc = GB[:, n, 8 + hd:9 + hd]
        ngc = t_[:, hd:hd + 1]
        bg = t_[:, 16 + hd:17 + hd]
        cols = d["cols"]
        kb.ts("dve", d["DG"][:, 0:128], idf.ap, gc, ALU.mult)
        kb.ts("dve", d["DG"][:, 128:256], idf.ap, beta, ALU.mult)
        kb.mm(d["PSG"].ap, onesf.ap, d["DG"].ap)
        kb.copy("act", d["Gs"].ap, d["PSG"].ap)
        kb.mm(d["PSK"].ap, kTb, kTb)
        kb.mm(d["PSQ"].ap, kTb, qTb)
        yield
        G_ = d["Gs"][:, 0:128]
        kb.tt("dve", d["tmp"].ap, G_, NMsT.ap, ALU.add)
        kb.act(d["DTs"].ap, d["tmp"].ap, AF.Exp, bias=ngc)
        kb.tt("pool", d["DTi"].ap, d["DTs"].ap, idf.ap, ALU.add)
        kb.tt("dve", d["Dn"].ap, G_, NPs.ap, ALU.add)
        kb.act(d["Dn"].ap, d["Dn"].ap, AF.Exp, bias=gc, scale=-1.0)
        kb.act(d["EGbc"].ap, G_, AF.Exp)
        kb.act(cols[:, 0:1], gc, AF.Exp, bias=d["Gs"][:, 127:128], scale=-1.0)
        kb.act(cols[:, 1:2], d["Gs"][:, 127:128], AF.Exp)
        yield
        kb.stt("dve", d["A"].ap, d["PSK"].ap, beta, d["Dn"].ap, ALU.mult, ALU.mult)
        kb.tt("dve", d["B"].ap, d["PSK"].ap, d["DTs"].ap, ALU.mult)
        kb.tt("pool", d["B"].ap, d["B"].ap, d["Gs"][:, 128:256], ALU.mult)
        kb.tt("dve", d["QKT"].ap, d["PSQ"].ap, d["DTi"].ap, ALU.mult)
        kb.tt("pool", d["Ad"].ap, d["A"].ap, BDm.ap, ALU.mult)
        kb.tt("pool", d["Ao"].ap, d["A"].ap, d["Ad"].ap, ALU.subtract)
        kb.tt("pool", d["Bd"].ap, d["B"].ap, BDm.ap, ALU.mult)
        kb.tt("pool", d["Bo"].ap, d["B"].ap, d["Bd"].ap, ALU.subtract)
        kb.tt("dve", d["P"].ap, idf.ap, d["Ad"].ap, ALU.subtract)
        kb.tt("dve", d["Pt"].ap, idf.ap, d["Bd"].ap, ALU.subtract)
        yield
        def split(src, hi, lo, e="dve"):
            kb.copy("act", d[hi].ap, src)
            kb.tt(e, d[lo].ap, src, d[hi].ap, ALU.subtract)

        def mm3(out, A_, B_):
            kb.mm(out, d[A_[0]].ap, d[B_[0]].ap, start=True, stop=False)
            kb.mm(out, d[A_[0]].ap, d[B_[1]].ap, start=False, stop=False)
            kb.mm(out, d[A_[1]].ap, d[B_[0]].ap, start=False, stop=True)
        split(d["Ad"].ap, "Adh", "Adl", "pool")
        split(d["Bd"].ap, "Bdh", "Bdl", "pool")
        split(d["Pt"].ap, "Pth", "Ptl")
        split(d["Ao"].ap, "Aoh", "Aol", "pool")
        yield
        Xc, Xtc = ("Adh", "Adl"), ("Bdh", "Bdl")
        PT = ("Pth", "Ptl")
        for k in range(1, 6):
            Xn = ("Xah", "Xal") if k % 2 else ("Xbh", "Xbl")
            Xtn = ("Xtah", "Xtal") if k % 2 else ("Xtbh", "Xtbl")
            mm3(d["C0"].ap, Xtc, Xc)
            if k < 5:
                mm3(d["C1"].ap, Xc, Xtc)
            yield
            split(d["C0"].ap, Xn[0], Xn[1])
            if k < 5:
                split(d["C1"].ap, Xtn[0], Xtn[1])
            yield
            mm3(d["C2"].ap, PT, Xn)
            mm3(d["C3"].ap, Xn, PT)
            yield
            kb.tt("dve", d["P"].ap, d["P"].ap, d["C2"].ap, ALU.add)
            kb.tt("dve", d["Pt"].ap, d["Pt"].ap, d["C3"].ap, ALU.add)
            split(d["Pt"].ap, "Pth", "Ptl")
            Xc, Xtc = Xn, Xtn
            yield
        split(d["P"].ap, "Ph", "Pl", "pool")
        mm3(d["C0"].ap, ("Aoh", "Aol"), PT)
        yield
        split(d["C0"].ap, "Zh", "Zl")
        yield
        mm3(d["C1"].ap, ("Ph", "Pl"), ("Zh", "Zl"))
        kb.tt("dve", d["R"].ap, d["Pt"].ap, d["C1"].ap, ALU.subtract)
        kb.tr(d["C2b"], kTb, idb.ap)
        kb.tr(d["C3b"], vTb, idb.ap)
        kb.ts("dve", d["kbg"].ap, d["C2b"], bg, ALU.mult)
        kb.ts("dve", d["kd"].ap, d["C2b"], cols[:, 0:1], ALU.mult)
        kb.ts("dve", d["vb"].ap, d["C3b"], beta, ALU.mult)
        kb.tt("pool", d["qdT"].ap, qT, d["EGbc"].ap, ALU.mult)
        yield
        kb.mm(d["D0"].ap, d["kbg"].ap, d["R"].ap)
        kb.mm(d["D1"].ap, d["R"].ap, d["vb"].ap)
        kb.copy("act", d["WT"].ap, d["D0"].ap)
        kb.copy("act", d["U0s"].ap, d["D1"].ap)
        yield
        S_ = ST[hd]
        Sb_ = STb[hd]
        kb.mm(d["D2"].ap, d["WT"].ap, Sb_.ap)
        kb.tt("dve", d["Us"].ap, d["U0s"].ap, d["D2"].ap, ALU.subtract)
        kb.mm(d["D3"].ap, d["qdT"].ap, Sb_.ap, start=True, stop=False)
        kb.mm(d["D3"].ap, d["QKT"].ap, d["Us"].ap, start=False, stop=True)
        kb.memset("dve", cols[:, 2:3], 0.0)
        kb.act(d["tmp"].ap, d["D3"].ap, AF.Square, accum=cols[:, 2:3])
        kb.act(cols[:, 3:4], cols[:, 2:3], AF.Sqrt, bias=EPS, scale=1.0 / HD_D)
        kb.recip(cols[:, 4:5], cols[:, 3:4])
        kb.stt("dve", d["on"].ap, d["D3"].ap, cols[:, 4:5], onrm.ap, ALU.mult, ALU.mult)
        kb.mm(d["C0"].ap, d["kd"].ap, d["Us"].ap)
        kb.stt("dve", S_.ap, S_.ap, cols[:, 1:2], d["C0"].ap, ALU.mult, ALU.add)
        kb.copy("act", Sb_.ap, S_.ap)
        kb.tt("pool", og[:, hd * 128:(hd + 1) * 128], d["on"].ap, sgt[:, hd * 128:(hd + 1) * 128], ALU.mult)
        yield

    PSot = View(sets[G - 1]["bA"], sets[G - 1]["bA"].t[:].bitcast(BF16))
    HG = G // 2
    import os
    SKEW = int(os.environ.get("DBG_SKEW", "22"))

    def prologue(n):
        X = X24[n % 2]
        Xb = X24b[n % 2]
        kb.copy("act", Xb[:, 0:12, :], X[:, 0:12, :])
        kb.copy("dve", Xb[:, 12:24, :], X[:, 12:24, :])
        kb.dma(SGt[n % 2].ap, self.SGD[n * 128:(n + 1) * 128, :])
        t_ = tl[n % 2]
        kb.ts("dve", t_[:, 0:8], GB[:, n, 8:16], -1.0, ALU.mult)
        kb.act(t_[:, 8:16], GB[:, n, 8:16], AF.Exp)
        kb.tt("dve", t_[:, 16:24], t_[:, 8:16], GB[:, n, 0:8], ALU.mult)

    def epilogue(n):
        og = Og[n % 2]
        for c in range(8):
            kb.tr(PSot[:, c * 128:(c + 1) * 128], og[:, c * 128:(c + 1) * 128], idb.ap)
        ot = OgT[n % 2]
        kb.copy("act", ot.ap, PSot.re("p (c t) -> p c t", c=8))
        kb.dma(OGTv[:, :, n * 128:(n + 1) * 128], ot.ap, q="pool")
        if n + 2 < NT:
            kb.dma(X24[n % 2].ap, Xv[:, :, (n + 2) * 128:(n + 3) * 128])

    def item(n, grp):
        if grp == 0:
            prologue(n)
        gens = [head(n, grp * HG + g, sets[grp * HG + g], X24[n % 2], X24b[n % 2], SGt[n % 2], Og[n % 2], tl[n % 2])
                for g in range(HG)]
        alive = list(gens)
        while alive:
            nxt = []
            for g_ in alive:
                try:
                    next(g_)
                    nxt.append(g_)
                except StopIteration:
                    pass
            alive = nxt
            yield
        if grp == 1:
            epilogue(n)

    kb.dma(X24[0].ap, Xv[:, :, 0:128])
    if NT > 1:
        kb.dma(X24[1].ap, Xv[:, :, 128:256])
    items = [item(n, grp) for n in range(NT) for grp in range(2)]
    active = []
    nxt_item = 0
    while nxt_item < len(items) or active:
        if nxt_item < len(items) and (not active or (len(active) < 2 and active[-1][1] >= SKEW)):
            active.append([items[nxt_item], 0])
            nxt_item += 1
        still = []
        for ent in active:
            try:
                next(ent[0])
                ent[1] += 1
                still.append(ent)
            except StopIteration:
                pass
        active = still
    kb.end_phase()


Prog.phase_D1 = phase_D1
Prog.phase_D2 = phase_D2


def dn_layer(self, l):
    self.phase_D1(l)
    self.phase_D2(l)
    self.phase_R(self.OGT, 8, self.inp["dn_w_out"][l // 2], self.inp["norm_mix_post"][l], self.H)


Prog.dn_layer = dn_layer


_SEQ = 8192
_NT = (_SEQ + NMETA + NPAD) // 128


def build_full(NT):
    p = Prog(NT)
    p.consts()
    p.phase_init()
    for l in range(4):
        if l % 2 == 0:
            p.attn_layer(l)
        else:
            p.dn_layer(l)
        p.ffn(l, final=(l == 3))
    p.finish()
    return p


def kernel(**inputs):
    x = np.asarray(inputs["x"], dtype=np.float32)
    B = x.shape[0]
    meta = np.asarray(inputs["meta_tokens"], dtype=np.float32)
    p = build_full(_NT)
    shared = {k: np.ascontiguousarray(np.asarray(inputs[k], dtype=np.float32)) for k in p.inp if k != "xpad"}
    real = {0: 0, 1: 1, 4: 2, 5: 3}
    zshared = {k: np.zeros_like(v) for k, v in shared.items()}
    zx = np.zeros((NPAD + NMETA + x.shape[1], D), np.float32)
    in_maps = []
    for c in range(8):
        if c in real and real[c] < B:
            b = real[c]
            xpad = np.concatenate([np.zeros((NPAD, D), np.float32), meta, x[b]], axis=0)
            m = dict(shared)
            m["xpad"] = np.ascontiguousarray(xpad)
        else:
            m = dict(zshared)
            m["xpad"] = zx
        in_maps.append(m)
    res = run_bass_kernel_spmd(p.nc, in_maps, core_ids=list(range(8)))
    core_of = {b: c for c, b in real.items()}
    out = np.stack([np.asarray(res.results[core_of[b]]["y"], dtype=np.float32) for b in range(B)], axis=0)
    return out
```

```python
import numpy as np
from contextlib import ExitStack
import concourse.bass as bass
import concourse.mybir as mybir
from concourse.bass_utils import run_bass_kernel_spmd

F32 = mybir.dt.float32
BF16 = mybir.dt.bfloat16
AF = mybir.ActivationFunctionType
ALU = mybir.AluOpType
AX = mybir.AxisListType

D = 1024
NH_A = 16
HD_A = 64
NH_D = 8
HD_D = 128
FFN = 2816
EPS = 1e-6
NPAD = 112
NMETA = 16
NEGBIG = -30000.0


class Buf:
    def __init__(self, t, name, space):
        self.t = t
        self.name = name
        self.space = space
        self.w = {}
        self.r = {}
        self.sm = {}
        self.sem = None

    def __getitem__(self, k):
        return View(self, self.t[k])

    @property
    def ap(self):
        return View(self, self.t[:])


class View:
    def __init__(self, buf, ap):
        self.buf = buf
        self.ap = ap

    def __getitem__(self, k):
        return View(self.buf, self.ap[k])

    def re(self, s, **kw):
        return View(self.buf, self.ap.rearrange(s, **kw))

    def bc(self, shape):
        return View(self.buf, self.ap.to_broadcast(shape))


def _ap(x):
    return x.ap if isinstance(x, View) else x


class KB:
    def __init__(self):
        self.nc = bass.Bass("TRN2", target_bir_lowering=False)
        nc = self.nc
        self.es = ExitStack()
        self.eng = {"pe": nc.tensor, "dve": nc.vector, "act": nc.scalar, "pool": nc.gpsimd, "sp": nc.sync}
        self.cnt = {}
        self.esem = {}
        for e in ("pe", "dve", "act", "pool"):
            self.esem[e] = self.es.enter_context(nc.semaphore("cnt_" + e))
            self.cnt[e] = 0
        self.seen = {e: {} for e in self.eng}
        self.sem_pool = []
        self.live_sems = []
        self.phase_stack = None
        self.phase_bufs = []
        self.uid = 0
        self.dq = 0
        self.noself = False
        import os
        self.fast_self = tuple(x for x in os.environ.get("FAST_SELF", "dve").split(",") if x)

    def phase(self):
        self.phase_stack = ExitStack()
        self.phase_bufs = []

    def end_phase(self):
        self.barrier()
        for b in self.phase_bufs:
            if b.sem is not None:
                self.sem_pool.append(b.sem)
                b.sem = None
        self.phase_stack.close()
        self.phase_stack = None
        self.phase_bufs = []

    def carve(self, buf, slices):
        out = []
        for (a, b) in slices:
            sbuf = Buf(buf.t[:, a:b], buf.name + "_%d" % a, buf.space)
            sbuf.w = buf.w
            sbuf.r = buf.r
            sbuf.sm = buf.sm
            self.phase_bufs.append(sbuf)
            out.append(sbuf)
        return out

    def sb(self, name, shape, dt):
        self.uid += 1
        t = self.phase_stack.enter_context(self.nc.sbuf_tensor("%s_%d" % (name, self.uid), list(shape), dt))
        b = Buf(t, name, "sb")
        self.phase_bufs.append(b)
        return b

    def ps(self, name, shape, dt):
        self.uid += 1
        t = self.phase_stack.enter_context(self.nc.psum_tensor("%s_%d" % (name, self.uid), list(shape), dt))
        b = Buf(t, name, "ps")
        self.phase_bufs.append(b)
        return b

    def _getsem(self, b):
        if b.sem is None:
            if self.sem_pool:
                b.sem = self.sem_pool.pop()
            else:
                s = self.es.enter_context(self.nc.semaphore("dsem%d" % len(self.live_sems)))
                b.sem = [s, 0]
                self.live_sems.append(b.sem)
        return b.sem

    def _wait(self, e, deps):
        seen = self.seen[e]
        for sid, (sem, val) in deps.items():
            if seen.get(sid, 0) < val:
                self.eng[e].wait_ge(sem, val)
                seen[sid] = val

    def _deps(self, outs, ins):
        deps = {}

        def add(d):
            for sid, (sem, val) in d.items():
                if sid not in deps or deps[sid][1] < val:
                    deps[sid] = (sem, val)
        for v in ins:
            if isinstance(v, View):
                add(v.buf.w)
        for v in outs:
            if isinstance(v, View):
                add(v.buf.w)
                add(v.buf.r)
        return deps

    def _record(self, outs, ins, sem, val):
        sid = id(sem)
        for v in ins:
            if isinstance(v, View):
                v.buf.r[sid] = (sem, val)
        for v in outs:
            if isinstance(v, View):
                v.buf.w[sid] = (sem, val)

    def op(self, e, fn, outs, ins, pe_acc=False):
        deps = self._deps(outs, ins)
        small = True
        if e == "pe":
            deps.pop(id(self.esem[e]), None)
        elif e in self.fast_self:
            sh = _ap(outs[0]).shape
            n = 1
            for v_ in sh[1:]:
                n *= v_
            small = n < 128
            sid = id(self.esem[e])
            if not small and sid in deps:
                need = 0
                for v in list(ins) + list(outs):
                    if isinstance(v, View):
                        need = max(need, v.buf.sm.get(sid, 0))
                if need > 0:
                    deps[sid] = (self.esem[e], need)
                else:
                    deps.pop(sid)
        self._wait(e, deps)
        ins_ = fn()
        self.cnt[e] += 1
        ins_.then_inc(self.esem[e], 1)
        self._record(outs, ins, self.esem[e], self.cnt[e])
        if small and e in self.fast_self:
            sid = id(self.esem[e])
            for v in list(ins) + list(outs):
                if isinstance(v, View):
                    v.buf.sm[sid] = self.cnt[e]
        return ins_

    def dma(self, out, in_, q=None, **kw):
        if q is None:
            q = "sp"
        sbv = out if isinstance(out, View) else in_
        assert isinstance(sbv, View)
        outs = [out] if isinstance(out, View) else []
        ins = [in_] if isinstance(in_, View) else []
        deps = self._deps(outs, ins)
        self._wait(q, deps)
        se = self._getsem(sbv.buf)
        self.eng[q].dma_start(out=_ap(out), in_=_ap(in_), **kw).then_inc(se[0], 16)
        se[1] += 16
        self._record(outs, ins, se[0], se[1])

    def barrier(self):
        allsems = {}
        for e in ("pe", "dve", "act", "pool"):
            allsems[id(self.esem[e])] = (self.esem[e], self.cnt[e])
        for se in self.live_sems:
            allsems[id(se[0])] = (se[0], se[1])
        for e in self.eng:
            self._wait(e, {k: v for k, v in allsems.items() if v[1] > 0})

    def mm(self, out, lhsT, rhs, start=True, stop=True):
        return self.op("pe", lambda: self.nc.tensor.matmul(_ap(out), lhsT=_ap(lhsT), rhs=_ap(rhs), start=start, stop=stop),
                       [out], [lhsT, rhs])

    def tr(self, out, in_, ident):
        return self.op("pe", lambda: self.nc.tensor.transpose(_ap(out), _ap(in_), _ap(ident)), [out], [in_, ident])

    def act(self, out, in_, func, bias=None, scale=None, accum=None, e="act"):
        kw = {}
        ins = [in_]
        outs = [out]
        if bias is not None:
            kw["bias"] = _ap(bias)
            ins.append(bias)
        if scale is not None:
            kw["scale"] = _ap(scale)
            ins.append(scale)
        if accum is not None:
            kw["accum_out"] = _ap(accum)
            outs.append(accum)
        return self.op("act", lambda: self.nc.scalar.activation(out=_ap(out), in_=_ap(in_), func=func, **kw), outs, ins)

    def tt(self, e, out, a, b, op):
        return self.op(e, lambda: self.eng[e].tensor_tensor(out=_ap(out), in0=_ap(a), in1=_ap(b), op=op), [out], [a, b])

    def ts(self, e, out, a, s1, op0, s2=None, op1=None, accum=None):
        kw = {}
        outs = [out]
        if op1 is not None:
            kw["op1"] = op1
        if accum is not None:
            kw["accum_out"] = _ap(accum)
            outs.append(accum)
        return self.op(e, lambda: self.eng[e].tensor_scalar(out=_ap(out), in0=_ap(a), scalar1=_ap(s1), scalar2=_ap(s2), op0=op0, **kw),
                       outs, [a, s1, s2])

    def stt(self, e, out, a, s, b, op0, op1):
        return self.op(e, lambda: self.eng[e].scalar_tensor_tensor(out=_ap(out), in0=_ap(a), scalar=_ap(s), in1=_ap(b), op0=op0, op1=op1),
                       [out], [a, s, b])

    def copy(self, e, out, in_):
        if e == "act":
            return self.op("act", lambda: self.nc.scalar.copy(out=_ap(out), in_=_ap(in_)), [out], [in_])
        return self.op(e, lambda: self.eng[e].tensor_copy(out=_ap(out), in_=_ap(in_)), [out], [in_])

    def amul(self, out, in_, mul):
        return self.op("act", lambda: self.nc.scalar.mul(out=_ap(out), in_=_ap(in_), mul=_ap(mul)), [out], [in_, mul])

    def memset(self, e, out, val):
        return self.op(e, lambda: self.eng[e].memset(_ap(out), val), [out], [])

    def recip(self, out, in_):
        return self.op("dve", lambda: self.nc.vector.reciprocal(out=_ap(out), in_=_ap(in_)), [out], [in_])

    def asel(self, out, in_, pattern, cmp, fill, base, cm):
        return self.op("pool", lambda: self.nc.gpsimd.affine_select(out=_ap(out), in_=_ap(in_), pattern=pattern, compare_op=cmp,
                                                                      fill=fill, base=base, channel_multiplier=cm), [out], [in_])


class Prog:
    def __init__(self, NT, layers=(0, 1, 2, 3), final_out=True):
        self.NT = NT
        self.LP = NT * 128
        self.kb = KB()
        self.nc = self.kb.nc
        self.layers = layers
        nc = self.nc
        LP = self.LP
        S = LP - 128
        shapes = {"xpad": [LP, D], "norm_mix_pre": [4, D], "norm_mix_post": [4, D], "norm_ffn_pre": [4, D], "norm_ffn_post": [4, D],
                  "attn_w_in": [2, D, 4112], "attn_b_forget": [2, 16], "attn_q_norm": [2, 64], "attn_k_norm": [2, 64],
                  "attn_w_out": [2, D, D], "dn_w_in": [2, D, 4112], "dn_conv": [2, 4, 3072], "dn_a_log": [2, 8], "dn_dt_bias": [2, 8],
                  "dn_o_norm": [2, 128], "dn_w_out": [2, D, D], "ffn_w_up": [4, D, 2 * FFN], "ffn_conv": [4, 3, 2 * FFN],
                  "ffn_w_down": [4, FFN, D]}

        class Lazy(dict):
            def __missing__(d, name):
                d[name] = nc.dram_tensor(name, list(shapes[name]), F32, kind="ExternalInput").ap()
                return d[name]
        self.inp = Lazy()
        self.y = nc.dram_tensor("y", [S, D], F32, kind="ExternalOutput").ap()

        def dscr(name, shape, dt):
            return nc.dram_tensor(name, list(shape), dt, kind="Internal").ap()
        self.H = dscr("H", [LP, D], F32)
        self.HT = dscr("HT", [FFN, LP], BF16)
        self.OGT = dscr("OGT", [D, LP], BF16)
        self.SG = dscr("SG", [D, LP], BF16)
        self.QT = dscr("QT", [16, 65, LP], BF16)
        self.KT = dscr("KT", [16, 65, LP], BF16)
        self.VA = dscr("VA", [LP, 16, 65], BF16)
        self.CT = dscr("CT", [128, NT, 16], F32)
        self.QKVD = dscr("QKVD", [24, 128, LP], F32)
        self.SGD = dscr("SGD", [LP, D], BF16)
        self.GBD = dscr("GBD", [128, NT, 16], F32)
        self.CAR = dscr("CAR", [128, NT + 1, 16], F32)

    def consts(self):
        kb = self.kb
        kb.phase()
        es = kb.phase_stack
        self.idf = kb.sb("idf", [128, 128], F32)
        self.idb = kb.sb("idb", [128, 128], BF16)
        kb.memset("pool", self.idf.ap, 1.0)
        kb.asel(self.idf.ap, self.idf.ap, [[-1, 128]], ALU.is_equal, 0.0, 0, 1)
        kb.copy("dve", self.idb.ap, self.idf.ap)
        self.const_stack = es
        kb.phase_stack = None
        kb.phase_bufs = []

    def blocks(self):
        out = []
        t = 0
        while t < self.NT:
            n = min(4, self.NT - t)
            out.append((t, n))
            t += n
        return out

    def load_cols(self, dst, src2d, n, stage, pst):
        kb = self.kb
        kb.dma(stage[0:n, :], src2d)
        kb.tr(pst[:, 0:n], stage[0:n, :], self.idf[0:n, 0:n])
        kb.copy("dve", dst, pst[:, 0:n])

    def load_w(self, dst, W, nch, ncols, gcol, stages, col0=0):
        kb = self.kb
        SEG = 2048
        k = 0
        for c in range(nch):
            for s0 in range(0, ncols, SEG):
                w = min(SEG, ncols - s0)
                st = stages[k % len(stages)]
                e = ("dve", "act")[k % 2]
                k += 1
                kb.dma(st[:, 0:w], W[c * 128:(c + 1) * 128, s0:s0 + w])
                if gcol is not None:
                    if e == "act":
                        kb.amul(dst[:, c, col0 + s0:col0 + s0 + w], st[:, 0:w], gcol[:, c:c + 1])
                    else:
                        kb.ts(e, dst[:, c, col0 + s0:col0 + s0 + w], st[:, 0:w], gcol[:, c:c + 1], ALU.mult)
                else:
                    kb.copy(e, dst[:, c, col0 + s0:col0 + s0 + w], st[:, 0:w])

    def prenormT(self, ht, aT_dst, abf, ss, pst, junk, k):
        kb = self.kb
        kb.memset("dve", ss[:, 0:1], 0.0)
        kb.act(junk.ap, ht.ap, AF.Square, accum=ss[:, 0:1])
        kb.act(ss[:, 1:2], ss[:, 0:1], AF.Sqrt, bias=EPS, scale=1.0 / D)
        kb.recip(ss[:, 2:3], ss[:, 1:2])
        kb.ts("dve", abf.ap, ht.ap, ss[:, 2:3], ALU.mult)
        for c in range(8):
            kb.tr(pst[:, c * 128:(c + 1) * 128], abf[:, c * 128:(c + 1) * 128], self.idb.ap)
        kb.copy(("act", "dve")[k % 2], aT_dst, pst.ap.re("p (c t) -> p c t", c=8))

    def phase_R(self, srcT, nch, W, gpost, hsrc, final=False):
        kb = self.kb
        kb.phase()
        Wb = kb.sb("Wb", [128, nch, D], BF16)
        stages = [kb.sb("wst", [128, 2048], F32) for _ in range(2)]
        self.load_w(Wb, W, nch, D, None, stages)
        g = kb.sb("g", [128, D], F32)
        kb.dma(g.ap, gpost.partition_broadcast(128))
        Sb = [kb.sb("Sb", [128, nch, 512], BF16) for _ in range(2)]
        hts = [kb.sb("ht", [128, D], F32) for _ in range(3)]
        fn = [kb.sb("fn", [128, D], F32) for _ in range(2)]
        ssb = [kb.sb("ss", [128, 8], F32) for _ in range(2)]
        junk = kb.sb("junk", [128, 512], BF16)
        PS = [[kb.ps("psa", [128, 512], F32), kb.ps("psb", [128, 512], F32)] for _ in range(2)]
        srcv = srcT.rearrange("(c p) t -> p c t", p=128)
        blocks = self.blocks()
        kb.dma(Sb[0][:, :, 0:blocks[0][1] * 128], srcv[:, :, 0:blocks[0][1] * 128])
        ti = 0
        for bi, (t0, n) in enumerate(blocks):
            if bi + 1 < len(blocks):
                t1, n1 = blocks[bi + 1]
                kb.dma(Sb[(bi + 1) % 2][:, :, 0:n1 * 128], srcv[:, :, t1 * 128:(t1 + n1) * 128])
            S = Sb[bi % 2]
            for j in range(n):
                tile = t0 + j
                ht = hts[ti % 3]
                kb.dma(ht.ap, hsrc[tile * 128:(tile + 1) * 128, :])
                pa, pb = PS[ti % 2]
                ss = ssb[ti % 2]
                f = fn[ti % 2]
                for c in range(nch):
                    kb.mm(pa.ap, S[:, c, j * 128:(j + 1) * 128], Wb[:, c, 0:512], start=(c == 0), stop=(c == nch - 1))
                for c in range(nch):
                    kb.mm(pb.ap, S[:, c, j * 128:(j + 1) * 128], Wb[:, c, 512:1024], start=(c == 0), stop=(c == nch - 1))
                kb.memset("dve", ss[:, 0:2], 0.0)
                kb.act(junk.ap, pa.ap, AF.Square, accum=ss[:, 0:1])
                kb.act(junk.ap, pb.ap, AF.Square, accum=ss[:, 1:2])
                kb.tt("dve", ss[:, 2:3], ss[:, 0:1], ss[:, 1:2], ALU.add)
                kb.act(ss[:, 3:4], ss[:, 2:3], AF.Sqrt, bias=EPS, scale=1.0 / D)
                kb.recip(ss[:, 4:5], ss[:, 3:4])
                kb.stt("dve", f[:, 0:512], pa.ap, ss[:, 4:5], g[:, 0:512], ALU.mult, ALU.mult)
                kb.stt("dve", f[:, 512:1024], pb.ap, ss[:, 4:5], g[:, 512:1024], ALU.mult, ALU.mult)
                kb.tt("dve", f.ap, f.ap, ht.ap, ALU.add)
                if final:
                    if tile >= 1:
                        kb.dma(self.y[(tile - 1) * 128:tile * 128, :], f.ap, q="pool")
                else:
                    kb.dma(self.H[tile * 128:(tile + 1) * 128, :], f.ap, q="pool")
                ti += 1
        kb.end_phase()

    def phase_F1(self, l):
        kb = self.kb
        kb.phase()
        Wup = kb.sb("Wup", [128, 8, 2 * FFN], BF16)
        stages = [kb.sb("wst", [128, 2048], F32) for _ in range(2)]
        cst = kb.sb("cst", [128, 128], F32)
        gcol = kb.sb("gcol", [128, 8], F32)
        cw = kb.sb("cw", [128, 3, 44], F32)
        pst = kb.ps("pst", [128, 1024], BF16)
        psc = kb.ps("psc", [128, 128], F32)
        self.load_cols(gcol.ap, self.inp["norm_ffn_pre"][l].rearrange("(c p) -> c p", p=128), 8, cst, psc)
        for k in range(3):
            self.load_cols(cw[:, k, :], self.inp["ffn_conv"][l, k].rearrange("(c p) -> c p", p=128), 44, cst, psc)
        self.load_w(Wup, self.inp["ffn_w_up"][l], 8, 2 * FFN, gcol, stages)
        halo = kb.sb("halo", [128, 44, 2], F32)
        kb.memset("pool", halo.ap, 0.0)
        hts = [kb.sb("ht", [128, D], F32) for _ in range(2)]
        abf = [kb.sb("abf", [128, D], BF16) for _ in range(2)]
        ssb = [kb.sb("ss", [128, 8], F32) for _ in range(2)]
        junk = kb.sb("junk", [128, D], BF16)
        aT = [kb.sb("aT", [128, 8, 512], BF16) for _ in range(2)]
        Hb = [kb.sb("Hb", [128, 22, 512], BF16) for _ in range(2)]
        Ug = [kb.sb("Ug", [128, 514], F32) for _ in range(2)]
        Uu = [kb.sb("Uu", [128, 514], F32) for _ in range(2)]
        yg = [kb.sb("yg", [128, 512], F32) for _ in range(2)]
        yu = [kb.sb("yu", [128, 512], F32) for _ in range(2)]
        gg = [kb.sb("gg", [128, 512], F32) for _ in range(2)]
        ptmp = [kb.sb("ptmp", [128, 512], F32) for _ in range(2)]
        PSg = [kb.ps("psg", [128, 512], F32) for _ in range(2)]
        PSu = [kb.ps("psu", [128, 512], F32) for _ in range(2)]
        HTv = self.HT.rearrange("(c p) t -> p c t", p=128)
        hsrc = self.H
        ti = 0
        it = 0
        for bi, (t0, n) in enumerate(self.blocks()):
            TW = n * 128
            A = aT[bi % 2]
            for j in range(n):
                tile = t0 + j
                ht = hts[ti % 2]
                kb.dma(ht.ap, hsrc[tile * 128:(tile + 1) * 128, :])
                self.prenormT(ht, A[:, :, j * 128:(j + 1) * 128], abf[ti % 2], ssb[ti % 2], pst, junk, ti)
                ti += 1
            HB = Hb[bi % 2]
            for fc in range(22):
                pg = PSg[it % 2]; pu = PSu[it % 2]
                ug = Ug[it % 2]; uu = Uu[it % 2]
                for c in range(8):
                    kb.mm(pg[:, 0:TW], Wup[:, c, fc * 128:(fc + 1) * 128], A[:, c, 0:TW], start=(c == 0), stop=(c == 7))
                for c in range(8):
                    kb.mm(pu[:, 0:TW], Wup[:, c, FFN + fc * 128:FFN + (fc + 1) * 128], A[:, c, 0:TW], start=(c == 0), stop=(c == 7))
                a = yg[it % 2]; b = yu[it % 2]
                for (u_, p_, y_, hc) in ((ug, pg, a, fc), (uu, pu, b, 22 + fc)):
                    kb.copy("act", u_[:, 2:2 + TW], p_[:, 0:TW])
                    kb.amul(y_[:, 0:TW], p_[:, 0:TW], cw[:, 2, hc:hc + 1])
                    kb.copy("pool", u_[:, 0:2], halo[:, hc, :])
                    kb.copy("pool", halo[:, hc, :], u_[:, TW:TW + 2])
                    kb.stt("dve", y_[:, 0:TW], u_[:, 1:1 + TW], cw[:, 1, hc:hc + 1], y_[:, 0:TW], ALU.mult, ALU.add)
                    kb.stt("dve", y_[:, 0:TW], u_[:, 0:TW], cw[:, 0, hc:hc + 1], y_[:, 0:TW], ALU.mult, ALU.add)
                kb.act(gg[it % 2][:, 0:TW], a[:, 0:TW], AF.Gelu_apprx_tanh)
                kb.tt("dve", HB[:, fc, 0:TW], gg[it % 2][:, 0:TW], b[:, 0:TW], ALU.mult)
                it += 1
            kb.dma(HTv[:, :, t0 * 128:t0 * 128 + TW], HB[:, :, 0:TW], q="pool")
        kb.end_phase()

    def phase_init(self):
        kb = self.kb
        kb.phase()
        hts = [kb.sb("ht", [128, D], F32) for _ in range(4)]
        for t in range(self.NT):
            ht = hts[t % 4]
            kb.dma(ht.ap, self.inp["xpad"][t * 128:(t + 1) * 128, :])
            kb.dma(self.H[t * 128:(t + 1) * 128, :], ht.ap, q="pool")
        kb.end_phase()

    def ffn(self, l, final):
        self.phase_F1(l)
        self.phase_R(self.HT, 22, self.inp["ffn_w_down"][l], self.inp["norm_ffn_post"][l], self.H, final=final)

    def finish(self):
        self.const_stack.close()
        self.kb.es.close()


def phase_A1(self, l):
    kb = self.kb
    j = l // 2
    NT = self.NT
    kb.phase()
    Win = kb.sb("Win", [128, 8, 4112], BF16)
    stages = [kb.sb("wst", [128, 2048], F32) for _ in range(2)]
    cst = kb.sb("cst", [128, 128], F32)
    gcol = kb.sb("gcol", [128, 8], F32)
    pst = kb.ps("pst", [128, 1024], BF16)
    PSq = [kb.ps("psq", [128, 512], F32) for _ in range(2)]
    PSs = kb.ps("pss", [128, 512], F32)
    PSv = kb.ps("psv", [128, 1024], F32)
    PSm = kb.ps("psm", [128, 512], F32)
    PSr = kb.ps("psr", [16, 512], BF16)
    self.load_cols(gcol.ap, self.inp["norm_mix_pre"][l].rearrange("(c p) -> c p", p=128), 8, cst, PSs)
    self.load_w(Win, self.inp["attn_w_in"][j], 8, 4112, gcol, stages)
    qg = kb.sb("qg", [128, 2], F32)
    for half in range(2):
        kb.dma(qg[half * 64:(half + 1) * 64, 0:1], self.inp["attn_q_norm"][j].rearrange("(p o) -> p o", o=1))
        kb.dma(qg[half * 64:(half + 1) * 64, 1:2], self.inp["attn_k_norm"][j].rearrange("(p o) -> p o", o=1))
    kb.ts("dve", qg[:, 0:1], qg[:, 0:1], 0.125, ALU.mult)
    bfg = kb.sb("bfg", [128, 16], F32)
    kb.dma(bfg.ap, self.inp["attn_b_forget"][j].partition_broadcast(128))
    BD = kb.sb("BD", [128, 128], BF16)
    kb.memset("pool", BD.ap, 1.0)
    kb.asel(BD[:, 0:64], BD[:, 0:64], [[0, 64]], ALU.is_ge, 0.0, 63, -1)
    kb.asel(BD[:, 64:128], BD[:, 64:128], [[0, 64]], ALU.is_ge, 0.0, -64, 1)
    tri = kb.sb("tri", [128, 128], F32)
    kb.memset("pool", tri.ap, 1.0)
    kb.asel(tri.ap, tri.ap, [[1, 128]], ALU.is_ge, 0.0, 0, -1)
    onesf = kb.sb("onesf", [128, 128], F32)
    kb.memset("pool", onesf.ap, 1.0)
    onesb = kb.sb("onesb", [16, 512], BF16)
    kb.memset("pool", onesb.ap, 1.0)
    Ctok = kb.sb("Ctok", [128, NT, 16], F32)
    CAR = kb.sb("CAR", [128, NT + 1, 16], F32)
    kb.memset("dve", CAR[:, 0, :], 0.0)
    hts = [kb.sb("ht", [128, D], F32) for _ in range(2)]
    abf = [kb.sb("abf", [128, D], BF16) for _ in range(2)]
    ssb = [kb.sb("ss", [128, 8], F32) for _ in range(2)]
    junk = kb.sb("junk", [128, D], BF16)
    aT = [kb.sb("aT", [128, 8, 512], BF16) for _ in range(2)]
    sq = [kb.sb("sq", [128, 512], BF16) for _ in range(2)]
    rms = [kb.sb("rms", [128, 512], F32) for _ in range(2)]
    qn = [kb.sb("qn", [128, 512], BF16) for _ in range(3)]
    Va = [kb.sb("Va", [128, 16, 65], BF16) for _ in range(2)]
    V0 = kb.sb("V0", [128, 16, 65], BF16)
    for v in Va + [V0]:
        kb.memset("pool", v.ap, 1.0)
    kb.memset("pool", V0[0:NPAD, :, 64:65], 0.0)
    xs = [kb.sb("xs", [128, 48], F32) for _ in range(2)]
    rb = [kb.sb("rb", [128, 16], BF16) for _ in range(2)]
    rT = [kb.sb("rT", [16, 512], BF16) for _ in range(2)]
    ti = 0
    it = 0
    for bi, (t0, n) in enumerate(self.blocks()):
        TW = n * 128
        c0 = t0 * 128
        A = aT[bi % 2]
        for jj in range(n):
            tile = t0 + jj
            ht = hts[ti % 2]
            kb.dma(ht.ap, self.H[tile * 128:(tile + 1) * 128, :])
            self.prenormT(ht, A[:, :, jj * 128:(jj + 1) * 128], abf[ti % 2], ssb[ti % 2], pst, junk, ti)
            ti += 1
        def projA(idx, it_):
            pq_ = PSq[it_ % 2]
            col_ = (idx // 8) * 1024 + (idx % 8) * 128
            for c in range(8):
                kb.mm(pq_[:, 0:TW], Win[:, c, col_:col_ + 128], A[:, c, 0:TW], start=(c == 0), stop=(c == 7))
        projA(0, it)
        for which in range(2):
            dst = self.QT if which == 0 else self.KT
            for ch in range(8):
                pq = PSq[it % 2]
                s_ = sq[it % 2]
                kb.act(s_[:, 0:TW], pq[:, 0:TW], AF.Square)
                if which * 8 + ch + 1 < 16:
                    projA(which * 8 + ch + 1, it + 1)
                kb.mm(PSs[:, 0:TW], BD.ap, s_[:, 0:TW])
                r_ = rms[it % 2]
                kb.act(r_[:, 0:TW], PSs[:, 0:TW], AF.Sqrt, bias=EPS, scale=1.0 / 64)
                kb.recip(r_[:, 0:TW], r_[:, 0:TW])
                q_ = qn[it % 3]
                kb.stt("dve", q_[:, 0:TW], pq[:, 0:TW], qg[:, which:which + 1], r_[:, 0:TW], ALU.mult, ALU.mult)
                kb.dma(dst[2 * ch, 0:64, c0:c0 + TW], q_[0:64, 0:TW], q="pool")
                kb.dma(dst[2 * ch + 1, 0:64, c0:c0 + TW], q_[64:128, 0:TW], q="pool")
                it += 1
        for ch in range(8):
            pq = PSq[it % 2]
            col = 3072 + ch * 128
            for c in range(8):
                kb.mm(pq[:, 0:TW], Win[:, c, col:col + 128], A[:, c, 0:TW], start=(c == 0), stop=(c == 7))
            q_ = qn[it % 3]
            kb.act(q_[:, 0:TW], pq[:, 0:TW], AF.Sigmoid)
            kb.dma(self.SG[ch * 128:(ch + 1) * 128, c0:c0 + TW], q_[:, 0:TW], q="pool")
            it += 1
        kb.dma(self.KT[:, 64, c0:c0 + TW], onesb[:, 0:TW], q="pool")
        for jj in range(n):
            tile = t0 + jj
            tc_ = slice(jj * 128, (jj + 1) * 128)
            for half in range(2):
                for c in range(8):
                    kb.mm(PSv[:, half * 512:(half + 1) * 512], A[:, c, tc_], Win[:, c, 2048 + half * 512:2048 + (half + 1) * 512],
                          start=(c == 0), stop=(c == 7))
            V = V0 if tile == 0 else Va[tile % 2]
            kb.copy("act", V[:, :, 0:64], PSv.ap.re("p (h d) -> p h d", h=16))
            kb.dma(self.VA[tile * 128:(tile + 1) * 128, :, :], V.ap, q="pool")
            for c in range(8):
                kb.mm(PSm[:, 0:16], A[:, c, tc_], Win[:, c, 4096:4112], start=(c == 0), stop=(c == 7))
            x = xs[tile % 2]
            kb.tt("dve", x[:, 0:16], PSm[:, 0:16], bfg.ap, ALU.add)
            kb.act(x[:, 16:32], x[:, 0:16], AF.Exp, scale=-1.0)
            kb.act(x[:, 32:48], x[:, 16:32], AF.Ln, bias=1.0)
            kb.mm(PSm[:, 16:32], tri.ap, x[:, 32:48])
            kb.mm(PSm[:, 32:48], onesf.ap, x[:, 32:48])
            kb.tt("dve", Ctok[:, tile, :], PSm[:, 16:32], CAR[:, tile, :], ALU.add)
            kb.tt("dve", CAR[:, tile + 1, :], PSm[:, 32:48], CAR[:, tile, :], ALU.add)
        R_ = rT[bi % 2]
        for jj in range(n):
            tile = t0 + jj
            r2 = rb[tile % 2]
            kb.tt("dve", r2.ap, CAR[:, t0 + n, :], Ctok[:, tile, :], ALU.subtract)
            kb.tr(PSr[:, jj * 128:(jj + 1) * 128], r2.ap, self.idb.ap)
        kb.copy("dve", R_[:, 0:TW], PSr[:, 0:TW])
        kb.dma(self.QT[:, 64, c0:c0 + TW], R_[:, 0:TW], q="pool")
    kb.dma(self.CT, Ctok.ap, q="pool")
    kb.dma(self.CAR, CAR.ap, q="pool")
    kb.end_phase()


def phase_A2(self, l):
    kb = self.kb
    NT = self.NT
    LP = self.LP
    kb.phase()
    Ctok = kb.sb("Ctok", [128, NT, 16], F32)
    CAR = kb.sb("CAR", [128, NT + 1, 16], F32)
    kb.dma(Ctok.ap, self.CT)
    kb.dma(CAR.ap, self.CAR)
    masks = []
    for jj in range(4):
        m = kb.sb("mask", [128, 512], BF16)
        kb.memset("pool", m.ap, 0.0)
        kb.asel(m.ap, m.ap, [[1, 512]], ALU.is_ge, NEGBIG, -128 * jj, -1)
        masks.append(m)
    onesf = kb.sb("onesf", [128, 64], F32)
    kb.memset("pool", onesf.ap, 1.0)
    KTh = [kb.sb("KTh", [128, LP], BF16) for _ in range(2)]
    VAh = [kb.sb("VAh", [128, NT, 65], BF16) for _ in range(2)]
    QTb = [kb.sb("QTb", [128, 512], BF16) for _ in range(2)]
    for b_ in KTh + QTb:
        kb.memset("pool", b_[64:128, :], 0.0)
    SGb = [kb.sb("SGb", [64, 512], BF16) for _ in range(2)]
    bias = [kb.sb("bias", [128, NT], F32) for _ in range(2)]
    P = [kb.sb("P", [128, 512], BF16) for _ in range(4)]
    oacc = [kb.sb("oacc", [65, 512], F32) for _ in range(2)]
    rden = [kb.sb("rden", [128, 512], F32) for _ in range(2)]
    for b_ in rden:
        kb.memset("pool", b_.ap, 0.0)
    sel = kb.sb("sel", [128, 64], F32)
    kb.memset("pool", sel.ap, 0.0)
    kb.memset("pool", sel[64:96, :], 1.0)
    kb.asel(sel[64:96, :], sel[64:96, :], [[0, 64]], ALU.is_ge, 0.0, 0, -1)
    tmp = [kb.sb("tmp", [64, 512], F32) for _ in range(2)]
    og = [kb.sb("og", [64, 512], BF16) for _ in range(2)]
    PSs = [kb.ps("pss", [128, 512], F32) for _ in range(4)]
    PSo = [kb.ps("pso", [65, 512], F32) for _ in range(2)]
    PSb = [kb.ps("psb", [64, 512], F32) for _ in range(2)]
    VAv = self.VA.rearrange("(t p) h e -> p t h e", p=128)
    blocks = self.blocks()
    it = 0
    ip = 0
    def load_head(hh):
        kb.dma(KTh[hh % 2][0:65, :], self.KT[hh])
        for a in range(0, NT, 16):
            b = min(NT, a + 16)
            kb.dma(VAh[hh % 2][:, a:b, :], VAv[:, a:b, hh, :])
    load_head(0)
    for h in range(16):
        if h + 1 < 16:
            load_head(h + 1)
        K_ = KTh[h % 2]
        V_ = VAh[h % 2]
        for bi, (t0, n) in enumerate(blocks):
            TW = n * 128
            c0 = t0 * 128
            Q_ = QTb[it % 2]
            S_ = SGb[it % 2]
            kb.dma(Q_[0:65, 0:TW], self.QT[h, :, c0:c0 + TW])
            kb.dma(S_[:, 0:TW], self.SG[h * 64:(h + 1) * 64, c0:c0 + TW])
            nkt = t0 + n
            b_ = bias[it % 2]
            kb.ts("dve", b_[:, 0:nkt], Ctok[:, 0:nkt, h], CAR[:, t0 + n, h:h + 1], ALU.subtract)
            po = PSo[it % 2]
            def emit_s(kt, ipk):
                ps = PSs[ipk % 4]
                diag = kt >= t0
                kb.mm(ps[:, 0:TW], K_[:, kt * 128:(kt + 1) * 128], Q_[:, 0:TW], start=True, stop=not diag)
                if diag:
                    kb.mm(ps[:, 0:TW], self.idb.ap, masks[kt - t0][:, 0:TW], start=False, stop=True)
                p_ = P[ipk % 4]
                kb.act(p_[:, 0:TW], ps[:, 0:TW], AF.Exp, bias=b_[:, kt:kt + 1])
            emit_s(0, ip)
            if nkt > 1:
                emit_s(1, ip + 1)
            for kt in range(nkt):
                if kt + 2 < nkt:
                    emit_s(kt + 2, ip + 2)
                kb.mm(po[:, 0:TW], V_[:, kt, :], P[ip % 4][:, 0:TW], start=(kt == 0), stop=(kt == nkt - 1))
                ip += 1
            oa = oacc[it % 2]
            rd = rden[it % 2]
            kb.copy("act", oa[:, 0:TW], po[:, 0:TW])
            kb.ts("dve", rd[64:65, 0:TW], oa[64:65, 0:TW], 1e-30, ALU.max)
            kb.recip(rd[64:65, 0:TW], rd[64:65, 0:TW])
            pb = PSb[it % 2]
            kb.mm(pb[:, 0:TW], sel.ap, rd[:, 0:TW])
            t_ = tmp[it % 2]
            kb.tt("dve", t_[:, 0:TW], oa[0:64, 0:TW], pb[:, 0:TW], ALU.mult)
            o_ = og[it % 2]
            kb.tt("pool", o_[:, 0:TW], t_[:, 0:TW], S_[:, 0:TW], ALU.mult)
            kb.dma(self.OGT[h * 64:(h + 1) * 64, c0:c0 + TW], o_[:, 0:TW], q="pool")
            it += 1
    kb.end_phase()


Prog.phase_A1 = phase_A1
Prog.phase_A2 = phase_A2


def attn_layer(self, l):
    self.phase_A1(l)
    self.phase_A2(l)
    self.phase_R(self.OGT, 8, self.inp["attn_w_out"][l // 2], self.inp["norm_mix_post"][l], self.H)


Prog.attn_layer = attn_layer


def phase_D1(self, l):
    kb = self.kb
    j = l // 2
    NT = self.NT
    kb.phase()
    Win = kb.sb("Win", [128, 8, 4112], BF16)
    stages = [kb.sb("wst", [128, 2048], F32) for _ in range(2)]
    cst = kb.sb("cst", [128, 128], F32)
    gcol = kb.sb("gcol", [128, 8], F32)
    cw = kb.sb("cw", [128, 4, 24], F32)
    pst = kb.ps("pst", [128, 1024], BF16)
    PSq = [kb.ps("psq", [128, 512], F32) for _ in range(2)]
    PSs = kb.ps("pss", [128, 512], F32)
    PSv = kb.ps("psv", [128, 1024], F32)
    PSm = kb.ps("psm", [128, 512], F32)
    self.load_cols(gcol.ap, self.inp["norm_mix_pre"][l].rearrange("(c p) -> c p", p=128), 8, cst, PSs)
    for k in range(4):
        self.load_cols(cw[:, k, :], self.inp["dn_conv"][j, k].rearrange("(c p) -> c p", p=128), 24, cst, PSs)
    self.load_w(Win, self.inp["dn_w_in"][j], 8, 4112, gcol, stages)
    dtb = kb.sb("dtb", [128, 8], F32)
    nA = kb.sb("nA", [128, 8], F32)
    kb.dma(dtb.ap, self.inp["dn_dt_bias"][j].partition_broadcast(128))
    kb.dma(nA.ap, self.inp["dn_a_log"][j].partition_broadcast(128))
    kb.act(nA.ap, nA.ap, AF.Exp)
    kb.ts("dve", nA.ap, nA.ap, -1.0, ALU.mult)
    tri = kb.sb("tri", [128, 128], F32)
    kb.memset("pool", tri.ap, 1.0)
    kb.asel(tri.ap, tri.ap, [[1, 128]], ALU.is_ge, 0.0, 0, -1)
    onesb = kb.sb("onesb", [128, 128], BF16)
    kb.memset("pool", onesb.ap, 1.0)
    halo = kb.sb("halo", [128, 24, 3], F32)
    kb.memset("pool", halo.ap, 0.0)
    GB = kb.sb("GB", [128, NT, 16], F32)
    hts = [kb.sb("ht", [128, D], F32) for _ in range(2)]
    abf = [kb.sb("abf", [128, D], BF16) for _ in range(2)]
    ssb = [kb.sb("ss", [128, 8], F32) for _ in range(2)]
    junk = kb.sb("junk", [128, D], BF16)
    aT = [kb.sb("aT", [128, 8, 512], BF16) for _ in range(2)]
    U = [kb.sb("U", [128, 515], F32) for _ in range(2)]
    yv = [kb.sb("yv", [128, 512], F32) for _ in range(2)]
    z = [kb.sb("z", [128, 512], F32) for _ in range(3)]
    sq = [kb.sb("sq", [128, 512], BF16) for _ in range(2)]
    rms = [kb.sb("rms", [128, 512], F32) for _ in range(2)]
    sg = [kb.sb("sg", [128, D], BF16) for _ in range(2)]
    xs = [kb.sb("xs", [128, 32], F32) for _ in range(2)]
    ti = 0
    it = 0
    for bi, (t0, n) in enumerate(self.blocks()):
        TW = n * 128
        c0 = t0 * 128
        A = aT[bi % 2]
        for jj in range(n):
            tile = t0 + jj
            ht = hts[ti % 2]
            kb.dma(ht.ap, self.H[tile * 128:(tile + 1) * 128, :])
            self.prenormT(ht, A[:, :, jj * 128:(jj + 1) * 128], abf[ti % 2], ssb[ti % 2], pst, junk, ti)
            ti += 1
        def stageA(fc, it_):
            pq = PSq[it_ % 2]
            for c in range(8):
                kb.mm(pq[:, 0:TW], Win[:, c, fc * 128:(fc + 1) * 128], A[:, c, 0:TW], start=(c == 0), stop=(c == 7))
            u = U[it_ % 2]
            kb.copy("pool", u[:, 0:3], halo[:, fc, :])
            kb.copy("act", u[:, 3:3 + TW], pq[:, 0:TW])
            kb.copy("pool", halo[:, fc, :], u[:, TW:TW + 3])
            y = yv[it_ % 2]
            kb.ts("dve", y[:, 0:TW], u[:, 3:3 + TW], cw[:, 3, fc:fc + 1], ALU.mult)
            for k in (2, 1, 0):
                kb.stt("dve", y[:, 0:TW], u[:, k:k + TW], cw[:, k, fc:fc + 1], y[:, 0:TW], ALU.mult, ALU.add)

        def stageB(fc, it_):
            y = yv[it_ % 2]
            z_ = z[it_ % 3]
            kb.act(z_[:, 0:TW], y[:, 0:TW], AF.Silu)
            if fc < 16:
                s_ = sq[it_ % 2]
                kb.act(s_[:, 0:TW], z_[:, 0:TW], AF.Square)
                kb.mm(PSs[:, 0:TW], onesb.ap, s_[:, 0:TW])
                r_ = rms[it_ % 2]
                kb.act(r_[:, 0:TW], PSs[:, 0:TW], AF.Sqrt, bias=EPS)
                kb.recip(r_[:, 0:TW], r_[:, 0:TW])
                if fc < 8:
                    kb.stt("dve", z_[:, 0:TW], z_[:, 0:TW], float(HD_D ** -0.5), r_[:, 0:TW], ALU.mult, ALU.mult)
                else:
                    kb.tt("dve", z_[:, 0:TW], z_[:, 0:TW], r_[:, 0:TW], ALU.mult)
            kb.dma(self.QKVD[fc, :, c0:c0 + TW], z_[:, 0:TW], q="pool")

        stageA(0, it)
        for fc in range(24):
            if fc + 1 < 24:
                stageA(fc + 1, it + 1)
            stageB(fc, it)
            it += 1
        for jj in range(n):
            tile = t0 + jj
            tc_ = slice(jj * 128, (jj + 1) * 128)
            for half in range(2):
                for c in range(8):
                    kb.mm(PSv[:, half * 512:(half + 1) * 512], A[:, c, tc_], Win[:, c, 3072 + half * 512:3072 + (half + 1) * 512],
                          start=(c == 0), stop=(c == 7))
            s2 = sg[tile % 2]
            kb.act(s2.ap, PSv.ap, AF.Silu)
            kb.dma(self.SGD[tile * 128:(tile + 1) * 128, :], s2.ap, q="pool")
            for c in range(8):
                kb.mm(PSm[:, 0:16], A[:, c, tc_], Win[:, c, 4096:4112], start=(c == 0), stop=(c == 7))
            x = xs[tile % 2]
            kb.act(GB[:, tile, 0:8], PSm[:, 0:8], AF.Sigmoid)
            kb.tt("dve", x[:, 0:8], PSm[:, 8:16], dtb.ap, ALU.add)
            kb.act(x[:, 8:16], x[:, 0:8], AF.Exp)
            kb.act(x[:, 16:24], x[:, 8:16], AF.Ln, bias=1.0)
            kb.tt("dve", x[:, 24:32], x[:, 16:24], nA.ap, ALU.mult)
            kb.mm(PSm[:, 16:24], tri.ap, x[:, 24:32])
            kb.copy("dve", GB[:, tile, 8:16], PSm[:, 16:24])
    kb.dma(self.GBD, GB.ap, q="pool")
    kb.end_phase()


def phase_D2(self, l, G=8):
    kb = self.kb
    j = l // 2
    NT = self.NT
    kb.phase()
    GB = kb.sb("GB", [128, NT, 16], F32)
    kb.dma(GB.ap, self.GBD)
    onrm = kb.sb("onrm", [128, 128], F32)
    kb.dma(onrm.ap, self.inp["dn_o_norm"][j].partition_broadcast(128))
    onesf = kb.sb("onesf", [128, 128], F32)
    kb.memset("pool", onesf.ap, 1.0)
    idf = self.idf
    idb = self.idb
    NMsT = kb.sb("NMsT", [128, 128], F32)
    kb.memset("pool", NMsT.ap, 0.0)
    kb.asel(NMsT.ap, NMsT.ap, [[1, 128]], ALU.is_gt, NEGBIG, 0, -1)
    NPs = kb.sb("NPs", [128, 128], F32)
    kb.memset("pool", NPs.ap, 0.0)
    kb.asel(NPs.ap, NPs.ap, [[-1, 128]], ALU.is_gt, -NEGBIG, 0, 1)
    BDm = kb.sb("BDm", [128, 128], F32)
    kb.memset("pool", BDm.ap, 1.0)
    kb.asel(BDm[:, 0:64], BDm[:, 0:64], [[0, 64]], ALU.is_ge, 0.0, 63, -1)
    kb.asel(BDm[:, 64:128], BDm[:, 64:128], [[0, 64]], ALU.is_ge, 0.0, -64, 1)
    ST = [kb.sb("ST", [128, 128], F32) for _ in range(8)]
    STb = [kb.sb("STb", [128, 128], BF16) for _ in range(8)]
    for s_ in ST + STb:
        kb.memset("dve", s_.ap, 0.0)
    X24 = [kb.sb("X24", [128, 24, 128], F32) for _ in range(2)]
    X24b = [kb.sb("X24b", [128, 24, 128], BF16) for _ in range(2)]
    SGt = [kb.sb("SGt", [128, D], BF16) for _ in range(2)]
    Og = [kb.sb("Og", [128, D], BF16) for _ in range(2)]
    OgT = [kb.sb("OgT", [128, 8, 128], BF16) for _ in range(2)]
    tl = [kb.sb("tl", [128, 32], F32) for _ in range(2)]
    sets = []
    f32names = ["DG", "Gs", "DTs", "DTi", "Dn", "A", "B", "Ad", "Ao", "Bd", "Bo", "P", "Pt",
                "U0s", "EGbc", "tmp", "on"]
    bf16names = ["R", "QKT", "kbg", "kd", "vb", "WT", "Us", "qdT",
                 "Adh", "Adl", "Bdh", "Bdl", "Xah", "Xal", "Xbh", "Xbl", "Xtah", "Xtal", "Xtbh", "Xtbl",
                 "Pth", "Ptl", "Ph", "Pl", "Aoh", "Aol", "Zh", "Zl"]
    for g in range(G):
        d = {}
        for nm in f32names:
            w = 256 if nm in ("DG", "Gs") else 128
            d[nm] = kb.sb(nm, [128, w], F32)
        for nm in bf16names:
            d[nm] = kb.sb(nm, [128, 128], BF16)
        d["cols"] = kb.sb("cols", [128, 8], F32)
        bA = kb.ps("bA", [128, 512], F32)
        d["bA"] = bA
        d["PSG"], d["PSK"], d["PSQ"] = kb.carve(bA, [(0, 256), (256, 384), (384, 512)])
        d["C0"], d["C1"], d["C2"], d["C3"] = kb.carve(bA, [(0, 128), (128, 256), (256, 384), (384, 512)])
        d["D0"], d["D1"], d["D2"], d["D3"] = d["C0"], d["C1"], d["C2"], d["C3"]
        d["C2b"] = View(d["C2"], d["C2"].t.bitcast(BF16)[:, 0:128])
        d["C3b"] = View(d["C3"], d["C3"].t.bitcast(BF16)[:, 0:128])
        sets.append(d)
    Xv = self.QKVD.rearrange("f d t -> d f t")
    OGTv = self.OGT.rearrange("(c p) t -> p c t", p=128)

    def head(n, hd, d, X, Xb, sgt, og, t_):
        qT = X[:, hd, :]
        qTb = Xb[:, hd, :]
        kTb = Xb[:, 8 + hd, :]
        vTb = Xb[:, 16 + hd, :]
        beta = GB[:, n, hd:hd + 1]
        gc = GB[:, n, 8 + hd:9 + hd]
        ngc = t_[:, hd:hd + 1]
        bg = t_[:, 16 + hd:17 + hd]
        cols = d["cols"]
        kb.ts("dve", d["DG"][:, 0:128], idf.ap, gc, ALU.mult)
        kb.ts("dve", d["DG"][:, 128:256], idf.ap, beta, ALU.mult)
        kb.mm(d["PSG"].ap, onesf.ap, d["DG"].ap)
        kb.copy("act", d["Gs"].ap, d["PSG"].ap)
        kb.mm(d["PSK"].ap, kTb, kTb)
        kb.mm(d["PSQ"].ap, kTb, qTb)
        yield
        G_ = d["Gs"][:, 0:128]
        kb.tt("dve", d["tmp"].ap, G_, NMsT.ap, ALU.add)
        kb.act(d["DTs"].ap, d["tmp"].ap, AF.Exp, bias=ngc)
        kb.tt("pool", d["DTi"].ap, d["DTs"].ap, idf.ap, ALU.add)
        kb.tt("dve", d["Dn"].ap, G_, NPs.ap, ALU.add)
        kb.act(d["Dn"].ap, d["Dn"].ap, AF.Exp, bias=gc, scale=-1.0)
        kb.act(d["EGbc"].ap, G_, AF.Exp)
        kb.act(cols[:, 0:1], gc, AF.Exp, bias=d["Gs"][:, 127:128], scale=-1.0)
        kb.act(cols[:, 1:2], d["Gs"][:, 127:128], AF.Exp)
        yield
        kb.stt("dve", d["A"].ap, d["PSK"].ap, beta, d["Dn"].ap, ALU.mult, ALU.mult)
        kb.tt("dve", d["B"].ap, d["PSK"].ap, d["DTs"].ap, ALU.mult)
        kb.tt("pool", d["B"].ap, d["B"].ap, d["Gs"][:, 128:256], ALU.mult)
        kb.tt("dve", d["QKT"].ap, d["PSQ"].ap, d["DTi"].ap, ALU.mult)
        kb.tt("pool", d["Ad"].ap, d["A"].ap, BDm.ap, ALU.mult)
        kb.tt("pool", d["Ao"].ap, d["A"].ap, d["Ad"].ap, ALU.subtract)
        kb.tt("pool", d["Bd"].ap, d["B"].ap, BDm.ap, ALU.mult)
        kb.tt("pool", d["Bo"].ap, d["B"].ap, d["Bd"].ap, ALU.subtract)
        kb.tt("dve", d["P"].ap, idf.ap, d["Ad"].ap, ALU.subtract)
        kb.tt("dve", d["Pt"].ap, idf.ap, d["Bd"].ap, ALU.subtract)
        yield
        def split(src, hi, lo, e="dve"):
            kb.copy("act", d[hi].ap, src)
            kb.tt(e, d[lo].ap, src, d[hi].ap, ALU.subtract)

        def mm3(out, A_, B_):
            kb.mm(out, d[A_[0]].ap, d[B_[0]].ap, start=True, stop=False)
            kb.mm(out, d[A_[0]].ap, d[B_[1]].ap, start=False, stop=False)
            kb.mm(out, d[A_[1]].ap, d[B_[0]].ap, start=False, stop=True)
        split(d["Ad"].ap, "Adh", "Adl", "pool")
        split(d["Bd"].ap, "Bdh", "Bdl", "pool")
        split(d["Pt"].ap, "Pth", "Ptl")
        split(d["Ao"].ap, "Aoh", "Aol", "pool")
        yield
        Xc, Xtc = ("Adh", "Adl"), ("Bdh", "Bdl")
        PT = ("Pth", "Ptl")
        for k in range(1, 6):
            Xn = ("Xah", "Xal") if k % 2 else ("Xbh", "Xbl")
            Xtn = ("Xtah", "Xtal") if k % 2 else ("Xtbh", "Xtbl")
            mm3(d["C0"].ap, Xtc, Xc)
            if k < 5:
                mm3(d["C1"].ap, Xc, Xtc)
            yield
            split(d["C0"].ap, Xn[0], Xn[1])
            if k < 5:
                split(d["C1"].ap, Xtn[0], Xtn[1])
            yield
            mm3(d["C2"].ap, PT, Xn)
            mm3(d["C3"].ap, Xn, PT)
            yield
            kb.tt("dve", d["P"].ap, d["P"].ap, d["C2"].ap, ALU.add)
            kb.tt("dve", d["Pt"].ap, d["Pt"].ap, d["C3"].ap, ALU.add)
            split(d["Pt"].ap, "Pth", "Ptl")
            Xc, Xtc = Xn, Xtn
            yield
        split(d["P"].ap, "Ph", "Pl", "pool")
        mm3(d["C0"].ap, ("Aoh", "Aol"), PT)
        yield
        split(d["C0"].ap, "Zh", "Zl")
        yield
        mm3(d["C1"].ap, ("Ph", "Pl"), ("Zh", "Zl"))
        kb.tt("dve", d["R"].ap, d["Pt"].ap, d["C1"].ap, ALU.subtract)
        kb.tr(d["C2b"], kTb, idb.ap)
        kb.tr(d["C3b"], vTb, idb.ap)
        kb.ts("dve", d["kbg"].ap, d["C2b"], bg, ALU.mult)
        kb.ts("dve", d["kd"].ap, d["C2b"], cols[:, 0:1], ALU.mult)
        kb.ts("dve", d["vb"].ap, d["C3b"], beta, ALU.mult)
        kb.tt("pool", d["qdT"].ap, qT, d["EGbc"].ap, ALU.mult)
        yield
        kb.mm(d["D0"].ap, d["kbg"].ap, d["R"].ap)
        kb.mm(d["D1"].ap, d["R"].ap, d["vb"].ap)
        kb.copy("act", d["WT"].ap, d["D0"].ap)
        kb.copy("act", d["U0s"].ap, d["D1"].ap)
        yield
        S_ = ST[hd]
        Sb_ = STb[hd]
        kb.mm(d["D2"].ap, d["WT"].ap, Sb_.ap)
        kb.tt("dve", d["Us"].ap, d["U0s"].ap, d["D2"].ap, ALU.subtract)
        kb.mm(d["D3"].ap, d["qdT"].ap, Sb_.ap, start=True, stop=False)
        kb.mm(d["D3"].ap, d["QKT"].ap, d["Us"].ap, start=False, stop=True)
        kb.memset("dve", cols[:, 2:3], 0.0)
        kb.act(d["tmp"].ap, d["D3"].ap, AF.Square, accum=cols[:, 2:3])
        kb.act(cols[:, 3:4], cols[:, 2:3], AF.Sqrt, bias=EPS, scale=1.0 / HD_D)
        kb.recip(cols[:, 4:5], cols[:, 3:4])
        kb.stt("dve", d["on"].ap, d["D3"].ap, cols[:, 4:5], onrm.ap, ALU.mult, ALU.mult)
        kb.mm(d["C0"].ap, d["kd"].ap, d["Us"].ap)
        kb.stt("dve", S_.ap, S_.ap, cols[:, 1:2], d["C0"].ap, ALU.mult, ALU.add)
        kb.copy("act", Sb_.ap, S_.ap)
        kb.tt("pool", og[:, hd * 128:(hd + 1) * 128], d["on"].ap, sgt[:, hd * 128:(hd + 1) * 128], ALU.mult)
        yield

    PSot = View(sets[G - 1]["bA"], sets[G - 1]["bA"].t[:].bitcast(BF16))
    HG = G // 2
    import os
    SKEW = int(os.environ.get("DBG_SKEW", "22"))

    def prologue(n):
        X = X24[n % 2]
        Xb = X24b[n % 2]
        kb.copy("act", Xb[:, 0:12, :], X[:, 0:12, :])
        kb.copy("dve", Xb[:, 12:24, :], X[:, 12:24, :])
        kb.dma(SGt[n % 2].ap, self.SGD[n * 128:(n + 1) * 128, :])
        t_ = tl[n % 2]
        kb.ts("dve", t_[:, 0:8], GB[:, n, 8:16], -1.0, ALU.mult)
        kb.act(t_[:, 8:16], GB[:, n, 8:16], AF.Exp)
        kb.tt("dve", t_[:, 16:24], t_[:, 8:16], GB[:, n, 0:8], ALU.mult)

    def epilogue(n):
        og = Og[n % 2]
        for c in range(8):
            kb.tr(PSot[:, c * 128:(c + 1) * 128], og[:, c * 128:(c + 1) * 128], idb.ap)
        ot = OgT[n % 2]
        kb.copy("act", ot.ap, PSot.re("p (c t) -> p c t", c=8))
        kb.dma(OGTv[:, :, n * 128:(n + 1) * 128], ot.ap, q="pool")
        if n + 2 < NT:
            kb.dma(X24[n % 2].ap, Xv[:, :, (n + 2) * 128:(n + 3) * 128])

    def item(n, grp):
        if grp == 0:
            prologue(n)
        gens = [head(n, grp * HG + g, sets[grp * HG + g], X24[n % 2], X24b[n % 2], SGt[n % 2], Og[n % 2], tl[n % 2])
                for g in range(HG)]
        alive = list(gens)
        while alive:
            nxt = []
            for g_ in alive:
                try:
                    next(g_)
                    nxt.append(g_)
                except StopIteration:
                    pass
            alive = nxt
            yield
        if grp == 1:
            epilogue(n)

    kb.dma(X24[0].ap, Xv[:, :, 0:128])
    if NT > 1:
        kb.dma(X24[1].ap, Xv[:, :, 128:256])
    items = [item(n, grp) for n in range(NT) for grp in range(2)]
    active = []
    nxt_item = 0
    while nxt_item < len(items) or active:
        if nxt_item < len(items) and (not active or (len(active) < 2 and active[-1][1] >= SKEW)):
            active.append([items[nxt_item], 0])
            nxt_item += 1
        still = []
        for ent in active:
            try:
                next(ent[0])
                ent[1] += 1
                still.append(ent)
            except StopIteration:
                pass
        active = still
    kb.end_phase()


Prog.phase_D1 = phase_D1
Prog.phase_D2 = phase_D2


def dn_layer(self, l):
    self.phase_D1(l)
    self.phase_D2(l)
    self.phase_R(self.OGT, 8, self.inp["dn_w_out"][l // 2], self.inp["norm_mix_post"][l], self.H)


Prog.dn_layer = dn_layer


_SEQ = 8192
_NT = (_SEQ + NMETA + NPAD) // 128


def build_full(NT):
    p = Prog(NT)
    p.consts()
    p.phase_init()
    for l in range(4):
        if l % 2 == 0:
            p.attn_layer(l)
        else:
            p.dn_layer(l)
        p.ffn(l, final=(l == 3))
    p.finish()
    return p


def kernel(**inputs):
    x = np.asarray(inputs["x"], dtype=np.float32)
    B = x.shape[0]
    meta = np.asarray(inputs["meta_tokens"], dtype=np.float32)
    p = build_full(_NT)
    shared = {k: np.ascontiguousarray(np.asarray(inputs[k], dtype=np.float32)) for k in p.inp if k != "xpad"}
    real = {0: 0, 1: 1, 4: 2, 5: 3}
    zshared = {k: np.zeros_like(v) for k, v in shared.items()}
    zx = np.zeros((NPAD + NMETA + x.shape[1], D), np.float32)
    in_maps = []
    for c in range(8):
        if c in real and real[c] < B:
            b = real[c]
            xpad = np.concatenate([np.zeros((NPAD, D), np.float32), meta, x[b]], axis=0)
            m = dict(shared)
            m["xpad"] = np.ascontiguousarray(xpad)
        else:
            m = dict(zshared)
            m["xpad"] = zx
        in_maps.append(m)
    res = run_bass_kernel_spmd(p.nc, in_maps, core_ids=list(range(8)))
    core_of = {b: c for c, b in real.items()}
    out = np.stack([np.asarray(res.results[core_of[b]]["y"], dtype=np.float32) for b in range(B)], axis=0)
    return out
```
